# Optimizing a Trainium2 kernel written in Bass

```python
import math
import jax
import jax.numpy as jnp
from jax import lax
import numpy as np

D_MODEL = 1024
BATCH = 4
SEQ = 4096
DEPTH = 2
DEC_BATCH = 128
DEC_SEQ = 1
PAST_LEN = 8192
PAGE_SIZE = 128

GDN_HEADS = 8
GDN_DK = 128
GDN_DV = 128
GDN_QK_W = GDN_HEADS * GDN_DK
GDN_V_W = GDN_HEADS * GDN_DV
CONV_W = 4
CONV_CH = 2 * GDN_QK_W + GDN_V_W
CHUNK = 64
N_Q_HEADS = 16
N_KV_HEADS = 4
HEAD_DIM = 64
Q_GROUP = N_Q_HEADS // N_KV_HEADS
ATT_W = N_Q_HEADS * HEAD_DIM
KV_W = N_KV_HEADS * HEAD_DIM
WINDOW = 128
BLOCK = 128
EPS = 1e-6

kernel_name = 'yoco_gdn_swa_sink_step'


def rmsnorm(x, g):
    xf = x.astype(jnp.float32)
    y = xf * lax.rsqrt(jnp.mean(xf * xf, axis=-1, keepdims=True) + EPS)
    return (y * g.astype(jnp.float32)).astype(x.dtype)


def l2norm(x):
    xf = x.astype(jnp.float32)
    return xf * lax.rsqrt(jnp.sum(xf * xf, axis=-1, keepdims=True) + EPS)


def alibi_slopes(n):
    return jnp.exp2(-8.0 * jnp.arange(1, n + 1, dtype=jnp.float32) / n)


def causal_conv_silu(u, buf, w):
    L = u.shape[1]
    full = jnp.concatenate([buf.astype(u.dtype), u], axis=1)
    out = sum(full[:, i:i + L] * w[i] for i in range(CONV_W))
    return jax.nn.silu(out), full[:, L:]


def _to_chunks(a, n, c):
    a = a.reshape((a.shape[0], n, c) + a.shape[2:])
    return jnp.moveaxis(a, 3, 2)


def gated_delta_rule(q, k, v, g, beta, s0):
    b, L, h, dk = q.shape
    dv = v.shape[-1]
    c = min(CHUNK, L)
    pad = (-L) % c
    if pad:
        q, k, v, g, beta = [jnp.pad(a, [(0, 0), (0, pad)] + [(0, 0)] * (a.ndim - 2)) for a in (q, k, v, g, beta)]
    n = (L + pad) // c
    qc, kc, vc, gc, bc = [_to_chunks(a, n, c) for a in (q, k, v, g, beta)]
    G = jnp.cumsum(gc, axis=-1)
    idx = jnp.arange(c)
    causal = idx[:, None] >= idx[None, :]
    strict = idx[:, None] > idx[None, :]
    decay = jnp.exp(jnp.where(causal, G[..., :, None] - G[..., None, :], -jnp.inf))
    kb = kc * bc[..., None]
    m = jnp.where(strict, jnp.einsum('bnhid,bnhjd->bnhij', kb, kc) * decay, 0.0)
    eye = jnp.eye(c, dtype=jnp.float32)
    rhs = jnp.concatenate([vc * bc[..., None], kb * jnp.exp(G)[..., None]], axis=-1)
    sol = lax.linalg.triangular_solve(eye + m, rhs, left_side=True, lower=True, unit_diagonal=True)
    u_val, w_k = sol[..., :dv], sol[..., dv:]
    qk = jnp.where(causal, jnp.einsum('bnhid,bnhjd->bnhij', qc, kc) * decay, 0.0)
    q_dec = qc * jnp.exp(G)[..., None]
    k_dec = kc * jnp.exp(G[..., -1:] - G)[..., None]
    g_tot = jnp.exp(G[..., -1])

    def step(s, xs):
        u_i, w_i, qk_i, qd_i, kd_i, gt_i = xs
        v_new = u_i - jnp.einsum('bhcd,bhde->bhce', w_i, s)
        o_i = jnp.einsum('bhcd,bhde->bhce', qd_i, s) + jnp.einsum('bhij,bhje->bhie', qk_i, v_new)
        s = s * gt_i[..., None, None] + jnp.einsum('bhcd,bhce->bhde', kd_i, v_new)
        return s, o_i

    xs = tuple(jnp.moveaxis(a, 1, 0) for a in (u_val, w_k, qk, q_dec, k_dec, g_tot))
    s_fin, o = lax.scan(step, s0, xs)
    o = jnp.moveaxis(o, (0, 2), (1, 3)).reshape(b, n * c, h, dv)[:, :L]
    return o, s_fin


def gdn_layer(x, conv_buf, s0, norm_g, w_in, conv_w, a_log, dt_bias, o_norm, w_out):
    b, L, _ = x.shape
    proj = rmsnorm(x, norm_g) @ w_in
    qkv, gate, a, beta_logit = jnp.split(proj, [CONV_CH, CONV_CH + GDN_V_W, CONV_CH + GDN_V_W + GDN_HEADS], axis=-1)
    qkv, new_buf = causal_conv_silu(qkv, conv_buf, conv_w)
    q, k, v = jnp.split(qkv, [GDN_QK_W, 2 * GDN_QK_W], axis=-1)
    q = l2norm(q.reshape(b, L, GDN_HEADS, GDN_DK)) * (GDN_DK ** -0.5)
    k = l2norm(k.reshape(b, L, GDN_HEADS, GDN_DK))
    v = v.reshape(b, L, GDN_HEADS, GDN_DV).astype(jnp.float32)
    beta = jax.nn.sigmoid(beta_logit.astype(jnp.float32))
    g = -jnp.exp(a_log.astype(jnp.float32)) * jax.nn.softplus(a.astype(jnp.float32) + dt_bias.astype(jnp.float32))
    o, s_fin = gated_delta_rule(q, k, v, g, beta, s0.astype(jnp.float32))
    o = rmsnorm(o, o_norm) * jax.nn.silu(gate.reshape(b, L, GDN_HEADS, GDN_DV).astype(jnp.float32))
    y = o.reshape(b, L, GDN_V_W).astype(x.dtype) @ w_out
    return x + y, new_buf, s_fin.astype(s0.dtype)


def shared_kv(h, kv_norm, w_kv, k_norm):
    b, L, _ = h.shape
    k, v = jnp.split(rmsnorm(h, kv_norm) @ w_kv, 2, axis=-1)
    k = rmsnorm(k.reshape(b, L, N_KV_HEADS, HEAD_DIM), k_norm)
    return k, v.reshape(b, L, N_KV_HEADS, HEAD_DIM)


def band_blocks(t, nb):
    tb = t.reshape((t.shape[0], nb, BLOCK) + t.shape[2:])
    prev = jnp.concatenate([jnp.zeros_like(tb[:, :1]), tb[:, :-1]], axis=1)
    return jnp.concatenate([prev, tb], axis=2)


def windowed_sink_attention(q, k, v, dist, valid, sinks):
    slopes = alibi_slopes(N_Q_HEADS).reshape(N_KV_HEADS, Q_GROUP, 1, 1)
    s = jnp.einsum('bnqhgd,bnshd->bnhgqs', q, k, preferred_element_type=jnp.float32) * (HEAD_DIM ** -0.5)
    s = jnp.where(valid[None, :, None, None], s - slopes * dist.astype(jnp.float32), -jnp.inf)
    sink = sinks.astype(jnp.float32).reshape(1, 1, N_KV_HEADS, Q_GROUP, 1, 1)
    mx = jnp.maximum(jnp.max(s, axis=-1, keepdims=True), sink)
    p = jnp.exp(s - mx)
    p = p / (jnp.sum(p, axis=-1, keepdims=True) + jnp.exp(sink - mx))
    return jnp.einsum('bnhgqs,bnshd->bnqhgd', p.astype(v.dtype), v)


def swa_layer(x, kb, vb, dist, valid, norm_g, w_in, q_norm, sinks, w_out):
    b, L, _ = x.shape
    n_blk = kb.shape[1]
    q, gate = jnp.split(rmsnorm(x, norm_g) @ w_in, 2, axis=-1)
    q = rmsnorm(q.reshape(b, L, N_Q_HEADS, HEAD_DIM), q_norm)
    q = q.reshape(b, n_blk, L // n_blk, N_KV_HEADS, Q_GROUP, HEAD_DIM)
    o = windowed_sink_attention(q, kb, vb, dist, valid, sinks).reshape(b, L, ATT_W)
    o = o * jax.nn.silu(gate)
    return x + o @ w_out


def setup_inputs(seed: int = 0) -> dict:
    key = jax.random.key(seed)
    ks = jax.random.split(key, 24)
    n_a = DEPTH // 2
    n_b = DEPTH - n_a
    f32 = jnp.float32

    def nrm(k, shape, scale):
        return jax.random.normal(k, shape, f32) * scale

    def gain(k, shape):
        return 1.0 + 0.05 * jax.random.normal(k, shape, f32)

    in_a = CONV_CH + GDN_V_W + 2 * GDN_HEADS
    dt = jnp.exp(jax.random.uniform(ks[9], (n_a, GDN_HEADS), f32, math.log(1e-3), math.log(1e-1)))
    return {
        'x_prompt': nrm(ks[0], (BATCH, SEQ, D_MODEL), 1.0),
        'x_sample': nrm(ks[1], (DEC_BATCH, DEC_SEQ, D_MODEL), 1.0),
        'state_conv': nrm(ks[2], (n_a, DEC_BATCH, CONV_W - 1, CONV_CH), 1.0),
        'state_ssm': nrm(ks[3], (n_a, DEC_BATCH, GDN_HEADS, GDN_DK, GDN_DV), GDN_DK ** -0.5),
        'cache_k_win': nrm(ks[4], (DEC_BATCH, WINDOW, N_KV_HEADS, HEAD_DIM), 1.0),
        'cache_v_win': nrm(ks[5], (DEC_BATCH, WINDOW, N_KV_HEADS, HEAD_DIM), 1.0),
        'norm_a': gain(ks[6], (n_a, D_MODEL)),
        'w_in_a': nrm(ks[7], (n_a, D_MODEL, in_a), D_MODEL ** -0.5),
        'conv_w_a': nrm(ks[8], (n_a, CONV_W, CONV_CH), CONV_W ** -0.5),
        'a_log': jnp.log(jax.random.uniform(ks[10], (n_a, GDN_HEADS), f32, 1.0, 16.0)),
        'dt_bias': dt + jnp.log(-jnp.expm1(-dt)),
        'o_norm_a': gain(ks[11], (n_a, GDN_DV)),
        'w_out_a': nrm(ks[12], (n_a, GDN_V_W, D_MODEL), GDN_V_W ** -0.5),
        'kv_norm': gain(ks[13], (D_MODEL,)),
        'w_kv': nrm(ks[14], (D_MODEL, 2 * KV_W), D_MODEL ** -0.5),
        'k_norm': gain(ks[15], (HEAD_DIM,)),
        'norm_b': gain(ks[16], (n_b, D_MODEL)),
        'w_in_b': nrm(ks[17], (n_b, D_MODEL, 2 * ATT_W), D_MODEL ** -0.5),
        'q_norm': gain(ks[18], (n_b, HEAD_DIM)),
        'sinks': nrm(ks[19], (n_b, N_Q_HEADS), 0.5),
        'w_out_b': nrm(ks[20], (n_b, ATT_W, D_MODEL), ATT_W ** -0.5),
    }


def reference(x_prompt, x_sample, state_conv, state_ssm, cache_k_win, cache_v_win,
              norm_a, w_in_a, conv_w_a, a_log, dt_bias, o_norm_a, w_out_a,
              kv_norm, w_kv, k_norm, norm_b, w_in_b, q_norm, sinks, w_out_b):
    n_a = DEPTH // 2
    bp, lp, _ = x_prompt.shape
    ls = x_sample.shape[1]
    nb = lp // BLOCK
    qi = jnp.arange(BLOCK)[:, None]
    kj = jnp.arange(2 * BLOCK)[None, :]
    dist_p = qi - kj + BLOCK
    valid_p = (dist_p >= 0) & (dist_p <= WINDOW) & ((jnp.arange(nb)[:, None, None] > 0) | (kj >= BLOCK))
    dist_s = jnp.arange(ls)[:, None] - jnp.arange(-WINDOW, ls)[None, :]
    valid_s = ((dist_s >= 0) & (dist_s <= WINDOW))[None]

    hp, hs = x_prompt, x_sample
    conv_p, ssm_p, conv_s, ssm_s = [], [], [], []
    for layer in range(DEPTH):
        if layer < n_a:
            wa = (norm_a[layer], w_in_a[layer], conv_w_a[layer], a_log[layer], dt_bias[layer],
                  o_norm_a[layer], w_out_a[layer])
            hp, cbuf, st = gdn_layer(hp, jnp.zeros((bp, CONV_W - 1, CONV_CH), hp.dtype),
                                     jnp.zeros((bp,) + state_ssm.shape[2:], state_ssm.dtype), *wa)
            conv_p.append(cbuf)
            ssm_p.append(st)
            hs, cbuf, st = gdn_layer(hs, state_conv[layer], state_ssm[layer], *wa)
            conv_s.append(cbuf)
            ssm_s.append(st)
        else:
            if layer == n_a:
                kp, vp = shared_kv(hp, kv_norm, w_kv, k_norm)
                kn, vn = shared_kv(hs, kv_norm, w_kv, k_norm)
                k_win_p, v_win_p = kp[:, -WINDOW:], vp[:, -WINDOW:]
                k_all_s = jnp.concatenate([cache_k_win.astype(kn.dtype), kn], axis=1)
                v_all_s = jnp.concatenate([cache_v_win.astype(vn.dtype), vn], axis=1)
                k_win_s, v_win_s = k_all_s[:, -WINDOW:], v_all_s[:, -WINDOW:]
                k_band, v_band = band_blocks(kp, nb), band_blocks(vp, nb)
                k_s_blk, v_s_blk = k_all_s[:, None], v_all_s[:, None]
            j = layer - n_a
            wb = (norm_b[j], w_in_b[j], q_norm[j], sinks[j], w_out_b[j])
            hp = swa_layer(hp, k_band, v_band, dist_p, valid_p, *wb)
            hs = swa_layer(hs, k_s_blk, v_s_blk, dist_s, valid_s, *wb)
    conv_prompt, ssm_prompt = jnp.stack(conv_p), jnp.stack(ssm_p)
    conv_sample, ssm_sample = jnp.stack(conv_s), jnp.stack(ssm_s)
    return (hp, hs, conv_prompt, ssm_prompt, k_win_p, v_win_p, conv_sample, ssm_sample, k_win_s, v_win_s)
```

```python
import contextlib
import numpy as np
import concourse.bass as bass
import concourse.mybir as mybir

F32 = mybir.dt.float32
BF16 = mybir.dt.bfloat16
I32 = mybir.dt.int32
AF = mybir.ActivationFunctionType
ALU = mybir.AluOpType


class Buf:
    def __init__(self, t, name):
        self.t = t
        self.name = name
        self.w = None
        self.r = []
        self.dkey = None

    def __getitem__(self, k):
        return self.t[k]


TABLE_AWARE = False


class _Rec:
    def __init__(self):
        self.call = None

    def __getattr__(self, name):
        def f(*a, **k):
            self.call = (name, a, k)
            return self
        return f


def _fsize(ap):
    try:
        return int(ap.free_size())
    except Exception:
        return 128


def _nbytes(ap):
    try:
        return int(ap.nbytes())
    except Exception:
        return 65536


class KB:
    DEFER = True

    def __init__(self):
        self.nc = bass.Bass("TRN2", target_bir_lowering=False)
        nc = self.nc
        self.es = contextlib.ExitStack()
        self.eng = {"pe": nc.tensor, "act": nc.scalar, "dve": nc.vector,
                    "pool": nc.gpsimd, "sp": nc.sync}
        self.semh = {}
        self.cnt = {}
        self.seen = {e: {} for e in self.eng}
        for e in ("pe", "act", "dve", "pool"):
            self.semh[e] = self.es.enter_context(nc.semaphore("s_" + e))
            self.cnt[e] = 0
        self.nbuf = 0
        self.pend = []
        self.scope = contextlib.ExitStack()
        self.scopes = []
        self.dma_keys = []
        self.nins = 0

    def sb(self, shape, dt, name=None):
        self.nbuf += 1
        name = f"sb{self.nbuf}_" + (name or "b")
        t = self.scope.enter_context(self.nc.sbuf_tensor(name, list(shape), dt))
        return Buf(t, name)

    def push(self):
        self.scopes.append(self.scope)
        self.scope = contextlib.ExitStack()

    def pop(self):
        self.barrier()
        self.scope.close()
        self.scope = self.scopes.pop()

    def new_scope(self):
        self.barrier()
        self.scope.close()
        self.scope = contextlib.ExitStack()

    def ps(self, shape, dt, name=None):
        self.nbuf += 1
        name = f"ps{self.nbuf}_" + (name or "p")
        t = self.es.enter_context(self.nc.psum_tensor(name, list(shape), dt))
        return Buf(t, name)

    def dram(self, name, shape, dt, kind):
        t = self.nc.dram_tensor(name, list(shape), dt, kind=kind)
        return Buf(t.ap(), name)

    def _deps(self, R, W):
        need = {}
        for b in R:
            if b.w is not None:
                k, c = b.w
                need[k] = max(need.get(k, 0), c)
        for b in W:
            if b.w is not None:
                k, c = b.w
                need[k] = max(need.get(k, 0), c)
            for (k, c) in b.r:
                need[k] = max(need.get(k, 0), c)
        return need

    def _wait(self, e, need):
        E = self.eng[e]
        seen = self.seen[e]
        for k, c in need.items():
            if k == e and e == "pe":
                continue
            if seen.get(k, 0) >= c:
                continue
            E.wait_ge(self.semh[k], c)
            seen[k] = c

    def _mark(self, tok, R, W):
        for b in R:
            b.r.append(tok)
            if len(b.r) > 64:
                d = {}
                for k, c in b.r:
                    d[k] = max(d.get(k, 0), c)
                b.r = list(d.items())
        for b in W:
            b.w = tok
            b.r = []

    def op(self, e, fn, R=(), W=()):
        if self.DEFER:
            rec = _Rec()
            fn(rec)
            name, a, k = rec.call
            out = k.get("out", a[0] if a else None)
            n = _fsize(out) if out is not None else 128
            if e == "pe":
                if name == "transpose":
                    c = 0.07
                else:
                    c = 0.03 + max(n, 64) / 2400.0
                    l = k.get("lhsT")
                    if l is not None and l.dtype == F32:
                        c *= 4
            elif e == "dve":
                c = 0.06 + n / 960.0
            elif e == "act":
                c = 0.2 + n / 1200.0
            else:
                c = 0.1 + n / 480.0
            tb = 0
            if e == "act":
                fnc = k.get("func")
                if fnc == AF.Ln:
                    tb = 1
                elif fnc == AF.Tanh:
                    tb = 2
            self.pend.append(("op", e, rec.call, tuple(R), tuple(W), c, c, tb))
            return None
        return self._op_now(e, fn, R, W)

    def _op_now(self, e, fn, R=(), W=()):
        self._wait(e, self._deps(R, W))
        ins = fn(self.eng[e])
        self.cnt[e] += 1
        ins.then_inc(self.semh[e], 1)
        self._mark((e, self.cnt[e]), R, W)
        self.nins += 1
        return ins

    def dma(self, q, out, in_, R=(), W=(), key=None, **kw):
        if self.DEFER:
            lat = 1.5 + _nbytes(out) / 150000.0
            W = tuple(W) if any(b is key for b in W) else tuple(W) + (key,)
            self.pend.append(("dma", q, (out, in_, key, kw), tuple(R), W, 0.06, lat))
            return None
        return self._dma_now(q, out, in_, R, W, key, **kw)

    def _dma_now(self, q, out, in_, R=(), W=(), key=None, **kw):
        if key.dkey is None:
            key.dkey = "d_" + key.name
            self.semh[key.dkey] = self.es.enter_context(self.nc.semaphore(key.dkey))
            self.cnt[key.dkey] = 0
            self.dma_keys.append(key.dkey)
        self._wait(q, self._deps(R, W))
        ins = self.eng[q].dma_start(out=out, in_=in_, **kw)
        self.cnt[key.dkey] += 16
        ins.then_inc(self.semh[key.dkey], 16)
        self._mark((key.dkey, self.cnt[key.dkey]), R, W)
        self.nins += 1
        return ins

    def dma_multi(self, q, pairs, R=(), W=(), key=None):
        if self.DEFER:
            nb = sum(_nbytes(o) for o, _ in pairs)
            W = tuple(W) if any(b is key for b in W) else tuple(W) + (key,)
            self.pend.append(("dmam", q, (list(pairs), key), tuple(R), W, 0.06 * len(pairs), 1.5 + nb / 150000.0))
            return None
        return self._dma_multi_now(q, pairs, R, W, key)

    def _dma_multi_now(self, q, pairs, R=(), W=(), key=None):
        if key.dkey is None:
            key.dkey = "d_" + key.name
            self.semh[key.dkey] = self.es.enter_context(self.nc.semaphore(key.dkey))
            self.cnt[key.dkey] = 0
            self.dma_keys.append(key.dkey)
        self._wait(q, self._deps(R, W))
        for (out, in_) in pairs:
            ins = self.eng[q].dma_start(out=out, in_=in_)
            self.cnt[key.dkey] += 16
            ins.then_inc(self.semh[key.dkey], 16)
            self.nins += 1
        self._mark((key.dkey, self.cnt[key.dkey]), R, W)

    def ind_dma(self, out, in_, idx_ap, nrows, R=(), W=(), key=None):
        q = "pool"
        if key.dkey is None:
            key.dkey = "d_" + key.name
            self.semh[key.dkey] = self.es.enter_context(self.nc.semaphore(key.dkey))
            self.cnt[key.dkey] = 0
            self.dma_keys.append(key.dkey)
        self._wait(q, self._deps(R, W))
        ins = self.eng[q].indirect_dma_start(out=out, out_offset=None, in_=in_,
                                             in_offset=bass.IndirectOffsetOnAxis(ap=idx_ap, axis=0),
                                             bounds_check=nrows - 1, oob_is_err=False)
        self.cnt[key.dkey] += 16
        ins.then_inc(self.semh[key.dkey], 16)
        self._mark((key.dkey, self.cnt[key.dkey]), R, W)
        self.nins += 1

    def ind_dma_multi(self, items, in_, nrows, R=(), W=(), key=None):
        if self.DEFER:
            nb = sum(_nbytes(o) for o, _, _ in items)
            W = tuple(W) if any(b is key for b in W) else tuple(W) + (key,)
            self.pend.append(("indm", "pool", (list(items), nrows, key), tuple(R), W, 1.0 * len(items), 3.0 + nb / 150000.0))
            return None
        return self._ind_dma_multi_now(items, in_, nrows, R, W, key)

    def _ind_dma_multi_now(self, items, in_, nrows, R=(), W=(), key=None):
        q = "pool"
        if key.dkey is None:
            key.dkey = "d_" + key.name
            self.semh[key.dkey] = self.es.enter_context(self.nc.semaphore(key.dkey))
            self.cnt[key.dkey] = 0
            self.dma_keys.append(key.dkey)
        self._wait(q, self._deps(R, W))
        for (out, idx_ap, src) in items:
            ins = self.eng[q].indirect_dma_start(out=out, out_offset=None, in_=src,
                                                 in_offset=bass.IndirectOffsetOnAxis(ap=idx_ap, axis=0),
                                                 bounds_check=nrows - 1, oob_is_err=False)
            self.cnt[key.dkey] += 16
            ins.then_inc(self.semh[key.dkey], 16)
            self.nins += 1
        self._mark((key.dkey, self.cnt[key.dkey]), R, W)

    def all_gather(self, in_buf, out_buf, groups):
        key = "cc_" + out_buf.name
        self.semh[key] = self.es.enter_context(self.nc.semaphore(key))
        self.cnt[key] = 0
        self.dma_keys.append(key)
        self._wait("pool", self._deps([in_buf], [out_buf]))
        ins = self.eng["pool"].collective_compute("AllGather", ALU.bypass, replica_groups=groups,
                                                  ins=[in_buf.t], outs=[out_buf.t])
        self.cnt[key] += 1
        ins.then_inc(self.semh[key], 1)
        self._mark((key, 1), [in_buf], [out_buf])
        self.nins += 1

    def flush(self):
        P = self.pend
        self.pend = []
        M = len(P)
        if M == 0:
            return
        lastw = {}
        readers = {}
        deps = [None] * M
        succ = [[] for _ in range(M)]
        for j, rec_ in enumerate(P):
            e, R, W = rec_[1], rec_[3], rec_[4]
            d = set()
            for b in R:
                i = lastw.get(id(b))
                if i is not None:
                    d.add(i)
            for b in W:
                i = lastw.get(id(b))
                if i is not None:
                    d.add(i)
                for i in readers.get(id(b), ()):
                    d.add(i)
            d.discard(j)
            deps[j] = d
            for i in d:
                succ[i].append(j)
            for b in R:
                readers.setdefault(id(b), []).append(j)
            for b in W:
                lastw[id(b)] = j
                readers[id(b)] = []
        tail = [0.0] * M
        for j in range(M - 1, -1, -1):
            t = 0.0
            for k in succ[j]:
                if tail[k] > t:
                    t = tail[k]
            tail[j] = t + P[j][6]
        ndep = [len(d) for d in deps]
        dr = [0.0] * M
        fin = [0.0] * M
        free = {}
        cand = [j for j in range(M) if ndep[j] == 0]
        order = []
        LOOK = 96
        acttab = 0
        import heapq
        heapq.heapify(cand)
        pool = []
        while cand or pool:
            while cand and len(pool) < LOOK:
                pool.append(heapq.heappop(cand))
            best = None
            bk = None
            for j in pool:
                e = P[j][1]
                st = dr[j]
                f = free.get(e, 0.0)
                if f > st:
                    st = f
                if TABLE_AWARE and e == "act" and len(P[j]) > 7 and P[j][7] and P[j][7] != acttab:
                    st += 1.3
                key = (round(st, 2), -tail[j], j)
                if bk is None or key < bk:
                    bk = key
                    best = j
            pool.remove(best)
            j = best
            e = P[j][1]
            st = max(dr[j], free.get(e, 0.0))
            if TABLE_AWARE and e == "act" and len(P[j]) > 7 and P[j][7]:
                if P[j][7] != acttab:
                    st += 1.3
                acttab = P[j][7]
            free[e] = st + P[j][5]
            fin[j] = st + P[j][6]
            order.append(j)
            for k in succ[j]:
                t = fin[j] + (0.05 if (P[k][1] == e and P[j][0] == "op") else 0.5)
                if t > dr[k]:
                    dr[k] = t
                ndep[k] -= 1
                if ndep[k] == 0:
                    heapq.heappush(cand, k)
        self.est = getattr(self, "est", 0.0) + max(fin) if fin else 0.0
        sv = self.DEFER
        self.DEFER = False
        try:
            for j in order:
                kind, e, pay, R, W = P[j][:5]
                if kind == "op":
                    name, a, k = pay
                    self._op_now(e, lambda eng: getattr(eng, name)(*a, **k), R, W)
                elif kind == "dma":
                    out, in_, key, kw = pay
                    self._dma_now(e, out, in_, R, W, key, **kw)
                elif kind == "dmam":
                    pairs, key = pay
                    self._dma_multi_now(e, pairs, R, W, key)
                else:
                    items, nrows, key = pay
                    self._ind_dma_multi_now(items, None, nrows, R, W, key)
        finally:
            self.DEFER = sv

    def barrier(self):
        self.flush()
        for e in self.eng:
            need = {k: c for k, c in self.cnt.items() if c > 0 and k != e}
            self._wait(e, need)

    def finish(self):
        self.flush()
        need = {k: self.cnt[k] for k in self.dma_keys if self.cnt[k] > 0}
        self._wait("sp", need)


import numpy as np

NH = 4
DK = 128
EPS = 1e-6
NLEV = 7


def host_consts():
    c = {}
    c["ident"] = np.eye(128, dtype=np.float32)
    i = np.arange(128)
    c["U"] = (i[:, None] <= i[None, :]).astype(np.float32)
    mi = (i[None, :] >= i[:, None]).astype(np.float32)
    ms = np.where(i[None, :] > i[:, None], 0.0, -30000.0).astype(np.float32)
    c["maskUi"] = np.tile(mi[:, None, :], (1, NH, 1)).copy()
    c["maskUs"] = np.tile(ms[:, None, :], (1, NH, 1)).copy()
    lm = np.zeros((128, NLEV, NH, 128), np.float32)
    for l in range(NLEV):
        s = 1 << l
        bi = i // s
        m = ((bi[:, None] % 2 == 1) & (bi[None, :] == bi[:, None] - 1)).astype(np.float32)
        lm[:, l, :, :] = m[:, None, :]
    c["lmask"] = lm
    c["negm"] = np.where(i[None, :] >= i[:, None], 0.0, -30000.0).astype(np.float32)
    return c


def phase_a(kb, x_d, wa_d, na_d, cw_d, alog_d, dtb_d, ong_d, cst, o_scr, ssm_d, convo_d, NSB, psT, psF, row0=0, smp=None, col0=0):
    nc = kb.nc
    NF = 4 * NH
    NCOL = NF * 128 + 2 * NH
    SBT = 512

    ident_f = kb.sb([128, 128], F32, "ident_f")
    ident_b = kb.sb([128, 128], BF16, "ident_b")
    ones_b = kb.sb([128, 128], BF16, "ones_b")
    ones_f = kb.sb([128, 128], F32, "ones_f")
    U_f = kb.sb([128, 128], F32, "U_f")
    mUs = kb.sb([128, NH, 128], F32, "mUs")
    lmask = kb.sb([128, NLEV, NH, 128], BF16, "lmask")
    cbias = kb.sb([128, 8], F32, "cbias")
    na = kb.sb([128, 8], F32, "na")
    cw = kb.sb([128, 3 * NH, 4], F32, "cw")
    negA = kb.sb([128, NH], F32, "negA")
    dtb = kb.sb([128, NH], F32, "dtb")
    ong = kb.sb([128, 1], F32, "ong")
    cload = kb.sb([128, 1], F32, "cload")
    negm_f = kb.sb([128, 128], F32, "negm_f")
    negm = kb.sb([128, 128], BF16, "negm")
    negms = kb.sb([128, 128], BF16, "negms")
    identb4 = kb.sb([128, NH, 128], BF16, "identb4")

    lds = []

    def ld(dst, src):
        lds.append((dst, src))
    ld(ident_f, cst["ident"][:, :]); ld(U_f, cst["U"][:, :])
    ld(mUs, cst["maskUs"][:, :, :])
    ld(na, na_d[:, :]); ld(cw, cw_d[:, :, :]); ld(negA, alog_d[:, :]); ld(dtb, dtb_d[:, :])
    ld(ong, ong_d[:, :]); ld(negm_f, cst["negm"][:, :])
    kb.dma_multi("sp", [(d_[:], s_) for d_, s_ in lds], W=[d_ for d_, _ in lds], key=cload)
    kb.op("dve", lambda e: e.tensor_copy(ident_b[:], ident_f[:]), R=[ident_f], W=[ident_b])
    kb.op("pool", lambda e: e.memset(ones_b[:], 1.0), W=[ones_b])
    kb.op("dve", lambda e: e.tensor_copy(negm[:], negm_f[:]), R=[negm_f], W=[negm])
    U_b = kb.sb([128, 128], BF16, "U_b")
    kb.op("dve", lambda e: e.tensor_copy(U_b[:], U_f[:]), R=[U_f], W=[U_b])
    kb.op("dve", lambda e: e.tensor_copy(negms[:], mUs[:, 0, :]), R=[mUs], W=[negms])
    for h in range(NH):
        kb.op("dve", lambda e, h=h: e.tensor_copy(identb4[:, h, :], ident_f[:]), R=[ident_f], W=[identb4])
    kb.op("pool", lambda e: e.memset(ones_f[:], 1.0), W=[ones_f])
    kb.push()
    lmask_f = kb.sb([128, NLEV, NH, 128], F32, "lmask_f")
    kb.dma("sp", lmask_f[:], cst["lmask"][:, :, :, :], W=[lmask_f], key=lmask_f)
    kb.op("dve", lambda e: e.tensor_copy(lmask[:], lmask_f[:]), R=[lmask_f], W=[lmask])
    kb.pop()
    for j, v in enumerate([4 * EPS, 4 * EPS * 128, EPS, 1.0, 0.0]):
        kb.op("pool", lambda e, j=j, v=v: e.memset(cbias[:, j:j + 1], v), W=[cbias])
    kb.op("act", lambda e: e.activation(out=negA[:], in_=negA[:], func=AF.Exp), R=[negA], W=[negA])
    kb.op("dve", lambda e: e.tensor_scalar(negA[:], negA[:], -1.0, None, op0=ALU.mult), R=[negA], W=[negA])
    kb.op("dve", lambda e: e.tensor_scalar(ong[:], ong[:], 0.5, None, op0=ALU.mult), R=[ong], W=[ong])

    W = kb.sb([128, 8, NCOL], BF16, "Wa")
    kb.push()
    stg = [kb.sb([128, NCOL], F32, f"stg{i}") for i in range(2)]
    wa_v = wa_d.rearrange("(kc p) n -> p kc n", p=128)
    for kc in range(8):
        s = stg[kc % 2]
        kb.dma(("sp", "act")[kc % 2], s[:], wa_v[:, kc, :], W=[s], key=s)
        eng = "act" if kc % 2 == 0 else "dve"
        if eng == "act":
            kb.op("act", lambda e, kc=kc, s=s: e.activation(out=W[:, kc, :], in_=s[:], func=AF.Copy,
                                                         scale=na[:, kc:kc + 1]), R=[s, na], W=[W])
        else:
            kb.op("dve", lambda e, kc=kc, s=s: e.tensor_scalar(W[:, kc, :], s[:], na[:, kc:kc + 1], None,
                                                            op0=ALU.mult), R=[s, na], W=[W])

    pfi = [0]

    def PS(ring=0):
        p = psF[pfi[0] % len(psF)]
        pfi[0] += 1
        return p

    if smp is not None:
        zt = kb.sb([128, NH, 128], BF16, "zpad")
        kb.op("pool", lambda e: e.memset(zt[:], 0.0), W=[zt])
        kb.dma_multi("sp", [(dst, zt[:]) for dst in o_scr.loc(0)] + [(o_scr.sloc(g_), zt[:]) for g_ in range(2)],
                     R=[zt], W=[o_scr], key=zt)
        sample_a(kb, smp, locals(), PS, psT, row0)
    kb.pop()

    xt = [kb.sb([128, 1024], F32, f"xt{i}") for i in range(3)]
    xs = [kb.sb([128, 1024], BF16, f"xs{i}") for i in range(2)]
    rr = kb.sb([128, 4], F32, "rr")
    junk = kb.sb([128, 1024], BF16, "junk")
    xsT_1 = kb.sb([128, 8, SBT], BF16, "xsT")
    xsT_2 = [xsT_1, xsT_1]
    pre = kb.sb([128, 3 * NH, SBT + 3], F32, "pre")
    preb = [Buf(pre.t, f"pre{i}") for i in range(3 * NH)]
    acc = [kb.sb([128, SBT], F32, f"acc{i}") for i in range(2)]
    tnh = [kb.sb([128, SBT], F32, f"tnh{i}") for i in range(2)]
    qkv_2 = [kb.sb([128, 3 * NH, SBT], BF16, f"qkv{i_}") for i_ in range(2)]
    qb_2 = [[Buf(qkv_2[i_].t, f"qkv{i_}_{i}") for i in range(3 * NH)] for i_ in range(2)]
    gs_2 = [kb.sb([128, NH, SBT], BF16, f"gs{i_}") for i_ in range(2)]
    sqb = [kb.sb([128, SBT], BF16, f"sqb{i}") for i in range(2)]
    rb = [kb.sb([128, SBT], F32, f"rb{i}") for i in range(2)]
    ab_2 = [kb.sb([128, 4, 2 * NH], F32, f"ab{i_}") for i_ in range(2)]
    t1_2 = [kb.sb([128, 4, NH], F32, f"t1{i_}") for i_ in range(2)]
    gg_2 = [kb.sb([128, 4, NH], F32, f"gg{i_}") for i_ in range(2)]
    gh16_2 = [kb.sb([128, 4, NH], BF16, f"gh16{i_}") for i_ in range(2)]
    gh32_2 = [kb.sb([128, 4, NH], F32, f"gh32{i_}") for i_ in range(2)]
    gl32_2 = [kb.sb([128, 4, NH], F32, f"gl32{i_}") for i_ in range(2)]
    lnb_2 = [kb.sb([128, 4, NH], F32, f"lnb{i_}") for i_ in range(2)]
    beta_2 = [kb.sb([128, 4, NH], F32, f"beta{i_}") for i_ in range(2)]
    Gs_2 = [kb.sb([128, 2 * NH], F32, f"Gs{i_}") for i_ in range(2)]
    negG_2 = [kb.sb([128, NH], F32, f"negG{i_}") for i_ in range(2)]
    nGb_2 = [kb.sb([128, NH], F32, f"nGb{i_}") for i_ in range(2)]
    negeG_2 = [kb.sb([128, NH], F32, f"negeG{i_}") for i_ in range(2)]
    kdsc_2 = [kb.sb([128, NH], F32, f"kdsc{i_}") for i_ in range(2)]
    gtot_2 = [kb.sb([128, NH], F32, f"gtot{i_}") for i_ in range(2)]
    gB_2 = [kb.sb([128, 2, NH, 128], BF16, f"gB{i_}") for i_ in range(2)]
    E_2 = [kb.sb([128, NH, 128], F32, f"E{i_}") for i_ in range(2)]
    Eb_2 = [kb.sb([128, NH, 128], F32, f"Eb{i_}") for i_ in range(2)]
    eGb_2 = [kb.sb([128, NH, 128], F32, f"eGb{i_}") for i_ in range(2)]
    MT_2 = [kb.sb([128, NH, 128], BF16, f"MT{i_}") for i_ in range(2)]
    qkT_2 = [kb.sb([128, NH, 128], BF16, f"qkT{i_}") for i_ in range(2)]
    qdT_2 = [kb.sb([128, NH, 128], BF16, f"qdT{i_}") for i_ in range(2)]
    T_2 = [kb.sb([128, NH, 128], BF16, f"T{i_}") for i_ in range(2)]
    TT_2 = [kb.sb([128, NH, 128], BF16, f"TT{i_}") for i_ in range(2)]
    Pm_2 = [kb.sb([128, NH, 128], BF16, f"Pm{i_}") for i_ in range(2)]
    kd_2 = [kb.sb([128, NH, 128], BF16, f"kd{i_}") for i_ in range(2)]
    vtok_2 = [kb.sb([128, NH, 128], F32, f"vtok{i_}") for i_ in range(2)]
    Rb_2 = [kb.sb([128, NH, 128], BF16, f"Rb{i_}") for i_ in range(2)]
    vnew_2 = [kb.sb([128, NH, 128], BF16, f"vnew{i_}") for i_ in range(2)]
    S32 = kb.sb([128, NH, 128], F32, "S32")
    Sbf = kb.sb([128, NH, 128], BF16, "Sbf")
    osq_2 = [kb.sb([128, NH, 128], BF16, f"osq{i_}") for i_ in range(2)]
    rinv_2 = [kb.sb([128, NH, 128], F32, f"rinv{i_}") for i_ in range(2)]
    otmp_2 = [kb.sb([128, NH, 128], F32, f"otmp{i_}") for i_ in range(2)]
    oTf = [kb.sb([128, NH, SBT], BF16, f"oTf{i}") for i in range(2)]

    def v3(p):
        return p.t[:, :].rearrange("p (a b) -> p a b", a=NH)

    kb.op("pool", lambda e: e.memset(pre[:], 0.0), W=preb)
    kb.op("pool", lambda e: e.memset(S32[:], 0.0), W=[S32])
    kb.op("pool", lambda e: e.memset(Sbf[:], 0.0), W=[Sbf])

    def bc(buf, blk=None):
        a = buf[:, :] if blk is None else buf[:, blk, :]
        return a.unsqueeze(2).to_broadcast([128, NH, 128])

    for sbi in range(NSB):
        tok0 = sbi * SBT
        sp_ = sbi % 2
        xsT, ab, t1, gg, lnb, beta, gs = (xsT_2[sp_], ab_2[sp_], t1_2[sp_], gg_2[sp_], lnb_2[sp_], beta_2[sp_], gs_2[sp_])
        qkv, qb = qkv_2[sp_], qb_2[sp_]
        gh16, gh32, gl32 = gh16_2[sp_], gh32_2[sp_], gl32_2[sp_]
        for b4 in range(4):
            xb = xt[(sbi * 4 + b4) % 3]
            xsb = xs[b4 % 2]
            kb.dma("sp", xb[:], x_d[tok0 + b4 * 128: tok0 + (b4 + 1) * 128, :], W=[xb], key=xb)
            kb.op("act", lambda e: e.activation(out=junk[:], in_=xb[:], func=AF.Square,
                                                accum_out=rr[:, 0:1]), R=[xb], W=[junk, rr])
            kb.op("act", lambda e: e.activation(out=rr[:, 1:2], in_=rr[:, 0:1], func=AF.Ln,
                                                scale=1.0 / 1024, bias=cbias[:, 2:3]), R=[rr, cbias], W=[rr])
            kb.op("act", lambda e: e.activation(out=rr[:, 2:3], in_=rr[:, 1:2], func=AF.Exp, scale=-0.5), R=[rr], W=[rr])
            kb.op("act", lambda e: e.activation(out=xsb[:], in_=xb[:], func=AF.Copy, scale=rr[:, 2:3]),
                  R=[xb, rr], W=[xsb])
            pt = psT[b4 % 2]
            ptB = pt.t[:, :]
            for kc in range(8):
                kb.op("pe", lambda e, kc=kc: e.transpose(out=ptB[:, kc * 128:(kc + 1) * 128],
                                                         in_=xsb[:, kc * 128:(kc + 1) * 128],
                                                         identity=ident_b[:]), R=[xsb, ident_b], W=[pt])
            ptv = ptB.rearrange("p (k t) -> p k t", k=8)
            kb.op("act", lambda e: e.activation(out=xsT[:, :, b4 * 128:(b4 + 1) * 128], in_=ptv, func=AF.Copy),
                  R=[pt], W=[xsT])
        pab = PS()
        for b4 in range(4):
            for kc in range(8):
                kb.op("pe", lambda e, kc=kc: e.matmul(pab[:, b4 * 2 * NH:(b4 + 1) * 2 * NH],
                                                     lhsT=xsT[:, kc, b4 * 128:(b4 + 1) * 128],
                                                     rhs=W[:, kc, NF * 128:NF * 128 + 2 * NH],
                                                     start=(kc == 0), stop=(kc == 7)), R=[xsT, W], W=[pab])
        kb.op("dve", lambda e: e.tensor_copy(ab[:], pab[:, 0:8 * NH].rearrange("p (a b) -> p a b", a=4)),
              R=[pab], W=[ab])
        kb.op("dve", lambda e: e.tensor_tensor(t1[:], ab[:, :, 0:NH],
                                               dtb[:, :].unsqueeze(1).to_broadcast([128, 4, NH]), op=ALU.add),
              R=[ab, dtb], W=[t1])
        kb.op("act", lambda e: e.activation(out=t1[:], in_=t1[:], func=AF.Exp), R=[t1], W=[t1])
        kb.op("act", lambda e: e.activation(out=lnb[:], in_=ab[:, :, NH:2 * NH], func=AF.Exp, scale=-1.0),
              R=[ab], W=[lnb])
        kb.op("act", lambda e: e.activation(out=t1[:], in_=t1[:], func=AF.Ln, bias=cbias[:, 3:4]),
              R=[t1, cbias], W=[t1])
        kb.op("act", lambda e: e.activation(out=lnb[:], in_=lnb[:], func=AF.Ln, bias=cbias[:, 3:4]),
              R=[lnb, cbias], W=[lnb])
        kb.op("dve", lambda e: e.tensor_tensor(gg[:], t1[:], negA[:, :].unsqueeze(1).to_broadcast([128, 4, NH]),
                                               op=ALU.mult), R=[t1, negA], W=[gg])
        kb.op("dve", lambda e: e.tensor_scalar(lnb[:], lnb[:], -1.0, None, op0=ALU.mult), R=[lnb], W=[lnb])
        kb.op("dve", lambda e: e.tensor_copy(gh16[:], gg[:]), R=[gg], W=[gh16])
        kb.op("dve", lambda e: e.tensor_copy(gh32[:], gh16[:]), R=[gh16], W=[gh32])
        kb.op("dve", lambda e: e.tensor_tensor(gl32[:], gg[:], gh32[:], op=ALU.subtract), R=[gg, gh32], W=[gl32])
        kb.op("act", lambda e: e.activation(out=beta[:], in_=lnb[:], func=AF.Exp), R=[lnb], W=[beta])

        for ft in range(NF):
            pp = PS()
            for kc in range(8):
                kb.op("pe", lambda e, kc=kc: e.matmul(pp[:, :], lhsT=W[:, kc, ft * 128:(ft + 1) * 128],
                                                     rhs=xsT[:, kc, :], start=(kc == 0), stop=(kc == 7)),
                      R=[W, xsT], W=[pp])
            a_ = acc[ft % 2]
            t_ = tnh[ft % 2]
            if ft < 3 * NH:
                kb.op("act", lambda e: e.activation(out=pre[:, ft, 3:SBT + 3], in_=pp[:, :], func=AF.Copy),
                      R=[pp], W=[preb[ft]])
                kb.op("act", lambda e: e.activation(out=a_[:], in_=pp[:, :], func=AF.Copy,
                                                    scale=cw[:, ft, 3:4]), R=[pp, cw], W=[a_])
                for tap in range(3):
                    eng = "dve"
                    kb.op(eng, lambda e, tap=tap: e.scalar_tensor_tensor(
                        out=a_[:], in0=pre[:, ft, tap:tap + SBT], scalar=cw[:, ft, tap:tap + 1], in1=a_[:],
                        op0=ALU.mult, op1=ALU.add), R=[preb[ft], cw, a_], W=[a_])
                kb.op("act", lambda e: e.activation(out=pre[:, ft, 0:3], in_=pre[:, ft, SBT:SBT + 3], func=AF.Copy),
                      R=[preb[ft]], W=[preb[ft]])
                kb.op("act", lambda e: e.activation(out=t_[:], in_=a_[:], func=AF.Tanh, scale=0.5),
                      R=[a_], W=[t_])
                kb.op("dve", lambda e: e.scalar_tensor_tensor(out=qkv[:, ft, :], in0=t_[:], scalar=1.0,
                                                              in1=a_[:], op0=ALU.add, op1=ALU.mult),
                      R=[t_, a_], W=[qb[ft]])
            else:
                h = ft - 3 * NH
                kb.op("act", lambda e: e.activation(out=t_[:], in_=pp[:, :], func=AF.Tanh, scale=0.5),
                      R=[pp], W=[t_])
                kb.op("dve", lambda e: e.scalar_tensor_tensor(out=gs[:, h, :], in0=t_[:], scalar=1.0,
                                                              in1=pp[:, :], op0=ALU.add, op1=ALU.mult),
                      R=[t_, pp], W=[gs])
        for ft in range(2 * NH):
            s_ = sqb[ft % 2]
            r_ = rb[ft % 2]
            pn = PS()
            kb.op("act", lambda e: e.activation(out=s_[:], in_=qkv[:, ft, :], func=AF.Square), R=[qb[ft]], W=[s_])
            kb.op("pe", lambda e: e.matmul(pn[:, :], lhsT=ones_b[:], rhs=s_[:], start=True, stop=True),
                  R=[ones_b, s_], W=[pn])
            isq = ft < NH
            kb.op("act", lambda e: e.activation(out=r_[:], in_=pn[:, :], func=AF.Ln,
                                                scale=(128.0 if isq else 1.0),
                                                bias=cbias[:, 1:2] if isq else cbias[:, 0:1]),
                  R=[pn, cbias], W=[r_])
            kb.op("act", lambda e: e.activation(out=r_[:], in_=r_[:], func=AF.Exp, scale=-0.5), R=[r_], W=[r_])
            kb.op("dve", lambda e: e.tensor_tensor(qkv[:, ft, :], qkv[:, ft, :], r_[:], op=ALU.mult),
                  R=[qb[ft], r_], W=[qb[ft]])

        for b4 in range(4):
            c0 = b4 * 128
            cs = slice(c0, c0 + 128)
            cp_ = (sbi * 4 + b4) % 2
            (Gs, negG, nGb, negeG, kdsc, gtot, gB, E, Eb, eGb, MT, qkT, qdT, T, TT, Pm, kd, vtok, Rb, vnew, osq, rinv, otmp) = (
                Gs_2[cp_], negG_2[cp_], nGb_2[cp_], negeG_2[cp_], kdsc_2[cp_], gtot_2[cp_], gB_2[cp_], E_2[cp_], Eb_2[cp_], eGb_2[cp_],
                MT_2[cp_], qkT_2[cp_], qdT_2[cp_], T_2[cp_], TT_2[cp_], Pm_2[cp_], kd_2[cp_], vtok_2[cp_], Rb_2[cp_], vnew_2[cp_],
                osq_2[cp_], rinv_2[cp_], otmp_2[cp_])
            ptk = psT[(sbi * 4 + b4) % 2]
            ptv_ = ptk
            ptkB = ptk.t[:, :]
            for h in range(NH):
                kb.op("pe", lambda e, h=h: e.transpose(out=ptkB[:, h * 128:(h + 1) * 128],
                                                       in_=qkv[:, NH + h, cs], identity=ident_b[:]),
                      R=[qb[NH + h], ident_b], W=[ptk])
            for h in range(NH):
                kb.op("pe", lambda e, h=h: e.transpose(out=ptkB[:, (NH + h) * 128:(NH + h + 1) * 128],
                                                       in_=qkv[:, 2 * NH + h, cs], identity=ident_b[:]),
                      R=[qb[2 * NH + h], ident_b], W=[ptv_])
            pg = PS()
            kb.op("pe", lambda e: e.matmul(pg[:, 0:NH], lhsT=U_f[:], rhs=gg[:, b4, :], start=True, stop=True),
                  R=[U_f, gg], W=[pg])
            kb.op("pe", lambda e: e.matmul(pg[:, NH:2 * NH], lhsT=ones_f[:], rhs=gg[:, b4, :], start=True, stop=True),
                  R=[ones_f, gg], W=[pg])
            kb.op("dve", lambda e: e.tensor_copy(Gs[:], pg[:, 0:2 * NH]), R=[pg], W=[Gs])
            kb.op("dve", lambda e: e.tensor_scalar(negG[:], Gs[:, 0:NH], -1.0, None, op0=ALU.mult), R=[Gs], W=[negG])
            kb.op("dve", lambda e: e.tensor_tensor(nGb[:], lnb[:, b4, :], Gs[:, 0:NH], op=ALU.subtract),
                  R=[lnb, Gs], W=[nGb])
            kb.op("act", lambda e: e.activation(out=negeG[:], in_=Gs[:, 0:NH], func=AF.Exp), R=[Gs], W=[negeG])
            kb.op("dve", lambda e: e.tensor_scalar(negeG[:], negeG[:], -1.0, None, op0=ALU.mult), R=[negeG], W=[negeG])
            kb.op("dve", lambda e: e.tensor_tensor(kdsc[:], Gs[:, NH:2 * NH], Gs[:, 0:NH], op=ALU.subtract),
                  R=[Gs], W=[kdsc])
            kb.op("act", lambda e: e.activation(out=kdsc[:], in_=kdsc[:], func=AF.Exp), R=[kdsc], W=[kdsc])
            kb.op("act", lambda e: e.activation(out=gtot[:], in_=Gs[:, NH:2 * NH], func=AF.Exp), R=[Gs], W=[gtot])
            for h in range(NH):
                kb.op("act", lambda e, h=h: e.activation(out=kd[:, h, :], in_=ptkB[:, h * 128:(h + 1) * 128], func=AF.Copy,
                                                        scale=kdsc[:, h:h + 1]), R=[ptk, kdsc], W=[kd])
            kb.op("act", lambda e: e.activation(out=vtok[:], in_=ptkB[:, NH * 128:2 * NH * 128].rearrange("p (a b) -> p a b", a=NH),
                                                func=AF.Copy, scale=0.5), R=[ptv_], W=[vtok])
            for h in range(NH):
                kb.op("act", lambda e, h=h: e.activation(out=gB[:, 0, h, :], in_=ones_f[:], func=AF.Copy, scale=gh32[:, b4, h:h + 1]),
                      R=[ones_f, gh32], W=[gB])
                kb.op("act", lambda e, h=h: e.activation(out=gB[:, 1, h, :], in_=ones_f[:], func=AF.Copy, scale=gl32[:, b4, h:h + 1]),
                      R=[ones_f, gl32], W=[gB])
            pgb = PS()
            pgm = PS()
            for h in range(NH):
                kb.op("pe", lambda e, h=h: e.matmul(pgb[:, h * 128:(h + 1) * 128], lhsT=gB[:, 0, h, :], rhs=U_b[:],
                                                   start=True, stop=False), R=[gB, U_b], W=[pgb])
                kb.op("pe", lambda e, h=h: e.matmul(pgb[:, h * 128:(h + 1) * 128], lhsT=gB[:, 1, h, :], rhs=U_b[:],
                                                   start=False, stop=True), R=[gB, U_b], W=[pgb])
            for h in range(NH):
                kb.op("pe", lambda e, h=h: e.matmul(pgm[:, h * 128:(h + 1) * 128], lhsT=gB[:, 0, h, :], rhs=U_b[:],
                                                   start=True, stop=False), R=[gB, U_b], W=[pgm])
                kb.op("pe", lambda e, h=h: e.matmul(pgm[:, h * 128:(h + 1) * 128], lhsT=gB[:, 1, h, :], rhs=U_b[:],
                                                   start=False, stop=False), R=[gB, U_b], W=[pgm])
                kb.op("pe", lambda e, h=h: e.matmul(pgm[:, h * 128:(h + 1) * 128], lhsT=ident_b[:], rhs=negm[:],
                                                   start=False, stop=True), R=[ident_b, negm], W=[pgm])
            pgs = PS()
            for h in range(NH):
                kb.op("pe", lambda e, h=h: e.matmul(pgs[:, h * 128:(h + 1) * 128], lhsT=gB[:, 0, h, :], rhs=U_b[:],
                                                   start=True, stop=False), R=[gB, U_b], W=[pgs])
                kb.op("pe", lambda e, h=h: e.matmul(pgs[:, h * 128:(h + 1) * 128], lhsT=gB[:, 1, h, :], rhs=U_b[:],
                                                   start=False, stop=False), R=[gB, U_b], W=[pgs])
                kb.op("pe", lambda e, h=h: e.matmul(pgs[:, h * 128:(h + 1) * 128], lhsT=ident_b[:], rhs=negms[:],
                                                   start=False, stop=True), R=[ident_b, negms], W=[pgs])
            for h in range(NH):
                kb.op("act", lambda e, h=h: e.activation(out=E[:, h, :], in_=pgm[:, h * 128:(h + 1) * 128], func=AF.Exp,
                                                        bias=negG[:, h:h + 1]), R=[pgm, negG], W=[E])
                kb.op("act", lambda e, h=h: e.activation(out=Eb[:, h, :], in_=pgs[:, h * 128:(h + 1) * 128], func=AF.Exp,
                                                        bias=nGb[:, h:h + 1]), R=[pgs, nGb], W=[Eb])
            kb.op("act", lambda e: e.activation(out=eGb[:], in_=v3(pgb), func=AF.Exp), R=[pgb], W=[eGb])
            pA = PS()
            pKQ = PS()
            for h in range(NH):
                kb.op("pe", lambda e, h=h: e.matmul(pA[:, h * 128:(h + 1) * 128], lhsT=qkv[:, NH + h, cs],
                                                   rhs=qkv[:, NH + h, cs], start=True, stop=True), R=[qb[NH + h]], W=[pA])
                kb.op("pe", lambda e, h=h: e.matmul(pKQ[:, h * 128:(h + 1) * 128], lhsT=qkv[:, NH + h, cs],
                                                   rhs=qkv[:, h, cs], start=True, stop=True), R=[qb[NH + h], qb[h]], W=[pKQ])
            kb.op("dve", lambda e: e.tensor_tensor(MT[:], v3(pA), Eb[:], op=ALU.mult), R=[pA, Eb], W=[MT])
            kb.op("dve", lambda e: e.tensor_tensor(qkT[:], v3(pKQ), E[:], op=ALU.mult), R=[pKQ, E], W=[qkT])
            kb.op("dve", lambda e: e.tensor_tensor(qdT[:], qkv[:, 0:NH, cs], eGb[:], op=ALU.mult), R=qb[0:NH] + [eGb], W=[qdT])
            for l in range(NLEV):
                pP = PS()
                if l == 0:
                    for h in range(NH):
                        kb.op("pe", lambda e, h=h: e.matmul(pP[:, h * 128:(h + 1) * 128], lhsT=MT[:, h, :], rhs=ident_b[:],
                                                           start=True, stop=True), R=[MT, ident_b], W=[pP])
                    kb.op("dve", lambda e: e.tensor_tensor(Pm[:], v3(pP), lmask[:, 0, :, :], op=ALU.mult), R=[pP, lmask], W=[Pm])
                    pQT = PS()
                    for h in range(NH):
                        kb.op("pe", lambda e, h=h: e.matmul(pQT[:, h * 128:(h + 1) * 128], lhsT=Pm[:, h, :], rhs=ident_b[:],
                                                           start=True, stop=True), R=[Pm, ident_b], W=[pQT])
                    kb.op("dve", lambda e: e.tensor_tensor(T[:], identb4[:], Pm[:], op=ALU.subtract), R=[identb4, Pm], W=[T])
                    kb.op("dve", lambda e: e.tensor_tensor(TT[:], identb4[:], v3(pQT), op=ALU.subtract), R=[identb4, pQT], W=[TT])
                    continue
                for h in range(NH):
                    kb.op("pe", lambda e, h=h: e.matmul(pP[:, h * 128:(h + 1) * 128], lhsT=MT[:, h, :], rhs=T[:, h, :],
                                                       start=True, stop=True), R=[MT, T], W=[pP])
                kb.op("dve", lambda e: e.tensor_tensor(Pm[:], v3(pP), lmask[:, l, :, :], op=ALU.mult),
                      R=[pP, lmask], W=[Pm])
                last = (l == NLEV - 1)
                pQT = PS()
                if not last:
                    pQ = PS()
                for h in range(NH):
                    if not last:
                        kb.op("pe", lambda e, h=h: e.matmul(pQ[:, h * 128:(h + 1) * 128], lhsT=TT[:, h, :], rhs=Pm[:, h, :],
                                                           start=True, stop=True), R=[TT, Pm], W=[pQ])
                    kb.op("pe", lambda e, h=h: e.matmul(pQT[:, h * 128:(h + 1) * 128], lhsT=Pm[:, h, :], rhs=TT[:, h, :],
                                                       start=True, stop=True), R=[Pm, TT], W=[pQT])
                if not last:
                    kb.op("dve", lambda e: e.tensor_tensor(T[:], T[:], v3(pQ), op=ALU.subtract), R=[T, pQ], W=[T])
                kb.op("dve", lambda e: e.tensor_tensor(TT[:], TT[:], v3(pQT), op=ALU.subtract), R=[TT, pQT], W=[TT])
            pKS = PS()
            for h in range(NH):
                kb.op("pe", lambda e, h=h: e.matmul(pKS[:, h * 128:(h + 1) * 128], lhsT=qkv[:, NH + h, cs], rhs=Sbf[:, h, :],
                                                   start=True, stop=True), R=[qb[NH + h], Sbf], W=[pKS])
            for h in range(NH):
                kb.op("dve", lambda e, h=h: e.scalar_tensor_tensor(out=Rb[:, h, :], in0=pKS[:, h * 128:(h + 1) * 128],
                                                                   scalar=negeG[:, h:h + 1], in1=vtok[:, h, :],
                                                                   op0=ALU.mult, op1=ALU.add), R=[pKS, negeG, vtok], W=[Rb])
            pX = PS()
            for h in range(NH):
                kb.op("pe", lambda e, h=h: e.matmul(pX[:, h * 128:(h + 1) * 128], lhsT=TT[:, h, :], rhs=Rb[:, h, :],
                                                   start=True, stop=True), R=[TT, Rb], W=[pX])
            for h in range(NH):
                kb.op("act", lambda e, h=h: e.activation(out=vnew[:, h, :], in_=pX[:, h * 128:(h + 1) * 128], func=AF.Copy,
                                                        scale=beta[:, b4, h:h + 1]), R=[pX, beta], W=[vnew])
            pO = PS()
            pS = PS()
            for h in range(NH):
                kb.op("pe", lambda e, h=h: e.matmul(pO[:, h * 128:(h + 1) * 128], lhsT=Sbf[:, h, :], rhs=qdT[:, h, :],
                                                   start=True, stop=False), R=[Sbf, qdT], W=[pO])
                kb.op("pe", lambda e, h=h: e.matmul(pO[:, h * 128:(h + 1) * 128], lhsT=vnew[:, h, :], rhs=qkT[:, h, :],
                                                   start=False, stop=True), R=[vnew, qkT], W=[pO])
            for h in range(NH):
                kb.op("pe", lambda e, h=h: e.matmul(pS[:, h * 128:(h + 1) * 128], lhsT=kd[:, h, :], rhs=vnew[:, h, :],
                                                   start=True, stop=True), R=[kd, vnew], W=[pS])
            for h in range(NH):
                kb.op("dve", lambda e, h=h: e.scalar_tensor_tensor(out=S32[:, h, :], in0=S32[:, h, :], scalar=gtot[:, h:h + 1],
                                                                   in1=pS[:, h * 128:(h + 1) * 128], op0=ALU.mult, op1=ALU.add),
                      R=[S32, gtot, pS], W=[S32])
            kb.op("act", lambda e: e.activation(out=Sbf[:], in_=S32[:], func=AF.Copy), R=[S32], W=[Sbf])
            kb.op("act", lambda e: e.activation(out=osq[:], in_=v3(pO), func=AF.Square), R=[pO], W=[osq])
            pN = PS()
            kb.op("pe", lambda e: e.matmul(pN[:, :], lhsT=ones_b[:], rhs=osq[:].rearrange("p a b -> p (a b)"),
                                           start=True, stop=True), R=[ones_b, osq], W=[pN])
            kb.op("act", lambda e: e.activation(out=rinv[:], in_=v3(pN), func=AF.Ln, scale=1.0 / 128,
                                                bias=cbias[:, 2:3]), R=[pN, cbias], W=[rinv])
            kb.op("act", lambda e: e.activation(out=rinv[:], in_=rinv[:], func=AF.Exp, scale=-0.5), R=[rinv], W=[rinv])
            kb.op("dve", lambda e: e.tensor_tensor(otmp[:], v3(pO), rinv[:], op=ALU.mult), R=[pO, rinv], W=[otmp])
            of = oTf[sbi % 2]
            kb.op("dve", lambda e: e.scalar_tensor_tensor(out=of[:, :, cs], in0=otmp[:], scalar=ong[:, 0:1],
                                                          in1=gs[:, :, cs], op0=ALU.mult, op1=ALU.mult),
                  R=[otmp, ong, gs], W=[of])
        of = oTf[sbi % 2]
        kb0 = (col0 + tok0) // 128
        prs = []
        for j in range(SBT // 128):
            for dst in o_scr.loc(kb0 + j):
                prs.append((dst, of[:, :, j * 128:(j + 1) * 128]))
        kb.dma_multi("sp", prs, R=[of], W=[o_scr], key=of)
    for h in range(NH):
        kb.dma("sp", ssm_d[h, :, :], S32[:, h, :], R=[S32], W=[ssm_d], key=S32)
    kb.dma("sp", convo_d[:, :, :], pre[:, :, 0:3], R=preb, W=[convo_d], key=pre)


def sample_a(kb, smps, L, PS, psT, row0):
    NS = 16
    W, cw, negA, dtb, ong, cbias = L["W"], L["cw"], L["negA"], L["dtb"], L["ong"], L["cbias"]
    ident_f, ident_b, ones_b, ones_f = L["ident_f"], L["ident_b"], L["ones_b"], L["ones_f"]
    NF = 4 * NH
    eye_d = smps[0]["eye"]
    xs_t = kb.sb([128, 1024], F32, "s_x")
    xsb = kb.sb([128, 1024], BF16, "s_xb")
    junk = kb.sb([128, 1024], BF16, "s_junk")
    rr = kb.sb([128, 4], F32, "s_rr")
    xsT = kb.sb([128, 8, 128], BF16, "s_xsT")
    sct = kb.sb([128, 3, 3 * NH * 128], F32, "s_sct")
    scT = kb.sb([128, 3 * NH, 3, NS], F32, "s_scT")
    crow = kb.sb([128, 3 * NH * 128], F32, "s_crow")
    ab = kb.sb([128, 2 * NH], F32, "s_ab")
    pf = kb.sb([128, NF, NS], F32, "s_pf")
    t_ = kb.sb([128, 3 * NH, NS], F32, "s_t")
    u_ = kb.sb([128, 3 * NH, NS], F32, "s_u")
    c2 = kb.sb([128, 3 * NH, NS], F32, "s_c2")
    gsil = kb.sb([128, NH, NS], F32, "s_gsil")
    sq = kb.sb([128, 2 * NH, NS], BF16, "s_sq")
    rbs = kb.sb([128, 2 * NH, NS], F32, "s_rbs")
    qkn = kb.sb([128, 2 * NH, NS], F32, "s_qkn")
    vf = kb.sb([128, NH, NS], F32, "s_vf")
    eyeb = kb.sb([128, NS, NS], F32, "s_eyeb")
    eyep = kb.sb([128, NS], F32, "s_eyep")
    Kexp = kb.sb([128, NH, NS, NS], BF16, "s_Kexp")
    Qexp = kb.sb([128, NH, NS, NS], BF16, "s_Qexp")
    S0b = [kb.sb([128, NS, 128], BF16, f"s_S0b{i}") for i in range(4)]
    tokb = kb.sb([128, NH, 128], BF16, "s_tokb")
    tok = kb.sb([128, 3 * NH, 128], F32, "s_tok")
    sm = kb.sb([128, 8 * NH], F32, "s_sm")
    KS = kb.sb([128, NH, 128], F32, "s_KS")
    QS = kb.sb([128, NH, 128], F32, "s_QS")
    vn = kb.sb([128, NH, 128], F32, "s_vn")
    ot = kb.sb([128, NH, 128], F32, "s_ot")
    tm = kb.sb([128, NH, 128], F32, "s_tm")
    osT = kb.sb([128, NH, NS], BF16, "s_osT")
    Egx = kb.sb([128, NS, NH], F32, "s_Egx")
    egb = kb.sb([128, NS, NH], F32, "s_egb")
    S0 = [kb.sb([128, NS, 128], F32, f"s_S0{i}") for i in range(4)]
    Vexp = [kb.sb([128, NS, 128], BF16, f"s_Vexp{i}") for i in range(4)]
    Sout = [kb.sb([128, 4, 128], F32, f"s_Sout{i}") for i in range(4)]
    for b_ in (xs_t, sct, tok, sm, vn, ot, eyep, Egx, Vexp[0], Vexp[1], Vexp[2], Vexp[3]):
        kb.op("pool", lambda e, b_=b_: e.memset(b_[:], 0.0), W=[b_])
    kb.dma("sp", eyeb[:], eye_d[:, :, :], W=[eyeb], key=eyeb)
    kb.dma("sp", eyep[0:NS, :], eye_d[0, :, :], W=[eyep], key=eyep)
    for smp in smps:
        _sample_a_group(kb, smp, locals(), L, PS, psT)


def _sample_a_group(kb, smp, A, L, PS, psT):
    NS = 16
    NF = 4 * NH
    W, cw, negA, dtb, ong, cbias = L["W"], L["cw"], L["negA"], L["dtb"], L["ong"], L["cbias"]
    ident_f, ident_b, ones_b, ones_f = L["ident_f"], L["ident_b"], L["ones_b"], L["ones_f"]
    xs_d, sc_d, ss_d, convs_d, ssms_d, os_scr = (smp[k] for k in ("xs", "sc", "ss", "convs", "ssms", "os_scr"))
    (xs_t, xsb, junk, rr, xsT, sct, scT, crow, ab, pf, t_, u_, c2, gsil, sq, rbs, qkn, vf, eyeb, eyep, Kexp, Qexp, tok, sm, KS, QS, vn, ot, tm,
     osT, Egx, egb, S0, Vexp, Sout, S0b, tokb) = (A[k] for k in (
        "xs_t", "xsb", "junk", "rr", "xsT", "sct", "scT", "crow", "ab", "pf", "t_", "u_", "c2", "gsil", "sq", "rbs", "qkn", "vf", "eyeb", "eyep",
        "Kexp", "Qexp", "tok", "sm", "KS", "QS", "vn", "ot", "tm", "osT", "Egx", "egb", "S0", "Vexp", "Sout", "S0b", "tokb"))
    kb.dma("sp", xs_t[0:NS, :], xs_d[:, :], W=[xs_t], key=xs_t)
    kb.dma("sp", sct[0:NS, :, :], sc_d[:, :, :], W=[sct], key=sct)
    kb.dma("sp", convs_d[:, 0:2, :], sc_d[:, 1:3, :], W=[], key=crow)
    kb.op("act", lambda e: e.activation(out=junk[:], in_=xs_t[:], func=AF.Square, accum_out=rr[:, 0:1]), R=[xs_t], W=[junk, rr])
    kb.op("act", lambda e: e.activation(out=rr[:, 1:2], in_=rr[:, 0:1], func=AF.Ln, scale=1.0 / 1024, bias=cbias[:, 2:3]), R=[rr, cbias], W=[rr])
    kb.op("act", lambda e: e.activation(out=rr[:, 2:3], in_=rr[:, 1:2], func=AF.Exp, scale=-0.5), R=[rr], W=[rr])
    kb.op("dve", lambda e: e.tensor_scalar(xsb[:], xs_t[:], rr[:, 2:3], None, op0=ALU.mult), R=[xs_t, rr], W=[xsb])
    pt = psT[0]
    ptB = pt.t[:, :]
    for kc in range(8):
        kb.op("pe", lambda e, kc=kc: e.transpose(out=ptB[:, kc * 128:(kc + 1) * 128], in_=xsb[:, kc * 128:(kc + 1) * 128], identity=ident_b[:]),
              R=[xsb, ident_b], W=[pt])
    kb.op("act", lambda e: e.activation(out=xsT[:], in_=ptB.rearrange("p (k t) -> p k t", k=8), func=AF.Copy), R=[pt], W=[xsT])
    ppf = PS()
    for ft in range(NF):
        for kc in range(8):
            kb.op("pe", lambda e, kc=kc: e.matmul(ppf[:, ft * NS:(ft + 1) * NS], lhsT=W[:, kc, ft * 128:(ft + 1) * 128], rhs=xsT[:, kc, 0:NS],
                                                 start=(kc == 0), stop=(kc == 7)), R=[W, xsT], W=[ppf])
    kb.op("act", lambda e: e.activation(out=pf[:], in_=ppf[:, 0:NF * NS].rearrange("p (a b) -> p a b", a=NF), func=AF.Copy), R=[ppf], W=[pf])
    for j in range(3):
        pc = PS()
        for kc in range(8):
            kb.op("pe", lambda e, kc=kc: e.matmul(pc[:, :], lhsT=xsT[:, kc, :], rhs=W[:, kc, j * 512:(j + 1) * 512], start=(kc == 0), stop=(kc == 7)),
                  R=[xsT, W], W=[pc])
        kb.op("act", lambda e: e.activation(out=crow[:, j * 512:(j + 1) * 512], in_=pc[:, :], func=AF.Copy), R=[pc], W=[crow])
    kb.dma("sp", convs_d[:, 2, :], crow[0:NS, :], R=[crow], W=[], key=crow)
    pab = PS()
    for kc in range(8):
        kb.op("pe", lambda e, kc=kc: e.matmul(pab[:, 0:2 * NH], lhsT=xsT[:, kc, :], rhs=W[:, kc, NF * 128:NF * 128 + 2 * NH], start=(kc == 0), stop=(kc == 7)),
              R=[xsT, W], W=[pab])
    kb.op("dve", lambda e: e.tensor_copy(ab[:], pab[:, 0:2 * NH]), R=[pab], W=[ab])
    g_ = sm[:, 0:NH]; eg_ = sm[:, NH:2 * NH]; be_ = sm[:, 2 * NH:3 * NH]; qk_ = sm[:, 3 * NH:4 * NH]
    tp_ = sm[:, 4 * NH:5 * NH]; ri_ = sm[:, 5 * NH:6 * NH]; tq_ = sm[:, 6 * NH:7 * NH]
    kb.op("dve", lambda e: e.tensor_tensor(tp_, ab[:, 0:NH], dtb[:, :], op=ALU.add), R=[ab, dtb], W=[sm])
    kb.op("act", lambda e: e.activation(out=tp_, in_=tp_, func=AF.Exp), R=[sm], W=[sm])
    kb.op("act", lambda e: e.activation(out=tq_, in_=ab[:, NH:2 * NH], func=AF.Exp, scale=-1.0), R=[ab], W=[sm])
    kb.op("act", lambda e: e.activation(out=tp_, in_=tp_, func=AF.Ln, bias=cbias[:, 3:4]), R=[sm, cbias], W=[sm])
    kb.op("act", lambda e: e.activation(out=tq_, in_=tq_, func=AF.Ln, bias=cbias[:, 3:4]), R=[sm, cbias], W=[sm])
    kb.op("dve", lambda e: e.tensor_tensor(g_, tp_, negA[:, :], op=ALU.mult), R=[sm, negA], W=[sm])
    kb.op("act", lambda e: e.activation(out=eg_, in_=g_, func=AF.Exp), R=[sm], W=[sm])
    kb.op("act", lambda e: e.activation(out=be_, in_=tq_, func=AF.Exp, scale=-1.0), R=[sm], W=[sm])
    psc = [PS(), PS()]
    idx = 0
    for ft in range(3 * NH):
        for tap in range(3):
            bank, off = (0, idx * NS) if idx < 32 else (1, (idx - 32) * NS)
            kb.op("pe", lambda e: e.transpose(out=psc[bank][:, off:off + NS], in_=sct[:, tap, ft * 128:(ft + 1) * 128][:, :], identity=ident_f[:])
                  if False else e.matmul(psc[bank][:, off:off + NS], lhsT=sct[:, tap, ft * 128:(ft + 1) * 128], rhs=ident_f[:, 0:NS], start=True, stop=True),
                  R=[sct, ident_f], W=[psc[bank]])
            idx += 1
    scTf = scT[:].rearrange("p a b c -> p (a b c)")
    kb.op("act", lambda e: e.activation(out=scTf[:, 0:512], in_=psc[0][:, 0:512], func=AF.Copy), R=[psc[0]], W=[scT])
    kb.op("act", lambda e: e.activation(out=scTf[:, 512:576], in_=psc[1][:, 0:64], func=AF.Copy), R=[psc[1]], W=[scT])
    def cwb(tap):
        return cw[:, :, tap:tap + 1].to_broadcast([128, 3 * NH, NS])
    kb.op("dve", lambda e: e.tensor_tensor(t_[:], scT[:, :, 0, :], cwb(0), op=ALU.mult), R=[scT, cw], W=[t_])
    for tap in (1, 2):
        kb.op("dve", lambda e, tap=tap: e.tensor_tensor(u_[:], scT[:, :, tap, :], cwb(tap), op=ALU.mult), R=[scT, cw], W=[u_])
        kb.op("dve", lambda e: e.tensor_tensor(t_[:], t_[:], u_[:], op=ALU.add), R=[t_, u_], W=[t_])
    kb.op("dve", lambda e: e.tensor_tensor(u_[:], pf[:, 0:3 * NH, :], cwb(3), op=ALU.mult), R=[pf, cw], W=[u_])
    kb.op("dve", lambda e: e.tensor_tensor(t_[:], t_[:], u_[:], op=ALU.add), R=[t_, u_], W=[t_])
    kb.op("act", lambda e: e.activation(out=u_[:], in_=t_[:], func=AF.Tanh, scale=0.5), R=[t_], W=[u_])
    kb.op("dve", lambda e: e.scalar_tensor_tensor(out=c2[:], in0=u_[:], scalar=1.0, in1=t_[:], op0=ALU.add, op1=ALU.mult), R=[u_, t_], W=[c2])
    kb.op("act", lambda e: e.activation(out=gsil[:], in_=pf[:, 3 * NH:4 * NH, :], func=AF.Tanh, scale=0.5), R=[pf], W=[gsil])
    kb.op("dve", lambda e: e.scalar_tensor_tensor(out=gsil[:], in0=gsil[:], scalar=1.0, in1=pf[:, 3 * NH:4 * NH, :], op0=ALU.add, op1=ALU.mult),
          R=[gsil, pf], W=[gsil])
    kb.op("act", lambda e: e.activation(out=sq[:], in_=c2[:, 0:2 * NH, :], func=AF.Square), R=[c2], W=[sq])
    pn = PS()
    kb.op("pe", lambda e: e.matmul(pn[:, 0:2 * NH * NS], lhsT=ones_b[:], rhs=sq[:].rearrange("p a b -> p (a b)"), start=True, stop=True),
          R=[ones_b, sq], W=[pn])
    pn3 = pn.t[:, 0:2 * NH * NS].rearrange("p (a b) -> p a b", a=2 * NH)
    kb.op("act", lambda e: e.activation(out=rbs[:, 0:NH, :], in_=pn3[:, 0:NH, :], func=AF.Ln, scale=128.0, bias=cbias[:, 1:2]), R=[pn, cbias], W=[rbs])
    kb.op("act", lambda e: e.activation(out=rbs[:, NH:2 * NH, :], in_=pn3[:, NH:2 * NH, :], func=AF.Ln, scale=1.0, bias=cbias[:, 0:1]), R=[pn, cbias], W=[rbs])
    kb.op("act", lambda e: e.activation(out=rbs[:], in_=rbs[:], func=AF.Exp, scale=-0.5), R=[rbs], W=[rbs])
    kb.op("dve", lambda e: e.tensor_tensor(qkn[:], c2[:, 0:2 * NH, :], rbs[:], op=ALU.mult), R=[c2, rbs], W=[qkn])
    kb.op("dve", lambda e: e.tensor_scalar(vf[:], c2[:, 2 * NH:3 * NH, :], 0.5, None, op0=ALU.mult), R=[c2], W=[vf])
    ptk = [PS(), PS(), PS()]
    for i in range(3 * NH):
        src = qkn[:, i, :] if i < 2 * NH else vf[:, i - 2 * NH, :]
        bank, off = i // 4, (i % 4) * 128
        kb.op("pe", lambda e: e.matmul(ptk[bank][0:NS, off:off + 128], lhsT=src, rhs=ident_f[:], start=True, stop=True),
              R=[qkn, vf, ident_f], W=[ptk[bank]])
    for bank in range(3):
        kb.op("act", lambda e, bank=bank: e.activation(out=tok[0:NS, bank * 4:(bank + 1) * 4, :],
                                                       in_=ptk[bank][0:NS, :].rearrange("p (a b) -> p a b", a=4), func=AF.Copy),
              R=[ptk[bank]], W=[tok])
    q_t = tok[:, 0:NH, :]; k_t = tok[:, NH:2 * NH, :]; v_t = tok[:, 2 * NH:3 * NH, :]
    kb.op("dve", lambda e: e.tensor_tensor(tm[:], q_t, k_t, op=ALU.mult), R=[tok], W=[tm])
    kb.op("dve", lambda e: e.tensor_reduce(out=qk_, in_=tm[:], axis=mybir.AxisListType.X, op=ALU.add), R=[tm], W=[sm])
    for h in range(NH):
        kb.op("dve", lambda e, h=h: e.tensor_tensor(Kexp[:, h, :, :], eyeb[:], qkn[:, NH + h, :].unsqueeze(2).to_broadcast([128, NS, NS]), op=ALU.mult),
              R=[eyeb, qkn], W=[Kexp])
        kb.op("dve", lambda e, h=h: e.tensor_tensor(Qexp[:, h, :, :], eyeb[:], qkn[:, h, :].unsqueeze(2).to_broadcast([128, NS, NS]), op=ALU.mult),
              R=[eyeb, qkn], W=[Qexp])
    kb.op("dve", lambda e: e.tensor_tensor(Egx[:], eyep[:, :].unsqueeze(2).to_broadcast([128, NS, NH]),
                                           eg_.unsqueeze(1).to_broadcast([128, NS, NH]), op=ALU.mult), R=[eyep, sm], W=[Egx])
    peg = PS()
    kb.op("pe", lambda e: e.matmul(peg[:, 0:NS * NH], lhsT=ones_f[:], rhs=Egx[:].rearrange("p a b -> p (a b)"), start=True, stop=True),
          R=[ones_f, Egx], W=[peg])
    kb.op("act", lambda e: e.activation(out=egb[:], in_=peg[:, 0:NS * NH].rearrange("p (a b) -> p a b", a=NS), func=AF.Copy), R=[peg], W=[egb])

    def bcs(ap):
        return ap.unsqueeze(2).to_broadcast([128, NH, 128])
    for h in range(NH):
        s0 = S0[h % 4]
        kb.dma("sp", s0[:], ss_d[:, h, :, :].rearrange("n k v -> k n v"), W=[s0], key=s0)
        s0b = S0b[h % 4]
        kb.op("act", lambda e: e.activation(out=s0b[:], in_=s0[:], func=AF.Copy), R=[s0], W=[s0b])
        if h == 0:
            kb.op("pool", lambda e: e.tensor_copy(tokb[:], tok[:, NH:2 * NH, :]), R=[tok], W=[tokb])
        pks = PS()
        for n in range(NS):
            kb.op("pe", lambda e, n=n: e.matmul(pks[0:NS, 0:128], lhsT=Kexp[:, h, n, :], rhs=s0b[:, n, :], start=(n == 0), stop=(n == NS - 1)),
                  R=[Kexp, s0b], W=[pks])
        for n in range(NS):
            kb.op("pe", lambda e, n=n: e.matmul(pks[0:NS, 128:256], lhsT=Qexp[:, h, n, :], rhs=s0b[:, n, :], start=(n == 0), stop=(n == NS - 1)),
                  R=[Qexp, s0b], W=[pks])
        kb.op("act", lambda e: e.activation(out=KS[0:NS, h, :], in_=pks[0:NS, 0:128], func=AF.Copy), R=[pks], W=[KS])
        kb.op("act", lambda e: e.activation(out=QS[0:NS, h, :], in_=pks[0:NS, 128:256], func=AF.Copy), R=[pks], W=[QS])
        r16 = slice(0, NS)
        kb.op("dve", lambda e: e.scalar_tensor_tensor(out=tm[r16, h, :], in0=KS[r16, h, :], scalar=eg_[r16, h:h + 1], in1=v_t[r16, h, :],
                                                      op0=ALU.mult, op1=ALU.subtract), R=[KS, sm, tok], W=[tm])
        kb.op("dve", lambda e: e.tensor_scalar(vn[r16, h, :], tm[r16, h, :], be_[r16, h:h + 1], -1.0, op0=ALU.mult, op1=ALU.mult),
              R=[tm, sm], W=[vn])
        kb.op("dve", lambda e: e.tensor_scalar(tm[r16, h, :], QS[r16, h, :], eg_[r16, h:h + 1], None, op0=ALU.mult), R=[QS, sm], W=[tm])
        kb.op("dve", lambda e: e.scalar_tensor_tensor(out=ot[r16, h, :], in0=vn[r16, h, :], scalar=qk_[r16, h:h + 1], in1=tm[r16, h, :],
                                                      op0=ALU.mult, op1=ALU.add), R=[vn, sm, tm], W=[ot])
        vx = Vexp[h % 4]
        kb.op("dve", lambda e: e.tensor_tensor(vx[:], vn[:, h, :].unsqueeze(1).to_broadcast([128, NS, 128]),
                                               eyep[:, :].unsqueeze(2).to_broadcast([128, NS, 128]), op=ALU.mult), R=[vn, eyep], W=[vx])
        for n4 in range(NS // 4):
            pss = PS()
            so = Sout[n4 % 4]
            for j in range(4):
                n = n4 * 4 + j
                kb.op("pe", lambda e, n=n, j=j: e.matmul(pss[:, j * 128:(j + 1) * 128], lhsT=tokb[:, h, :], rhs=vx[:, n, :], start=True, stop=True),
                      R=[tokb, vx], W=[pss])
            for j in range(4):
                n = n4 * 4 + j
                kb.op("dve", lambda e, n=n, j=j: e.scalar_tensor_tensor(out=so[:, j, :], in0=s0[:, n, :], scalar=egb[:, n, h:h + 1],
                                                                        in1=pss[:, j * 128:(j + 1) * 128], op0=ALU.mult, op1=ALU.add),
                      R=[s0, egb, pss], W=[so])
            kb.dma("sp", ssms_d[n4 * 4:(n4 + 1) * 4, h, :, :].rearrange("n k v -> k n v"), so[:], R=[so], W=[], key=so)
    kb.op("dve", lambda e: e.tensor_tensor(tm[:], ot[:], ot[:], op=ALU.mult), R=[ot], W=[tm])
    kb.op("dve", lambda e: e.tensor_reduce(out=ri_, in_=tm[:], axis=mybir.AxisListType.X, op=ALU.add), R=[tm], W=[sm])
    kb.op("act", lambda e: e.activation(out=ri_, in_=ri_, func=AF.Ln, scale=1.0 / 128, bias=cbias[:, 2:3]), R=[sm, cbias], W=[sm])
    kb.op("act", lambda e: e.activation(out=ri_, in_=ri_, func=AF.Exp, scale=-0.5), R=[sm], W=[sm])
    kb.op("dve", lambda e: e.tensor_tensor(ot[:], ot[:], bcs(ri_), op=ALU.mult), R=[ot, sm], W=[ot])
    pot = PS()
    for h in range(NH):
        kb.op("pe", lambda e, h=h: e.matmul(pot[:, h * NS:(h + 1) * NS], lhsT=ot[:, h, :], rhs=ident_f[:, 0:NS], start=True, stop=True),
              R=[ot, ident_f], W=[pot])
    kb.op("dve", lambda e: e.scalar_tensor_tensor(out=osT[:], in0=pot[:, 0:NH * NS].rearrange("p (a b) -> p a b", a=NH), scalar=ong[:, 0:1],
                                                  in1=gsil[:], op0=ALU.mult, op1=ALU.mult), R=[pot, ong, gsil], W=[osT])
    kb.dma("sp", os_scr.sloc(smp["grp"])[:, :, 0:NS], osT[:], R=[osT], W=[os_scr], key=osT)

import numpy as np

EPS = 1e-6
NQ = 16
NS = 16


def host_consts_b():
    c = {}
    i = np.arange(128)
    slopes = np.exp2(-8.0 * np.arange(1, NQ + 1, dtype=np.float32) / NQ).astype(np.float32)
    bias = np.zeros((128, 2, NQ, 128), np.float32)
    jj = i[:, None]
    ii = i[None, :]
    for h in range(NQ):
        cur = np.where(ii >= jj, -slopes[h] * (ii - jj), -30000.0)
        prv = np.where(jj >= ii, -slopes[h] * (128 + ii - jj), -30000.0)
        bias[:, 0, h, :] = prv
        bias[:, 1, h, :] = cur
    c["abias"] = bias
    bo = np.zeros((128, 128), np.float32)
    bo[:64, :64] = 1.0
    bo[64:, 64:] = 1.0
    c["bones"] = bo
    eo = np.zeros((128, 2, 128), np.float32)
    eo[:, 0, :64] = 1.0
    eo[:, 1, 64:] = 1.0
    c["eones"] = eo
    bs = np.zeros((NS * 4, 4, 128), np.float32)
    for n in range(NS):
        for g in range(4):
            for hh in range(4):
                bs[n * 4 + g, hh, :] = -slopes[4 * g + hh] * (128 - i)
    c["sbias"] = bs
    c["eye16"] = np.tile(np.eye(16, dtype=np.float32)[None], (128, 1, 1)).copy()
    pm = np.zeros((128, 64), np.float32)
    pm[np.arange(128), np.arange(128) % 64] = 1.0
    c["pairM"] = pm
    return c


def alloc_weights_b(kb):
    Woa = kb.sb([128, 8, 1024], BF16, "Woa")
    Wkv = kb.sb([128, 8, 512], BF16, "Wkv")
    Wb = kb.sb([128, 8, 2048], BF16, "Wb")
    Wob = kb.sb([128, 8, 1024], BF16, "Wob")
    gk = kb.sb([128, 8], F32, "gkv")
    gb = kb.sb([128, 8], F32, "gnb")
    return Woa, Wkv, Wb, Wob, gk, gb


def load_weights_b(kb, W6, woa_d, wkv_d, kvn_d, wb_d, nb_d, wob_d):
    Woa, Wkv, Wb, Wob, gk, gb = W6
    stg = [kb.sb([128, 2048], F32, f"stgb{i}") for i in range(4)]
    kb.dma("sp", gk[:], kvn_d[:, :], W=[gk], key=gk)
    kb.dma("sp", gb[:], nb_d[:, :], W=[gb], key=gb)
    i = 0
    for (dst, src, n, g) in ((Woa, woa_d, 1024, None), (Wkv, wkv_d, 512, gk), (Wb, wb_d, 2048, gb), (Wob, wob_d, 1024, None)):
        sv = src.rearrange("(kc p) n -> p kc n", p=128)
        for kc in range(8):
            s = stg[i % 4]
            q_ = ("sp", "act")[i % 2]
            i += 1
            kb.dma(q_, s[:, 0:n], sv[:, kc, :], W=[s], key=s)
            if g is None:
                kb.op("act" if kc % 2 else "dve",
                      (lambda e: e.activation(out=dst[:, kc, :], in_=s[:, 0:n], func=AF.Copy)) if kc % 2 else
                      (lambda e: e.tensor_copy(dst[:, kc, :], s[:, 0:n])), R=[s], W=[dst])
            else:
                kb.op("act" if kc % 2 else "dve",
                      (lambda e: e.activation(out=dst[:, kc, :], in_=s[:, 0:n], func=AF.Copy, scale=g[:, kc:kc + 1])) if kc % 2 else
                      (lambda e: e.tensor_scalar(dst[:, kc, :], s[:, 0:n], g[:, kc:kc + 1], None, op0=ALU.mult)),
                      R=[s, g], W=[dst])


def phase_b(kb, cb, x_d, o_scr, y_d, kwin_d, vwin_d, kng_d, qng_d, snk_d, Wts, NBLK, psT, psF, smp=None, idx_d=None, ab1_d=None, NBT=None, wsrc=None):
    Woa, Wkv, Wb, Wob = Wts[:4]
    pfi = [0]

    def PS():
        p = psF[pfi[0] % len(psF)]
        pfi[0] += 1
        return p

    def v4(p, a=4):
        return p.t[:, :].rearrange("p (a b) -> p a b", a=a)

    ident_f = kb.sb([128, 128], F32, "b_ident_f")
    ident_b = kb.sb([128, 128], BF16, "b_ident_b")
    bones_f = kb.sb([128, 128], F32, "bones_f")
    bones = kb.sb([128, 128], BF16, "bones")
    eones_f = kb.sb([128, 2, 128], F32, "eones_f")
    eones = kb.sb([128, 2, 128], BF16, "eones")
    kng = kb.sb([128, 64], F32, "kng")
    qng = kb.sb([128, 1], F32, "qng")
    esink = kb.sb([128, 8], F32, "esink")
    cbias = kb.sb([128, 4], F32, "b_cbias")
    cl = kb.sb([128, 1], F32, "b_cload")
    lds = []

    def ld(dst, src):
        lds.append((dst, src))
    ld(ident_f, cb["ident"][:, :]); ld(bones_f, cb["bones"][:, :]); ld(eones_f, cb["eones"][:, :, :])
    ld(kng, kng_d[:, :]); ld(qng, qng_d[:, :]); ld(esink, snk_d[:, :])
    kb.dma_multi("sp", [(d_[:], s_) for d_, s_ in lds], W=[d_ for d_, _ in lds], key=cl)
    kb.op("dve", lambda e: e.tensor_copy(ident_b[:], ident_f[:]), R=[ident_f], W=[ident_b])
    kb.op("dve", lambda e: e.tensor_copy(bones[:], bones_f[:]), R=[bones_f], W=[bones])
    kb.op("dve", lambda e: e.tensor_copy(eones[:], eones_f[:]), R=[eones_f], W=[eones])
    kb.op("act", lambda e: e.activation(out=esink[:], in_=esink[:], func=AF.Exp), R=[esink], W=[esink])
    kb.op("dve", lambda e: e.tensor_scalar(qng[:], qng[:], 0.125, None, op0=ALU.mult), R=[qng], W=[qng])
    for j, v in enumerate([EPS, 1.0]):
        kb.op("pool", lambda e, j=j, v=v: e.memset(cbias[:, j:j + 1], v), W=[cbias])

    idxt = kb.sb([128, (NBLK + 1) * 2], I32, "b_idx")
    kb.dma("sp", idxt[:], idx_d[:, :], W=[idxt], key=idxt)
    hs = kb.sb([128, 1024], BF16, "b_hs")
    junk = kb.sb([128, 1024], BF16, "b_junk")
    rr = kb.sb([128, 4], F32, "b_rr")
    hT = kb.sb([128, 8, 128], BF16, "b_hT")
    def rms_and_T(src, ntok, hT_out):
        kb.op("act", lambda e: e.activation(out=junk[0:ntok, :], in_=src[0:ntok, :], func=AF.Square,
                                            accum_out=rr[0:ntok, 0:1]), R=[src], W=[junk, rr])
        kb.op("act", lambda e: e.activation(out=rr[0:ntok, 1:2], in_=rr[0:ntok, 0:1], func=AF.Ln,
                                            scale=1.0 / 1024, bias=cbias[0:ntok, 0:1]), R=[rr, cbias], W=[rr])
        kb.op("act", lambda e: e.activation(out=rr[0:ntok, 2:3], in_=rr[0:ntok, 1:2], func=AF.Exp, scale=-0.5), R=[rr], W=[rr])
        kb.op("act", lambda e: e.activation(out=hs[0:ntok, :], in_=src[0:ntok, :], func=AF.Copy, scale=rr[0:ntok, 2:3]),
              R=[src, rr], W=[hs])
        pt = psT[0]
        ptB = pt.t[:, :]
        for kc in range(8):
            kb.op("pe", lambda e, kc=kc: e.transpose(out=ptB[:, kc * 128:kc * 128 + ntok], in_=hs[0:ntok, kc * 128:(kc + 1) * 128],
                                                     identity=ident_b[0:ntok, 0:ntok]), R=[hs, ident_b], W=[pt])
        kb.op("act", lambda e: e.activation(out=hT_out[:, :, 0:ntok],
                                            in_=ptB.rearrange("p (k t) -> p k t", k=8)[:, :, 0:ntok], func=AF.Copy),
              R=[pt], W=[hT_out])

    kb.push()
    load_weights_b(kb, Wts, *wsrc)
    if smp is not None:
        sample_b(kb, smp, locals(), PS, psT, rms_and_T)
    kb.pop()
    kb.push()
    abias = kb.sb([128, 2, NQ, 128], F32, "abias")
    kb.dma("sp", abias[:], cb["abias"][:, :, :, :], W=[abias], key=abias)
    abias1 = kb.sb([128, NQ, 128], F32, "abias1")
    kb.dma("sp", abias1[:], ab1_d[:, :, :], W=[abias1], key=abias1)
    xb = [kb.sb([128, 1024], F32, f"b_x{i}") for i in range(2)]
    ob = [kb.sb([128, 8, 128], BF16, f"b_o{i}") for i in range(2)]
    hb_2 = [kb.sb([128, 1024], F32, f"b_hb{i_}") for i_ in range(2)]
    kv_2 = [kb.sb([128, 512], F32, f"b_kv{i_}") for i_ in range(2)]
    kss_2 = [kb.sb([128, 8], F32, f"b_kss{i_}") for i_ in range(2)]
    kn_2 = [kb.sb([128, 4, 64], F32, f"b_kn{i_}") for i_ in range(2)]
    kdup_2 = [kb.sb([128, 4, 2, 64], BF16, f"b_kdup{i_}") for i_ in range(2)]
    kT2 = [kb.sb([128, 4, 128], BF16, f"b_kT2{i}") for i in range(2)]
    vE = [kb.sb([128, 4, 128], BF16, f"b_vE{i}") for i in range(2)]
    vO = [kb.sb([128, 4, 128], BF16, f"b_vO{i}") for i in range(2)]
    sq_2 = [kb.sb([128, 4, 128], BF16, f"b_sq{i_}") for i_ in range(2)]
    rq_2 = [kb.sb([128, 4, 128], F32, f"b_rq{i_}") for i_ in range(2)]
    qT_2 = [kb.sb([128, 2, 8, 128], BF16, f"b_qT{i_}") for i_ in range(2)]
    tg_2 = [kb.sb([128, 4, 128], F32, f"b_tg{i_}") for i_ in range(2)]
    gsl_2 = [kb.sb([128, 8, 128], BF16, f"b_gsl{i_}") for i_ in range(2)]
    ssb = [kb.sb([128, 4, 128], F32, f"b_ss{i}") for i in range(2)]
    PT = [kb.sb([128, 4, 128], BF16, f"b_PT{i}") for i in range(4)]
    den_2 = [kb.sb([128, 4, 128], F32, f"b_den{i_}") for i_ in range(2)]
    o1_2 = [kb.sb([128, 4, 128], F32, f"b_o1{i_}") for i_ in range(2)]
    oTb_2 = [kb.sb([128, 8, 128], BF16, f"b_oTb{i_}") for i_ in range(2)]
    yb = [kb.sb([128, 1024], F32, f"b_y{i}") for i in range(2)]
    for i_ in range(2):
        kb.op("pool", lambda e, i_=i_: e.memset(qT_2[i_][:], 0.0), W=[qT_2[i_]])
    for i in range(2):
        kb.op("pool", lambda e, i=i: e.memset(vE[i][:], 0.0), W=[vE[i]])
        kb.op("pool", lambda e, i=i: e.memset(vO[i][:], 0.0), W=[vO[i]])

    for blk in range(NBLK):
        t0 = blk * 128
        par = blk % 2
        x_ = xb[par]
        o_ = ob[par]
        hb, kv, kss, kn, kdup, sq, rq, qT, tg, gsl, den, o1, oTb = (hb_2[par], kv_2[par], kss_2[par], kn_2[par], kdup_2[par], sq_2[par],
                                                                  rq_2[par], qT_2[par], tg_2[par], gsl_2[par], den_2[par], o1_2[par], oTb_2[par])
        kb.dma("sp", x_[:], x_d[t0:t0 + 128, :], W=[x_], key=x_)
        kb.ind_dma_multi([(o_[:, 4 * hg:4 * hg + 4, :].rearrange("p h t -> p (h t)"), idxt[:, blk * 2 + hg:blk * 2 + hg + 1], o_scr.gsrc(blk)) for hg in range(2)],
                         None, o_scr.gnr(blk), R=[o_scr.gbuf(blk), idxt], W=[o_], key=o_)
        for half in range(2):
            ph = PS()
            for h in range(8):
                kb.op("pe", lambda e, h=h: e.matmul(ph[:, :], lhsT=o_[:, h, :], rhs=Woa[:, h, half * 512:(half + 1) * 512],
                                                   start=(h == 0), stop=(h == 7)), R=[o_, Woa], W=[ph])
            kb.op("dve", lambda e: e.tensor_tensor(hb[:, half * 512:(half + 1) * 512], ph[:, :], x_[:, half * 512:(half + 1) * 512],
                                                   op=ALU.add), R=[ph, x_], W=[hb])
        rms_and_T(hb, 128, hT)
        pk = PS()
        for kc in range(8):
            kb.op("pe", lambda e, kc=kc: e.matmul(pk[:, :], lhsT=hT[:, kc, :], rhs=Wkv[:, kc, :], start=(kc == 0), stop=(kc == 7)),
                  R=[hT, Wkv], W=[pk])
        kb.op("act", lambda e: e.activation(out=kv[:], in_=pk[:, :], func=AF.Copy), R=[pk], W=[kv])
        for g in range(4):
            kb.op("act", lambda e, g=g: e.activation(out=junk[:, 0:64], in_=kv[:, g * 64:(g + 1) * 64], func=AF.Square,
                                                    accum_out=kss[:, g:g + 1]), R=[kv], W=[junk, kss])
        kb.op("act", lambda e: e.activation(out=kss[:, 4:8], in_=kss[:, 0:4], func=AF.Ln, scale=1.0 / 64, bias=cbias[:, 0:1]),
              R=[kss, cbias], W=[kss])
        kb.op("act", lambda e: e.activation(out=kss[:, 4:8], in_=kss[:, 4:8], func=AF.Exp, scale=-0.5), R=[kss], W=[kss])
        kb.op("dve", lambda e: e.tensor_tensor(kn[:], kv[:, 0:256].rearrange("p (g d) -> p g d", g=4),
                                               kss[:, 4:8].unsqueeze(2).to_broadcast([128, 4, 64]), op=ALU.mult), R=[kv, kss], W=[kn])
        kb.op("dve", lambda e: e.tensor_tensor(kn[:], kn[:], kng[:, :].unsqueeze(1).to_broadcast([128, 4, 64]), op=ALU.mult),
              R=[kn, kng], W=[kn])
        kb.op("act", lambda e: e.activation(out=kdup[:, :, 0, :], in_=kn[:], func=AF.Copy), R=[kn], W=[kdup])
        kb.op("act", lambda e: e.activation(out=kdup[:, :, 1, :], in_=kn[:], func=AF.Copy), R=[kn], W=[kdup])
        vv = kv[:, 256:512].rearrange("p (g d) -> p g d", g=4)
        kb.op("act", lambda e: e.activation(out=vE[par][:, :, 0:64], in_=vv, func=AF.Copy), R=[kv], W=[vE[par]])
        kb.op("act", lambda e: e.activation(out=vO[par][:, :, 64:128], in_=vv, func=AF.Copy), R=[kv], W=[vO[par]])
        pt = psT[1]
        ptB = pt.t[:, :]
        for g in range(4):
            kb.op("pe", lambda e, g=g: e.transpose(out=ptB[:, g * 128:(g + 1) * 128], in_=kdup[:, g, :, :].rearrange("p a d -> p (a d)"),
                                                   identity=ident_b[:]), R=[kdup, ident_b], W=[pt])
        kb.op("act", lambda e: e.activation(out=kT2[par][:], in_=ptB[:, 0:512].rearrange("p (g t) -> p g t", g=4), func=AF.Copy),
              R=[pt], W=[kT2[par]])
        if blk == NBLK - 1:
            kb.dma("sp", kwin_d[:, :], kn[:].rearrange("p g d -> p (g d)"), R=[kn], W=[], key=kn)
            kb.dma("sp", vwin_d[:, :], kv[:, 256:512], R=[kv], W=[], key=kv)
        if blk == 0:
            continue
        for grp in range(4):
            pq = PS()
            for cc in range(4):
                col0 = (grp * 4 + cc) * 128
                for kc in range(8):
                    kb.op("pe", lambda e, kc=kc: e.matmul(pq[:, cc * 128:(cc + 1) * 128], lhsT=Wb[:, kc, col0:col0 + 128], rhs=hT[:, kc, :],
                                                         start=(kc == 0), stop=(kc == 7)), R=[Wb, hT], W=[pq])
            if grp < 2:
                kb.op("act", lambda e: e.activation(out=sq[:], in_=v4(pq), func=AF.Square), R=[pq], W=[sq])
                pn = PS()
                kb.op("pe", lambda e: e.matmul(pn[:, :], lhsT=bones[:], rhs=sq[:].rearrange("p a b -> p (a b)"), start=True, stop=True),
                      R=[bones, sq], W=[pn])
                kb.op("act", lambda e: e.activation(out=rq[:], in_=v4(pn), func=AF.Ln, scale=1.0 / 64, bias=cbias[:, 0:1]),
                      R=[pn, cbias], W=[rq])
                kb.op("act", lambda e: e.activation(out=rq[:], in_=rq[:], func=AF.Exp, scale=-0.5), R=[rq], W=[rq])
                for hf in range(2):
                    ps_ = slice(hf * 64, (hf + 1) * 64)
                    kb.op("dve", lambda e: e.scalar_tensor_tensor(out=qT[ps_, hf, grp * 4:(grp + 1) * 4, :], in0=v4(pq)[ps_], scalar=qng[ps_, 0:1],
                                                                  in1=rq[ps_], op0=ALU.mult, op1=ALU.mult), R=[pq, qng, rq], W=[qT])
            else:
                kb.op("act", lambda e: e.activation(out=tg[:], in_=v4(pq), func=AF.Tanh, scale=0.5), R=[pq], W=[tg])
                kb.op("dve", lambda e: e.scalar_tensor_tensor(out=gsl[:, (grp - 2) * 4:(grp - 1) * 4, :], in0=tg[:], scalar=1.0, in1=v4(pq),
                                                              op0=ALU.add, op1=ALU.mult), R=[tg, pq], W=[gsl])
        kbs = [0, 1]
        pO = [psF[0], psF[1]]
        pD = [psF[2], psF[3]]
        sidx = 0
        for g in range(4):
            pts = []
            for ki, kbk in enumerate(kbs):
                kpar = par if kbk == 1 else 1 - par
                psS = psF[4 + sidx % 2]
                sidx += 1
                for hh in range(4):
                    head = 4 * g + hh
                    c, hf = head // 2, head % 2
                    kb.op("pe", lambda e, hh=hh, c=c, hf=hf: e.matmul(psS[:, hh * 128:(hh + 1) * 128],
                                                                     lhsT=kT2[kpar][:, g, :],
                                                                     rhs=qT[:, hf, c, :], start=True, stop=True),
                          R=[kT2[kpar], qT], W=[psS])
                s_ = ssb[ki]
                bsrc = abias1[:, 4 * g:4 * g + 4, :] if (blk == 1 and kbk == 0) else abias[:, kbk, 4 * g:4 * g + 4, :]
                kb.op("dve", lambda e: e.tensor_tensor(s_[:], v4(psS), bsrc, op=ALU.add),
                      R=[psS, abias, abias1], W=[s_])
                p_ = PT[(g % 2) * 2 + ki]
                kb.op("act", lambda e: e.activation(out=p_[:], in_=s_[:], func=AF.Exp), R=[s_], W=[p_])
                pts.append((p_, kpar))
            for hh in range(4):
                head = 4 * g + hh
                c, hf = head // 2, head % 2
                bank, cc = c // 4, c % 4
                first = (hf == 0)
                for ki, (p_, kpar) in enumerate(pts):
                    vsrc = vE[kpar] if hf == 0 else vO[kpar]
                    st = first and ki == 0
                    sp_ = (hf == 1) and ki == len(pts) - 1
                    kb.op("pe", lambda e: e.matmul(pO[bank][:, cc * 128:(cc + 1) * 128], lhsT=vsrc[:, g, :], rhs=p_[:, hh, :],
                                                   start=st, stop=sp_), R=[vsrc, p_], W=[pO[bank]])
                    kb.op("pe", lambda e: e.matmul(pD[bank][:, cc * 128:(cc + 1) * 128], lhsT=eones[:, hf, :], rhs=p_[:, hh, :],
                                                   start=st, stop=sp_), R=[eones, p_], W=[pD[bank]])
        for bank in range(2):
            kb.op("dve", lambda e: e.tensor_tensor(den[:], v4(pD[bank]),
                                                   esink[:, bank * 4:(bank + 1) * 4].unsqueeze(2).to_broadcast([128, 4, 128]), op=ALU.add),
                  R=[pD[bank], esink], W=[den])
            kb.op("dve", lambda e: e.reciprocal(den[:], den[:]), R=[den], W=[den])
            kb.op("dve", lambda e: e.tensor_tensor(o1[:], v4(pO[bank]), den[:], op=ALU.mult), R=[pO[bank], den], W=[o1])
            kb.op("dve", lambda e: e.scalar_tensor_tensor(out=oTb[:, bank * 4:(bank + 1) * 4, :], in0=o1[:], scalar=0.5,
                                                          in1=gsl[:, bank * 4:(bank + 1) * 4, :], op0=ALU.mult, op1=ALU.mult),
                  R=[o1, gsl], W=[oTb])
        y_ = yb[par]
        for half in range(2):
            py = PS()
            for c in range(8):
                kb.op("pe", lambda e, c=c: e.matmul(py[:, :], lhsT=oTb[:, c, :], rhs=Wob[:, c, half * 512:(half + 1) * 512],
                                                   start=(c == 0), stop=(c == 7)), R=[oTb, Wob], W=[py])
            kb.op("dve", lambda e: e.tensor_tensor(y_[:, half * 512:(half + 1) * 512], py[:, :], hb[:, half * 512:(half + 1) * 512], op=ALU.add),
                  R=[py, hb], W=[y_])
        kb.dma("sp", y_d[t0 - 128:t0, :], y_[:], R=[y_], W=[], key=y_)

    kb.pop()


def sample_b(kb, smp, L, PS, psT, rms_and_T):
    NS_ = 16
    Woa, Wkv, Wb, Wob = L["Woa"], L["Wkv"], L["Wb"], L["Wob"]
    cbias, kng, ident_b, hT, rr = L["cbias"], L["kng"], L["ident_b"], L["hT"], L["rr"]
    xs_d, os_scr, ys_d, ck_d, cv_d, kws_d, vws_d = (smp[k] for k in ("xs", "os_scr", "ys", "ck", "cv", "kws", "vws"))
    q_scr, o2_scr, kn_scr, vn_scr = (smp[k] for k in ("q_scr", "o2_scr", "kn_scr", "vn_scr"))
    qgr_d, sb_d, sk_d = smp["qngr"], smp["sbias"], smp["snk64"]
    X = mybir.AxisListType.X
    xs_t = kb.sb([128, 1024], F32, "t_x")
    hs_t = kb.sb([128, 1024], F32, "t_h")
    osT = kb.sb([128, 8, 128], BF16, "t_osT")
    idxt, NBLK = L["idxt"], L["NBLK"]
    kv = kb.sb([128, 512], F32, "t_kv")
    kss = kb.sb([128, 8], F32, "t_kss")
    junk = kb.sb([128, 64], BF16, "t_junk")
    kn = kb.sb([128, 4, 64], F32, "t_kn")
    qg = kb.sb([128, 2048], F32, "t_qg")
    tq = kb.sb([128, 16, 64], F32, "t_tq")
    qss = kb.sb([128, 32], F32, "t_qss")
    qgr = kb.sb([128, 64], F32, "t_qgr")
    gsl = kb.sb([128, 1024], F32, "t_gsl")
    q64 = kb.sb([128, 4, 64], F32, "t_q64")
    kn64 = kb.sb([64, 64], F32, "t_kn64")
    vn64 = kb.sb([64, 64], F32, "t_vn64")
    bufA = kb.sb([128, 64, 64], F32, "t_bufA")
    bufB = kb.sb([128, 64, 64], F32, "t_bufB")
    s64 = kb.sb([128, 4, 64], F32, "t_s64")
    sbias = kb.sb([128, 4, 64], F32, "t_sbias")
    part = kb.sb([128, 4 * 64 + 4], F32, "t_part")
    pairM = kb.sb([128, 64], F32, "t_pairM")
    esk = kb.sb([64, 4], F32, "t_esk")
    sm = kb.sb([64, 16], F32, "t_sm")
    o64 = kb.sb([64, 4, 64], F32, "t_o64")
    t64 = kb.sb([64, 4, 64], F32, "t_t64")
    ot = kb.sb([128, 1024], F32, "t_ot")
    ob = kb.sb([128, 1024], BF16, "t_ob")
    oT = kb.sb([128, 8, 128], BF16, "t_oT")
    ys = kb.sb([128, 1024], F32, "t_ys")
    for b_ in (xs_t, hs_t, ot):
        kb.op("pool", lambda e, b_=b_: e.memset(b_[:], 0.0), W=[b_])
    kb.dma("sp", xs_t[0:NS_, :], xs_d[:, :], W=[xs_t], key=xs_t)
    kb.ind_dma_multi([(osT[:, 4 * hg:4 * hg + 4, :].rearrange("p h t -> p (h t)"), idxt[:, NBLK * 2 + hg:NBLK * 2 + hg + 1], os_scr.gsrc(NBLK)) for hg in range(2)],
                     None, os_scr.gnr(NBLK), R=[os_scr.gbuf(NBLK), idxt], W=[osT], key=osT)
    kb.dma("sp", qgr[:], qgr_d[:, :], W=[qgr], key=qgr)
    kb.op("dve", lambda e: e.tensor_scalar(qgr[:], qgr[:], 0.125, None, op0=ALU.mult), R=[qgr], W=[qgr])
    kb.dma_multi("sp", [(sbias[jh * 64:(jh + 1) * 64, :, :], sb_d[:, :, jh * 64:(jh + 1) * 64]) for jh in range(2)], W=[sbias], key=sbias)
    kb.dma("sp", pairM[:], smp["pairM"][:, :], W=[pairM], key=pairM)
    kb.dma("sp", esk[:], sk_d[:, :], W=[esk], key=esk)
    kb.op("act", lambda e: e.activation(out=esk[:], in_=esk[:], func=AF.Exp), R=[esk], W=[esk])
    kb.dma("sp", kws_d[:, 0:127, :], ck_d[:, 1:128, :], W=[], key=junk)
    kb.dma("sp", vws_d[:, 0:127, :], cv_d[:, 1:128, :], W=[], key=junk)
    for half in range(2):
        ph = PS()
        for h in range(8):
            kb.op("pe", lambda e, h=h: e.matmul(ph[0:NS_, :], lhsT=osT[:, h, 0:NS_], rhs=Woa[:, h, half * 512:(half + 1) * 512],
                                               start=(h == 0), stop=(h == 7)), R=[osT, Woa], W=[ph])
        kb.op("dve", lambda e: e.tensor_tensor(hs_t[0:NS_, half * 512:(half + 1) * 512], ph[0:NS_, :], xs_t[0:NS_, half * 512:(half + 1) * 512],
                                               op=ALU.add), R=[ph, xs_t], W=[hs_t])
    rms_and_T(hs_t, 128, hT)
    pk = PS()
    for kc in range(8):
        kb.op("pe", lambda e, kc=kc: e.matmul(pk[:, :], lhsT=hT[:, kc, :], rhs=Wkv[:, kc, :], start=(kc == 0), stop=(kc == 7)), R=[hT, Wkv], W=[pk])
    kb.op("act", lambda e: e.activation(out=kv[:], in_=pk[:, :], func=AF.Copy), R=[pk], W=[kv])
    for g in range(4):
        kb.op("act", lambda e, g=g: e.activation(out=junk[:, 0:64], in_=kv[:, g * 64:(g + 1) * 64], func=AF.Square, accum_out=kss[:, g:g + 1]),
              R=[kv], W=[junk, kss])
    kb.op("act", lambda e: e.activation(out=kss[:, 4:8], in_=kss[:, 0:4], func=AF.Ln, scale=1.0 / 64, bias=cbias[:, 0:1]), R=[kss, cbias], W=[kss])
    kb.op("act", lambda e: e.activation(out=kss[:, 4:8], in_=kss[:, 4:8], func=AF.Exp, scale=-0.5), R=[kss], W=[kss])
    kb.op("dve", lambda e: e.tensor_tensor(kn[:], kv[:, 0:256].rearrange("p (g d) -> p g d", g=4),
                                           kss[:, 4:8].unsqueeze(2).to_broadcast([128, 4, 64]), op=ALU.mult), R=[kv, kss], W=[kn])
    kb.op("dve", lambda e: e.tensor_tensor(kn[:], kn[:], kng[:, :].unsqueeze(1).to_broadcast([128, 4, 64]), op=ALU.mult), R=[kn, kng], W=[kn])
    kb.dma("sp", kws_d[:, 127, :], kn[0:NS_].rearrange("p g d -> p (g d)"), R=[kn], W=[], key=kn)
    kb.dma("sp", vws_d[:, 127, :], kv[0:NS_, 256:512], R=[kv], W=[], key=kv)
    kb.dma("sp", kn_scr.t[:, :], kn[0:NS_].rearrange("p g d -> p (g d)"), R=[kn], W=[kn_scr], key=kn)
    kb.dma("sp", vn_scr.t[:, :], kv[0:NS_, 256:512], R=[kv], W=[vn_scr], key=kv)
    for j in range(4):
        pq = PS()
        for kc in range(8):
            kb.op("pe", lambda e, kc=kc: e.matmul(pq[:, :], lhsT=hT[:, kc, :], rhs=Wb[:, kc, j * 512:(j + 1) * 512], start=(kc == 0), stop=(kc == 7)),
                  R=[hT, Wb], W=[pq])
        kb.op("act", lambda e: e.activation(out=qg[:, j * 512:(j + 1) * 512], in_=pq[:, :], func=AF.Copy), R=[pq], W=[qg])
    q3 = qg[:, 0:1024].rearrange("p (h d) -> p h d", h=16)
    kb.op("dve", lambda e: e.tensor_tensor(tq[:], q3, q3, op=ALU.mult), R=[qg], W=[tq])
    kb.op("dve", lambda e: e.tensor_reduce(out=qss[:, 0:16], in_=tq[:], axis=X, op=ALU.add), R=[tq], W=[qss])
    kb.op("act", lambda e: e.activation(out=qss[:, 16:32], in_=qss[:, 0:16], func=AF.Ln, scale=1.0 / 64, bias=cbias[:, 0:1]), R=[qss, cbias], W=[qss])
    kb.op("act", lambda e: e.activation(out=qss[:, 16:32], in_=qss[:, 16:32], func=AF.Exp, scale=-0.5), R=[qss], W=[qss])
    kb.op("dve", lambda e: e.tensor_tensor(tq[:], q3, qss[:, 16:32].unsqueeze(2).to_broadcast([128, 16, 64]), op=ALU.mult), R=[qg, qss], W=[tq])
    kb.op("dve", lambda e: e.tensor_tensor(tq[:], tq[:], qgr[:, :].unsqueeze(1).to_broadcast([128, 16, 64]), op=ALU.mult), R=[tq, qgr], W=[tq])
    kb.dma("sp", q_scr.t[:, :], tq[0:NS_].rearrange("p h d -> p (h d)"), R=[tq], W=[q_scr], key=tq)
    kb.op("act", lambda e: e.activation(out=gsl[:], in_=qg[:, 1024:2048], func=AF.Tanh, scale=0.5), R=[qg], W=[gsl])
    kb.op("dve", lambda e: e.scalar_tensor_tensor(out=gsl[:], in0=gsl[:], scalar=1.0, in1=qg[:, 1024:2048], op0=ALU.add, op1=ALU.mult),
          R=[gsl, qg], W=[gsl])
    kb.dma_multi("sp", [(q64[jh * 64:(jh + 1) * 64], q_scr.t[:, :].rearrange("n (g hh d) -> (n g) hh d", g=4, hh=4)) for jh in range(2)],
                 R=[q_scr], W=[q64], key=q64)
    kb.dma("sp", kn64[:], kn_scr.t[:, :].rearrange("n (g d) -> (n g) d", g=4), R=[kn_scr], W=[kn64], key=kn64)
    kb.dma("sp", vn64[:], vn_scr.t[:, :].rearrange("n (g d) -> (n g) d", g=4), R=[vn_scr], W=[vn64], key=vn64)
    kb.dma_multi("sp", [(bufA[jh * 64 + 4 * n:jh * 64 + 4 * n + 4, :, :], ck_d[n, jh * 64:(jh + 1) * 64, :].rearrange("j (g d) -> g j d", g=4))
                        for n in range(NS_) for jh in range(2)], W=[bufA], key=bufA)
    for hh in range(4):
        kb.op("dve", lambda e, hh=hh: e.tensor_tensor(bufB[:], bufA[:], q64[:, hh, :].unsqueeze(1).to_broadcast([128, 64, 64]), op=ALU.mult),
              R=[bufA, q64], W=[bufB])
        kb.op("dve", lambda e, hh=hh: e.tensor_reduce(out=s64[:, hh, :], in_=bufB[:], axis=X, op=ALU.add), R=[bufB], W=[s64])
    kb.op("dve", lambda e: e.tensor_tensor(s64[:], s64[:], sbias[:], op=ALU.add), R=[s64, sbias], W=[s64])
    kb.op("act", lambda e: e.activation(out=s64[:], in_=s64[:], func=AF.Exp), R=[s64], W=[s64])
    kb.op("dve", lambda e: e.tensor_reduce(out=part[:, 256:260], in_=s64[:], axis=X, op=ALU.add), R=[s64], W=[part])
    kb.op("dve", lambda e: e.tensor_tensor(t64[:], q64[0:64], kn64[:, :].unsqueeze(1).to_broadcast([64, 4, 64]), op=ALU.mult), R=[q64, kn64], W=[t64])
    kb.op("dve", lambda e: e.tensor_reduce(out=sm[:, 4:8], in_=t64[:], axis=X, op=ALU.add), R=[t64], W=[sm])
    kb.op("act", lambda e: e.activation(out=sm[:, 4:8], in_=sm[:, 4:8], func=AF.Exp), R=[sm], W=[sm])
    kb.dma_multi("sp", [(bufB[jh * 64 + 4 * n:jh * 64 + 4 * n + 4, :, :], cv_d[n, jh * 64:(jh + 1) * 64, :].rearrange("j (g d) -> g j d", g=4))
                        for n in range(NS_) for jh in range(2)], W=[bufB], key=bufB)
    for hh in range(4):
        kb.op("dve", lambda e, hh=hh: e.tensor_tensor(bufA[:].rearrange("p j d -> p d j"), bufB[:].rearrange("p j d -> p d j"),
                                                      s64[:, hh, :].unsqueeze(1).to_broadcast([128, 64, 64]), op=ALU.mult),
              R=[bufB, s64], W=[bufA])
        kb.op("dve", lambda e, hh=hh: e.tensor_reduce(out=part[:, hh * 64:(hh + 1) * 64], in_=bufA[:].rearrange("p j d -> p d j"), axis=X, op=ALU.add),
              R=[bufA], W=[part])
    pcm = PS()
    kb.op("pe", lambda e: e.matmul(pcm[0:64, 0:260], lhsT=pairM[:], rhs=part[:], start=True, stop=True), R=[pairM, part], W=[pcm])
    kb.op("act", lambda e: e.activation(out=o64[:], in_=pcm[0:64, 0:256].rearrange("p (a b) -> p a b", a=4), func=AF.Copy), R=[pcm], W=[o64])
    kb.op("act", lambda e: e.activation(out=sm[:, 0:4], in_=pcm[0:64, 256:260], func=AF.Copy), R=[pcm], W=[sm])
    kb.op("dve", lambda e: e.tensor_tensor(sm[:, 8:12], sm[:, 0:4], sm[:, 4:8], op=ALU.add), R=[sm], W=[sm])
    kb.op("dve", lambda e: e.tensor_tensor(sm[:, 8:12], sm[:, 8:12], esk[:], op=ALU.add), R=[sm, esk], W=[sm])
    kb.op("dve", lambda e: e.reciprocal(sm[:, 8:12], sm[:, 8:12]), R=[sm], W=[sm])
    kb.op("dve", lambda e: e.tensor_tensor(t64[:], vn64[:, :].unsqueeze(1).to_broadcast([64, 4, 64]),
                                           sm[:, 4:8].unsqueeze(2).to_broadcast([64, 4, 64]), op=ALU.mult), R=[vn64, sm], W=[t64])
    kb.op("dve", lambda e: e.tensor_tensor(o64[:], o64[:], t64[:], op=ALU.add), R=[o64, t64], W=[o64])
    kb.op("dve", lambda e: e.tensor_tensor(o64[:], o64[:], sm[:, 8:12].unsqueeze(2).to_broadcast([64, 4, 64]), op=ALU.mult), R=[o64, sm], W=[o64])
    kb.dma("sp", o2_scr.t[:, :].rearrange("n (g hh d) -> (n g) hh d", g=4, hh=4), o64[:], R=[o64], W=[o2_scr], key=o64)
    kb.dma("sp", ot[0:NS_, :], o2_scr.t[:, :], R=[o2_scr], W=[ot], key=ot)
    kb.op("dve", lambda e: e.scalar_tensor_tensor(out=ob[:], in0=ot[:], scalar=0.5, in1=gsl[:], op0=ALU.mult, op1=ALU.mult), R=[ot, gsl], W=[ob])
    pt = psT[1]
    ptB = pt.t[:, :]
    for c in range(8):
        kb.op("pe", lambda e, c=c: e.transpose(out=ptB[:, c * 128:(c + 1) * 128], in_=ob[:, c * 128:(c + 1) * 128], identity=ident_b[:]),
              R=[ob, ident_b], W=[pt])
    kb.op("act", lambda e: e.activation(out=oT[:], in_=ptB.rearrange("p (k t) -> p k t", k=8), func=AF.Copy), R=[pt], W=[oT])
    for half in range(2):
        py = PS()
        for c in range(8):
            kb.op("pe", lambda e, c=c: e.matmul(py[:, :], lhsT=oT[:, c, :], rhs=Wob[:, c, half * 512:(half + 1) * 512], start=(c == 0), stop=(c == 7)),
                  R=[oT, Wob], W=[py])
        kb.op("dve", lambda e: e.tensor_tensor(ys[:, half * 512:(half + 1) * 512], py[:, :], hs_t[:, half * 512:(half + 1) * 512], op=ALU.add),
              R=[py, hs_t], W=[ys])
    kb.dma("sp", ys_d[:, :], ys[0:NS_, :], R=[ys], W=[], key=ys)

from concourse.bass_utils import run_bass_kernel_spmd

NHG = 4
T_FULL = 4096
GROUPS = [[0, 4], [1, 5], [2, 6], [3, 7]]


KCH = 3
GATHER_BARRIER = False


class Exchange(Buf):
    def __init__(self, kb, T):
        Buf.__init__(self, None, "xch")
        self.kb = kb
        self.HB = T // 2 // 128
        self.NBLK = self.HB + 1
        nslot = self.NBLK + 1
        self.nch = (nslot + KCH - 1) // KCH
        self.kc = [2 * min(KCH, nslot - c * KCH) for c in range(self.nch)]
        self.i = [kb.dram(f"xi{c}", [128 * self.kc[c], 512], BF16, "Internal") for c in range(self.nch)]
        self.g = [kb.dram(f"xg{c}", [2 * 128 * self.kc[c], 512], BF16, "Internal") for c in range(self.nch)]
        self.iv = [self.i[c].t.rearrange("(p l) (h t) -> p l h t", l=self.kc[c], t=128) for c in range(self.nch)]

    def _slot(self, k, s):
        c = k // KCH
        return c, (k - c * KCH) * 2 + s

    def loc(self, colblk):
        out = []
        for s in range(2):
            k = colblk - s * self.HB
            if 0 <= k < self.NBLK:
                c, l = self._slot(k, s)
                out.append(self.iv[c][:, l, :, :])
        return out

    def sloc(self, grp):
        c, l = self._slot(self.NBLK, grp)
        return self.iv[c][:, l, :, :]

    def gsrc(self, k):
        return self.g[k // KCH].t[:, :]

    def gbuf(self, k):
        return self.g[k // KCH]

    def gnr(self, k):
        return 2 * 128 * self.kc[k // KCH]

    def row(self, k, s, hg, p):
        c, l = self._slot(k, s)
        return hg * 128 * self.kc[c] + p * self.kc[c] + l

    def gather(self):
        kb = self.kb
        for c in range(self.nch):
            key = f"cc{c}"
            kb.semh[key] = kb.es.enter_context(kb.nc.semaphore(key))
            kb.cnt[key] = 0
            kb.dma_keys.append(key)
            kb.flush()
            kb._wait("pool", kb._deps([self], []))
            ins = kb.eng["pool"].collective_compute("AllGather", ALU.bypass, replica_groups=GROUPS,
                                                    ins=[self.i[c].t], outs=[self.g[c].t])
            kb.cnt[key] += 1
            ins.then_inc(kb.semh[key], 1)
            kb.nins += 1
            self.g[c].w = (key, 1)
            self.g[c].r = []
        if GATHER_BARRIER:
            kb.barrier()


def _prep_a(inp, hg):
    heads = [hg * NHG + i for i in range(NHG)]
    w = inp["w_in_a"][0]
    cols = []
    for base in (0, 1024, 2048, 3072):
        for h in heads:
            cols.append(np.arange(base + h * 128, base + (h + 1) * 128))
    cols.append(np.array([4096 + h for h in heads]))
    cols.append(np.array([4104 + h for h in heads]))
    cols = np.concatenate(cols)
    d = {}
    d["wa"] = np.ascontiguousarray(w[:, cols])
    cwf = inp["conv_w_a"][0]
    cw = np.zeros((128, 3 * NHG, 4), np.float32)
    for g, base in enumerate((0, 1024, 2048)):
        for i, h in enumerate(heads):
            cw[:, g * NHG + i, :] = cwf[:, base + h * 128: base + (h + 1) * 128].T
    d["cw"] = cw
    d["alog"] = np.ascontiguousarray(np.tile(inp["a_log"][0][heads][None, :], (128, 1)).astype(np.float32))
    d["dtb"] = np.ascontiguousarray(np.tile(inp["dt_bias"][0][heads][None, :], (128, 1)).astype(np.float32))
    return d


def build(T=T_FULL):
    kb = KB()
    NSB = T // 512
    HALF = T // 2
    NBLK = HALF // 128 + 1
    NBT = 1 + T // 128 + 2
    I = lambda n, s: kb.dram(n, s, F32, "ExternalInput")
    O = lambda n, s: kb.dram(n, s, F32, "ExternalOutput")
    x_d = I("x", [T, 1024])
    xB_d = I("xB", [HALF + 128, 1024])
    na_d = I("na", [128, 8])
    ong_d = I("ong", [128, 1])
    wa_d = I("wa", [1024, 16 * 128 + 8]); cw_d = I("cw", [128, 12, 4]); alog_d = I("alog", [128, 4]); dtb_d = I("dtb", [128, 4])
    hc = host_consts()
    hcb = host_consts_b()
    cst = {k: I("c_" + k, list(v.shape)) for k, v in hc.items()}
    cstb = {k: I("cb_" + k, list(v.shape)) for k, v in hcb.items()}
    woa_d = I("woa", [1024, 1024]); wkv_d = I("wkv", [1024, 512]); kvn_d = I("kvn", [128, 8])
    wb_d = I("wb", [1024, 2048]); nb_d = I("nb", [128, 8]); wob_d = I("wob", [1024, 1024])
    kng_d = I("kng", [128, 64]); qng_d = I("qng", [128, 1]); snk_d = I("snk", [128, 8])
    idx_d = kb.dram("idxtab", [128, (NBLK + 1) * 2], I32, "ExternalInput")
    ab1_d = I("abias1", [128, 16, 128])
    y_d = O("y", [HALF, 1024])
    ssm_d = O("ssm", [4, 128, 128])
    convo_d = O("convo", [128, 12, 3])
    kwin_d = O("kwin", [128, 256]); vwin_d = O("vwin", [128, 256])
    xch = Exchange(kb, T)
    xs32_d = I("xs32", [32, 1024]); sc_d = I("sc", [32, 3, 1536]); ss_d = I("ss", [32, 4, 128, 128])
    xs_d = I("xs", [16, 1024])
    ck_d = I("ck", [16, 128, 256]); cv_d = I("cv", [16, 128, 256])
    qngr_d = I("qngr", [128, 64]); snk64_d = I("snk64", [64, 4])
    convs_d = O("convs", [32, 3, 1536]); ssms_d = O("ssms", [32, 4, 128, 128])
    kws_d = O("kws", [16, 128, 256]); vws_d = O("vws", [16, 128, 256]); ys_d = O("ys", [16, 1024])
    q_scr = kb.dram("q_scr", [16, 1024], F32, "Internal"); o2_scr = kb.dram("o2_scr", [16, 1024], F32, "Internal")
    kn_scr = kb.dram("kn_scr", [16, 256], F32, "Internal"); vn_scr = kb.dram("vn_scr", [16, 256], F32, "Internal")
    psT = [kb.ps([128, 1024], BF16, f"psT{i}") for i in range(2)]
    psF = [kb.ps([128, 512], F32, f"psF{i}") for i in range(6)]
    cst_t = {k: v.t for k, v in cst.items()}
    smps = []
    for grp in range(2):
        r = slice(16 * grp, 16 * grp + 16)
        smps.append(dict(xs=xs32_d.t[r, :], sc=sc_d.t[r, :, :], ss=ss_d.t[r, :, :, :], convs=convs_d.t[r, :, :], ssms=ssms_d.t[r, :, :, :],
                         os_scr=xch, eye=cstb["eye16"].t, grp=grp))
    phase_a(kb, x_d.t, wa_d.t, na_d.t, cw_d.t, alog_d.t, dtb_d.t, ong_d.t, cst_t, xch, ssm_d, convo_d, NSB, psT, psF,
            row0=0, smp=smps, col0=128)
    kb.new_scope()
    xch.gather()
    Wts = alloc_weights_b(kb)
    cb_t = {k: v.t for k, v in cstb.items()}
    cb_t["ident"] = cst["ident"].t
    phase_b(kb, cb_t, xB_d.t, xch, y_d.t, kwin_d.t, vwin_d.t, kng_d.t, qng_d.t, snk_d.t, Wts, NBLK, psT, psF,
            smp=dict(xs=xs_d.t, os_scr=xch, ys=ys_d.t, ck=ck_d.t, cv=cv_d.t, kws=kws_d.t, vws=vws_d.t, q_scr=q_scr, o2_scr=o2_scr,
                     kn_scr=kn_scr, vn_scr=vn_scr, qngr=qngr_d.t, sbias=cstb["sbias"].t, snk64=snk64_d.t, pairM=cstb["pairM"].t),
            idx_d=idx_d.t, ab1_d=ab1_d.t, NBT=NBT, wsrc=(woa_d.t, wkv_d.t, kvn_d.t, wb_d.t, nb_d.t, wob_d.t))
    kb.finish()
    return kb


def make_inputs(inp, T=T_FULL):
    HALF = T // 2
    NBLK = HALF // 128 + 1
    NBT = 1 + T // 128 + 2
    hc = host_consts()
    hcb = host_consts_b()
    shared = {}
    shared["na"] = np.ascontiguousarray(inp["norm_a"][0].reshape(8, 128).T)
    shared["ong"] = np.ascontiguousarray(inp["o_norm_a"][0].reshape(128, 1))
    for k, v in hc.items():
        shared["c_" + k] = v
    for k, v in hcb.items():
        shared["cb_" + k] = v
    shared["woa"] = np.ascontiguousarray(inp["w_out_a"][0])
    shared["wkv"] = np.ascontiguousarray(inp["w_kv"])
    shared["kvn"] = np.ascontiguousarray(inp["kv_norm"].reshape(8, 128).T)
    shared["wb"] = np.ascontiguousarray(inp["w_in_b"][0])
    shared["nb"] = np.ascontiguousarray(inp["norm_b"][0].reshape(8, 128).T)
    shared["wob"] = np.ascontiguousarray(inp["w_out_b"][0])
    shared["kng"] = np.ascontiguousarray(np.tile(inp["k_norm"][None, :], (128, 1)).astype(np.float32))
    shared["qng"] = np.ascontiguousarray(np.tile(inp["q_norm"][0], 2).reshape(128, 1).astype(np.float32))
    sk = inp["sinks"][0]
    snk = np.zeros((128, 8), np.float32)
    for c in range(8):
        snk[:64, c] = sk[2 * c]
        snk[64:, c] = sk[2 * c + 1]
    shared["snk"] = snk
    shared["qngr"] = np.ascontiguousarray(np.tile(inp["q_norm"][0][None, :], (128, 1)).astype(np.float32))
    s64 = np.zeros((64, 4), np.float32)
    for n in range(16):
        for g in range(4):
            s64[n * 4 + g, :] = sk[4 * g:4 * g + 4]
    shared["snk64"] = s64
    chan = []
    for hg in range(2):
        cols = []
        for base in (0, 1024, 2048):
            for i in range(NHG):
                h = hg * NHG + i
                cols.append(np.arange(base + h * 128, base + (h + 1) * 128))
        chan.append(np.concatenate(cols))
    pa = [_prep_a(inp, hg) for hg in range(2)]
    p = np.arange(128)
    maps = []
    for c in range(8):
        b, s = c % 4, c // 4
        m = dict(shared)
        m.update(pa[s])
        m["x"] = np.ascontiguousarray(inp["x_prompt"][b, :T])
        xB = np.zeros((HALF + 128, 1024), np.float32)
        lo = s * HALF - 128
        if lo < 0:
            xB[128:] = inp["x_prompt"][b, 0:HALF]
        else:
            xB[:] = inp["x_prompt"][b, lo:lo + HALF + 128]
        m["xB"] = xB
        idx = np.zeros((128, (NBLK + 1) * 2), np.int32)
        nslot = NBLK + 1
        for kk in range(nslot):
            cch = kk // KCH
            kc = 2 * min(KCH, nslot - cch * KCH)
            l = (kk - cch * KCH) * 2 + s
            for hg in range(2):
                idx[:, kk * 2 + hg] = hg * 128 * kc + p * kc + l
        m["idxtab"] = idx
        m["abias1"] = np.ascontiguousarray(hcb["abias"][:, 0]) if s == 1 else np.full((128, 16, 128), -30000.0, np.float32)
        n0 = 32 * b
        m["xs32"] = np.ascontiguousarray(inp["x_sample"][n0:n0 + 32, 0, :])
        m["sc"] = np.ascontiguousarray(inp["state_conv"][0, n0:n0 + 32][:, :, chan[s]])
        m["ss"] = np.ascontiguousarray(inp["state_ssm"][0, n0:n0 + 32, s * NHG:(s + 1) * NHG])
        n1 = n0 + 16 * s
        m["xs"] = np.ascontiguousarray(inp["x_sample"][n1:n1 + 16, 0, :])
        m["ck"] = np.ascontiguousarray(inp["cache_k_win"][n1:n1 + 16].reshape(16, 128, 256))
        m["cv"] = np.ascontiguousarray(inp["cache_v_win"][n1:n1 + 16].reshape(16, 128, 256))
        maps.append(m)
    return maps


def assemble(R, T=T_FULL):
    B = 4
    HALF = T // 2
    y_p = np.zeros((B, T, 1024), np.float32)
    conv_p = np.zeros((1, B, 3, 3072), np.float32)
    ssm_p = np.zeros((1, B, 8, 128, 128), np.float32)
    kw_p = np.zeros((B, 128, 4, 64), np.float32)
    vw_p = np.zeros((B, 128, 4, 64), np.float32)
    y_s = np.zeros((128, 1, 1024), np.float32)
    conv_s = np.zeros((1, 128, 3, 3072), np.float32)
    ssm_s = np.zeros((1, 128, 8, 128, 128), np.float32)
    kw_s = np.zeros((128, 128, 4, 64), np.float32)
    vw_s = np.zeros((128, 128, 4, 64), np.float32)
    for c in range(8):
        b, s = c % 4, c // 4
        r = R[c]
        y_p[b, s * HALF:(s + 1) * HALF] = r["y"]
        ssm_p[0, b, s * NHG:(s + 1) * NHG] = r["ssm"]
        n0 = 32 * b
        ssm_s[0, n0:n0 + 32, s * NHG:(s + 1) * NHG] = r["ssms"]
        for g, base in enumerate((0, 1024, 2048)):
            for i in range(NHG):
                h = s * NHG + i
                conv_p[0, b, :, base + h * 128: base + (h + 1) * 128] = r["convo"][:, g * NHG + i, :].T
                conv_s[0, n0:n0 + 32, :, base + h * 128: base + (h + 1) * 128] = r["convs"][:, :, (g * NHG + i) * 128:(g * NHG + i + 1) * 128]
        if s == 1:
            kw_p[b] = r["kwin"].reshape(128, 4, 64)
            vw_p[b] = r["vwin"].reshape(128, 4, 64)
        n1 = n0 + 16 * s
        y_s[n1:n1 + 16, 0] = r["ys"]
        kw_s[n1:n1 + 16] = r["kws"].reshape(16, 128, 4, 64)
        vw_s[n1:n1 + 16] = r["vws"].reshape(16, 128, 4, 64)
    return (y_p, y_s, conv_p, ssm_p, kw_p, vw_p, conv_s, ssm_s, kw_s, vw_s)


_CACHE = {}


def kernel(**inp):
    inp = {k: np.asarray(v) for k, v in inp.items()}
    if "kb" not in _CACHE:
        _CACHE["kb"] = build()
    kb = _CACHE["kb"]
    maps = make_inputs(inp)
    res = run_bass_kernel_spmd(kb.nc, maps, core_ids=list(range(8)))
    return assemble(res.results)
```

```python
import contextlib
import numpy as np
import concourse.bass as bass
import concourse.mybir as mybir

F32 = mybir.dt.float32
BF16 = mybir.dt.bfloat16
I32 = mybir.dt.int32
AF = mybir.ActivationFunctionType
ALU = mybir.AluOpType


class Buf:
    def __init__(self, t, name):
        self.t = t
        self.name = name
        self.w = None
        self.r = []
        self.dkey = None

    def __getitem__(self, k):
        return self.t[k]


TABLE_AWARE = False


class _Rec:
    def __init__(self):
        self.call = None

    def __getattr__(self, name):
        def f(*a, **k):
            self.call = (name, a, k)
            return self
        return f


def _fsize(ap):
    try:
        return int(ap.free_size())
    except Exception:
        return 128


def _nbytes(ap):
    try:
        return int(ap.nbytes())
    except Exception:
        return 65536


class KB:
    DEFER = True

    def __init__(self):
        self.nc = bass.Bass("TRN2", target_bir_lowering=False)
        nc = self.nc
        self.es = contextlib.ExitStack()
        self.eng = {"pe": nc.tensor, "act": nc.scalar, "dve": nc.vector,
                    "pool": nc.gpsimd, "sp": nc.sync}
        self.semh = {}
        self.cnt = {}
        self.seen = {e: {} for e in self.eng}
        for e in ("pe", "act", "dve", "pool"):
            self.semh[e] = self.es.enter_context(nc.semaphore("s_" + e))
            self.cnt[e] = 0
        self.nbuf = 0
        self.pend = []
        self.scope = contextlib.ExitStack()
        self.scopes = []
        self.dma_keys = []
        self.nins = 0

    def sb(self, shape, dt, name=None):
        self.nbuf += 1
        name = f"sb{self.nbuf}_" + (name or "b")
        t = self.scope.enter_context(self.nc.sbuf_tensor(name, list(shape), dt))
        return Buf(t, name)

    def push(self):
        self.scopes.append(self.scope)
        self.scope = contextlib.ExitStack()

    def pop(self):
        self.barrier()
        self.scope.close()
        self.scope = self.scopes.pop()

    def new_scope(self):
        self.barrier()
        self.scope.close()
        self.scope = contextlib.ExitStack()

    def ps(self, shape, dt, name=None):
        self.nbuf += 1
        name = f"ps{self.nbuf}_" + (name or "p")
        t = self.es.enter_context(self.nc.psum_tensor(name, list(shape), dt))
        return Buf(t, name)

    def dram(self, name, shape, dt, kind):
        t = self.nc.dram_tensor(name, list(shape), dt, kind=kind)
        return Buf(t.ap(), name)

    def _deps(self, R, W):
        need = {}
        for b in R:
            if b.w is not None:
                k, c = b.w
                need[k] = max(need.get(k, 0), c)
        for b in W:
            if b.w is not None:
                k, c = b.w
                need[k] = max(need.get(k, 0), c)
            for (k, c) in b.r:
                need[k] = max(need.get(k, 0), c)
        return need

    def _wait(self, e, need):
        E = self.eng[e]
        seen = self.seen[e]
        for k, c in need.items():
            if k == e and e == "pe":
                continue
            if seen.get(k, 0) >= c:
                continue
            E.wait_ge(self.semh[k], c)
            seen[k] = c

    def _mark(self, tok, R, W):
        for b in R:
            b.r.append(tok)
            if len(b.r) > 64:
                d = {}
                for k, c in b.r:
                    d[k] = max(d.get(k, 0), c)
                b.r = list(d.items())
        for b in W:
            b.w = tok
            b.r = []

    def op(self, e, fn, R=(), W=()):
        if self.DEFER:
            rec = _Rec()
            fn(rec)
            name, a, k = rec.call
            out = k.get("out", a[0] if a else None)
            n = _fsize(out) if out is not None else 128
            if e == "pe":
                if name == "transpose":
                    c = 0.07
                else:
                    c = 0.03 + max(n, 64) / 2400.0
                    l = k.get("lhsT")
                    if l is not None and l.dtype == F32:
                        c *= 4
            elif e == "dve":
                c = 0.06 + n / 960.0
            elif e == "act":
                c = 0.2 + n / 1200.0
            else:
                c = 0.1 + n / 480.0
            tb = 0
            if e == "act":
                fnc = k.get("func")
                if fnc == AF.Ln:
                    tb = 1
                elif fnc == AF.Tanh:
                    tb = 2
            self.pend.append(("op", e, rec.call, tuple(R), tuple(W), c, c, tb))
            return None
        return self._op_now(e, fn, R, W)

    def _op_now(self, e, fn, R=(), W=()):
        self._wait(e, self._deps(R, W))
        ins = fn(self.eng[e])
        self.cnt[e] += 1
        ins.then_inc(self.semh[e], 1)
        self._mark((e, self.cnt[e]), R, W)
        self.nins += 1
        return ins

    def dma(self, q, out, in_, R=(), W=(), key=None, **kw):
        if self.DEFER:
            lat = 1.5 + _nbytes(out) / 150000.0
            W = tuple(W) if any(b is key for b in W) else tuple(W) + (key,)
            self.pend.append(("dma", q, (out, in_, key, kw), tuple(R), W, 0.06, lat))
            return None
        return self._dma_now(q, out, in_, R, W, key, **kw)

    def _dma_now(self, q, out, in_, R=(), W=(), key=None, **kw):
        if key.dkey is None:
            key.dkey = "d_" + key.name
            self.semh[key.dkey] = self.es.enter_context(self.nc.semaphore(key.dkey))
            self.cnt[key.dkey] = 0
            self.dma_keys.append(key.dkey)
        self._wait(q, self._deps(R, W))
        ins = self.eng[q].dma_start(out=out, in_=in_, **kw)
        self.cnt[key.dkey] += 16
        ins.then_inc(self.semh[key.dkey], 16)
        self._mark((key.dkey, self.cnt[key.dkey]), R, W)
        self.nins += 1
        return ins

    def dma_multi(self, q, pairs, R=(), W=(), key=None):
        if self.DEFER:
            nb = sum(_nbytes(o) for o, _ in pairs)
            W = tuple(W) if any(b is key for b in W) else tuple(W) + (key,)
            self.pend.append(("dmam", q, (list(pairs), key), tuple(R), W, 0.06 * len(pairs), 1.5 + nb / 150000.0))
            return None
        return self._dma_multi_now(q, pairs, R, W, key)

    def _dma_multi_now(self, q, pairs, R=(), W=(), key=None):
        if key.dkey is None:
            key.dkey = "d_" + key.name
            self.semh[key.dkey] = self.es.enter_context(self.nc.semaphore(key.dkey))
            self.cnt[key.dkey] = 0
            self.dma_keys.append(key.dkey)
        self._wait(q, self._deps(R, W))
        for (out, in_) in pairs:
            ins = self.eng[q].dma_start(out=out, in_=in_)
            self.cnt[key.dkey] += 16
            ins.then_inc(self.semh[key.dkey], 16)
            self.nins += 1
        self._mark((key.dkey, self.cnt[key.dkey]), R, W)

    def ind_dma(self, out, in_, idx_ap, nrows, R=(), W=(), key=None):
        q = "pool"
        if key.dkey is None:
            key.dkey = "d_" + key.name
            self.semh[key.dkey] = self.es.enter_context(self.nc.semaphore(key.dkey))
            self.cnt[key.dkey] = 0
            self.dma_keys.append(key.dkey)
        self._wait(q, self._deps(R, W))
        ins = self.eng[q].indirect_dma_start(out=out, out_offset=None, in_=in_,
                                             in_offset=bass.IndirectOffsetOnAxis(ap=idx_ap, axis=0),
                                             bounds_check=nrows - 1, oob_is_err=False)
        self.cnt[key.dkey] += 16
        ins.then_inc(self.semh[key.dkey], 16)
        self._mark((key.dkey, self.cnt[key.dkey]), R, W)
        self.nins += 1

    def ind_dma_multi(self, items, in_, nrows, R=(), W=(), key=None):
        if self.DEFER:
            nb = sum(_nbytes(o) for o, _, _ in items)
            W = tuple(W) if any(b is key for b in W) else tuple(W) + (key,)
            self.pend.append(("indm", "pool", (list(items), nrows, key), tuple(R), W, 1.0 * len(items), 3.0 + nb / 150000.0))
            return None
        return self._ind_dma_multi_now(items, in_, nrows, R, W, key)

    def _ind_dma_multi_now(self, items, in_, nrows, R=(), W=(), key=None):
        q = "pool"
        if key.dkey is None:
            key.dkey = "d_" + key.name
            self.semh[key.dkey] = self.es.enter_context(self.nc.semaphore(key.dkey))
            self.cnt[key.dkey] = 0
            self.dma_keys.append(key.dkey)
        self._wait(q, self._deps(R, W))
        for (out, idx_ap, src) in items:
            ins = self.eng[q].indirect_dma_start(out=out, out_offset=None, in_=src,
                                                 in_offset=bass.IndirectOffsetOnAxis(ap=idx_ap, axis=0),
                                                 bounds_check=nrows - 1, oob_is_err=False)
            self.cnt[key.dkey] += 16
            ins.then_inc(self.semh[key.dkey], 16)
            self.nins += 1
        self._mark((key.dkey, self.cnt[key.dkey]), R, W)

    def all_gather(self, in_buf, out_buf, groups):
        key = "cc_" + out_buf.name
        self.semh[key] = self.es.enter_context(self.nc.semaphore(key))
        self.cnt[key] = 0
        self.dma_keys.append(key)
        self._wait("pool", self._deps([in_buf], [out_buf]))
        ins = self.eng["pool"].collective_compute("AllGather", ALU.bypass, replica_groups=groups,
                                                  ins=[in_buf.t], outs=[out_buf.t])
        self.cnt[key] += 1
        ins.then_inc(self.semh[key], 1)
        self._mark((key, 1), [in_buf], [out_buf])
        self.nins += 1

    def flush(self):
        P = self.pend
        self.pend = []
        M = len(P)
        if M == 0:
            return
        lastw = {}
        readers = {}
        deps = [None] * M
        succ = [[] for _ in range(M)]
        for j, rec_ in enumerate(P):
            e, R, W = rec_[1], rec_[3], rec_[4]
            d = set()
            for b in R:
                i = lastw.get(id(b))
                if i is not None:
                    d.add(i)
            for b in W:
                i = lastw.get(id(b))
                if i is not None:
                    d.add(i)
                for i in readers.get(id(b), ()):
                    d.add(i)
            d.discard(j)
            deps[j] = d
            for i in d:
                succ[i].append(j)
            for b in R:
                readers.setdefault(id(b), []).append(j)
            for b in W:
                lastw[id(b)] = j
                readers[id(b)] = []
        tail = [0.0] * M
        for j in range(M - 1, -1, -1):
            t = 0.0
            for k in succ[j]:
                if tail[k] > t:
                    t = tail[k]
            tail[j] = t + P[j][6]
        ndep = [len(d) for d in deps]
        dr = [0.0] * M
        fin = [0.0] * M
        free = {}
        cand = [j for j in range(M) if ndep[j] == 0]
        order = []
        LOOK = 96
        acttab = 0
        import heapq
        heapq.heapify(cand)
        pool = []
        while cand or pool:
            while cand and len(pool) < LOOK:
                pool.append(heapq.heappop(cand))
            best = None
            bk = None
            for j in pool:
                e = P[j][1]
                st = dr[j]
                f = free.get(e, 0.0)
                if f > st:
                    st = f
                if TABLE_AWARE and e == "act" and len(P[j]) > 7 and P[j][7] and P[j][7] != acttab:
                    st += 1.3
                key = (round(st, 2), -tail[j], j)
                if bk is None or key < bk:
                    bk = key
                    best = j
            pool.remove(best)
            j = best
            e = P[j][1]
            st = max(dr[j], free.get(e, 0.0))
            if TABLE_AWARE and e == "act" and len(P[j]) > 7 and P[j][7]:
                if P[j][7] != acttab:
                    st += 1.3
                acttab = P[j][7]
            free[e] = st + P[j][5]
            fin[j] = st + P[j][6]
            order.append(j)
            for k in succ[j]:
                t = fin[j] + (0.0 if (P[k][1] == e and P[j][0] == "op") else 0.1)
                if t > dr[k]:
                    dr[k] = t
                ndep[k] -= 1
                if ndep[k] == 0:
                    heapq.heappush(cand, k)
        self.est = getattr(self, "est", 0.0) + max(fin) if fin else 0.0
        sv = self.DEFER
        self.DEFER = False
        try:
            for j in order:
                kind, e, pay, R, W = P[j][:5]
                if kind == "op":
                    name, a, k = pay
                    self._op_now(e, lambda eng: getattr(eng, name)(*a, **k), R, W)
                elif kind == "dma":
                    out, in_, key, kw = pay
                    self._dma_now(e, out, in_, R, W, key, **kw)
                elif kind == "dmam":
                    pairs, key = pay
                    self._dma_multi_now(e, pairs, R, W, key)
                else:
                    items, nrows, key = pay
                    self._ind_dma_multi_now(items, None, nrows, R, W, key)
        finally:
            self.DEFER = sv

    def barrier(self):
        self.flush()
        for e in self.eng:
            need = {k: c for k, c in self.cnt.items() if c > 0 and k != e}
            self._wait(e, need)

    def finish(self):
        self.flush()
        need = {k: self.cnt[k] for k in self.dma_keys if self.cnt[k] > 0}
        self._wait("sp", need)


import numpy as np

NH = 4
DK = 128
EPS = 1e-6
NLEV = 7


def host_consts():
    c = {}
    c["ident"] = np.eye(128, dtype=np.float32)
    i = np.arange(128)
    c["U"] = (i[:, None] <= i[None, :]).astype(np.float32)
    mi = (i[None, :] >= i[:, None]).astype(np.float32)
    ms = np.where(i[None, :] > i[:, None], 0.0, -30000.0).astype(np.float32)
    c["maskUi"] = np.tile(mi[:, None, :], (1, NH, 1)).copy()
    c["maskUs"] = np.tile(ms[:, None, :], (1, NH, 1)).copy()
    lm = np.zeros((128, NLEV, NH, 128), np.float32)
    for l in range(NLEV):
        s = 1 << l
        bi = i // s
        m = ((bi[:, None] % 2 == 1) & (bi[None, :] == bi[:, None] - 1)).astype(np.float32)
        lm[:, l, :, :] = m[:, None, :]
    c["lmask"] = lm
    c["negm"] = np.where(i[None, :] >= i[:, None], 0.0, -30000.0).astype(np.float32)
    return c


def phase_a(kb, x_d, wa_d, na_d, cw_d, alog_d, dtb_d, ong_d, cst, o_scr, ssm_d, convo_d, NSB, psT, psF, row0=0, smp=None, col0=0):
    nc = kb.nc
    NF = 4 * NH
    NCOL = NF * 128 + 2 * NH
    SBT = 512

    ident_f = kb.sb([128, 128], F32, "ident_f")
    ident_b = kb.sb([128, 128], BF16, "ident_b")
    ones_b = kb.sb([128, 128], BF16, "ones_b")
    ones_f = kb.sb([128, 128], F32, "ones_f")
    U_f = kb.sb([128, 128], F32, "U_f")
    mUs = kb.sb([128, NH, 128], F32, "mUs")
    lmask = kb.sb([128, NLEV, NH, 128], BF16, "lmask")
    cbias = kb.sb([128, 8], F32, "cbias")
    na = kb.sb([128, 8], F32, "na")
    cw = kb.sb([128, 3 * NH, 4], F32, "cw")
    negA = kb.sb([128, NH], F32, "negA")
    dtb = kb.sb([128, NH], F32, "dtb")
    ong = kb.sb([128, 1], F32, "ong")
    cload = kb.sb([128, 1], F32, "cload")
    negm_f = kb.sb([128, 128], F32, "negm_f")
    negm = kb.sb([128, 128], BF16, "negm")
    negms = kb.sb([128, 128], BF16, "negms")
    identb4 = kb.sb([128, NH, 128], BF16, "identb4")

    lds = []

    def ld(dst, src):
        lds.append((dst, src))
    ld(ident_f, cst["ident"][:, :]); ld(U_f, cst["U"][:, :])
    ld(mUs, cst["maskUs"][:, :, :])
    ld(na, na_d[:, :]); ld(cw, cw_d[:, :, :]); ld(negA, alog_d[:, :]); ld(dtb, dtb_d[:, :])
    ld(ong, ong_d[:, :]); ld(negm_f, cst["negm"][:, :])
    kb.dma_multi("sp", [(d_[:], s_) for d_, s_ in lds], W=[d_ for d_, _ in lds], key=cload)
    kb.op("dve", lambda e: e.tensor_copy(ident_b[:], ident_f[:]), R=[ident_f], W=[ident_b])
    kb.op("pool", lambda e: e.memset(ones_b[:], 1.0), W=[ones_b])
    kb.op("dve", lambda e: e.tensor_copy(negm[:], negm_f[:]), R=[negm_f], W=[negm])
    U_b = kb.sb([128, 128], BF16, "U_b")
    kb.op("dve", lambda e: e.tensor_copy(U_b[:], U_f[:]), R=[U_f], W=[U_b])
    kb.op("dve", lambda e: e.tensor_copy(negms[:], mUs[:, 0, :]), R=[mUs], W=[negms])
    for h in range(NH):
        kb.op("dve", lambda e, h=h: e.tensor_copy(identb4[:, h, :], ident_f[:]), R=[ident_f], W=[identb4])
    kb.op("pool", lambda e: e.memset(ones_f[:], 1.0), W=[ones_f])
    kb.push()
    lmask_f = kb.sb([128, NLEV, NH, 128], F32, "lmask_f")
    kb.dma("sp", lmask_f[:], cst["lmask"][:, :, :, :], W=[lmask_f], key=lmask_f)
    kb.op("dve", lambda e: e.tensor_copy(lmask[:], lmask_f[:]), R=[lmask_f], W=[lmask])
    kb.pop()
    for j, v in enumerate([4 * EPS, 4 * EPS * 128, EPS, 1.0, 0.0]):
        kb.op("pool", lambda e, j=j, v=v: e.memset(cbias[:, j:j + 1], v), W=[cbias])
    kb.op("act", lambda e: e.activation(out=negA[:], in_=negA[:], func=AF.Exp), R=[negA], W=[negA])
    kb.op("dve", lambda e: e.tensor_scalar(negA[:], negA[:], -1.0, None, op0=ALU.mult), R=[negA], W=[negA])
    kb.op("dve", lambda e: e.tensor_scalar(ong[:], ong[:], 0.5, None, op0=ALU.mult), R=[ong], W=[ong])

    W = kb.sb([128, 8, NCOL], BF16, "Wa")
    kb.push()
    stg = [kb.sb([128, NCOL], F32, f"stg{i}") for i in range(2)]
    wa_v = wa_d.rearrange("(kc p) n -> p kc n", p=128)
    for kc in range(8):
        s = stg[kc % 2]
        kb.dma(("sp", "act")[kc % 2], s[:], wa_v[:, kc, :], W=[s], key=s)
        eng = "act" if kc % 2 == 0 else "dve"
        if eng == "act":
            kb.op("act", lambda e, kc=kc, s=s: e.activation(out=W[:, kc, :], in_=s[:], func=AF.Copy,
                                                         scale=na[:, kc:kc + 1]), R=[s, na], W=[W])
        else:
            kb.op("dve", lambda e, kc=kc, s=s: e.tensor_scalar(W[:, kc, :], s[:], na[:, kc:kc + 1], None,
                                                            op0=ALU.mult), R=[s, na], W=[W])

    pfi = [0]

    def PS(ring=0):
        p = psF[pfi[0] % len(psF)]
        pfi[0] += 1
        return p

    if smp is not None:
        zt = kb.sb([128, NH, 128], BF16, "zpad")
        kb.op("pool", lambda e: e.memset(zt[:], 0.0), W=[zt])
        kb.dma_multi("sp", [(dst, zt[:]) for dst in o_scr.loc(0)] + [(o_scr.sloc(g_), zt[:]) for g_ in range(2)],
                     R=[zt], W=[o_scr], key=zt)
        sample_a(kb, smp, locals(), PS, psT, row0)
    kb.pop()

    xt = [kb.sb([128, 1024], F32, f"xt{i}") for i in range(3)]
    xs = [kb.sb([128, 1024], BF16, f"xs{i}") for i in range(2)]
    rr = kb.sb([128, 4], F32, "rr")
    junk = kb.sb([128, 1024], BF16, "junk")
    xsT_1 = kb.sb([128, 8, SBT], BF16, "xsT")
    xsT_2 = [xsT_1, xsT_1]
    pre = kb.sb([128, 3 * NH, SBT + 3], F32, "pre")
    preb = [Buf(pre.t, f"pre{i}") for i in range(3 * NH)]
    acc = [kb.sb([128, SBT], F32, f"acc{i}") for i in range(2)]
    tnh = [kb.sb([128, SBT], F32, f"tnh{i}") for i in range(2)]
    qkv_2 = [kb.sb([128, 3 * NH, SBT], BF16, f"qkv{i_}") for i_ in range(2)]
    qb_2 = [[Buf(qkv_2[i_].t, f"qkv{i_}_{i}") for i in range(3 * NH)] for i_ in range(2)]
    gs_2 = [kb.sb([128, NH, SBT], BF16, f"gs{i_}") for i_ in range(2)]
    sqb = [kb.sb([128, SBT], BF16, f"sqb{i}") for i in range(2)]
    rb = [kb.sb([128, SBT], F32, f"rb{i}") for i in range(2)]
    ab_2 = [kb.sb([128, 4, 2 * NH], F32, f"ab{i_}") for i_ in range(2)]
    t1_2 = [kb.sb([128, 4, NH], F32, f"t1{i_}") for i_ in range(2)]
    gg_2 = [kb.sb([128, 4, NH], F32, f"gg{i_}") for i_ in range(2)]
    gh16_2 = [kb.sb([128, 4, NH], BF16, f"gh16{i_}") for i_ in range(2)]
    gh32_2 = [kb.sb([128, 4, NH], F32, f"gh32{i_}") for i_ in range(2)]
    gl32_2 = [kb.sb([128, 4, NH], F32, f"gl32{i_}") for i_ in range(2)]
    lnb_2 = [kb.sb([128, 4, NH], F32, f"lnb{i_}") for i_ in range(2)]
    beta_2 = [kb.sb([128, 4, NH], F32, f"beta{i_}") for i_ in range(2)]
    Gs_2 = [kb.sb([128, 2 * NH], F32, f"Gs{i_}") for i_ in range(2)]
    negG_2 = [kb.sb([128, NH], F32, f"negG{i_}") for i_ in range(2)]
    nGb_2 = [kb.sb([128, NH], F32, f"nGb{i_}") for i_ in range(2)]
    negeG_2 = [kb.sb([128, NH], F32, f"negeG{i_}") for i_ in range(2)]
    kdsc_2 = [kb.sb([128, NH], F32, f"kdsc{i_}") for i_ in range(2)]
    gtot_2 = [kb.sb([128, NH], F32, f"gtot{i_}") for i_ in range(2)]
    gB_2 = [kb.sb([128, 2, NH, 128], BF16, f"gB{i_}") for i_ in range(2)]
    E_2 = [kb.sb([128, NH, 128], F32, f"E{i_}") for i_ in range(2)]
    Eb_2 = [kb.sb([128, NH, 128], F32, f"Eb{i_}") for i_ in range(2)]
    eGb_2 = [kb.sb([128, NH, 128], F32, f"eGb{i_}") for i_ in range(2)]
    MT_2 = [kb.sb([128, NH, 128], BF16, f"MT{i_}") for i_ in range(2)]
    qkT_2 = [kb.sb([128, NH, 128], BF16, f"qkT{i_}") for i_ in range(2)]
    qdT_2 = [kb.sb([128, NH, 128], BF16, f"qdT{i_}") for i_ in range(2)]
    T_2 = [kb.sb([128, NH, 128], BF16, f"T{i_}") for i_ in range(2)]
    TT_2 = [kb.sb([128, NH, 128], BF16, f"TT{i_}") for i_ in range(2)]
    Pm_2 = [kb.sb([128, NH, 128], BF16, f"Pm{i_}") for i_ in range(2)]
    kd_2 = [kb.sb([128, NH, 128], BF16, f"kd{i_}") for i_ in range(2)]
    vtok_2 = [kb.sb([128, NH, 128], F32, f"vtok{i_}") for i_ in range(2)]
    Rb_2 = [kb.sb([128, NH, 128], BF16, f"Rb{i_}") for i_ in range(2)]
    vnew_2 = [kb.sb([128, NH, 128], BF16, f"vnew{i_}") for i_ in range(2)]
    S32 = kb.sb([128, NH, 128], F32, "S32")
    Sbf = kb.sb([128, NH, 128], BF16, "Sbf")
    osq_2 = [kb.sb([128, NH, 128], BF16, f"osq{i_}") for i_ in range(2)]
    rinv_2 = [kb.sb([128, NH, 128], F32, f"rinv{i_}") for i_ in range(2)]
    otmp_2 = [kb.sb([128, NH, 128], F32, f"otmp{i_}") for i_ in range(2)]
    oTf = [kb.sb([128, NH, SBT], BF16, f"oTf{i}") for i in range(2)]

    def v3(p):
        return p.t[:, :].rearrange("p (a b) -> p a b", a=NH)

    kb.op("pool", lambda e: e.memset(pre[:], 0.0), W=preb)
    kb.op("pool", lambda e: e.memset(S32[:], 0.0), W=[S32])
    kb.op("pool", lambda e: e.memset(Sbf[:], 0.0), W=[Sbf])

    def bc(buf, blk=None):
        a = buf[:, :] if blk is None else buf[:, blk, :]
        return a.unsqueeze(2).to_broadcast([128, NH, 128])

    for sbi in range(NSB):
        tok0 = sbi * SBT
        sp_ = sbi % 2
        xsT, ab, t1, gg, lnb, beta, gs = (xsT_2[sp_], ab_2[sp_], t1_2[sp_], gg_2[sp_], lnb_2[sp_], beta_2[sp_], gs_2[sp_])
        qkv, qb = qkv_2[sp_], qb_2[sp_]
        gh16, gh32, gl32 = gh16_2[sp_], gh32_2[sp_], gl32_2[sp_]
        for b4 in range(4):
            xb = xt[(sbi * 4 + b4) % 3]
            xsb = xs[b4 % 2]
            kb.dma("sp", xb[:], x_d[tok0 + b4 * 128: tok0 + (b4 + 1) * 128, :], W=[xb], key=xb)
            kb.op("act", lambda e: e.activation(out=junk[:], in_=xb[:], func=AF.Square,
                                                accum_out=rr[:, 0:1]), R=[xb], W=[junk, rr])
            kb.op("act", lambda e: e.activation(out=rr[:, 1:2], in_=rr[:, 0:1], func=AF.Ln,
                                                scale=1.0 / 1024, bias=cbias[:, 2:3]), R=[rr, cbias], W=[rr])
            kb.op("act", lambda e: e.activation(out=rr[:, 2:3], in_=rr[:, 1:2], func=AF.Exp, scale=-0.5), R=[rr], W=[rr])
            kb.op("act", lambda e: e.activation(out=xsb[:], in_=xb[:], func=AF.Copy, scale=rr[:, 2:3]),
                  R=[xb, rr], W=[xsb])
            pt = psT[b4 % 2]
            ptB = pt.t[:, :]
            for kc in range(8):
                kb.op("pe", lambda e, kc=kc: e.transpose(out=ptB[:, kc * 128:(kc + 1) * 128],
                                                         in_=xsb[:, kc * 128:(kc + 1) * 128],
                                                         identity=ident_b[:]), R=[xsb, ident_b], W=[pt])
            ptv = ptB.rearrange("p (k t) -> p k t", k=8)
            kb.op("act", lambda e: e.activation(out=xsT[:, :, b4 * 128:(b4 + 1) * 128], in_=ptv, func=AF.Copy),
                  R=[pt], W=[xsT])
        pab = PS()
        for b4 in range(4):
            for kc in range(8):
                kb.op("pe", lambda e, kc=kc: e.matmul(pab[:, b4 * 2 * NH:(b4 + 1) * 2 * NH],
                                                     lhsT=xsT[:, kc, b4 * 128:(b4 + 1) * 128],
                                                     rhs=W[:, kc, NF * 128:NF * 128 + 2 * NH],
                                                     start=(kc == 0), stop=(kc == 7)), R=[xsT, W], W=[pab])
        kb.op("dve", lambda e: e.tensor_copy(ab[:], pab[:, 0:8 * NH].rearrange("p (a b) -> p a b", a=4)),
              R=[pab], W=[ab])
        kb.op("dve", lambda e: e.tensor_tensor(t1[:], ab[:, :, 0:NH],
                                               dtb[:, :].unsqueeze(1).to_broadcast([128, 4, NH]), op=ALU.add),
              R=[ab, dtb], W=[t1])
        kb.op("act", lambda e: e.activation(out=t1[:], in_=t1[:], func=AF.Exp), R=[t1], W=[t1])
        kb.op("act", lambda e: e.activation(out=lnb[:], in_=ab[:, :, NH:2 * NH], func=AF.Exp, scale=-1.0),
              R=[ab], W=[lnb])
        kb.op("act", lambda e: e.activation(out=t1[:], in_=t1[:], func=AF.Ln, bias=cbias[:, 3:4]),
              R=[t1, cbias], W=[t1])
        kb.op("act", lambda e: e.activation(out=lnb[:], in_=lnb[:], func=AF.Ln, bias=cbias[:, 3:4]),
              R=[lnb, cbias], W=[lnb])
        kb.op("dve", lambda e: e.tensor_tensor(gg[:], t1[:], negA[:, :].unsqueeze(1).to_broadcast([128, 4, NH]),
                                               op=ALU.mult), R=[t1, negA], W=[gg])
        kb.op("dve", lambda e: e.tensor_scalar(lnb[:], lnb[:], -1.0, None, op0=ALU.mult), R=[lnb], W=[lnb])
        kb.op("dve", lambda e: e.tensor_copy(gh16[:], gg[:]), R=[gg], W=[gh16])
        kb.op("dve", lambda e: e.tensor_copy(gh32[:], gh16[:]), R=[gh16], W=[gh32])
        kb.op("dve", lambda e: e.tensor_tensor(gl32[:], gg[:], gh32[:], op=ALU.subtract), R=[gg, gh32], W=[gl32])
        kb.op("act", lambda e: e.activation(out=beta[:], in_=lnb[:], func=AF.Exp), R=[lnb], W=[beta])

        for ft in range(NF):
            pp = PS()
            for kc in range(8):
                kb.op("pe", lambda e, kc=kc: e.matmul(pp[:, :], lhsT=W[:, kc, ft * 128:(ft + 1) * 128],
                                                     rhs=xsT[:, kc, :], start=(kc == 0), stop=(kc == 7)),
                      R=[W, xsT], W=[pp])
            a_ = acc[ft % 2]
            t_ = tnh[ft % 2]
            if ft < 3 * NH:
                kb.op("act", lambda e: e.activation(out=pre[:, ft, 3:SBT + 3], in_=pp[:, :], func=AF.Copy),
                      R=[pp], W=[preb[ft]])
                kb.op("act", lambda e: e.activation(out=a_[:], in_=pp[:, :], func=AF.Copy,
                                                    scale=cw[:, ft, 3:4]), R=[pp, cw], W=[a_])
                for tap in range(3):
                    eng = "dve"
                    kb.op(eng, lambda e, tap=tap: e.scalar_tensor_tensor(
                        out=a_[:], in0=pre[:, ft, tap:tap + SBT], scalar=cw[:, ft, tap:tap + 1], in1=a_[:],
                        op0=ALU.mult, op1=ALU.add), R=[preb[ft], cw, a_], W=[a_])
                kb.op("act", lambda e: e.activation(out=pre[:, ft, 0:3], in_=pre[:, ft, SBT:SBT + 3], func=AF.Copy),
                      R=[preb[ft]], W=[preb[ft]])
                kb.op("act", lambda e: e.activation(out=t_[:], in_=a_[:], func=AF.Tanh, scale=0.5),
                      R=[a_], W=[t_])
                kb.op("dve", lambda e: e.scalar_tensor_tensor(out=qkv[:, ft, :], in0=t_[:], scalar=1.0,
                                                              in1=a_[:], op0=ALU.add, op1=ALU.mult),
                      R=[t_, a_], W=[qb[ft]])
            else:
                h = ft - 3 * NH
                kb.op("act", lambda e: e.activation(out=t_[:], in_=pp[:, :], func=AF.Tanh, scale=0.5),
                      R=[pp], W=[t_])
                kb.op("dve", lambda e: e.scalar_tensor_tensor(out=gs[:, h, :], in0=t_[:], scalar=1.0,
                                                              in1=pp[:, :], op0=ALU.add, op1=ALU.mult),
                      R=[t_, pp], W=[gs])
        for ft in range(2 * NH):
            s_ = sqb[ft % 2]
            r_ = rb[ft % 2]
            pn = PS()
            kb.op("act", lambda e: e.activation(out=s_[:], in_=qkv[:, ft, :], func=AF.Square), R=[qb[ft]], W=[s_])
            kb.op("pe", lambda e: e.matmul(pn[:, :], lhsT=ones_b[:], rhs=s_[:], start=True, stop=True),
                  R=[ones_b, s_], W=[pn])
            isq = ft < NH
            kb.op("act", lambda e: e.activation(out=r_[:], in_=pn[:, :], func=AF.Ln,
                                                scale=(128.0 if isq else 1.0),
                                                bias=cbias[:, 1:2] if isq else cbias[:, 0:1]),
                  R=[pn, cbias], W=[r_])
            kb.op("act", lambda e: e.activation(out=r_[:], in_=r_[:], func=AF.Exp, scale=-0.5), R=[r_], W=[r_])
            kb.op("dve", lambda e: e.tensor_tensor(qkv[:, ft, :], qkv[:, ft, :], r_[:], op=ALU.mult),
                  R=[qb[ft], r_], W=[qb[ft]])

        for b4 in range(4):
            c0 = b4 * 128
            cs = slice(c0, c0 + 128)
            cp_ = (sbi * 4 + b4) % 2
            (Gs, negG, nGb, negeG, kdsc, gtot, gB, E, Eb, eGb, MT, qkT, qdT, T, TT, Pm, kd, vtok, Rb, vnew, osq, rinv, otmp) = (
                Gs_2[cp_], negG_2[cp_], nGb_2[cp_], negeG_2[cp_], kdsc_2[cp_], gtot_2[cp_], gB_2[cp_], E_2[cp_], Eb_2[cp_], eGb_2[cp_],
                MT_2[cp_], qkT_2[cp_], qdT_2[cp_], T_2[cp_], TT_2[cp_], Pm_2[cp_], kd_2[cp_], vtok_2[cp_], Rb_2[cp_], vnew_2[cp_],
                osq_2[cp_], rinv_2[cp_], otmp_2[cp_])
            ptk = psT[(sbi * 4 + b4) % 2]
            ptv_ = ptk
            ptkB = ptk.t[:, :]
            for h in range(NH):
                kb.op("pe", lambda e, h=h: e.transpose(out=ptkB[:, h * 128:(h + 1) * 128],
                                                       in_=qkv[:, NH + h, cs], identity=ident_b[:]),
                      R=[qb[NH + h], ident_b], W=[ptk])
            for h in range(NH):
                kb.op("pe", lambda e, h=h: e.transpose(out=ptkB[:, (NH + h) * 128:(NH + h + 1) * 128],
                                                       in_=qkv[:, 2 * NH + h, cs], identity=ident_b[:]),
                      R=[qb[2 * NH + h], ident_b], W=[ptv_])
            pg = PS()
            kb.op("pe", lambda e: e.matmul(pg[:, 0:NH], lhsT=U_f[:], rhs=gg[:, b4, :], start=True, stop=True),
                  R=[U_f, gg], W=[pg])
            kb.op("pe", lambda e: e.matmul(pg[:, NH:2 * NH], lhsT=ones_f[:], rhs=gg[:, b4, :], start=True, stop=True),
                  R=[ones_f, gg], W=[pg])
            kb.op("dve", lambda e: e.tensor_copy(Gs[:], pg[:, 0:2 * NH]), R=[pg], W=[Gs])
            kb.op("dve", lambda e: e.tensor_scalar(negG[:], Gs[:, 0:NH], -1.0, None, op0=ALU.mult), R=[Gs], W=[negG])
            kb.op("dve", lambda e: e.tensor_tensor(nGb[:], lnb[:, b4, :], Gs[:, 0:NH], op=ALU.subtract),
                  R=[lnb, Gs], W=[nGb])
            kb.op("act", lambda e: e.activation(out=negeG[:], in_=Gs[:, 0:NH], func=AF.Exp), R=[Gs], W=[negeG])
            kb.op("dve", lambda e: e.tensor_scalar(negeG[:], negeG[:], -1.0, None, op0=ALU.mult), R=[negeG], W=[negeG])
            kb.op("dve", lambda e: e.tensor_tensor(kdsc[:], Gs[:, NH:2 * NH], Gs[:, 0:NH], op=ALU.subtract),
                  R=[Gs], W=[kdsc])
            kb.op("act", lambda e: e.activation(out=kdsc[:], in_=kdsc[:], func=AF.Exp), R=[kdsc], W=[kdsc])
            kb.op("act", lambda e: e.activation(out=gtot[:], in_=Gs[:, NH:2 * NH], func=AF.Exp), R=[Gs], W=[gtot])
            for h in range(NH):
                kb.op("act", lambda e, h=h: e.activation(out=kd[:, h, :], in_=ptkB[:, h * 128:(h + 1) * 128], func=AF.Copy,
                                                        scale=kdsc[:, h:h + 1]), R=[ptk, kdsc], W=[kd])
            kb.op("act", lambda e: e.activation(out=vtok[:], in_=ptkB[:, NH * 128:2 * NH * 128].rearrange("p (a b) -> p a b", a=NH),
                                                func=AF.Copy, scale=0.5), R=[ptv_], W=[vtok])
            for h in range(NH):
                kb.op("act", lambda e, h=h: e.activation(out=gB[:, 0, h, :], in_=ones_f[:], func=AF.Copy, scale=gh32[:, b4, h:h + 1]),
                      R=[ones_f, gh32], W=[gB])
                kb.op("act", lambda e, h=h: e.activation(out=gB[:, 1, h, :], in_=ones_f[:], func=AF.Copy, scale=gl32[:, b4, h:h + 1]),
                      R=[ones_f, gl32], W=[gB])
            pgb = PS()
            pgm = PS()
            for h in range(NH):
                kb.op("pe", lambda e, h=h: e.matmul(pgb[:, h * 128:(h + 1) * 128], lhsT=gB[:, 0, h, :], rhs=U_b[:],
                                                   start=True, stop=False), R=[gB, U_b], W=[pgb])
                kb.op("pe", lambda e, h=h: e.matmul(pgb[:, h * 128:(h + 1) * 128], lhsT=gB[:, 1, h, :], rhs=U_b[:],
                                                   start=False, stop=True), R=[gB, U_b], W=[pgb])
            for h in range(NH):
                kb.op("pe", lambda e, h=h: e.matmul(pgm[:, h * 128:(h + 1) * 128], lhsT=gB[:, 0, h, :], rhs=U_b[:],
                                                   start=True, stop=False), R=[gB, U_b], W=[pgm])
                kb.op("pe", lambda e, h=h: e.matmul(pgm[:, h * 128:(h + 1) * 128], lhsT=gB[:, 1, h, :], rhs=U_b[:],
                                                   start=False, stop=False), R=[gB, U_b], W=[pgm])
                kb.op("pe", lambda e, h=h: e.matmul(pgm[:, h * 128:(h + 1) * 128], lhsT=ident_b[:], rhs=negm[:],
                                                   start=False, stop=True), R=[ident_b, negm], W=[pgm])
            pgs = PS()
            for h in range(NH):
                kb.op("pe", lambda e, h=h: e.matmul(pgs[:, h * 128:(h + 1) * 128], lhsT=gB[:, 0, h, :], rhs=U_b[:],
                                                   start=True, stop=False), R=[gB, U_b], W=[pgs])
                kb.op("pe", lambda e, h=h: e.matmul(pgs[:, h * 128:(h + 1) * 128], lhsT=gB[:, 1, h, :], rhs=U_b[:],
                                                   start=False, stop=False), R=[gB, U_b], W=[pgs])
                kb.op("pe", lambda e, h=h: e.matmul(pgs[:, h * 128:(h + 1) * 128], lhsT=ident_b[:], rhs=negms[:],
                                                   start=False, stop=True), R=[ident_b, negms], W=[pgs])
            for h in range(NH):
                kb.op("act", lambda e, h=h: e.activation(out=E[:, h, :], in_=pgm[:, h * 128:(h + 1) * 128], func=AF.Exp,
                                                        bias=negG[:, h:h + 1]), R=[pgm, negG], W=[E])
                kb.op("act", lambda e, h=h: e.activation(out=Eb[:, h, :], in_=pgs[:, h * 128:(h + 1) * 128], func=AF.Exp,
                                                        bias=nGb[:, h:h + 1]), R=[pgs, nGb], W=[Eb])
            kb.op("act", lambda e: e.activation(out=eGb[:], in_=v3(pgb), func=AF.Exp), R=[pgb], W=[eGb])
            pA = PS()
            pKQ = PS()
            for h in range(NH):
                kb.op("pe", lambda e, h=h: e.matmul(pA[:, h * 128:(h + 1) * 128], lhsT=qkv[:, NH + h, cs],
                                                   rhs=qkv[:, NH + h, cs], start=True, stop=True), R=[qb[NH + h]], W=[pA])
                kb.op("pe", lambda e, h=h: e.matmul(pKQ[:, h * 128:(h + 1) * 128], lhsT=qkv[:, NH + h, cs],
                                                   rhs=qkv[:, h, cs], start=True, stop=True), R=[qb[NH + h], qb[h]], W=[pKQ])
            kb.op("dve", lambda e: e.tensor_tensor(MT[:], v3(pA), Eb[:], op=ALU.mult), R=[pA, Eb], W=[MT])
            kb.op("dve", lambda e: e.tensor_tensor(qkT[:], v3(pKQ), E[:], op=ALU.mult), R=[pKQ, E], W=[qkT])
            kb.op("dve", lambda e: e.tensor_tensor(qdT[:], qkv[:, 0:NH, cs], eGb[:], op=ALU.mult), R=qb[0:NH] + [eGb], W=[qdT])
            for l in range(NLEV):
                pP = PS()
                if l == 0:
                    for h in range(NH):
                        kb.op("pe", lambda e, h=h: e.matmul(pP[:, h * 128:(h + 1) * 128], lhsT=MT[:, h, :], rhs=ident_b[:],
                                                           start=True, stop=True), R=[MT, ident_b], W=[pP])
                    kb.op("dve", lambda e: e.tensor_tensor(Pm[:], v3(pP), lmask[:, 0, :, :], op=ALU.mult), R=[pP, lmask], W=[Pm])
                    pQT = PS()
                    for h in range(NH):
                        kb.op("pe", lambda e, h=h: e.matmul(pQT[:, h * 128:(h + 1) * 128], lhsT=Pm[:, h, :], rhs=ident_b[:],
                                                           start=True, stop=True), R=[Pm, ident_b], W=[pQT])
                    kb.op("dve", lambda e: e.tensor_tensor(T[:], identb4[:], Pm[:], op=ALU.subtract), R=[identb4, Pm], W=[T])
                    kb.op("dve", lambda e: e.tensor_tensor(TT[:], identb4[:], v3(pQT), op=ALU.subtract), R=[identb4, pQT], W=[TT])
                    continue
                for h in range(NH):
                    kb.op("pe", lambda e, h=h: e.matmul(pP[:, h * 128:(h + 1) * 128], lhsT=MT[:, h, :], rhs=T[:, h, :],
                                                       start=True, stop=True), R=[MT, T], W=[pP])
                kb.op("dve", lambda e: e.tensor_tensor(Pm[:], v3(pP), lmask[:, l, :, :], op=ALU.mult),
                      R=[pP, lmask], W=[Pm])
                last = (l == NLEV - 1)
                pQT = PS()
                if not last:
                    pQ = PS()
                for h in range(NH):
                    if not last:
                        kb.op("pe", lambda e, h=h: e.matmul(pQ[:, h * 128:(h + 1) * 128], lhsT=TT[:, h, :], rhs=Pm[:, h, :],
                                                           start=True, stop=True), R=[TT, Pm], W=[pQ])
                    kb.op("pe", lambda e, h=h: e.matmul(pQT[:, h * 128:(h + 1) * 128], lhsT=Pm[:, h, :], rhs=TT[:, h, :],
                                                       start=True, stop=True), R=[Pm, TT], W=[pQT])
                if not last:
                    kb.op("dve", lambda e: e.tensor_tensor(T[:], T[:], v3(pQ), op=ALU.subtract), R=[T, pQ], W=[T])
                kb.op("dve", lambda e: e.tensor_tensor(TT[:], TT[:], v3(pQT), op=ALU.subtract), R=[TT, pQT], W=[TT])
            pKS = PS()
            for h in range(NH):
                kb.op("pe", lambda e, h=h: e.matmul(pKS[:, h * 128:(h + 1) * 128], lhsT=qkv[:, NH + h, cs], rhs=Sbf[:, h, :],
                                                   start=True, stop=True), R=[qb[NH + h], Sbf], W=[pKS])
            for h in range(NH):
                kb.op("dve", lambda e, h=h: e.scalar_tensor_tensor(out=Rb[:, h, :], in0=pKS[:, h * 128:(h + 1) * 128],
                                                                   scalar=negeG[:, h:h + 1], in1=vtok[:, h, :],
                                                                   op0=ALU.mult, op1=ALU.add), R=[pKS, negeG, vtok], W=[Rb])
            pX = PS()
            for h in range(NH):
                kb.op("pe", lambda e, h=h: e.matmul(pX[:, h * 128:(h + 1) * 128], lhsT=TT[:, h, :], rhs=Rb[:, h, :],
                                                   start=True, stop=True), R=[TT, Rb], W=[pX])
            for h in range(NH):
                kb.op("act", lambda e, h=h: e.activation(out=vnew[:, h, :], in_=pX[:, h * 128:(h + 1) * 128], func=AF.Copy,
                                                        scale=beta[:, b4, h:h + 1]), R=[pX, beta], W=[vnew])
            pO = PS()
            pS = PS()
            for h in range(NH):
                kb.op("pe", lambda e, h=h: e.matmul(pO[:, h * 128:(h + 1) * 128], lhsT=Sbf[:, h, :], rhs=qdT[:, h, :],
                                                   start=True, stop=False), R=[Sbf, qdT], W=[pO])
                kb.op("pe", lambda e, h=h: e.matmul(pO[:, h * 128:(h + 1) * 128], lhsT=vnew[:, h, :], rhs=qkT[:, h, :],
                                                   start=False, stop=True), R=[vnew, qkT], W=[pO])
            for h in range(NH):
                kb.op("pe", lambda e, h=h: e.matmul(pS[:, h * 128:(h + 1) * 128], lhsT=kd[:, h, :], rhs=vnew[:, h, :],
                                                   start=True, stop=True), R=[kd, vnew], W=[pS])
            for h in range(NH):
                kb.op("dve", lambda e, h=h: e.scalar_tensor_tensor(out=S32[:, h, :], in0=S32[:, h, :], scalar=gtot[:, h:h + 1],
                                                                   in1=pS[:, h * 128:(h + 1) * 128], op0=ALU.mult, op1=ALU.add),
                      R=[S32, gtot, pS], W=[S32])
            kb.op("act", lambda e: e.activation(out=Sbf[:], in_=S32[:], func=AF.Copy), R=[S32], W=[Sbf])
            kb.op("act", lambda e: e.activation(out=osq[:], in_=v3(pO), func=AF.Square), R=[pO], W=[osq])
            pN = PS()
            kb.op("pe", lambda e: e.matmul(pN[:, :], lhsT=ones_b[:], rhs=osq[:].rearrange("p a b -> p (a b)"),
                                           start=True, stop=True), R=[ones_b, osq], W=[pN])
            kb.op("act", lambda e: e.activation(out=rinv[:], in_=v3(pN), func=AF.Ln, scale=1.0 / 128,
                                                bias=cbias[:, 2:3]), R=[pN, cbias], W=[rinv])
            kb.op("act", lambda e: e.activation(out=rinv[:], in_=rinv[:], func=AF.Exp, scale=-0.5), R=[rinv], W=[rinv])
            kb.op("dve", lambda e: e.tensor_tensor(otmp[:], v3(pO), rinv[:], op=ALU.mult), R=[pO, rinv], W=[otmp])
            of = oTf[sbi % 2]
            kb.op("dve", lambda e: e.scalar_tensor_tensor(out=of[:, :, cs], in0=otmp[:], scalar=ong[:, 0:1],
                                                          in1=gs[:, :, cs], op0=ALU.mult, op1=ALU.mult),
                  R=[otmp, ong, gs], W=[of])
        of = oTf[sbi % 2]
        kb0 = (col0 + tok0) // 128
        prs = []
        for j in range(SBT // 128):
            for dst in o_scr.loc(kb0 + j):
                prs.append((dst, of[:, :, j * 128:(j + 1) * 128]))
        kb.dma_multi("sp", prs, R=[of], W=[o_scr], key=of)
    for h in range(NH):
        kb.dma("sp", ssm_d[h, :, :], S32[:, h, :], R=[S32], W=[ssm_d], key=S32)
    kb.dma("sp", convo_d[:, :, :], pre[:, :, 0:3], R=preb, W=[convo_d], key=pre)


def sample_a(kb, smps, L, PS, psT, row0):
    NS = 16
    W, cw, negA, dtb, ong, cbias = L["W"], L["cw"], L["negA"], L["dtb"], L["ong"], L["cbias"]
    ident_f, ident_b, ones_b, ones_f = L["ident_f"], L["ident_b"], L["ones_b"], L["ones_f"]
    NF = 4 * NH
    eye_d = smps[0]["eye"]
    xs_t = kb.sb([128, 1024], F32, "s_x")
    xsb = kb.sb([128, 1024], BF16, "s_xb")
    junk = kb.sb([128, 1024], BF16, "s_junk")
    rr = kb.sb([128, 4], F32, "s_rr")
    xsT = kb.sb([128, 8, 128], BF16, "s_xsT")
    sct = kb.sb([128, 3, 3 * NH * 128], F32, "s_sct")
    scT = kb.sb([128, 3 * NH, 3, NS], F32, "s_scT")
    crow = kb.sb([128, 3 * NH * 128], F32, "s_crow")
    ab = kb.sb([128, 2 * NH], F32, "s_ab")
    pf = kb.sb([128, NF, NS], F32, "s_pf")
    t_ = kb.sb([128, 3 * NH, NS], F32, "s_t")
    u_ = kb.sb([128, 3 * NH, NS], F32, "s_u")
    c2 = kb.sb([128, 3 * NH, NS], F32, "s_c2")
    gsil = kb.sb([128, NH, NS], F32, "s_gsil")
    sq = kb.sb([128, 2 * NH, NS], BF16, "s_sq")
    rbs = kb.sb([128, 2 * NH, NS], F32, "s_rbs")
    qkn = kb.sb([128, 2 * NH, NS], F32, "s_qkn")
    vf = kb.sb([128, NH, NS], F32, "s_vf")
    eyeb = kb.sb([128, NS, NS], F32, "s_eyeb")
    eyep = kb.sb([128, NS], F32, "s_eyep")
    Kexp = kb.sb([128, NH, NS, NS], BF16, "s_Kexp")
    Qexp = kb.sb([128, NH, NS, NS], BF16, "s_Qexp")
    S0b = [kb.sb([128, NS, 128], BF16, f"s_S0b{i}") for i in range(4)]
    tokb = kb.sb([128, NH, 128], BF16, "s_tokb")
    tok = kb.sb([128, 3 * NH, 128], F32, "s_tok")
    sm = kb.sb([128, 8 * NH], F32, "s_sm")
    KS = kb.sb([128, NH, 128], F32, "s_KS")
    QS = kb.sb([128, NH, 128], F32, "s_QS")
    vn = kb.sb([128, NH, 128], F32, "s_vn")
    ot = kb.sb([128, NH, 128], F32, "s_ot")
    tm = kb.sb([128, NH, 128], F32, "s_tm")
    osT = kb.sb([128, NH, NS], BF16, "s_osT")
    Egx = kb.sb([128, NS, NH], F32, "s_Egx")
    egb = kb.sb([128, NS, NH], F32, "s_egb")
    S0 = [kb.sb([128, NS, 128], F32, f"s_S0{i}") for i in range(4)]
    Vexp = [kb.sb([128, NS, 128], BF16, f"s_Vexp{i}") for i in range(4)]
    Sout = [kb.sb([128, 4, 128], F32, f"s_Sout{i}") for i in range(4)]
    for b_ in (xs_t, sct, tok, sm, vn, ot, eyep, Egx, Vexp[0], Vexp[1], Vexp[2], Vexp[3]):
        kb.op("pool", lambda e, b_=b_: e.memset(b_[:], 0.0), W=[b_])
    kb.dma("sp", eyeb[:], eye_d[:, :, :], W=[eyeb], key=eyeb)
    kb.dma("sp", eyep[0:NS, :], eye_d[0, :, :], W=[eyep], key=eyep)
    for smp in smps:
        _sample_a_group(kb, smp, locals(), L, PS, psT)


def _sample_a_group(kb, smp, A, L, PS, psT):
    NS = 16
    NF = 4 * NH
    W, cw, negA, dtb, ong, cbias = L["W"], L["cw"], L["negA"], L["dtb"], L["ong"], L["cbias"]
    ident_f, ident_b, ones_b, ones_f = L["ident_f"], L["ident_b"], L["ones_b"], L["ones_f"]
    xs_d, sc_d, ss_d, convs_d, ssms_d, os_scr = (smp[k] for k in ("xs", "sc", "ss", "convs", "ssms", "os_scr"))
    (xs_t, xsb, junk, rr, xsT, sct, scT, crow, ab, pf, t_, u_, c2, gsil, sq, rbs, qkn, vf, eyeb, eyep, Kexp, Qexp, tok, sm, KS, QS, vn, ot, tm,
     osT, Egx, egb, S0, Vexp, Sout, S0b, tokb) = (A[k] for k in (
        "xs_t", "xsb", "junk", "rr", "xsT", "sct", "scT", "crow", "ab", "pf", "t_", "u_", "c2", "gsil", "sq", "rbs", "qkn", "vf", "eyeb", "eyep",
        "Kexp", "Qexp", "tok", "sm", "KS", "QS", "vn", "ot", "tm", "osT", "Egx", "egb", "S0", "Vexp", "Sout", "S0b", "tokb"))
    kb.dma("sp", xs_t[0:NS, :], xs_d[:, :], W=[xs_t], key=xs_t)
    kb.dma("sp", sct[0:NS, :, :], sc_d[:, :, :], W=[sct], key=sct)
    kb.dma("sp", convs_d[:, 0:2, :], sc_d[:, 1:3, :], W=[], key=crow)
    kb.op("act", lambda e: e.activation(out=junk[:], in_=xs_t[:], func=AF.Square, accum_out=rr[:, 0:1]), R=[xs_t], W=[junk, rr])
    kb.op("act", lambda e: e.activation(out=rr[:, 1:2], in_=rr[:, 0:1], func=AF.Ln, scale=1.0 / 1024, bias=cbias[:, 2:3]), R=[rr, cbias], W=[rr])
    kb.op("act", lambda e: e.activation(out=rr[:, 2:3], in_=rr[:, 1:2], func=AF.Exp, scale=-0.5), R=[rr], W=[rr])
    kb.op("dve", lambda e: e.tensor_scalar(xsb[:], xs_t[:], rr[:, 2:3], None, op0=ALU.mult), R=[xs_t, rr], W=[xsb])
    pt = psT[0]
    ptB = pt.t[:, :]
    for kc in range(8):
        kb.op("pe", lambda e, kc=kc: e.transpose(out=ptB[:, kc * 128:(kc + 1) * 128], in_=xsb[:, kc * 128:(kc + 1) * 128], identity=ident_b[:]),
              R=[xsb, ident_b], W=[pt])
    kb.op("act", lambda e: e.activation(out=xsT[:], in_=ptB.rearrange("p (k t) -> p k t", k=8), func=AF.Copy), R=[pt], W=[xsT])
    ppf = PS()
    for ft in range(NF):
        for kc in range(8):
            kb.op("pe", lambda e, kc=kc: e.matmul(ppf[:, ft * NS:(ft + 1) * NS], lhsT=W[:, kc, ft * 128:(ft + 1) * 128], rhs=xsT[:, kc, 0:NS],
                                                 start=(kc == 0), stop=(kc == 7)), R=[W, xsT], W=[ppf])
    kb.op("act", lambda e: e.activation(out=pf[:], in_=ppf[:, 0:NF * NS].rearrange("p (a b) -> p a b", a=NF), func=AF.Copy), R=[ppf], W=[pf])
    for j in range(3):
        pc = PS()
        for kc in range(8):
            kb.op("pe", lambda e, kc=kc: e.matmul(pc[:, :], lhsT=xsT[:, kc, :], rhs=W[:, kc, j * 512:(j + 1) * 512], start=(kc == 0), stop=(kc == 7)),
                  R=[xsT, W], W=[pc])
        kb.op("act", lambda e: e.activation(out=crow[:, j * 512:(j + 1) * 512], in_=pc[:, :], func=AF.Copy), R=[pc], W=[crow])
    kb.dma("sp", convs_d[:, 2, :], crow[0:NS, :], R=[crow], W=[], key=crow)
    pab = PS()
    for kc in range(8):
        kb.op("pe", lambda e, kc=kc: e.matmul(pab[:, 0:2 * NH], lhsT=xsT[:, kc, :], rhs=W[:, kc, NF * 128:NF * 128 + 2 * NH], start=(kc == 0), stop=(kc == 7)),
              R=[xsT, W], W=[pab])
    kb.op("dve", lambda e: e.tensor_copy(ab[:], pab[:, 0:2 * NH]), R=[pab], W=[ab])
    g_ = sm[:, 0:NH]; eg_ = sm[:, NH:2 * NH]; be_ = sm[:, 2 * NH:3 * NH]; qk_ = sm[:, 3 * NH:4 * NH]
    tp_ = sm[:, 4 * NH:5 * NH]; ri_ = sm[:, 5 * NH:6 * NH]; tq_ = sm[:, 6 * NH:7 * NH]
    kb.op("dve", lambda e: e.tensor_tensor(tp_, ab[:, 0:NH], dtb[:, :], op=ALU.add), R=[ab, dtb], W=[sm])
    kb.op("act", lambda e: e.activation(out=tp_, in_=tp_, func=AF.Exp), R=[sm], W=[sm])
    kb.op("act", lambda e: e.activation(out=tq_, in_=ab[:, NH:2 * NH], func=AF.Exp, scale=-1.0), R=[ab], W=[sm])
    kb.op("act", lambda e: e.activation(out=tp_, in_=tp_, func=AF.Ln, bias=cbias[:, 3:4]), R=[sm, cbias], W=[sm])
    kb.op("act", lambda e: e.activation(out=tq_, in_=tq_, func=AF.Ln, bias=cbias[:, 3:4]), R=[sm, cbias], W=[sm])
    kb.op("dve", lambda e: e.tensor_tensor(g_, tp_, negA[:, :], op=ALU.mult), R=[sm, negA], W=[sm])
    kb.op("act", lambda e: e.activation(out=eg_, in_=g_, func=AF.Exp), R=[sm], W=[sm])
    kb.op("act", lambda e: e.activation(out=be_, in_=tq_, func=AF.Exp, scale=-1.0), R=[sm], W=[sm])
    psc = [PS(), PS()]
    idx = 0
    for ft in range(3 * NH):
        for tap in range(3):
            bank, off = (0, idx * NS) if idx < 32 else (1, (idx - 32) * NS)
            kb.op("pe", lambda e: e.transpose(out=psc[bank][:, off:off + NS], in_=sct[:, tap, ft * 128:(ft + 1) * 128][:, :], identity=ident_f[:])
                  if False else e.matmul(psc[bank][:, off:off + NS], lhsT=sct[:, tap, ft * 128:(ft + 1) * 128], rhs=ident_f[:, 0:NS], start=True, stop=True),
                  R=[sct, ident_f], W=[psc[bank]])
            idx += 1
    scTf = scT[:].rearrange("p a b c -> p (a b c)")
    kb.op("act", lambda e: e.activation(out=scTf[:, 0:512], in_=psc[0][:, 0:512], func=AF.Copy), R=[psc[0]], W=[scT])
    kb.op("act", lambda e: e.activation(out=scTf[:, 512:576], in_=psc[1][:, 0:64], func=AF.Copy), R=[psc[1]], W=[scT])
    def cwb(tap):
        return cw[:, :, tap:tap + 1].to_broadcast([128, 3 * NH, NS])
    kb.op("dve", lambda e: e.tensor_tensor(t_[:], scT[:, :, 0, :], cwb(0), op=ALU.mult), R=[scT, cw], W=[t_])
    for tap in (1, 2):
        kb.op("dve", lambda e, tap=tap: e.tensor_tensor(u_[:], scT[:, :, tap, :], cwb(tap), op=ALU.mult), R=[scT, cw], W=[u_])
        kb.op("dve", lambda e: e.tensor_tensor(t_[:], t_[:], u_[:], op=ALU.add), R=[t_, u_], W=[t_])
    kb.op("dve", lambda e: e.tensor_tensor(u_[:], pf[:, 0:3 * NH, :], cwb(3), op=ALU.mult), R=[pf, cw], W=[u_])
    kb.op("dve", lambda e: e.tensor_tensor(t_[:], t_[:], u_[:], op=ALU.add), R=[t_, u_], W=[t_])
    kb.op("act", lambda e: e.activation(out=u_[:], in_=t_[:], func=AF.Tanh, scale=0.5), R=[t_], W=[u_])
    kb.op("dve", lambda e: e.scalar_tensor_tensor(out=c2[:], in0=u_[:], scalar=1.0, in1=t_[:], op0=ALU.add, op1=ALU.mult), R=[u_, t_], W=[c2])
    kb.op("act", lambda e: e.activation(out=gsil[:], in_=pf[:, 3 * NH:4 * NH, :], func=AF.Tanh, scale=0.5), R=[pf], W=[gsil])
    kb.op("dve", lambda e: e.scalar_tensor_tensor(out=gsil[:], in0=gsil[:], scalar=1.0, in1=pf[:, 3 * NH:4 * NH, :], op0=ALU.add, op1=ALU.mult),
          R=[gsil, pf], W=[gsil])
    kb.op("act", lambda e: e.activation(out=sq[:], in_=c2[:, 0:2 * NH, :], func=AF.Square), R=[c2], W=[sq])
    pn = PS()
    kb.op("pe", lambda e: e.matmul(pn[:, 0:2 * NH * NS], lhsT=ones_b[:], rhs=sq[:].rearrange("p a b -> p (a b)"), start=True, stop=True),
          R=[ones_b, sq], W=[pn])
    pn3 = pn.t[:, 0:2 * NH * NS].rearrange("p (a b) -> p a b", a=2 * NH)
    kb.op("act", lambda e: e.activation(out=rbs[:, 0:NH, :], in_=pn3[:, 0:NH, :], func=AF.Ln, scale=128.0, bias=cbias[:, 1:2]), R=[pn, cbias], W=[rbs])
    kb.op("act", lambda e: e.activation(out=rbs[:, NH:2 * NH, :], in_=pn3[:, NH:2 * NH, :], func=AF.Ln, scale=1.0, bias=cbias[:, 0:1]), R=[pn, cbias], W=[rbs])
    kb.op("act", lambda e: e.activation(out=rbs[:], in_=rbs[:], func=AF.Exp, scale=-0.5), R=[rbs], W=[rbs])
    kb.op("dve", lambda e: e.tensor_tensor(qkn[:], c2[:, 0:2 * NH, :], rbs[:], op=ALU.mult), R=[c2, rbs], W=[qkn])
    kb.op("dve", lambda e: e.tensor_scalar(vf[:], c2[:, 2 * NH:3 * NH, :], 0.5, None, op0=ALU.mult), R=[c2], W=[vf])
    ptk = [PS(), PS(), PS()]
    for i in range(3 * NH):
        src = qkn[:, i, :] if i < 2 * NH else vf[:, i - 2 * NH, :]
        bank, off = i // 4, (i % 4) * 128
        kb.op("pe", lambda e: e.matmul(ptk[bank][0:NS, off:off + 128], lhsT=src, rhs=ident_f[:], start=True, stop=True),
              R=[qkn, vf, ident_f], W=[ptk[bank]])
    for bank in range(3):
        kb.op("act", lambda e, bank=bank: e.activation(out=tok[0:NS, bank * 4:(bank + 1) * 4, :],
                                                       in_=ptk[bank][0:NS, :].rearrange("p (a b) -> p a b", a=4), func=AF.Copy),
              R=[ptk[bank]], W=[tok])
    q_t = tok[:, 0:NH, :]; k_t = tok[:, NH:2 * NH, :]; v_t = tok[:, 2 * NH:3 * NH, :]
    kb.op("dve", lambda e: e.tensor_tensor(tm[:], q_t, k_t, op=ALU.mult), R=[tok], W=[tm])
    kb.op("dve", lambda e: e.tensor_reduce(out=qk_, in_=tm[:], axis=mybir.AxisListType.X, op=ALU.add), R=[tm], W=[sm])
    for h in range(NH):
        kb.op("dve", lambda e, h=h: e.tensor_tensor(Kexp[:, h, :, :], eyeb[:], qkn[:, NH + h, :].unsqueeze(2).to_broadcast([128, NS, NS]), op=ALU.mult),
              R=[eyeb, qkn], W=[Kexp])
        kb.op("dve", lambda e, h=h: e.tensor_tensor(Qexp[:, h, :, :], eyeb[:], qkn[:, h, :].unsqueeze(2).to_broadcast([128, NS, NS]), op=ALU.mult),
              R=[eyeb, qkn], W=[Qexp])
    kb.op("dve", lambda e: e.tensor_tensor(Egx[:], eyep[:, :].unsqueeze(2).to_broadcast([128, NS, NH]),
                                           eg_.unsqueeze(1).to_broadcast([128, NS, NH]), op=ALU.mult), R=[eyep, sm], W=[Egx])
    peg = PS()
    kb.op("pe", lambda e: e.matmul(peg[:, 0:NS * NH], lhsT=ones_f[:], rhs=Egx[:].rearrange("p a b -> p (a b)"), start=True, stop=True),
          R=[ones_f, Egx], W=[peg])
    kb.op("act", lambda e: e.activation(out=egb[:], in_=peg[:, 0:NS * NH].rearrange("p (a b) -> p a b", a=NS), func=AF.Copy), R=[peg], W=[egb])

    def bcs(ap):
        return ap.unsqueeze(2).to_broadcast([128, NH, 128])
    for h in range(NH):
        s0 = S0[h % 4]
        kb.dma("sp", s0[:], ss_d[:, h, :, :].rearrange("n k v -> k n v"), W=[s0], key=s0)
        s0b = S0b[h % 4]
        kb.op("act", lambda e: e.activation(out=s0b[:], in_=s0[:], func=AF.Copy), R=[s0], W=[s0b])
        if h == 0:
            kb.op("pool", lambda e: e.tensor_copy(tokb[:], tok[:, NH:2 * NH, :]), R=[tok], W=[tokb])
        pks = PS()
        for n in range(NS):
            kb.op("pe", lambda e, n=n: e.matmul(pks[0:NS, 0:128], lhsT=Kexp[:, h, n, :], rhs=s0b[:, n, :], start=(n == 0), stop=(n == NS - 1)),
                  R=[Kexp, s0b], W=[pks])
        for n in range(NS):
            kb.op("pe", lambda e, n=n: e.matmul(pks[0:NS, 128:256], lhsT=Qexp[:, h, n, :], rhs=s0b[:, n, :], start=(n == 0), stop=(n == NS - 1)),
                  R=[Qexp, s0b], W=[pks])
        kb.op("act", lambda e: e.activation(out=KS[0:NS, h, :], in_=pks[0:NS, 0:128], func=AF.Copy), R=[pks], W=[KS])
        kb.op("act", lambda e: e.activation(out=QS[0:NS, h, :], in_=pks[0:NS, 128:256], func=AF.Copy), R=[pks], W=[QS])
        r16 = slice(0, NS)
        kb.op("dve", lambda e: e.scalar_tensor_tensor(out=tm[r16, h, :], in0=KS[r16, h, :], scalar=eg_[r16, h:h + 1], in1=v_t[r16, h, :],
                                                      op0=ALU.mult, op1=ALU.subtract), R=[KS, sm, tok], W=[tm])
        kb.op("dve", lambda e: e.tensor_scalar(vn[r16, h, :], tm[r16, h, :], be_[r16, h:h + 1], -1.0, op0=ALU.mult, op1=ALU.mult),
              R=[tm, sm], W=[vn])
        kb.op("dve", lambda e: e.tensor_scalar(tm[r16, h, :], QS[r16, h, :], eg_[r16, h:h + 1], None, op0=ALU.mult), R=[QS, sm], W=[tm])
        kb.op("dve", lambda e: e.scalar_tensor_tensor(out=ot[r16, h, :], in0=vn[r16, h, :], scalar=qk_[r16, h:h + 1], in1=tm[r16, h, :],
                                                      op0=ALU.mult, op1=ALU.add), R=[vn, sm, tm], W=[ot])
        vx = Vexp[h % 4]
        kb.op("dve", lambda e: e.tensor_tensor(vx[:], vn[:, h, :].unsqueeze(1).to_broadcast([128, NS, 128]),
                                               eyep[:, :].unsqueeze(2).to_broadcast([128, NS, 128]), op=ALU.mult), R=[vn, eyep], W=[vx])
        for n4 in range(NS // 4):
            pss = PS()
            so = Sout[n4 % 4]
            for j in range(4):
                n = n4 * 4 + j
                kb.op("pe", lambda e, n=n, j=j: e.matmul(pss[:, j * 128:(j + 1) * 128], lhsT=tokb[:, h, :], rhs=vx[:, n, :], start=True, stop=True),
                      R=[tokb, vx], W=[pss])
            for j in range(4):
                n = n4 * 4 + j
                kb.op("dve", lambda e, n=n, j=j: e.scalar_tensor_tensor(out=so[:, j, :], in0=s0[:, n, :], scalar=egb[:, n, h:h + 1],
                                                                        in1=pss[:, j * 128:(j + 1) * 128], op0=ALU.mult, op1=ALU.add),
                      R=[s0, egb, pss], W=[so])
            kb.dma("sp", ssms_d[n4 * 4:(n4 + 1) * 4, h, :, :].rearrange("n k v -> k n v"), so[:], R=[so], W=[], key=so)
    kb.op("dve", lambda e: e.tensor_tensor(tm[:], ot[:], ot[:], op=ALU.mult), R=[ot], W=[tm])
    kb.op("dve", lambda e: e.tensor_reduce(out=ri_, in_=tm[:], axis=mybir.AxisListType.X, op=ALU.add), R=[tm], W=[sm])
    kb.op("act", lambda e: e.activation(out=ri_, in_=ri_, func=AF.Ln, scale=1.0 / 128, bias=cbias[:, 2:3]), R=[sm, cbias], W=[sm])
    kb.op("act", lambda e: e.activation(out=ri_, in_=ri_, func=AF.Exp, scale=-0.5), R=[sm], W=[sm])
    kb.op("dve", lambda e: e.tensor_tensor(ot[:], ot[:], bcs(ri_), op=ALU.mult), R=[ot, sm], W=[ot])
    pot = PS()
    for h in range(NH):
        kb.op("pe", lambda e, h=h: e.matmul(pot[:, h * NS:(h + 1) * NS], lhsT=ot[:, h, :], rhs=ident_f[:, 0:NS], start=True, stop=True),
              R=[ot, ident_f], W=[pot])
    kb.op("dve", lambda e: e.scalar_tensor_tensor(out=osT[:], in0=pot[:, 0:NH * NS].rearrange("p (a b) -> p a b", a=NH), scalar=ong[:, 0:1],
                                                  in1=gsil[:], op0=ALU.mult, op1=ALU.mult), R=[pot, ong, gsil], W=[osT])
    kb.dma("sp", os_scr.sloc(smp["grp"])[:, :, 0:NS], osT[:], R=[osT], W=[os_scr], key=osT)

import numpy as np

EPS = 1e-6
NQ = 16
NS = 16


def host_consts_b():
    c = {}
    i = np.arange(128)
    slopes = np.exp2(-8.0 * np.arange(1, NQ + 1, dtype=np.float32) / NQ).astype(np.float32)
    bias = np.zeros((128, 2, NQ, 128), np.float32)
    jj = i[:, None]
    ii = i[None, :]
    for h in range(NQ):
        cur = np.where(ii >= jj, -slopes[h] * (ii - jj), -30000.0)
        prv = np.where(jj >= ii, -slopes[h] * (128 + ii - jj), -30000.0)
        bias[:, 0, h, :] = prv
        bias[:, 1, h, :] = cur
    c["abias"] = bias
    bo = np.zeros((128, 128), np.float32)
    bo[:64, :64] = 1.0
    bo[64:, 64:] = 1.0
    c["bones"] = bo
    eo = np.zeros((128, 2, 128), np.float32)
    eo[:, 0, :64] = 1.0
    eo[:, 1, 64:] = 1.0
    c["eones"] = eo
    bs = np.zeros((NS * 4, 4, 128), np.float32)
    for n in range(NS):
        for g in range(4):
            for hh in range(4):
                bs[n * 4 + g, hh, :] = -slopes[4 * g + hh] * (128 - i)
    c["sbias"] = bs
    c["eye16"] = np.tile(np.eye(16, dtype=np.float32)[None], (128, 1, 1)).copy()
    pm = np.zeros((128, 64), np.float32)
    pm[np.arange(128), np.arange(128) % 64] = 1.0
    c["pairM"] = pm
    return c


def alloc_weights_b(kb):
    Woa = kb.sb([128, 8, 1024], BF16, "Woa")
    Wkv = kb.sb([128, 8, 512], BF16, "Wkv")
    Wb = kb.sb([128, 8, 2048], BF16, "Wb")
    Wob = kb.sb([128, 8, 1024], BF16, "Wob")
    gk = kb.sb([128, 8], F32, "gkv")
    gb = kb.sb([128, 8], F32, "gnb")
    return Woa, Wkv, Wb, Wob, gk, gb


def load_weights_b(kb, W6, woa_d, wkv_d, kvn_d, wb_d, nb_d, wob_d):
    Woa, Wkv, Wb, Wob, gk, gb = W6
    stg = [kb.sb([128, 2048], F32, f"stgb{i}") for i in range(4)]
    kb.dma("sp", gk[:], kvn_d[:, :], W=[gk], key=gk)
    kb.dma("sp", gb[:], nb_d[:, :], W=[gb], key=gb)
    i = 0
    for (dst, src, n, g) in ((Woa, woa_d, 1024, None), (Wkv, wkv_d, 512, gk), (Wb, wb_d, 2048, gb), (Wob, wob_d, 1024, None)):
        sv = src.rearrange("(kc p) n -> p kc n", p=128)
        for kc in range(8):
            s = stg[i % 4]
            q_ = ("sp", "act")[i % 2]
            i += 1
            kb.dma(q_, s[:, 0:n], sv[:, kc, :], W=[s], key=s)
            if g is None:
                kb.op("act" if kc % 2 else "dve",
                      (lambda e: e.activation(out=dst[:, kc, :], in_=s[:, 0:n], func=AF.Copy)) if kc % 2 else
                      (lambda e: e.tensor_copy(dst[:, kc, :], s[:, 0:n])), R=[s], W=[dst])
            else:
                kb.op("act" if kc % 2 else "dve",
                      (lambda e: e.activation(out=dst[:, kc, :], in_=s[:, 0:n], func=AF.Copy, scale=g[:, kc:kc + 1])) if kc % 2 else
                      (lambda e: e.tensor_scalar(dst[:, kc, :], s[:, 0:n], g[:, kc:kc + 1], None, op0=ALU.mult)),
                      R=[s, g], W=[dst])


def phase_b(kb, cb, x_d, o_scr, y_d, kwin_d, vwin_d, kng_d, qng_d, snk_d, Wts, NBLK, psT, psF, smp=None, idx_d=None, ab1_d=None, NBT=None, wsrc=None):
    Woa, Wkv, Wb, Wob = Wts[:4]
    pfi = [0]

    def PS():
        p = psF[pfi[0] % len(psF)]
        pfi[0] += 1
        return p

    def v4(p, a=4):
        return p.t[:, :].rearrange("p (a b) -> p a b", a=a)

    ident_f = kb.sb([128, 128], F32, "b_ident_f")
    ident_b = kb.sb([128, 128], BF16, "b_ident_b")
    bones_f = kb.sb([128, 128], F32, "bones_f")
    bones = kb.sb([128, 128], BF16, "bones")
    eones_f = kb.sb([128, 2, 128], F32, "eones_f")
    eones = kb.sb([128, 2, 128], BF16, "eones")
    kng = kb.sb([128, 64], F32, "kng")
    qng = kb.sb([128, 1], F32, "qng")
    esink = kb.sb([128, 8], F32, "esink")
    cbias = kb.sb([128, 4], F32, "b_cbias")
    cl = kb.sb([128, 1], F32, "b_cload")
    lds = []

    def ld(dst, src):
        lds.append((dst, src))
    ld(ident_f, cb["ident"][:, :]); ld(bones_f, cb["bones"][:, :]); ld(eones_f, cb["eones"][:, :, :])
    ld(kng, kng_d[:, :]); ld(qng, qng_d[:, :]); ld(esink, snk_d[:, :])
    kb.dma_multi("sp", [(d_[:], s_) for d_, s_ in lds], W=[d_ for d_, _ in lds], key=cl)
    kb.op("dve", lambda e: e.tensor_copy(ident_b[:], ident_f[:]), R=[ident_f], W=[ident_b])
    kb.op("dve", lambda e: e.tensor_copy(bones[:], bones_f[:]), R=[bones_f], W=[bones])
    kb.op("dve", lambda e: e.tensor_copy(eones[:], eones_f[:]), R=[eones_f], W=[eones])
    kb.op("act", lambda e: e.activation(out=esink[:], in_=esink[:], func=AF.Exp), R=[esink], W=[esink])
    kb.op("dve", lambda e: e.tensor_scalar(qng[:], qng[:], 0.125, None, op0=ALU.mult), R=[qng], W=[qng])
    for j, v in enumerate([EPS, 1.0]):
        kb.op("pool", lambda e, j=j, v=v: e.memset(cbias[:, j:j + 1], v), W=[cbias])

    idxt = kb.sb([128, (NBLK + 1) * 2], I32, "b_idx")
    kb.dma("sp", idxt[:], idx_d[:, :], W=[idxt], key=idxt)
    hs = kb.sb([128, 1024], BF16, "b_hs")
    junk = kb.sb([128, 1024], BF16, "b_junk")
    rr = kb.sb([128, 4], F32, "b_rr")
    hT = kb.sb([128, 8, 128], BF16, "b_hT")
    def rms_and_T(src, ntok, hT_out):
        kb.op("act", lambda e: e.activation(out=junk[0:ntok, :], in_=src[0:ntok, :], func=AF.Square,
                                            accum_out=rr[0:ntok, 0:1]), R=[src], W=[junk, rr])
        kb.op("act", lambda e: e.activation(out=rr[0:ntok, 1:2], in_=rr[0:ntok, 0:1], func=AF.Ln,
                                            scale=1.0 / 1024, bias=cbias[0:ntok, 0:1]), R=[rr, cbias], W=[rr])
        kb.op("act", lambda e: e.activation(out=rr[0:ntok, 2:3], in_=rr[0:ntok, 1:2], func=AF.Exp, scale=-0.5), R=[rr], W=[rr])
        kb.op("act", lambda e: e.activation(out=hs[0:ntok, :], in_=src[0:ntok, :], func=AF.Copy, scale=rr[0:ntok, 2:3]),
              R=[src, rr], W=[hs])
        pt = psT[0]
        ptB = pt.t[:, :]
        for kc in range(8):
            kb.op("pe", lambda e, kc=kc: e.transpose(out=ptB[:, kc * 128:kc * 128 + ntok], in_=hs[0:ntok, kc * 128:(kc + 1) * 128],
                                                     identity=ident_b[0:ntok, 0:ntok]), R=[hs, ident_b], W=[pt])
        kb.op("act", lambda e: e.activation(out=hT_out[:, :, 0:ntok],
                                            in_=ptB.rearrange("p (k t) -> p k t", k=8)[:, :, 0:ntok], func=AF.Copy),
              R=[pt], W=[hT_out])

    kb.push()
    load_weights_b(kb, Wts, *wsrc)
    if smp is not None:
        sample_b(kb, smp, locals(), PS, psT, rms_and_T)
    kb.pop()
    kb.push()
    abias = kb.sb([128, 2, NQ, 128], F32, "abias")
    kb.dma("sp", abias[:], cb["abias"][:, :, :, :], W=[abias], key=abias)
    abias1 = kb.sb([128, NQ, 128], F32, "abias1")
    kb.dma("sp", abias1[:], ab1_d[:, :, :], W=[abias1], key=abias1)
    xb = [kb.sb([128, 1024], F32, f"b_x{i}") for i in range(2)]
    ob = [kb.sb([128, 8, 128], BF16, f"b_o{i}") for i in range(2)]
    hb_2 = [kb.sb([128, 1024], F32, f"b_hb{i_}") for i_ in range(2)]
    kv_2 = [kb.sb([128, 512], F32, f"b_kv{i_}") for i_ in range(2)]
    kss_2 = [kb.sb([128, 8], F32, f"b_kss{i_}") for i_ in range(2)]
    kn_2 = [kb.sb([128, 4, 64], F32, f"b_kn{i_}") for i_ in range(2)]
    kdup_2 = [kb.sb([128, 4, 2, 64], BF16, f"b_kdup{i_}") for i_ in range(2)]
    kT2 = [kb.sb([128, 4, 128], BF16, f"b_kT2{i}") for i in range(2)]
    vE = [kb.sb([128, 4, 128], BF16, f"b_vE{i}") for i in range(2)]
    vO = [kb.sb([128, 4, 128], BF16, f"b_vO{i}") for i in range(2)]
    sq_2 = [kb.sb([128, 4, 128], BF16, f"b_sq{i_}") for i_ in range(2)]
    rq_2 = [kb.sb([128, 4, 128], F32, f"b_rq{i_}") for i_ in range(2)]
    qT_2 = [kb.sb([128, 2, 8, 128], BF16, f"b_qT{i_}") for i_ in range(2)]
    tg_2 = [kb.sb([128, 4, 128], F32, f"b_tg{i_}") for i_ in range(2)]
    gsl_2 = [kb.sb([128, 8, 128], BF16, f"b_gsl{i_}") for i_ in range(2)]
    ssb = [kb.sb([128, 4, 128], F32, f"b_ss{i}") for i in range(2)]
    PT = [kb.sb([128, 4, 128], BF16, f"b_PT{i}") for i in range(4)]
    den_2 = [kb.sb([128, 4, 128], F32, f"b_den{i_}") for i_ in range(2)]
    o1_2 = [kb.sb([128, 4, 128], F32, f"b_o1{i_}") for i_ in range(2)]
    oTb_2 = [kb.sb([128, 8, 128], BF16, f"b_oTb{i_}") for i_ in range(2)]
    yb = [kb.sb([128, 1024], F32, f"b_y{i}") for i in range(2)]
    for i_ in range(2):
        kb.op("pool", lambda e, i_=i_: e.memset(qT_2[i_][:], 0.0), W=[qT_2[i_]])
    for i in range(2):
        kb.op("pool", lambda e, i=i: e.memset(vE[i][:], 0.0), W=[vE[i]])
        kb.op("pool", lambda e, i=i: e.memset(vO[i][:], 0.0), W=[vO[i]])

    for blk in range(NBLK):
        t0 = blk * 128
        par = blk % 2
        x_ = xb[par]
        o_ = ob[par]
        hb, kv, kss, kn, kdup, sq, rq, qT, tg, gsl, den, o1, oTb = (hb_2[par], kv_2[par], kss_2[par], kn_2[par], kdup_2[par], sq_2[par],
                                                                  rq_2[par], qT_2[par], tg_2[par], gsl_2[par], den_2[par], o1_2[par], oTb_2[par])
        kb.dma("sp", x_[:], x_d[t0:t0 + 128, :], W=[x_], key=x_)
        kb.ind_dma_multi([(o_[:, 4 * hg:4 * hg + 4, :].rearrange("p h t -> p (h t)"), idxt[:, blk * 2 + hg:blk * 2 + hg + 1], o_scr.gsrc(blk)) for hg in range(2)],
                         None, o_scr.gnr(blk), R=[o_scr.gbuf(blk), idxt], W=[o_], key=o_)
        for half in range(2):
            ph = PS()
            for h in range(8):
                kb.op("pe", lambda e, h=h: e.matmul(ph[:, :], lhsT=o_[:, h, :], rhs=Woa[:, h, half * 512:(half + 1) * 512],
                                                   start=(h == 0), stop=(h == 7)), R=[o_, Woa], W=[ph])
            kb.op("dve", lambda e: e.tensor_tensor(hb[:, half * 512:(half + 1) * 512], ph[:, :], x_[:, half * 512:(half + 1) * 512],
                                                   op=ALU.add), R=[ph, x_], W=[hb])
        rms_and_T(hb, 128, hT)
        pk = PS()
        for kc in range(8):
            kb.op("pe", lambda e, kc=kc: e.matmul(pk[:, :], lhsT=hT[:, kc, :], rhs=Wkv[:, kc, :], start=(kc == 0), stop=(kc == 7)),
                  R=[hT, Wkv], W=[pk])
        kb.op("act", lambda e: e.activation(out=kv[:], in_=pk[:, :], func=AF.Copy), R=[pk], W=[kv])
        for g in range(4):
            kb.op("act", lambda e, g=g: e.activation(out=junk[:, 0:64], in_=kv[:, g * 64:(g + 1) * 64], func=AF.Square,
                                                    accum_out=kss[:, g:g + 1]), R=[kv], W=[junk, kss])
        kb.op("act", lambda e: e.activation(out=kss[:, 4:8], in_=kss[:, 0:4], func=AF.Ln, scale=1.0 / 64, bias=cbias[:, 0:1]),
              R=[kss, cbias], W=[kss])
        kb.op("act", lambda e: e.activation(out=kss[:, 4:8], in_=kss[:, 4:8], func=AF.Exp, scale=-0.5), R=[kss], W=[kss])
        kb.op("dve", lambda e: e.tensor_tensor(kn[:], kv[:, 0:256].rearrange("p (g d) -> p g d", g=4),
                                               kss[:, 4:8].unsqueeze(2).to_broadcast([128, 4, 64]), op=ALU.mult), R=[kv, kss], W=[kn])
        kb.op("dve", lambda e: e.tensor_tensor(kn[:], kn[:], kng[:, :].unsqueeze(1).to_broadcast([128, 4, 64]), op=ALU.mult),
              R=[kn, kng], W=[kn])
        kb.op("act", lambda e: e.activation(out=kdup[:, :, 0, :], in_=kn[:], func=AF.Copy), R=[kn], W=[kdup])
        kb.op("act", lambda e: e.activation(out=kdup[:, :, 1, :], in_=kn[:], func=AF.Copy), R=[kn], W=[kdup])
        vv = kv[:, 256:512].rearrange("p (g d) -> p g d", g=4)
        kb.op("act", lambda e: e.activation(out=vE[par][:, :, 0:64], in_=vv, func=AF.Copy), R=[kv], W=[vE[par]])
        kb.op("act", lambda e: e.activation(out=vO[par][:, :, 64:128], in_=vv, func=AF.Copy), R=[kv], W=[vO[par]])
        pt = psT[1]
        ptB = pt.t[:, :]
        for g in range(4):
            kb.op("pe", lambda e, g=g: e.transpose(out=ptB[:, g * 128:(g + 1) * 128], in_=kdup[:, g, :, :].rearrange("p a d -> p (a d)"),
                                                   identity=ident_b[:]), R=[kdup, ident_b], W=[pt])
        kb.op("act", lambda e: e.activation(out=kT2[par][:], in_=ptB[:, 0:512].rearrange("p (g t) -> p g t", g=4), func=AF.Copy),
              R=[pt], W=[kT2[par]])
        if blk == NBLK - 1:
            kb.dma("sp", kwin_d[:, :], kn[:].rearrange("p g d -> p (g d)"), R=[kn], W=[], key=kn)
            kb.dma("sp", vwin_d[:, :], kv[:, 256:512], R=[kv], W=[], key=kv)
        if blk == 0:
            continue
        for grp in range(4):
            pq = PS()
            for cc in range(4):
                col0 = (grp * 4 + cc) * 128
                for kc in range(8):
                    kb.op("pe", lambda e, kc=kc: e.matmul(pq[:, cc * 128:(cc + 1) * 128], lhsT=Wb[:, kc, col0:col0 + 128], rhs=hT[:, kc, :],
                                                         start=(kc == 0), stop=(kc == 7)), R=[Wb, hT], W=[pq])
            if grp < 2:
                kb.op("act", lambda e: e.activation(out=sq[:], in_=v4(pq), func=AF.Square), R=[pq], W=[sq])
                pn = PS()
                kb.op("pe", lambda e: e.matmul(pn[:, :], lhsT=bones[:], rhs=sq[:].rearrange("p a b -> p (a b)"), start=True, stop=True),
                      R=[bones, sq], W=[pn])
                kb.op("act", lambda e: e.activation(out=rq[:], in_=v4(pn), func=AF.Ln, scale=1.0 / 64, bias=cbias[:, 0:1]),
                      R=[pn, cbias], W=[rq])
                kb.op("act", lambda e: e.activation(out=rq[:], in_=rq[:], func=AF.Exp, scale=-0.5), R=[rq], W=[rq])
                for hf in range(2):
                    ps_ = slice(hf * 64, (hf + 1) * 64)
                    kb.op("dve", lambda e: e.scalar_tensor_tensor(out=qT[ps_, hf, grp * 4:(grp + 1) * 4, :], in0=v4(pq)[ps_], scalar=qng[ps_, 0:1],
                                                                  in1=rq[ps_], op0=ALU.mult, op1=ALU.mult), R=[pq, qng, rq], W=[qT])
            else:
                kb.op("act", lambda e: e.activation(out=tg[:], in_=v4(pq), func=AF.Tanh, scale=0.5), R=[pq], W=[tg])
                kb.op("dve", lambda e: e.scalar_tensor_tensor(out=gsl[:, (grp - 2) * 4:(grp - 1) * 4, :], in0=tg[:], scalar=1.0, in1=v4(pq),
                                                              op0=ALU.add, op1=ALU.mult), R=[tg, pq], W=[gsl])
        kbs = [0, 1]
        pO = [psF[0], psF[1]]
        pD = [psF[2], psF[3]]
        sidx = 0
        for g in range(4):
            pts = []
            for ki, kbk in enumerate(kbs):
                kpar = par if kbk == 1 else 1 - par
                psS = psF[4 + sidx % 2]
                sidx += 1
                for hh in range(4):
                    head = 4 * g + hh
                    c, hf = head // 2, head % 2
                    kb.op("pe", lambda e, hh=hh, c=c, hf=hf: e.matmul(psS[:, hh * 128:(hh + 1) * 128],
                                                                     lhsT=kT2[kpar][:, g, :],
                                                                     rhs=qT[:, hf, c, :], start=True, stop=True),
                          R=[kT2[kpar], qT], W=[psS])
                s_ = ssb[ki]
                bsrc = abias1[:, 4 * g:4 * g + 4, :] if (blk == 1 and kbk == 0) else abias[:, kbk, 4 * g:4 * g + 4, :]
                kb.op("dve", lambda e: e.tensor_tensor(s_[:], v4(psS), bsrc, op=ALU.add),
                      R=[psS, abias, abias1], W=[s_])
                p_ = PT[(g % 2) * 2 + ki]
                kb.op("act", lambda e: e.activation(out=p_[:], in_=s_[:], func=AF.Exp), R=[s_], W=[p_])
                pts.append((p_, kpar))
            for hh in range(4):
                head = 4 * g + hh
                c, hf = head // 2, head % 2
                bank, cc = c // 4, c % 4
                first = (hf == 0)
                for ki, (p_, kpar) in enumerate(pts):
                    vsrc = vE[kpar] if hf == 0 else vO[kpar]
                    st = first and ki == 0
                    sp_ = (hf == 1) and ki == len(pts) - 1
                    kb.op("pe", lambda e: e.matmul(pO[bank][:, cc * 128:(cc + 1) * 128], lhsT=vsrc[:, g, :], rhs=p_[:, hh, :],
                                                   start=st, stop=sp_), R=[vsrc, p_], W=[pO[bank]])
                    kb.op("pe", lambda e: e.matmul(pD[bank][:, cc * 128:(cc + 1) * 128], lhsT=eones[:, hf, :], rhs=p_[:, hh, :],
                                                   start=st, stop=sp_), R=[eones, p_], W=[pD[bank]])
        for bank in range(2):
            kb.op("dve", lambda e: e.tensor_tensor(den[:], v4(pD[bank]),
                                                   esink[:, bank * 4:(bank + 1) * 4].unsqueeze(2).to_broadcast([128, 4, 128]), op=ALU.add),
                  R=[pD[bank], esink], W=[den])
            kb.op("dve", lambda e: e.reciprocal(den[:], den[:]), R=[den], W=[den])
            kb.op("dve", lambda e: e.tensor_tensor(o1[:], v4(pO[bank]), den[:], op=ALU.mult), R=[pO[bank], den], W=[o1])
            kb.op("dve", lambda e: e.scalar_tensor_tensor(out=oTb[:, bank * 4:(bank + 1) * 4, :], in0=o1[:], scalar=0.5,
                                                          in1=gsl[:, bank * 4:(bank + 1) * 4, :], op0=ALU.mult, op1=ALU.mult),
                  R=[o1, gsl], W=[oTb])
        y_ = yb[par]
        for half in range(2):
            py = PS()
            for c in range(8):
                kb.op("pe", lambda e, c=c: e.matmul(py[:, :], lhsT=oTb[:, c, :], rhs=Wob[:, c, half * 512:(half + 1) * 512],
                                                   start=(c == 0), stop=(c == 7)), R=[oTb, Wob], W=[py])
            kb.op("dve", lambda e: e.tensor_tensor(y_[:, half * 512:(half + 1) * 512], py[:, :], hb[:, half * 512:(half + 1) * 512], op=ALU.add),
                  R=[py, hb], W=[y_])
        kb.dma("sp", y_d[t0 - 128:t0, :], y_[:], R=[y_], W=[], key=y_)

    kb.pop()


def sample_b(kb, smp, L, PS, psT, rms_and_T):
    NS_ = 16
    Woa, Wkv, Wb, Wob = L["Woa"], L["Wkv"], L["Wb"], L["Wob"]
    cbias, kng, ident_b, hT, rr = L["cbias"], L["kng"], L["ident_b"], L["hT"], L["rr"]
    xs_d, os_scr, ys_d, ck_d, cv_d, kws_d, vws_d = (smp[k] for k in ("xs", "os_scr", "ys", "ck", "cv", "kws", "vws"))
    q_scr, o2_scr, kn_scr, vn_scr = (smp[k] for k in ("q_scr", "o2_scr", "kn_scr", "vn_scr"))
    qgr_d, sb_d, sk_d = smp["qngr"], smp["sbias"], smp["snk64"]
    X = mybir.AxisListType.X
    xs_t = kb.sb([128, 1024], F32, "t_x")
    hs_t = kb.sb([128, 1024], F32, "t_h")
    osT = kb.sb([128, 8, 128], BF16, "t_osT")
    idxt, NBLK = L["idxt"], L["NBLK"]
    kv = kb.sb([128, 512], F32, "t_kv")
    kss = kb.sb([128, 8], F32, "t_kss")
    junk = kb.sb([128, 64], BF16, "t_junk")
    kn = kb.sb([128, 4, 64], F32, "t_kn")
    qg = kb.sb([128, 2048], F32, "t_qg")
    tq = kb.sb([128, 16, 64], F32, "t_tq")
    qss = kb.sb([128, 32], F32, "t_qss")
    qgr = kb.sb([128, 64], F32, "t_qgr")
    gsl = kb.sb([128, 1024], F32, "t_gsl")
    q64 = kb.sb([128, 4, 64], F32, "t_q64")
    kn64 = kb.sb([64, 64], F32, "t_kn64")
    vn64 = kb.sb([64, 64], F32, "t_vn64")
    bufA = kb.sb([128, 64, 64], F32, "t_bufA")
    bufB = kb.sb([128, 64, 64], F32, "t_bufB")
    s64 = kb.sb([128, 4, 64], F32, "t_s64")
    sbias = kb.sb([128, 4, 64], F32, "t_sbias")
    part = kb.sb([128, 4 * 64 + 4], F32, "t_part")
    pairM = kb.sb([128, 64], F32, "t_pairM")
    esk = kb.sb([64, 4], F32, "t_esk")
    sm = kb.sb([64, 16], F32, "t_sm")
    o64 = kb.sb([64, 4, 64], F32, "t_o64")
    t64 = kb.sb([64, 4, 64], F32, "t_t64")
    ot = kb.sb([128, 1024], F32, "t_ot")
    ob = kb.sb([128, 1024], BF16, "t_ob")
    oT = kb.sb([128, 8, 128], BF16, "t_oT")
    ys = kb.sb([128, 1024], F32, "t_ys")
    for b_ in (xs_t, hs_t, ot):
        kb.op("pool", lambda e, b_=b_: e.memset(b_[:], 0.0), W=[b_])
    kb.dma("sp", xs_t[0:NS_, :], xs_d[:, :], W=[xs_t], key=xs_t)
    kb.ind_dma_multi([(osT[:, 4 * hg:4 * hg + 4, :].rearrange("p h t -> p (h t)"), idxt[:, NBLK * 2 + hg:NBLK * 2 + hg + 1], os_scr.gsrc(NBLK)) for hg in range(2)],
                     None, os_scr.gnr(NBLK), R=[os_scr.gbuf(NBLK), idxt], W=[osT], key=osT)
    kb.dma("sp", qgr[:], qgr_d[:, :], W=[qgr], key=qgr)
    kb.op("dve", lambda e: e.tensor_scalar(qgr[:], qgr[:], 0.125, None, op0=ALU.mult), R=[qgr], W=[qgr])
    kb.dma_multi("sp", [(sbias[jh * 64:(jh + 1) * 64, :, :], sb_d[:, :, jh * 64:(jh + 1) * 64]) for jh in range(2)], W=[sbias], key=sbias)
    kb.dma("sp", pairM[:], smp["pairM"][:, :], W=[pairM], key=pairM)
    kb.dma("sp", esk[:], sk_d[:, :], W=[esk], key=esk)
    kb.op("act", lambda e: e.activation(out=esk[:], in_=esk[:], func=AF.Exp), R=[esk], W=[esk])
    kb.dma("sp", kws_d[:, 0:127, :], ck_d[:, 1:128, :], W=[], key=junk)
    kb.dma("sp", vws_d[:, 0:127, :], cv_d[:, 1:128, :], W=[], key=junk)
    for half in range(2):
        ph = PS()
        for h in range(8):
            kb.op("pe", lambda e, h=h: e.matmul(ph[0:NS_, :], lhsT=osT[:, h, 0:NS_], rhs=Woa[:, h, half * 512:(half + 1) * 512],
                                               start=(h == 0), stop=(h == 7)), R=[osT, Woa], W=[ph])
        kb.op("dve", lambda e: e.tensor_tensor(hs_t[0:NS_, half * 512:(half + 1) * 512], ph[0:NS_, :], xs_t[0:NS_, half * 512:(half + 1) * 512],
                                               op=ALU.add), R=[ph, xs_t], W=[hs_t])
    rms_and_T(hs_t, 128, hT)
    pk = PS()
    for kc in range(8):
        kb.op("pe", lambda e, kc=kc: e.matmul(pk[:, :], lhsT=hT[:, kc, :], rhs=Wkv[:, kc, :], start=(kc == 0), stop=(kc == 7)), R=[hT, Wkv], W=[pk])
    kb.op("act", lambda e: e.activation(out=kv[:], in_=pk[:, :], func=AF.Copy), R=[pk], W=[kv])
    for g in range(4):
        kb.op("act", lambda e, g=g: e.activation(out=junk[:, 0:64], in_=kv[:, g * 64:(g + 1) * 64], func=AF.Square, accum_out=kss[:, g:g + 1]),
              R=[kv], W=[junk, kss])
    kb.op("act", lambda e: e.activation(out=kss[:, 4:8], in_=kss[:, 0:4], func=AF.Ln, scale=1.0 / 64, bias=cbias[:, 0:1]), R=[kss, cbias], W=[kss])
    kb.op("act", lambda e: e.activation(out=kss[:, 4:8], in_=kss[:, 4:8], func=AF.Exp, scale=-0.5), R=[kss], W=[kss])
    kb.op("dve", lambda e: e.tensor_tensor(kn[:], kv[:, 0:256].rearrange("p (g d) -> p g d", g=4),
                                           kss[:, 4:8].unsqueeze(2).to_broadcast([128, 4, 64]), op=ALU.mult), R=[kv, kss], W=[kn])
    kb.op("dve", lambda e: e.tensor_tensor(kn[:], kn[:], kng[:, :].unsqueeze(1).to_broadcast([128, 4, 64]), op=ALU.mult), R=[kn, kng], W=[kn])
    kb.dma("sp", kws_d[:, 127, :], kn[0:NS_].rearrange("p g d -> p (g d)"), R=[kn], W=[], key=kn)
    kb.dma("sp", vws_d[:, 127, :], kv[0:NS_, 256:512], R=[kv], W=[], key=kv)
    kb.dma("sp", kn_scr.t[:, :], kn[0:NS_].rearrange("p g d -> p (g d)"), R=[kn], W=[kn_scr], key=kn)
    kb.dma("sp", vn_scr.t[:, :], kv[0:NS_, 256:512], R=[kv], W=[vn_scr], key=kv)
    for j in range(4):
        pq = PS()
        for kc in range(8):
            kb.op("pe", lambda e, kc=kc: e.matmul(pq[:, :], lhsT=hT[:, kc, :], rhs=Wb[:, kc, j * 512:(j + 1) * 512], start=(kc == 0), stop=(kc == 7)),
                  R=[hT, Wb], W=[pq])
        kb.op("act", lambda e: e.activation(out=qg[:, j * 512:(j + 1) * 512], in_=pq[:, :], func=AF.Copy), R=[pq], W=[qg])
    q3 = qg[:, 0:1024].rearrange("p (h d) -> p h d", h=16)
    kb.op("dve", lambda e: e.tensor_tensor(tq[:], q3, q3, op=ALU.mult), R=[qg], W=[tq])
    kb.op("dve", lambda e: e.tensor_reduce(out=qss[:, 0:16], in_=tq[:], axis=X, op=ALU.add), R=[tq], W=[qss])
    kb.op("act", lambda e: e.activation(out=qss[:, 16:32], in_=qss[:, 0:16], func=AF.Ln, scale=1.0 / 64, bias=cbias[:, 0:1]), R=[qss, cbias], W=[qss])
    kb.op("act", lambda e: e.activation(out=qss[:, 16:32], in_=qss[:, 16:32], func=AF.Exp, scale=-0.5), R=[qss], W=[qss])
    kb.op("dve", lambda e: e.tensor_tensor(tq[:], q3, qss[:, 16:32].unsqueeze(2).to_broadcast([128, 16, 64]), op=ALU.mult), R=[qg, qss], W=[tq])
    kb.op("dve", lambda e: e.tensor_tensor(tq[:], tq[:], qgr[:, :].unsqueeze(1).to_broadcast([128, 16, 64]), op=ALU.mult), R=[tq, qgr], W=[tq])
    kb.dma("sp", q_scr.t[:, :], tq[0:NS_].rearrange("p h d -> p (h d)"), R=[tq], W=[q_scr], key=tq)
    kb.op("act", lambda e: e.activation(out=gsl[:], in_=qg[:, 1024:2048], func=AF.Tanh, scale=0.5), R=[qg], W=[gsl])
    kb.op("dve", lambda e: e.scalar_tensor_tensor(out=gsl[:], in0=gsl[:], scalar=1.0, in1=qg[:, 1024:2048], op0=ALU.add, op1=ALU.mult),
          R=[gsl, qg], W=[gsl])
    kb.dma_multi("sp", [(q64[jh * 64:(jh + 1) * 64], q_scr.t[:, :].rearrange("n (g hh d) -> (n g) hh d", g=4, hh=4)) for jh in range(2)],
                 R=[q_scr], W=[q64], key=q64)
    kb.dma("sp", kn64[:], kn_scr.t[:, :].rearrange("n (g d) -> (n g) d", g=4), R=[kn_scr], W=[kn64], key=kn64)
    kb.dma("sp", vn64[:], vn_scr.t[:, :].rearrange("n (g d) -> (n g) d", g=4), R=[vn_scr], W=[vn64], key=vn64)
    kb.dma_multi("sp", [(bufA[jh * 64 + 4 * n:jh * 64 + 4 * n + 4, :, :], ck_d[n, jh * 64:(jh + 1) * 64, :].rearrange("j (g d) -> g j d", g=4))
                        for n in range(NS_) for jh in range(2)], W=[bufA], key=bufA)
    for hh in range(4):
        kb.op("dve", lambda e, hh=hh: e.tensor_tensor(bufB[:], bufA[:], q64[:, hh, :].unsqueeze(1).to_broadcast([128, 64, 64]), op=ALU.mult),
              R=[bufA, q64], W=[bufB])
        kb.op("dve", lambda e, hh=hh: e.tensor_reduce(out=s64[:, hh, :], in_=bufB[:], axis=X, op=ALU.add), R=[bufB], W=[s64])
    kb.op("dve", lambda e: e.tensor_tensor(s64[:], s64[:], sbias[:], op=ALU.add), R=[s64, sbias], W=[s64])
    kb.op("act", lambda e: e.activation(out=s64[:], in_=s64[:], func=AF.Exp), R=[s64], W=[s64])
    kb.op("dve", lambda e: e.tensor_reduce(out=part[:, 256:260], in_=s64[:], axis=X, op=ALU.add), R=[s64], W=[part])
    kb.op("dve", lambda e: e.tensor_tensor(t64[:], q64[0:64], kn64[:, :].unsqueeze(1).to_broadcast([64, 4, 64]), op=ALU.mult), R=[q64, kn64], W=[t64])
    kb.op("dve", lambda e: e.tensor_reduce(out=sm[:, 4:8], in_=t64[:], axis=X, op=ALU.add), R=[t64], W=[sm])
    kb.op("act", lambda e: e.activation(out=sm[:, 4:8], in_=sm[:, 4:8], func=AF.Exp), R=[sm], W=[sm])
    kb.dma_multi("sp", [(bufB[jh * 64 + 4 * n:jh * 64 + 4 * n + 4, :, :], cv_d[n, jh * 64:(jh + 1) * 64, :].rearrange("j (g d) -> g j d", g=4))
                        for n in range(NS_) for jh in range(2)], W=[bufB], key=bufB)
    for hh in range(4):
        kb.op("dve", lambda e, hh=hh: e.tensor_tensor(bufA[:].rearrange("p j d -> p d j"), bufB[:].rearrange("p j d -> p d j"),
                                                      s64[:, hh, :].unsqueeze(1).to_broadcast([128, 64, 64]), op=ALU.mult),
              R=[bufB, s64], W=[bufA])
        kb.op("dve", lambda e, hh=hh: e.tensor_reduce(out=part[:, hh * 64:(hh + 1) * 64], in_=bufA[:].rearrange("p j d -> p d j"), axis=X, op=ALU.add),
              R=[bufA], W=[part])
    pcm = PS()
    kb.op("pe", lambda e: e.matmul(pcm[0:64, 0:260], lhsT=pairM[:], rhs=part[:], start=True, stop=True), R=[pairM, part], W=[pcm])
    kb.op("act", lambda e: e.activation(out=o64[:], in_=pcm[0:64, 0:256].rearrange("p (a b) -> p a b", a=4), func=AF.Copy), R=[pcm], W=[o64])
    kb.op("act", lambda e: e.activation(out=sm[:, 0:4], in_=pcm[0:64, 256:260], func=AF.Copy), R=[pcm], W=[sm])
    kb.op("dve", lambda e: e.tensor_tensor(sm[:, 8:12], sm[:, 0:4], sm[:, 4:8], op=ALU.add), R=[sm], W=[sm])
    kb.op("dve", lambda e: e.tensor_tensor(sm[:, 8:12], sm[:, 8:12], esk[:], op=ALU.add), R=[sm, esk], W=[sm])
    kb.op("dve", lambda e: e.reciprocal(sm[:, 8:12], sm[:, 8:12]), R=[sm], W=[sm])
    kb.op("dve", lambda e: e.tensor_tensor(t64[:], vn64[:, :].unsqueeze(1).to_broadcast([64, 4, 64]),
                                           sm[:, 4:8].unsqueeze(2).to_broadcast([64, 4, 64]), op=ALU.mult), R=[vn64, sm], W=[t64])
    kb.op("dve", lambda e: e.tensor_tensor(o64[:], o64[:], t64[:], op=ALU.add), R=[o64, t64], W=[o64])
    kb.op("dve", lambda e: e.tensor_tensor(o64[:], o64[:], sm[:, 8:12].unsqueeze(2).to_broadcast([64, 4, 64]), op=ALU.mult), R=[o64, sm], W=[o64])
    kb.dma("sp", o2_scr.t[:, :].rearrange("n (g hh d) -> (n g) hh d", g=4, hh=4), o64[:], R=[o64], W=[o2_scr], key=o64)
    kb.dma("sp", ot[0:NS_, :], o2_scr.t[:, :], R=[o2_scr], W=[ot], key=ot)
    kb.op("dve", lambda e: e.scalar_tensor_tensor(out=ob[:], in0=ot[:], scalar=0.5, in1=gsl[:], op0=ALU.mult, op1=ALU.mult), R=[ot, gsl], W=[ob])
    pt = psT[1]
    ptB = pt.t[:, :]
    for c in range(8):
        kb.op("pe", lambda e, c=c: e.transpose(out=ptB[:, c * 128:(c + 1) * 128], in_=ob[:, c * 128:(c + 1) * 128], identity=ident_b[:]),
              R=[ob, ident_b], W=[pt])
    kb.op("act", lambda e: e.activation(out=oT[:], in_=ptB.rearrange("p (k t) -> p k t", k=8), func=AF.Copy), R=[pt], W=[oT])
    for half in range(2):
        py = PS()
        for c in range(8):
            kb.op("pe", lambda e, c=c: e.matmul(py[:, :], lhsT=oT[:, c, :], rhs=Wob[:, c, half * 512:(half + 1) * 512], start=(c == 0), stop=(c == 7)),
                  R=[oT, Wob], W=[py])
        kb.op("dve", lambda e: e.tensor_tensor(ys[:, half * 512:(half + 1) * 512], py[:, :], hs_t[:, half * 512:(half + 1) * 512], op=ALU.add),
              R=[py, hs_t], W=[ys])
    kb.dma("sp", ys_d[:, :], ys[0:NS_, :], R=[ys], W=[], key=ys)

from concourse.bass_utils import run_bass_kernel_spmd

NHG = 4
T_FULL = 4096
GROUPS = [[0, 4], [1, 5], [2, 6], [3, 7]]


KCH = 3
GATHER_BARRIER = False


class Exchange(Buf):
    def __init__(self, kb, T):
        Buf.__init__(self, None, "xch")
        self.kb = kb
        self.HB = T // 2 // 128
        self.NBLK = self.HB + 1
        nslot = self.NBLK + 1
        self.nch = (nslot + KCH - 1) // KCH
        self.kc = [2 * min(KCH, nslot - c * KCH) for c in range(self.nch)]
        self.i = [kb.dram(f"xi{c}", [128 * self.kc[c], 512], BF16, "Internal") for c in range(self.nch)]
        self.g = [kb.dram(f"xg{c}", [2 * 128 * self.kc[c], 512], BF16, "Internal") for c in range(self.nch)]
        self.iv = [self.i[c].t.rearrange("(p l) (h t) -> p l h t", l=self.kc[c], t=128) for c in range(self.nch)]

    def _slot(self, k, s):
        c = k // KCH
        return c, (k - c * KCH) * 2 + s

    def loc(self, colblk):
        out = []
        for s in range(2):
            k = colblk - s * self.HB
            if 0 <= k < self.NBLK:
                c, l = self._slot(k, s)
                out.append(self.iv[c][:, l, :, :])
        return out

    def sloc(self, grp):
        c, l = self._slot(self.NBLK, grp)
        return self.iv[c][:, l, :, :]

    def gsrc(self, k):
        return self.g[k // KCH].t[:, :]

    def gbuf(self, k):
        return self.g[k // KCH]

    def gnr(self, k):
        return 2 * 128 * self.kc[k // KCH]

    def row(self, k, s, hg, p):
        c, l = self._slot(k, s)
        return hg * 128 * self.kc[c] + p * self.kc[c] + l

    def gather(self):
        kb = self.kb
        for c in range(self.nch):
            key = f"cc{c}"
            kb.semh[key] = kb.es.enter_context(kb.nc.semaphore(key))
            kb.cnt[key] = 0
            kb.dma_keys.append(key)
            kb.flush()
            kb._wait("pool", kb._deps([self], []))
            ins = kb.eng["pool"].collective_compute("AllGather", ALU.bypass, replica_groups=GROUPS,
                                                    ins=[self.i[c].t], outs=[self.g[c].t])
            kb.cnt[key] += 1
            ins.then_inc(kb.semh[key], 1)
            kb.nins += 1
            self.g[c].w = (key, 1)
            self.g[c].r = []
        if GATHER_BARRIER:
            kb.barrier()


def _prep_a(inp, hg):
    heads = [hg * NHG + i for i in range(NHG)]
    w = inp["w_in_a"][0]
    cols = []
    for base in (0, 1024, 2048, 3072):
        for h in heads:
            cols.append(np.arange(base + h * 128, base + (h + 1) * 128))
    cols.append(np.array([4096 + h for h in heads]))
    cols.append(np.array([4104 + h for h in heads]))
    cols = np.concatenate(cols)
    d = {}
    d["wa"] = np.ascontiguousarray(w[:, cols])
    cwf = inp["conv_w_a"][0]
    cw = np.zeros((128, 3 * NHG, 4), np.float32)
    for g, base in enumerate((0, 1024, 2048)):
        for i, h in enumerate(heads):
            cw[:, g * NHG + i, :] = cwf[:, base + h * 128: base + (h + 1) * 128].T
    d["cw"] = cw
    d["alog"] = np.ascontiguousarray(np.tile(inp["a_log"][0][heads][None, :], (128, 1)).astype(np.float32))
    d["dtb"] = np.ascontiguousarray(np.tile(inp["dt_bias"][0][heads][None, :], (128, 1)).astype(np.float32))
    return d


def build(T=T_FULL):
    kb = KB()
    NSB = T // 512
    HALF = T // 2
    NBLK = HALF // 128 + 1
    NBT = 1 + T // 128 + 2
    I = lambda n, s: kb.dram(n, s, F32, "ExternalInput")
    O = lambda n, s: kb.dram(n, s, F32, "ExternalOutput")
    x_d = I("x", [T, 1024])
    xB_d = I("xB", [HALF + 128, 1024])
    na_d = I("na", [128, 8])
    ong_d = I("ong", [128, 1])
    wa_d = I("wa", [1024, 16 * 128 + 8]); cw_d = I("cw", [128, 12, 4]); alog_d = I("alog", [128, 4]); dtb_d = I("dtb", [128, 4])
    hc = host_consts()
    hcb = host_consts_b()
    cst = {k: I("c_" + k, list(v.shape)) for k, v in hc.items()}
    cstb = {k: I("cb_" + k, list(v.shape)) for k, v in hcb.items()}
    woa_d = I("woa", [1024, 1024]); wkv_d = I("wkv", [1024, 512]); kvn_d = I("kvn", [128, 8])
    wb_d = I("wb", [1024, 2048]); nb_d = I("nb", [128, 8]); wob_d = I("wob", [1024, 1024])
    kng_d = I("kng", [128, 64]); qng_d = I("qng", [128, 1]); snk_d = I("snk", [128, 8])
    idx_d = kb.dram("idxtab", [128, (NBLK + 1) * 2], I32, "ExternalInput")
    ab1_d = I("abias1", [128, 16, 128])
    y_d = O("y", [HALF, 1024])
    ssm_d = O("ssm", [4, 128, 128])
    convo_d = O("convo", [128, 12, 3])
    kwin_d = O("kwin", [128, 256]); vwin_d = O("vwin", [128, 256])
    xch = Exchange(kb, T)
    xs32_d = I("xs32", [32, 1024]); sc_d = I("sc", [32, 3, 1536]); ss_d = I("ss", [32, 4, 128, 128])
    xs_d = I("xs", [16, 1024])
    ck_d = I("ck", [16, 128, 256]); cv_d = I("cv", [16, 128, 256])
    qngr_d = I("qngr", [128, 64]); snk64_d = I("snk64", [64, 4])
    convs_d = O("convs", [32, 3, 1536]); ssms_d = O("ssms", [32, 4, 128, 128])
    kws_d = O("kws", [16, 128, 256]); vws_d = O("vws", [16, 128, 256]); ys_d = O("ys", [16, 1024])
    q_scr = kb.dram("q_scr", [16, 1024], F32, "Internal"); o2_scr = kb.dram("o2_scr", [16, 1024], F32, "Internal")
    kn_scr = kb.dram("kn_scr", [16, 256], F32, "Internal"); vn_scr = kb.dram("vn_scr", [16, 256], F32, "Internal")
    psT = [kb.ps([128, 1024], BF16, f"psT{i}") for i in range(2)]
    psF = [kb.ps([128, 512], F32, f"psF{i}") for i in range(6)]
    cst_t = {k: v.t for k, v in cst.items()}
    smps = []
    for grp in range(2):
        r = slice(16 * grp, 16 * grp + 16)
        smps.append(dict(xs=xs32_d.t[r, :], sc=sc_d.t[r, :, :], ss=ss_d.t[r, :, :, :], convs=convs_d.t[r, :, :], ssms=ssms_d.t[r, :, :, :],
                         os_scr=xch, eye=cstb["eye16"].t, grp=grp))
    phase_a(kb, x_d.t, wa_d.t, na_d.t, cw_d.t, alog_d.t, dtb_d.t, ong_d.t, cst_t, xch, ssm_d, convo_d, NSB, psT, psF,
            row0=0, smp=smps, col0=128)
    kb.new_scope()
    xch.gather()
    Wts = alloc_weights_b(kb)
    cb_t = {k: v.t for k, v in cstb.items()}
    cb_t["ident"] = cst["ident"].t
    phase_b(kb, cb_t, xB_d.t, xch, y_d.t, kwin_d.t, vwin_d.t, kng_d.t, qng_d.t, snk_d.t, Wts, NBLK, psT, psF,
            smp=dict(xs=xs_d.t, os_scr=xch, ys=ys_d.t, ck=ck_d.t, cv=cv_d.t, kws=kws_d.t, vws=vws_d.t, q_scr=q_scr, o2_scr=o2_scr,
                     kn_scr=kn_scr, vn_scr=vn_scr, qngr=qngr_d.t, sbias=cstb["sbias"].t, snk64=snk64_d.t, pairM=cstb["pairM"].t),
            idx_d=idx_d.t, ab1_d=ab1_d.t, NBT=NBT, wsrc=(woa_d.t, wkv_d.t, kvn_d.t, wb_d.t, nb_d.t, wob_d.t))
    kb.finish()
    return kb


def make_inputs(inp, T=T_FULL):
    HALF = T // 2
    NBLK = HALF // 128 + 1
    NBT = 1 + T // 128 + 2
    hc = host_consts()
    hcb = host_consts_b()
    shared = {}
    shared["na"] = np.ascontiguousarray(inp["norm_a"][0].reshape(8, 128).T)
    shared["ong"] = np.ascontiguousarray(inp["o_norm_a"][0].reshape(128, 1))
    for k, v in hc.items():
        shared["c_" + k] = v
    for k, v in hcb.items():
        shared["cb_" + k] = v
    shared["woa"] = np.ascontiguousarray(inp["w_out_a"][0])
    shared["wkv"] = np.ascontiguousarray(inp["w_kv"])
    shared["kvn"] = np.ascontiguousarray(inp["kv_norm"].reshape(8, 128).T)
    shared["wb"] = np.ascontiguousarray(inp["w_in_b"][0])
    shared["nb"] = np.ascontiguousarray(inp["norm_b"][0].reshape(8, 128).T)
    shared["wob"] = np.ascontiguousarray(inp["w_out_b"][0])
    shared["kng"] = np.ascontiguousarray(np.tile(inp["k_norm"][None, :], (128, 1)).astype(np.float32))
    shared["qng"] = np.ascontiguousarray(np.tile(inp["q_norm"][0], 2).reshape(128, 1).astype(np.float32))
    sk = inp["sinks"][0]
    snk = np.zeros((128, 8), np.float32)
    for c in range(8):
        snk[:64, c] = sk[2 * c]
        snk[64:, c] = sk[2 * c + 1]
    shared["snk"] = snk
    shared["qngr"] = np.ascontiguousarray(np.tile(inp["q_norm"][0][None, :], (128, 1)).astype(np.float32))
    s64 = np.zeros((64, 4), np.float32)
    for n in range(16):
        for g in range(4):
            s64[n * 4 + g, :] = sk[4 * g:4 * g + 4]
    shared["snk64"] = s64
    chan = []
    for hg in range(2):
        cols = []
        for base in (0, 1024, 2048):
            for i in range(NHG):
                h = hg * NHG + i
                cols.append(np.arange(base + h * 128, base + (h + 1) * 128))
        chan.append(np.concatenate(cols))
    pa = [_prep_a(inp, hg) for hg in range(2)]
    p = np.arange(128)
    maps = []
    for c in range(8):
        b, s = c % 4, c // 4
        m = dict(shared)
        m.update(pa[s])
        m["x"] = np.ascontiguousarray(inp["x_prompt"][b, :T])
        xB = np.zeros((HALF + 128, 1024), np.float32)
        lo = s * HALF - 128
        if lo < 0:
            xB[128:] = inp["x_prompt"][b, 0:HALF]
        else:
            xB[:] = inp["x_prompt"][b, lo:lo + HALF + 128]
        m["xB"] = xB
        idx = np.zeros((128, (NBLK + 1) * 2), np.int32)
        nslot = NBLK + 1
        for kk in range(nslot):
            cch = kk // KCH
            kc = 2 * min(KCH, nslot - cch * KCH)
            l = (kk - cch * KCH) * 2 + s
            for hg in range(2):
                idx[:, kk * 2 + hg] = hg * 128 * kc + p * kc + l
        m["idxtab"] = idx
        m["abias1"] = np.ascontiguousarray(hcb["abias"][:, 0]) if s == 1 else np.full((128, 16, 128), -30000.0, np.float32)
        n0 = 32 * b
        m["xs32"] = np.ascontiguousarray(inp["x_sample"][n0:n0 + 32, 0, :])
        m["sc"] = np.ascontiguousarray(inp["state_conv"][0, n0:n0 + 32][:, :, chan[s]])
        m["ss"] = np.ascontiguousarray(inp["state_ssm"][0, n0:n0 + 32, s * NHG:(s + 1) * NHG])
        n1 = n0 + 16 * s
        m["xs"] = np.ascontiguousarray(inp["x_sample"][n1:n1 + 16, 0, :])
        m["ck"] = np.ascontiguousarray(inp["cache_k_win"][n1:n1 + 16].reshape(16, 128, 256))
        m["cv"] = np.ascontiguousarray(inp["cache_v_win"][n1:n1 + 16].reshape(16, 128, 256))
        maps.append(m)
    return maps


def assemble(R, T=T_FULL):
    B = 4
    HALF = T // 2
    y_p = np.zeros((B, T, 1024), np.float32)
    conv_p = np.zeros((1, B, 3, 3072), np.float32)
    ssm_p = np.zeros((1, B, 8, 128, 128), np.float32)
    kw_p = np.zeros((B, 128, 4, 64), np.float32)
    vw_p = np.zeros((B, 128, 4, 64), np.float32)
    y_s = np.zeros((128, 1, 1024), np.float32)
    conv_s = np.zeros((1, 128, 3, 3072), np.float32)
    ssm_s = np.zeros((1, 128, 8, 128, 128), np.float32)
    kw_s = np.zeros((128, 128, 4, 64), np.float32)
    vw_s = np.zeros((128, 128, 4, 64), np.float32)
    for c in range(8):
        b, s = c % 4, c // 4
        r = R[c]
        y_p[b, s * HALF:(s + 1) * HALF] = r["y"]
        ssm_p[0, b, s * NHG:(s + 1) * NHG] = r["ssm"]
        n0 = 32 * b
        ssm_s[0, n0:n0 + 32, s * NHG:(s + 1) * NHG] = r["ssms"]
        for g, base in enumerate((0, 1024, 2048)):
            for i in range(NHG):
                h = s * NHG + i
                conv_p[0, b, :, base + h * 128: base + (h + 1) * 128] = r["convo"][:, g * NHG + i, :].T
                conv_s[0, n0:n0 + 32, :, base + h * 128: base + (h + 1) * 128] = r["convs"][:, :, (g * NHG + i) * 128:(g * NHG + i + 1) * 128]
        if s == 1:
            kw_p[b] = r["kwin"].reshape(128, 4, 64)
            vw_p[b] = r["vwin"].reshape(128, 4, 64)
        n1 = n0 + 16 * s
        y_s[n1:n1 + 16, 0] = r["ys"]
        kw_s[n1:n1 + 16] = r["kws"].reshape(16, 128, 4, 64)
        vw_s[n1:n1 + 16] = r["vws"].reshape(16, 128, 4, 64)
    return (y_p, y_s, conv_p, ssm_p, kw_p, vw_p, conv_s, ssm_s, kw_s, vw_s)


_CACHE = {}


def kernel(**inp):
    inp = {k: np.asarray(v) for k, v in inp.items()}
    if "kb" not in _CACHE:
        _CACHE["kb"] = build()
    kb = _CACHE["kb"]
    maps = make_inputs(inp)
    res = run_bass_kernel_spmd(kb.nc, maps, core_ids=list(range(8)))
    return assemble(res.results)
```

```python
import contextlib
import numpy as np
import concourse.bass as bass
import concourse.mybir as mybir

F32 = mybir.dt.float32
BF16 = mybir.dt.bfloat16
I32 = mybir.dt.int32
AF = mybir.ActivationFunctionType
ALU = mybir.AluOpType


class Buf:
    def __init__(self, t, name):
        self.t = t
        self.name = name
        self.w = None
        self.r = []
        self.dkey = None

    def __getitem__(self, k):
        return self.t[k]


TABLE_AWARE = False


class _Rec:
    def __init__(self):
        self.call = None

    def __getattr__(self, name):
        def f(*a, **k):
            self.call = (name, a, k)
            return self
        return f


def _fsize(ap):
    try:
        return int(ap.free_size())
    except Exception:
        return 128


def _nbytes(ap):
    try:
        return int(ap.nbytes())
    except Exception:
        return 65536


class KB:
    DEFER = True

    def __init__(self):
        self.nc = bass.Bass("TRN2", target_bir_lowering=False)
        nc = self.nc
        self.es = contextlib.ExitStack()
        self.eng = {"pe": nc.tensor, "act": nc.scalar, "dve": nc.vector,
                    "pool": nc.gpsimd, "sp": nc.sync}
        self.semh = {}
        self.cnt = {}
        self.seen = {e: {} for e in self.eng}
        for e in ("pe", "act", "dve", "pool"):
            self.semh[e] = self.es.enter_context(nc.semaphore("s_" + e))
            self.cnt[e] = 0
        self.nbuf = 0
        self.pend = []
        self.scope = contextlib.ExitStack()
        self.scopes = []
        self.dma_keys = []
        self.nins = 0

    def sb(self, shape, dt, name=None):
        self.nbuf += 1
        name = f"sb{self.nbuf}_" + (name or "b")
        t = self.scope.enter_context(self.nc.sbuf_tensor(name, list(shape), dt))
        return Buf(t, name)

    def push(self):
        self.scopes.append(self.scope)
        self.scope = contextlib.ExitStack()

    def pop(self):
        self.barrier()
        self.scope.close()
        self.scope = self.scopes.pop()

    def new_scope(self):
        self.barrier()
        self.scope.close()
        self.scope = contextlib.ExitStack()

    def ps(self, shape, dt, name=None):
        self.nbuf += 1
        name = f"ps{self.nbuf}_" + (name or "p")
        t = self.es.enter_context(self.nc.psum_tensor(name, list(shape), dt))
        return Buf(t, name)

    def dram(self, name, shape, dt, kind):
        t = self.nc.dram_tensor(name, list(shape), dt, kind=kind)
        return Buf(t.ap(), name)

    def _deps(self, R, W):
        need = {}
        for b in R:
            if b.w is not None:
                k, c = b.w
                need[k] = max(need.get(k, 0), c)
        for b in W:
            if b.w is not None:
                k, c = b.w
                need[k] = max(need.get(k, 0), c)
            for (k, c) in b.r:
                need[k] = max(need.get(k, 0), c)
        return need

    def _wait(self, e, need):
        E = self.eng[e]
        seen = self.seen[e]
        for k, c in need.items():
            if k == e and e == "pe":
                continue
            if seen.get(k, 0) >= c:
                continue
            E.wait_ge(self.semh[k], c)
            seen[k] = c

    def _mark(self, tok, R, W):
        for b in R:
            b.r.append(tok)
            if len(b.r) > 64:
                d = {}
                for k, c in b.r:
                    d[k] = max(d.get(k, 0), c)
                b.r = list(d.items())
        for b in W:
            b.w = tok
            b.r = []

    def op(self, e, fn, R=(), W=()):
        if self.DEFER:
            rec = _Rec()
            fn(rec)
            name, a, k = rec.call
            out = k.get("out", a[0] if a else None)
            n = _fsize(out) if out is not None else 128
            if e == "pe":
                if name == "transpose":
                    c = 0.07
                else:
                    c = 0.03 + max(n, 64) / 2400.0
                    l = k.get("lhsT")
                    if l is not None and l.dtype == F32:
                        c *= 4
            elif e == "dve":
                c = 0.06 + n / 960.0
            elif e == "act":
                c = 0.2 + n / 1200.0
            else:
                c = 0.1 + n / 480.0
            tb = 0
            if e == "act":
                fnc = k.get("func")
                if fnc == AF.Ln:
                    tb = 1
                elif fnc == AF.Tanh:
                    tb = 2
            self.pend.append(("op", e, rec.call, tuple(R), tuple(W), c, c, tb))
            return None
        return self._op_now(e, fn, R, W)

    def _op_now(self, e, fn, R=(), W=()):
        self._wait(e, self._deps(R, W))
        ins = fn(self.eng[e])
        self.cnt[e] += 1
        ins.then_inc(self.semh[e], 1)
        self._mark((e, self.cnt[e]), R, W)
        self.nins += 1
        return ins

    def dma(self, q, out, in_, R=(), W=(), key=None, **kw):
        if self.DEFER:
            lat = 1.5 + _nbytes(out) / 150000.0
            W = tuple(W) if any(b is key for b in W) else tuple(W) + (key,)
            self.pend.append(("dma", q, (out, in_, key, kw), tuple(R), W, 0.06, lat))
            return None
        return self._dma_now(q, out, in_, R, W, key, **kw)

    def _dma_now(self, q, out, in_, R=(), W=(), key=None, **kw):
        if key.dkey is None:
            key.dkey = "d_" + key.name
            self.semh[key.dkey] = self.es.enter_context(self.nc.semaphore(key.dkey))
            self.cnt[key.dkey] = 0
            self.dma_keys.append(key.dkey)
        self._wait(q, self._deps(R, W))
        ins = self.eng[q].dma_start(out=out, in_=in_, **kw)
        self.cnt[key.dkey] += 16
        ins.then_inc(self.semh[key.dkey], 16)
        self._mark((key.dkey, self.cnt[key.dkey]), R, W)
        self.nins += 1
        return ins

    def dma_multi(self, q, pairs, R=(), W=(), key=None):
        if self.DEFER:
            nb = sum(_nbytes(o) for o, _ in pairs)
            W = tuple(W) if any(b is key for b in W) else tuple(W) + (key,)
            self.pend.append(("dmam", q, (list(pairs), key), tuple(R), W, 0.06 * len(pairs), 1.5 + nb / 150000.0))
            return None
        return self._dma_multi_now(q, pairs, R, W, key)

    def _dma_multi_now(self, q, pairs, R=(), W=(), key=None):
        if key.dkey is None:
            key.dkey = "d_" + key.name
            self.semh[key.dkey] = self.es.enter_context(self.nc.semaphore(key.dkey))
            self.cnt[key.dkey] = 0
            self.dma_keys.append(key.dkey)
        self._wait(q, self._deps(R, W))
        for (out, in_) in pairs:
            ins = self.eng[q].dma_start(out=out, in_=in_)
            self.cnt[key.dkey] += 16
            ins.then_inc(self.semh[key.dkey], 16)
            self.nins += 1
        self._mark((key.dkey, self.cnt[key.dkey]), R, W)

    def ind_dma(self, out, in_, idx_ap, nrows, R=(), W=(), key=None):
        q = "pool"
        if key.dkey is None:
            key.dkey = "d_" + key.name
            self.semh[key.dkey] = self.es.enter_context(self.nc.semaphore(key.dkey))
            self.cnt[key.dkey] = 0
            self.dma_keys.append(key.dkey)
        self._wait(q, self._deps(R, W))
        ins = self.eng[q].indirect_dma_start(out=out, out_offset=None, in_=in_,
                                             in_offset=bass.IndirectOffsetOnAxis(ap=idx_ap, axis=0),
                                             bounds_check=nrows - 1, oob_is_err=False)
        self.cnt[key.dkey] += 16
        ins.then_inc(self.semh[key.dkey], 16)
        self._mark((key.dkey, self.cnt[key.dkey]), R, W)
        self.nins += 1

    def ind_dma_multi(self, items, in_, nrows, R=(), W=(), key=None):
        if self.DEFER:
            nb = sum(_nbytes(o) for o, _, _ in items)
            W = tuple(W) if any(b is key for b in W) else tuple(W) + (key,)
            self.pend.append(("indm", "pool", (list(items), nrows, key), tuple(R), W, 1.0 * len(items), 3.0 + nb / 150000.0))
            return None
        return self._ind_dma_multi_now(items, in_, nrows, R, W, key)

    def _ind_dma_multi_now(self, items, in_, nrows, R=(), W=(), key=None):
        q = "pool"
        if key.dkey is None:
            key.dkey = "d_" + key.name
            self.semh[key.dkey] = self.es.enter_context(self.nc.semaphore(key.dkey))
            self.cnt[key.dkey] = 0
            self.dma_keys.append(key.dkey)
        self._wait(q, self._deps(R, W))
        for (out, idx_ap, src) in items:
            ins = self.eng[q].indirect_dma_start(out=out, out_offset=None, in_=src,
                                                 in_offset=bass.IndirectOffsetOnAxis(ap=idx_ap, axis=0),
                                                 bounds_check=nrows - 1, oob_is_err=False)
            self.cnt[key.dkey] += 16
            ins.then_inc(self.semh[key.dkey], 16)
            self.nins += 1
        self._mark((key.dkey, self.cnt[key.dkey]), R, W)

    def all_gather(self, in_buf, out_buf, groups):
        key = "cc_" + out_buf.name
        self.semh[key] = self.es.enter_context(self.nc.semaphore(key))
        self.cnt[key] = 0
        self.dma_keys.append(key)
        self._wait("pool", self._deps([in_buf], [out_buf]))
        ins = self.eng["pool"].collective_compute("AllGather", ALU.bypass, replica_groups=groups,
                                                  ins=[in_buf.t], outs=[out_buf.t])
        self.cnt[key] += 1
        ins.then_inc(self.semh[key], 1)
        self._mark((key, 1), [in_buf], [out_buf])
        self.nins += 1

    def flush(self):
        P = self.pend
        self.pend = []
        M = len(P)
        if M == 0:
            return
        lastw = {}
        readers = {}
        deps = [None] * M
        succ = [[] for _ in range(M)]
        for j, rec_ in enumerate(P):
            e, R, W = rec_[1], rec_[3], rec_[4]
            d = set()
            for b in R:
                i = lastw.get(id(b))
                if i is not None:
                    d.add(i)
            for b in W:
                i = lastw.get(id(b))
                if i is not None:
                    d.add(i)
                for i in readers.get(id(b), ()):
                    d.add(i)
            d.discard(j)
            deps[j] = d
            for i in d:
                succ[i].append(j)
            for b in R:
                readers.setdefault(id(b), []).append(j)
            for b in W:
                lastw[id(b)] = j
                readers[id(b)] = []
        tail = [0.0] * M
        for j in range(M - 1, -1, -1):
            t = 0.0
            for k in succ[j]:
                if tail[k] > t:
                    t = tail[k]
            tail[j] = t + P[j][6]
        ndep = [len(d) for d in deps]
        dr = [0.0] * M
        fin = [0.0] * M
        free = {}
        cand = [j for j in range(M) if ndep[j] == 0]
        order = []
        LOOK = 96
        acttab = 0
        import heapq
        heapq.heapify(cand)
        pool = []
        while cand or pool:
            while cand and len(pool) < LOOK:
                pool.append(heapq.heappop(cand))
            best = None
            bk = None
            for j in pool:
                e = P[j][1]
                st = dr[j]
                f = free.get(e, 0.0)
                if f > st:
                    st = f
                if TABLE_AWARE and e == "act" and len(P[j]) > 7 and P[j][7] and P[j][7] != acttab:
                    st += 1.3
                key = (round(st, 2), -tail[j], j)
                if bk is None or key < bk:
                    bk = key
                    best = j
            pool.remove(best)
            j = best
            e = P[j][1]
            st = max(dr[j], free.get(e, 0.0))
            if TABLE_AWARE and e == "act" and len(P[j]) > 7 and P[j][7]:
                if P[j][7] != acttab:
                    st += 1.3
                acttab = P[j][7]
            free[e] = st + P[j][5]
            fin[j] = st + P[j][6]
            order.append(j)
            for k in succ[j]:
                t = fin[j] + (0.0 if (P[k][1] == e and P[j][0] == "op") else 0.0)
                if t > dr[k]:
                    dr[k] = t
                ndep[k] -= 1
                if ndep[k] == 0:
                    heapq.heappush(cand, k)
        self.est = getattr(self, "est", 0.0) + max(fin) if fin else 0.0
        sv = self.DEFER
        self.DEFER = False
        try:
            for j in order:
                kind, e, pay, R, W = P[j][:5]
                if kind == "op":
                    name, a, k = pay
                    self._op_now(e, lambda eng: getattr(eng, name)(*a, **k), R, W)
                elif kind == "dma":
                    out, in_, key, kw = pay
                    self._dma_now(e, out, in_, R, W, key, **kw)
                elif kind == "dmam":
                    pairs, key = pay
                    self._dma_multi_now(e, pairs, R, W, key)
                else:
                    items, nrows, key = pay
                    self._ind_dma_multi_now(items, None, nrows, R, W, key)
        finally:
            self.DEFER = sv

    def barrier(self):
        self.flush()
        for e in self.eng:
            need = {k: c for k, c in self.cnt.items() if c > 0 and k != e}
            self._wait(e, need)

    def finish(self):
        self.flush()
        need = {k: self.cnt[k] for k in self.dma_keys if self.cnt[k] > 0}
        self._wait("sp", need)


import numpy as np

NH = 4
DK = 128
EPS = 1e-6
NLEV = 7


def host_consts():
    c = {}
    c["ident"] = np.eye(128, dtype=np.float32)
    i = np.arange(128)
    c["U"] = (i[:, None] <= i[None, :]).astype(np.float32)
    mi = (i[None, :] >= i[:, None]).astype(np.float32)
    ms = np.where(i[None, :] > i[:, None], 0.0, -30000.0).astype(np.float32)
    c["maskUi"] = np.tile(mi[:, None, :], (1, NH, 1)).copy()
    c["maskUs"] = np.tile(ms[:, None, :], (1, NH, 1)).copy()
    lm = np.zeros((128, NLEV, NH, 128), np.float32)
    for l in range(NLEV):
        s = 1 << l
        bi = i // s
        m = ((bi[:, None] % 2 == 1) & (bi[None, :] == bi[:, None] - 1)).astype(np.float32)
        lm[:, l, :, :] = m[:, None, :]
    c["lmask"] = lm
    c["negm"] = np.where(i[None, :] >= i[:, None], 0.0, -30000.0).astype(np.float32)
    return c


def phase_a(kb, x_d, wa_d, na_d, cw_d, alog_d, dtb_d, ong_d, cst, o_scr, ssm_d, convo_d, NSB, psT, psF, row0=0, smp=None, col0=0):
    nc = kb.nc
    NF = 4 * NH
    NCOL = NF * 128 + 2 * NH
    SBT = 512

    ident_f = kb.sb([128, 128], F32, "ident_f")
    ident_b = kb.sb([128, 128], BF16, "ident_b")
    ones_b = kb.sb([128, 128], BF16, "ones_b")
    ones_f = kb.sb([128, 128], F32, "ones_f")
    U_f = kb.sb([128, 128], F32, "U_f")
    mUs = kb.sb([128, NH, 128], F32, "mUs")
    lmask = kb.sb([128, NLEV, NH, 128], BF16, "lmask")
    cbias = kb.sb([128, 8], F32, "cbias")
    na = kb.sb([128, 8], F32, "na")
    cw = kb.sb([128, 3 * NH, 4], F32, "cw")
    negA = kb.sb([128, NH], F32, "negA")
    dtb = kb.sb([128, NH], F32, "dtb")
    ong = kb.sb([128, 1], F32, "ong")
    cload = kb.sb([128, 1], F32, "cload")
    negm_f = kb.sb([128, 128], F32, "negm_f")
    negm = kb.sb([128, 128], BF16, "negm")
    negms = kb.sb([128, 128], BF16, "negms")
    identb4 = kb.sb([128, NH, 128], BF16, "identb4")

    lds = []

    def ld(dst, src):
        lds.append((dst, src))
    ld(ident_f, cst["ident"][:, :]); ld(U_f, cst["U"][:, :])
    ld(mUs, cst["maskUs"][:, :, :])
    ld(na, na_d[:, :]); ld(cw, cw_d[:, :, :]); ld(negA, alog_d[:, :]); ld(dtb, dtb_d[:, :])
    ld(ong, ong_d[:, :]); ld(negm_f, cst["negm"][:, :])
    kb.dma_multi("sp", [(d_[:], s_) for d_, s_ in lds], W=[d_ for d_, _ in lds], key=cload)
    kb.op("dve", lambda e: e.tensor_copy(ident_b[:], ident_f[:]), R=[ident_f], W=[ident_b])
    kb.op("pool", lambda e: e.memset(ones_b[:], 1.0), W=[ones_b])
    kb.op("dve", lambda e: e.tensor_copy(negm[:], negm_f[:]), R=[negm_f], W=[negm])
    U_b = kb.sb([128, 128], BF16, "U_b")
    kb.op("dve", lambda e: e.tensor_copy(U_b[:], U_f[:]), R=[U_f], W=[U_b])
    kb.op("dve", lambda e: e.tensor_copy(negms[:], mUs[:, 0, :]), R=[mUs], W=[negms])
    for h in range(NH):
        kb.op("dve", lambda e, h=h: e.tensor_copy(identb4[:, h, :], ident_f[:]), R=[ident_f], W=[identb4])
    kb.op("pool", lambda e: e.memset(ones_f[:], 1.0), W=[ones_f])
    kb.push()
    lmask_f = kb.sb([128, NLEV, NH, 128], F32, "lmask_f")
    kb.dma("sp", lmask_f[:], cst["lmask"][:, :, :, :], W=[lmask_f], key=lmask_f)
    kb.op("dve", lambda e: e.tensor_copy(lmask[:], lmask_f[:]), R=[lmask_f], W=[lmask])
    kb.pop()
    for j, v in enumerate([4 * EPS, 4 * EPS * 128, EPS, 1.0, 0.0]):
        kb.op("pool", lambda e, j=j, v=v: e.memset(cbias[:, j:j + 1], v), W=[cbias])
    kb.op("act", lambda e: e.activation(out=negA[:], in_=negA[:], func=AF.Exp), R=[negA], W=[negA])
    kb.op("dve", lambda e: e.tensor_scalar(negA[:], negA[:], -1.0, None, op0=ALU.mult), R=[negA], W=[negA])
    kb.op("dve", lambda e: e.tensor_scalar(ong[:], ong[:], 0.5, None, op0=ALU.mult), R=[ong], W=[ong])

    W = kb.sb([128, 8, NCOL], BF16, "Wa")
    kb.push()
    stg = [kb.sb([128, NCOL], F32, f"stg{i}") for i in range(2)]
    wa_v = wa_d.rearrange("(kc p) n -> p kc n", p=128)
    for kc in range(8):
        s = stg[kc % 2]
        kb.dma(("sp", "act")[kc % 2], s[:], wa_v[:, kc, :], W=[s], key=s)
        eng = "act" if kc % 2 == 0 else "dve"
        if eng == "act":
            kb.op("act", lambda e, kc=kc, s=s: e.activation(out=W[:, kc, :], in_=s[:], func=AF.Copy,
                                                         scale=na[:, kc:kc + 1]), R=[s, na], W=[W])
        else:
            kb.op("dve", lambda e, kc=kc, s=s: e.tensor_scalar(W[:, kc, :], s[:], na[:, kc:kc + 1], None,
                                                            op0=ALU.mult), R=[s, na], W=[W])

    pfi = [0]

    def PS(ring=0):
        p = psF[pfi[0] % len(psF)]
        pfi[0] += 1
        return p

    if smp is not None:
        zt = kb.sb([128, NH, 128], BF16, "zpad")
        kb.op("pool", lambda e: e.memset(zt[:], 0.0), W=[zt])
        kb.dma_multi("sp", [(dst, zt[:]) for dst in o_scr.loc(0)] + [(o_scr.sloc(g_), zt[:]) for g_ in range(2)],
                     R=[zt], W=[o_scr], key=zt)
        sample_a(kb, smp, locals(), PS, psT, row0)
    kb.pop()

    xt = [kb.sb([128, 1024], F32, f"xt{i}") for i in range(3)]
    xs = [kb.sb([128, 1024], BF16, f"xs{i}") for i in range(2)]
    rr = kb.sb([128, 4], F32, "rr")
    junk = kb.sb([128, 1024], BF16, "junk")
    xsT_1 = kb.sb([128, 8, SBT], BF16, "xsT")
    xsT_2 = [xsT_1, xsT_1]
    pre = kb.sb([128, 3 * NH, SBT + 3], F32, "pre")
    preb = [Buf(pre.t, f"pre{i}") for i in range(3 * NH)]
    acc = [kb.sb([128, SBT], F32, f"acc{i}") for i in range(2)]
    tnh = [kb.sb([128, SBT], F32, f"tnh{i}") for i in range(2)]
    qkv_2 = [kb.sb([128, 3 * NH, SBT], BF16, f"qkv{i_}") for i_ in range(2)]
    qb_2 = [[Buf(qkv_2[i_].t, f"qkv{i_}_{i}") for i in range(3 * NH)] for i_ in range(2)]
    gs_2 = [kb.sb([128, NH, SBT], BF16, f"gs{i_}") for i_ in range(2)]
    sqb = [kb.sb([128, SBT], BF16, f"sqb{i}") for i in range(2)]
    rb = [kb.sb([128, SBT], F32, f"rb{i}") for i in range(2)]
    ab_2 = [kb.sb([128, 4, 2 * NH], F32, f"ab{i_}") for i_ in range(2)]
    t1_2 = [kb.sb([128, 4, NH], F32, f"t1{i_}") for i_ in range(2)]
    gg_2 = [kb.sb([128, 4, NH], F32, f"gg{i_}") for i_ in range(2)]
    gh16_2 = [kb.sb([128, 4, NH], BF16, f"gh16{i_}") for i_ in range(2)]
    gh32_2 = [kb.sb([128, 4, NH], F32, f"gh32{i_}") for i_ in range(2)]
    gl32_2 = [kb.sb([128, 4, NH], F32, f"gl32{i_}") for i_ in range(2)]
    lnb_2 = [kb.sb([128, 4, NH], F32, f"lnb{i_}") for i_ in range(2)]
    beta_2 = [kb.sb([128, 4, NH], F32, f"beta{i_}") for i_ in range(2)]
    Gs_2 = [kb.sb([128, 2 * NH], F32, f"Gs{i_}") for i_ in range(2)]
    negG_2 = [kb.sb([128, NH], F32, f"negG{i_}") for i_ in range(2)]
    nGb_2 = [kb.sb([128, NH], F32, f"nGb{i_}") for i_ in range(2)]
    negeG_2 = [kb.sb([128, NH], F32, f"negeG{i_}") for i_ in range(2)]
    kdsc_2 = [kb.sb([128, NH], F32, f"kdsc{i_}") for i_ in range(2)]
    gtot_2 = [kb.sb([128, NH], F32, f"gtot{i_}") for i_ in range(2)]
    gB_2 = [kb.sb([128, 2, NH, 128], BF16, f"gB{i_}") for i_ in range(2)]
    E_2 = [kb.sb([128, NH, 128], F32, f"E{i_}") for i_ in range(2)]
    Eb_2 = [kb.sb([128, NH, 128], F32, f"Eb{i_}") for i_ in range(2)]
    eGb_2 = [kb.sb([128, NH, 128], F32, f"eGb{i_}") for i_ in range(2)]
    MT_2 = [kb.sb([128, NH, 128], BF16, f"MT{i_}") for i_ in range(2)]
    qkT_2 = [kb.sb([128, NH, 128], BF16, f"qkT{i_}") for i_ in range(2)]
    qdT_2 = [kb.sb([128, NH, 128], BF16, f"qdT{i_}") for i_ in range(2)]
    T_2 = [kb.sb([128, NH, 128], BF16, f"T{i_}") for i_ in range(2)]
    TT_2 = [kb.sb([128, NH, 128], BF16, f"TT{i_}") for i_ in range(2)]
    Pm_2 = [kb.sb([128, NH, 128], BF16, f"Pm{i_}") for i_ in range(2)]
    kd_2 = [kb.sb([128, NH, 128], BF16, f"kd{i_}") for i_ in range(2)]
    vtok_2 = [kb.sb([128, NH, 128], F32, f"vtok{i_}") for i_ in range(2)]
    Rb_2 = [kb.sb([128, NH, 128], BF16, f"Rb{i_}") for i_ in range(2)]
    vnew_2 = [kb.sb([128, NH, 128], BF16, f"vnew{i_}") for i_ in range(2)]
    S32 = kb.sb([128, NH, 128], F32, "S32")
    Sbf = kb.sb([128, NH, 128], BF16, "Sbf")
    osq_2 = [kb.sb([128, NH, 128], BF16, f"osq{i_}") for i_ in range(2)]
    rinv_2 = [kb.sb([128, NH, 128], F32, f"rinv{i_}") for i_ in range(2)]
    otmp_2 = [kb.sb([128, NH, 128], F32, f"otmp{i_}") for i_ in range(2)]
    oTf = [kb.sb([128, NH, SBT], BF16, f"oTf{i}") for i in range(2)]

    def v3(p):
        return p.t[:, :].rearrange("p (a b) -> p a b", a=NH)

    kb.op("pool", lambda e: e.memset(pre[:], 0.0), W=preb)
    kb.op("pool", lambda e: e.memset(S32[:], 0.0), W=[S32])
    kb.op("pool", lambda e: e.memset(Sbf[:], 0.0), W=[Sbf])

    def bc(buf, blk=None):
        a = buf[:, :] if blk is None else buf[:, blk, :]
        return a.unsqueeze(2).to_broadcast([128, NH, 128])

    for sbi in range(NSB):
        tok0 = sbi * SBT
        sp_ = sbi % 2
        xsT, ab, t1, gg, lnb, beta, gs = (xsT_2[sp_], ab_2[sp_], t1_2[sp_], gg_2[sp_], lnb_2[sp_], beta_2[sp_], gs_2[sp_])
        qkv, qb = qkv_2[sp_], qb_2[sp_]
        gh16, gh32, gl32 = gh16_2[sp_], gh32_2[sp_], gl32_2[sp_]
        for b4 in range(4):
            xb = xt[(sbi * 4 + b4) % 3]
            xsb = xs[b4 % 2]
            kb.dma("sp", xb[:], x_d[tok0 + b4 * 128: tok0 + (b4 + 1) * 128, :], W=[xb], key=xb)
            kb.op("act", lambda e: e.activation(out=junk[:], in_=xb[:], func=AF.Square,
                                                accum_out=rr[:, 0:1]), R=[xb], W=[junk, rr])
            kb.op("act", lambda e: e.activation(out=rr[:, 1:2], in_=rr[:, 0:1], func=AF.Ln,
                                                scale=1.0 / 1024, bias=cbias[:, 2:3]), R=[rr, cbias], W=[rr])
            kb.op("act", lambda e: e.activation(out=rr[:, 2:3], in_=rr[:, 1:2], func=AF.Exp, scale=-0.5), R=[rr], W=[rr])
            kb.op("act", lambda e: e.activation(out=xsb[:], in_=xb[:], func=AF.Copy, scale=rr[:, 2:3]),
                  R=[xb, rr], W=[xsb])
            pt = psT[b4 % 2]
            ptB = pt.t[:, :]
            for kc in range(8):
                kb.op("pe", lambda e, kc=kc: e.transpose(out=ptB[:, kc * 128:(kc + 1) * 128],
                                                         in_=xsb[:, kc * 128:(kc + 1) * 128],
                                                         identity=ident_b[:]), R=[xsb, ident_b], W=[pt])
            ptv = ptB.rearrange("p (k t) -> p k t", k=8)
            kb.op("act", lambda e: e.activation(out=xsT[:, :, b4 * 128:(b4 + 1) * 128], in_=ptv, func=AF.Copy),
                  R=[pt], W=[xsT])
        pab = PS()
        for b4 in range(4):
            for kc in range(8):
                kb.op("pe", lambda e, kc=kc: e.matmul(pab[:, b4 * 2 * NH:(b4 + 1) * 2 * NH],
                                                     lhsT=xsT[:, kc, b4 * 128:(b4 + 1) * 128],
                                                     rhs=W[:, kc, NF * 128:NF * 128 + 2 * NH],
                                                     start=(kc == 0), stop=(kc == 7)), R=[xsT, W], W=[pab])
        kb.op("dve", lambda e: e.tensor_copy(ab[:], pab[:, 0:8 * NH].rearrange("p (a b) -> p a b", a=4)),
              R=[pab], W=[ab])
        kb.op("dve", lambda e: e.tensor_tensor(t1[:], ab[:, :, 0:NH],
                                               dtb[:, :].unsqueeze(1).to_broadcast([128, 4, NH]), op=ALU.add),
              R=[ab, dtb], W=[t1])
        kb.op("act", lambda e: e.activation(out=t1[:], in_=t1[:], func=AF.Exp), R=[t1], W=[t1])
        kb.op("act", lambda e: e.activation(out=lnb[:], in_=ab[:, :, NH:2 * NH], func=AF.Exp, scale=-1.0),
              R=[ab], W=[lnb])
        kb.op("act", lambda e: e.activation(out=t1[:], in_=t1[:], func=AF.Ln, bias=cbias[:, 3:4]),
              R=[t1, cbias], W=[t1])
        kb.op("act", lambda e: e.activation(out=lnb[:], in_=lnb[:], func=AF.Ln, bias=cbias[:, 3:4]),
              R=[lnb, cbias], W=[lnb])
        kb.op("dve", lambda e: e.tensor_tensor(gg[:], t1[:], negA[:, :].unsqueeze(1).to_broadcast([128, 4, NH]),
                                               op=ALU.mult), R=[t1, negA], W=[gg])
        kb.op("dve", lambda e: e.tensor_scalar(lnb[:], lnb[:], -1.0, None, op0=ALU.mult), R=[lnb], W=[lnb])
        kb.op("dve", lambda e: e.tensor_copy(gh16[:], gg[:]), R=[gg], W=[gh16])
        kb.op("dve", lambda e: e.tensor_copy(gh32[:], gh16[:]), R=[gh16], W=[gh32])
        kb.op("dve", lambda e: e.tensor_tensor(gl32[:], gg[:], gh32[:], op=ALU.subtract), R=[gg, gh32], W=[gl32])
        kb.op("act", lambda e: e.activation(out=beta[:], in_=lnb[:], func=AF.Exp), R=[lnb], W=[beta])

        for ft in range(NF):
            pp = PS()
            for kc in range(8):
                kb.op("pe", lambda e, kc=kc: e.matmul(pp[:, :], lhsT=W[:, kc, ft * 128:(ft + 1) * 128],
                                                     rhs=xsT[:, kc, :], start=(kc == 0), stop=(kc == 7)),
                      R=[W, xsT], W=[pp])
            a_ = acc[ft % 2]
            t_ = tnh[ft % 2]
            if ft < 3 * NH:
                kb.op("act", lambda e: e.activation(out=pre[:, ft, 3:SBT + 3], in_=pp[:, :], func=AF.Copy),
                      R=[pp], W=[preb[ft]])
                kb.op("act", lambda e: e.activation(out=a_[:], in_=pp[:, :], func=AF.Copy,
                                                    scale=cw[:, ft, 3:4]), R=[pp, cw], W=[a_])
                for tap in range(3):
                    eng = "dve"
                    kb.op(eng, lambda e, tap=tap: e.scalar_tensor_tensor(
                        out=a_[:], in0=pre[:, ft, tap:tap + SBT], scalar=cw[:, ft, tap:tap + 1], in1=a_[:],
                        op0=ALU.mult, op1=ALU.add), R=[preb[ft], cw, a_], W=[a_])
                kb.op("act", lambda e: e.activation(out=pre[:, ft, 0:3], in_=pre[:, ft, SBT:SBT + 3], func=AF.Copy),
                      R=[preb[ft]], W=[preb[ft]])
                kb.op("act", lambda e: e.activation(out=t_[:], in_=a_[:], func=AF.Tanh, scale=0.5),
                      R=[a_], W=[t_])
                kb.op("dve", lambda e: e.scalar_tensor_tensor(out=qkv[:, ft, :], in0=t_[:], scalar=1.0,
                                                              in1=a_[:], op0=ALU.add, op1=ALU.mult),
                      R=[t_, a_], W=[qb[ft]])
            else:
                h = ft - 3 * NH
                kb.op("act", lambda e: e.activation(out=t_[:], in_=pp[:, :], func=AF.Tanh, scale=0.5),
                      R=[pp], W=[t_])
                kb.op("dve", lambda e: e.scalar_tensor_tensor(out=gs[:, h, :], in0=t_[:], scalar=1.0,
                                                              in1=pp[:, :], op0=ALU.add, op1=ALU.mult),
                      R=[t_, pp], W=[gs])
        for ft in range(2 * NH):
            s_ = sqb[ft % 2]
            r_ = rb[ft % 2]
            pn = PS()
            kb.op("act", lambda e: e.activation(out=s_[:], in_=qkv[:, ft, :], func=AF.Square), R=[qb[ft]], W=[s_])
            kb.op("pe", lambda e: e.matmul(pn[:, :], lhsT=ones_b[:], rhs=s_[:], start=True, stop=True),
                  R=[ones_b, s_], W=[pn])
            isq = ft < NH
            kb.op("act", lambda e: e.activation(out=r_[:], in_=pn[:, :], func=AF.Ln,
                                                scale=(128.0 if isq else 1.0),
                                                bias=cbias[:, 1:2] if isq else cbias[:, 0:1]),
                  R=[pn, cbias], W=[r_])
            kb.op("act", lambda e: e.activation(out=r_[:], in_=r_[:], func=AF.Exp, scale=-0.5), R=[r_], W=[r_])
            kb.op("dve", lambda e: e.tensor_tensor(qkv[:, ft, :], qkv[:, ft, :], r_[:], op=ALU.mult),
                  R=[qb[ft], r_], W=[qb[ft]])

        for b4 in range(4):
            c0 = b4 * 128
            cs = slice(c0, c0 + 128)
            cp_ = (sbi * 4 + b4) % 2
            (Gs, negG, nGb, negeG, kdsc, gtot, gB, E, Eb, eGb, MT, qkT, qdT, T, TT, Pm, kd, vtok, Rb, vnew, osq, rinv, otmp) = (
                Gs_2[cp_], negG_2[cp_], nGb_2[cp_], negeG_2[cp_], kdsc_2[cp_], gtot_2[cp_], gB_2[cp_], E_2[cp_], Eb_2[cp_], eGb_2[cp_],
                MT_2[cp_], qkT_2[cp_], qdT_2[cp_], T_2[cp_], TT_2[cp_], Pm_2[cp_], kd_2[cp_], vtok_2[cp_], Rb_2[cp_], vnew_2[cp_],
                osq_2[cp_], rinv_2[cp_], otmp_2[cp_])
            ptk = psT[(sbi * 4 + b4) % 2]
            ptv_ = ptk
            ptkB = ptk.t[:, :]
            for h in range(NH):
                kb.op("pe", lambda e, h=h: e.transpose(out=ptkB[:, h * 128:(h + 1) * 128],
                                                       in_=qkv[:, NH + h, cs], identity=ident_b[:]),
                      R=[qb[NH + h], ident_b], W=[ptk])
            for h in range(NH):
                kb.op("pe", lambda e, h=h: e.transpose(out=ptkB[:, (NH + h) * 128:(NH + h + 1) * 128],
                                                       in_=qkv[:, 2 * NH + h, cs], identity=ident_b[:]),
                      R=[qb[2 * NH + h], ident_b], W=[ptv_])
            pg = PS()
            kb.op("pe", lambda e: e.matmul(pg[:, 0:NH], lhsT=U_f[:], rhs=gg[:, b4, :], start=True, stop=True),
                  R=[U_f, gg], W=[pg])
            kb.op("pe", lambda e: e.matmul(pg[:, NH:2 * NH], lhsT=ones_f[:], rhs=gg[:, b4, :], start=True, stop=True),
                  R=[ones_f, gg], W=[pg])
            kb.op("dve", lambda e: e.tensor_copy(Gs[:], pg[:, 0:2 * NH]), R=[pg], W=[Gs])
            kb.op("dve", lambda e: e.tensor_scalar(negG[:], Gs[:, 0:NH], -1.0, None, op0=ALU.mult), R=[Gs], W=[negG])
            kb.op("dve", lambda e: e.tensor_tensor(nGb[:], lnb[:, b4, :], Gs[:, 0:NH], op=ALU.subtract),
                  R=[lnb, Gs], W=[nGb])
            kb.op("act", lambda e: e.activation(out=negeG[:], in_=Gs[:, 0:NH], func=AF.Exp), R=[Gs], W=[negeG])
            kb.op("dve", lambda e: e.tensor_scalar(negeG[:], negeG[:], -1.0, None, op0=ALU.mult), R=[negeG], W=[negeG])
            kb.op("dve", lambda e: e.tensor_tensor(kdsc[:], Gs[:, NH:2 * NH], Gs[:, 0:NH], op=ALU.subtract),
                  R=[Gs], W=[kdsc])
            kb.op("act", lambda e: e.activation(out=kdsc[:], in_=kdsc[:], func=AF.Exp), R=[kdsc], W=[kdsc])
            kb.op("act", lambda e: e.activation(out=gtot[:], in_=Gs[:, NH:2 * NH], func=AF.Exp), R=[Gs], W=[gtot])
            for h in range(NH):
                kb.op("act", lambda e, h=h: e.activation(out=kd[:, h, :], in_=ptkB[:, h * 128:(h + 1) * 128], func=AF.Copy,
                                                        scale=kdsc[:, h:h + 1]), R=[ptk, kdsc], W=[kd])
            kb.op("act", lambda e: e.activation(out=vtok[:], in_=ptkB[:, NH * 128:2 * NH * 128].rearrange("p (a b) -> p a b", a=NH),
                                                func=AF.Copy, scale=0.5), R=[ptv_], W=[vtok])
            for h in range(NH):
                kb.op("act", lambda e, h=h: e.activation(out=gB[:, 0, h, :], in_=ones_f[:], func=AF.Copy, scale=gh32[:, b4, h:h + 1]),
                      R=[ones_f, gh32], W=[gB])
                kb.op("act", lambda e, h=h: e.activation(out=gB[:, 1, h, :], in_=ones_f[:], func=AF.Copy, scale=gl32[:, b4, h:h + 1]),
                      R=[ones_f, gl32], W=[gB])
            pgb = PS()
            pgm = PS()
            for h in range(NH):
                kb.op("pe", lambda e, h=h: e.matmul(pgb[:, h * 128:(h + 1) * 128], lhsT=gB[:, 0, h, :], rhs=U_b[:],
                                                   start=True, stop=False), R=[gB, U_b], W=[pgb])
                kb.op("pe", lambda e, h=h: e.matmul(pgb[:, h * 128:(h + 1) * 128], lhsT=gB[:, 1, h, :], rhs=U_b[:],
                                                   start=False, stop=True), R=[gB, U_b], W=[pgb])
            for h in range(NH):
                kb.op("pe", lambda e, h=h: e.matmul(pgm[:, h * 128:(h + 1) * 128], lhsT=gB[:, 0, h, :], rhs=U_b[:],
                                                   start=True, stop=False), R=[gB, U_b], W=[pgm])
                kb.op("pe", lambda e, h=h: e.matmul(pgm[:, h * 128:(h + 1) * 128], lhsT=gB[:, 1, h, :], rhs=U_b[:],
                                                   start=False, stop=False), R=[gB, U_b], W=[pgm])
                kb.op("pe", lambda e, h=h: e.matmul(pgm[:, h * 128:(h + 1) * 128], lhsT=ident_b[:], rhs=negm[:],
                                                   start=False, stop=True), R=[ident_b, negm], W=[pgm])
            pgs = PS()
            for h in range(NH):
                kb.op("pe", lambda e, h=h: e.matmul(pgs[:, h * 128:(h + 1) * 128], lhsT=gB[:, 0, h, :], rhs=U_b[:],
                                                   start=True, stop=False), R=[gB, U_b], W=[pgs])
                kb.op("pe", lambda e, h=h: e.matmul(pgs[:, h * 128:(h + 1) * 128], lhsT=gB[:, 1, h, :], rhs=U_b[:],
                                                   start=False, stop=False), R=[gB, U_b], W=[pgs])
                kb.op("pe", lambda e, h=h: e.matmul(pgs[:, h * 128:(h + 1) * 128], lhsT=ident_b[:], rhs=negms[:],
                                                   start=False, stop=True), R=[ident_b, negms], W=[pgs])
            for h in range(NH):
                kb.op("act", lambda e, h=h: e.activation(out=E[:, h, :], in_=pgm[:, h * 128:(h + 1) * 128], func=AF.Exp,
                                                        bias=negG[:, h:h + 1]), R=[pgm, negG], W=[E])
                kb.op("act", lambda e, h=h: e.activation(out=Eb[:, h, :], in_=pgs[:, h * 128:(h + 1) * 128], func=AF.Exp,
                                                        bias=nGb[:, h:h + 1]), R=[pgs, nGb], W=[Eb])
            kb.op("act", lambda e: e.activation(out=eGb[:], in_=v3(pgb), func=AF.Exp), R=[pgb], W=[eGb])
            pA = PS()
            pKQ = PS()
            for h in range(NH):
                kb.op("pe", lambda e, h=h: e.matmul(pA[:, h * 128:(h + 1) * 128], lhsT=qkv[:, NH + h, cs],
                                                   rhs=qkv[:, NH + h, cs], start=True, stop=True), R=[qb[NH + h]], W=[pA])
                kb.op("pe", lambda e, h=h: e.matmul(pKQ[:, h * 128:(h + 1) * 128], lhsT=qkv[:, NH + h, cs],
                                                   rhs=qkv[:, h, cs], start=True, stop=True), R=[qb[NH + h], qb[h]], W=[pKQ])
            kb.op("dve", lambda e: e.tensor_tensor(MT[:], v3(pA), Eb[:], op=ALU.mult), R=[pA, Eb], W=[MT])
            kb.op("dve", lambda e: e.tensor_tensor(qkT[:], v3(pKQ), E[:], op=ALU.mult), R=[pKQ, E], W=[qkT])
            kb.op("dve", lambda e: e.tensor_tensor(qdT[:], qkv[:, 0:NH, cs], eGb[:], op=ALU.mult), R=qb[0:NH] + [eGb], W=[qdT])
            for l in range(NLEV):
                pP = PS()
                if l == 0:
                    for h in range(NH):
                        kb.op("pe", lambda e, h=h: e.matmul(pP[:, h * 128:(h + 1) * 128], lhsT=MT[:, h, :], rhs=ident_b[:],
                                                           start=True, stop=True), R=[MT, ident_b], W=[pP])
                    kb.op("dve", lambda e: e.tensor_tensor(Pm[:], v3(pP), lmask[:, 0, :, :], op=ALU.mult), R=[pP, lmask], W=[Pm])
                    pQT = PS()
                    for h in range(NH):
                        kb.op("pe", lambda e, h=h: e.matmul(pQT[:, h * 128:(h + 1) * 128], lhsT=Pm[:, h, :], rhs=ident_b[:],
                                                           start=True, stop=True), R=[Pm, ident_b], W=[pQT])
                    kb.op("dve", lambda e: e.tensor_tensor(T[:], identb4[:], Pm[:], op=ALU.subtract), R=[identb4, Pm], W=[T])
                    kb.op("dve", lambda e: e.tensor_tensor(TT[:], identb4[:], v3(pQT), op=ALU.subtract), R=[identb4, pQT], W=[TT])
                    continue
                for h in range(NH):
                    kb.op("pe", lambda e, h=h: e.matmul(pP[:, h * 128:(h + 1) * 128], lhsT=MT[:, h, :], rhs=T[:, h, :],
                                                       start=True, stop=True), R=[MT, T], W=[pP])
                kb.op("dve", lambda e: e.tensor_tensor(Pm[:], v3(pP), lmask[:, l, :, :], op=ALU.mult),
                      R=[pP, lmask], W=[Pm])
                last = (l == NLEV - 1)
                pQT = PS()
                if not last:
                    pQ = PS()
                for h in range(NH):
                    if not last:
                        kb.op("pe", lambda e, h=h: e.matmul(pQ[:, h * 128:(h + 1) * 128], lhsT=TT[:, h, :], rhs=Pm[:, h, :],
                                                           start=True, stop=True), R=[TT, Pm], W=[pQ])
                    kb.op("pe", lambda e, h=h: e.matmul(pQT[:, h * 128:(h + 1) * 128], lhsT=Pm[:, h, :], rhs=TT[:, h, :],
                                                       start=True, stop=True), R=[Pm, TT], W=[pQT])
                if not last:
                    kb.op("dve", lambda e: e.tensor_tensor(T[:], T[:], v3(pQ), op=ALU.subtract), R=[T, pQ], W=[T])
                kb.op("dve", lambda e: e.tensor_tensor(TT[:], TT[:], v3(pQT), op=ALU.subtract), R=[TT, pQT], W=[TT])
            pKS = PS()
            for h in range(NH):
                kb.op("pe", lambda e, h=h: e.matmul(pKS[:, h * 128:(h + 1) * 128], lhsT=qkv[:, NH + h, cs], rhs=Sbf[:, h, :],
                                                   start=True, stop=True), R=[qb[NH + h], Sbf], W=[pKS])
            for h in range(NH):
                kb.op("dve", lambda e, h=h: e.scalar_tensor_tensor(out=Rb[:, h, :], in0=pKS[:, h * 128:(h + 1) * 128],
                                                                   scalar=negeG[:, h:h + 1], in1=vtok[:, h, :],
                                                                   op0=ALU.mult, op1=ALU.add), R=[pKS, negeG, vtok], W=[Rb])
            pX = PS()
            for h in range(NH):
                kb.op("pe", lambda e, h=h: e.matmul(pX[:, h * 128:(h + 1) * 128], lhsT=TT[:, h, :], rhs=Rb[:, h, :],
                                                   start=True, stop=True), R=[TT, Rb], W=[pX])
            for h in range(NH):
                kb.op("act", lambda e, h=h: e.activation(out=vnew[:, h, :], in_=pX[:, h * 128:(h + 1) * 128], func=AF.Copy,
                                                        scale=beta[:, b4, h:h + 1]), R=[pX, beta], W=[vnew])
            pO = PS()
            pS = PS()
            for h in range(NH):
                kb.op("pe", lambda e, h=h: e.matmul(pO[:, h * 128:(h + 1) * 128], lhsT=Sbf[:, h, :], rhs=qdT[:, h, :],
                                                   start=True, stop=False), R=[Sbf, qdT], W=[pO])
                kb.op("pe", lambda e, h=h: e.matmul(pO[:, h * 128:(h + 1) * 128], lhsT=vnew[:, h, :], rhs=qkT[:, h, :],
                                                   start=False, stop=True), R=[vnew, qkT], W=[pO])
            for h in range(NH):
                kb.op("pe", lambda e, h=h: e.matmul(pS[:, h * 128:(h + 1) * 128], lhsT=kd[:, h, :], rhs=vnew[:, h, :],
                                                   start=True, stop=True), R=[kd, vnew], W=[pS])
            for h in range(NH):
                kb.op("dve", lambda e, h=h: e.scalar_tensor_tensor(out=S32[:, h, :], in0=S32[:, h, :], scalar=gtot[:, h:h + 1],
                                                                   in1=pS[:, h * 128:(h + 1) * 128], op0=ALU.mult, op1=ALU.add),
                      R=[S32, gtot, pS], W=[S32])
            kb.op("act", lambda e: e.activation(out=Sbf[:], in_=S32[:], func=AF.Copy), R=[S32], W=[Sbf])
            kb.op("act", lambda e: e.activation(out=osq[:], in_=v3(pO), func=AF.Square), R=[pO], W=[osq])
            pN = PS()
            kb.op("pe", lambda e: e.matmul(pN[:, :], lhsT=ones_b[:], rhs=osq[:].rearrange("p a b -> p (a b)"),
                                           start=True, stop=True), R=[ones_b, osq], W=[pN])
            kb.op("act", lambda e: e.activation(out=rinv[:], in_=v3(pN), func=AF.Ln, scale=1.0 / 128,
                                                bias=cbias[:, 2:3]), R=[pN, cbias], W=[rinv])
            kb.op("act", lambda e: e.activation(out=rinv[:], in_=rinv[:], func=AF.Exp, scale=-0.5), R=[rinv], W=[rinv])
            kb.op("dve", lambda e: e.tensor_tensor(otmp[:], v3(pO), rinv[:], op=ALU.mult), R=[pO, rinv], W=[otmp])
            of = oTf[sbi % 2]
            kb.op("dve", lambda e: e.scalar_tensor_tensor(out=of[:, :, cs], in0=otmp[:], scalar=ong[:, 0:1],
                                                          in1=gs[:, :, cs], op0=ALU.mult, op1=ALU.mult),
                  R=[otmp, ong, gs], W=[of])
        of = oTf[sbi % 2]
        kb0 = (col0 + tok0) // 128
        prs = []
        for j in range(SBT // 128):
            for dst in o_scr.loc(kb0 + j):
                prs.append((dst, of[:, :, j * 128:(j + 1) * 128]))
        kb.dma_multi("sp", prs, R=[of], W=[o_scr], key=of)
    for h in range(NH):
        kb.dma("sp", ssm_d[h, :, :], S32[:, h, :], R=[S32], W=[ssm_d], key=S32)
    kb.dma("sp", convo_d[:, :, :], pre[:, :, 0:3], R=preb, W=[convo_d], key=pre)


def sample_a(kb, smps, L, PS, psT, row0):
    NS = 16
    W, cw, negA, dtb, ong, cbias = L["W"], L["cw"], L["negA"], L["dtb"], L["ong"], L["cbias"]
    ident_f, ident_b, ones_b, ones_f = L["ident_f"], L["ident_b"], L["ones_b"], L["ones_f"]
    NF = 4 * NH
    eye_d = smps[0]["eye"]
    xs_t = kb.sb([128, 1024], F32, "s_x")
    xsb = kb.sb([128, 1024], BF16, "s_xb")
    junk = kb.sb([128, 1024], BF16, "s_junk")
    rr = kb.sb([128, 4], F32, "s_rr")
    xsT = kb.sb([128, 8, 128], BF16, "s_xsT")
    sct = kb.sb([128, 3, 3 * NH * 128], F32, "s_sct")
    scT = kb.sb([128, 3 * NH, 3, NS], F32, "s_scT")
    crow = kb.sb([128, 3 * NH * 128], F32, "s_crow")
    ab = kb.sb([128, 2 * NH], F32, "s_ab")
    pf = kb.sb([128, NF, NS], F32, "s_pf")
    t_ = kb.sb([128, 3 * NH, NS], F32, "s_t")
    u_ = kb.sb([128, 3 * NH, NS], F32, "s_u")
    c2 = kb.sb([128, 3 * NH, NS], F32, "s_c2")
    gsil = kb.sb([128, NH, NS], F32, "s_gsil")
    sq = kb.sb([128, 2 * NH, NS], BF16, "s_sq")
    rbs = kb.sb([128, 2 * NH, NS], F32, "s_rbs")
    qkn = kb.sb([128, 2 * NH, NS], F32, "s_qkn")
    vf = kb.sb([128, NH, NS], F32, "s_vf")
    eyeb = kb.sb([128, NS, NS], F32, "s_eyeb")
    eyep = kb.sb([128, NS], F32, "s_eyep")
    Kexp = kb.sb([128, NH, NS, NS], BF16, "s_Kexp")
    Qexp = kb.sb([128, NH, NS, NS], BF16, "s_Qexp")
    S0b = [kb.sb([128, NS, 128], BF16, f"s_S0b{i}") for i in range(4)]
    tokb = kb.sb([128, NH, 128], BF16, "s_tokb")
    tok = kb.sb([128, 3 * NH, 128], F32, "s_tok")
    sm = kb.sb([128, 8 * NH], F32, "s_sm")
    KS = kb.sb([128, NH, 128], F32, "s_KS")
    QS = kb.sb([128, NH, 128], F32, "s_QS")
    vn = kb.sb([128, NH, 128], F32, "s_vn")
    ot = kb.sb([128, NH, 128], F32, "s_ot")
    tm = kb.sb([128, NH, 128], F32, "s_tm")
    osT = kb.sb([128, NH, NS], BF16, "s_osT")
    Egx = kb.sb([128, NS, NH], F32, "s_Egx")
    egb = kb.sb([128, NS, NH], F32, "s_egb")
    S0 = [kb.sb([128, NS, 128], F32, f"s_S0{i}") for i in range(4)]
    Vexp = [kb.sb([128, NS, 128], BF16, f"s_Vexp{i}") for i in range(4)]
    Sout = [kb.sb([128, 4, 128], F32, f"s_Sout{i}") for i in range(4)]
    for b_ in (xs_t, sct, tok, sm, vn, ot, eyep, Egx, Vexp[0], Vexp[1], Vexp[2], Vexp[3]):
        kb.op("pool", lambda e, b_=b_: e.memset(b_[:], 0.0), W=[b_])
    kb.dma("sp", eyeb[:], eye_d[:, :, :], W=[eyeb], key=eyeb)
    kb.dma("sp", eyep[0:NS, :], eye_d[0, :, :], W=[eyep], key=eyep)
    for smp in smps:
        _sample_a_group(kb, smp, locals(), L, PS, psT)


def _sample_a_group(kb, smp, A, L, PS, psT):
    NS = 16
    NF = 4 * NH
    W, cw, negA, dtb, ong, cbias = L["W"], L["cw"], L["negA"], L["dtb"], L["ong"], L["cbias"]
    ident_f, ident_b, ones_b, ones_f = L["ident_f"], L["ident_b"], L["ones_b"], L["ones_f"]
    xs_d, sc_d, ss_d, convs_d, ssms_d, os_scr = (smp[k] for k in ("xs", "sc", "ss", "convs", "ssms", "os_scr"))
    (xs_t, xsb, junk, rr, xsT, sct, scT, crow, ab, pf, t_, u_, c2, gsil, sq, rbs, qkn, vf, eyeb, eyep, Kexp, Qexp, tok, sm, KS, QS, vn, ot, tm,
     osT, Egx, egb, S0, Vexp, Sout, S0b, tokb) = (A[k] for k in (
        "xs_t", "xsb", "junk", "rr", "xsT", "sct", "scT", "crow", "ab", "pf", "t_", "u_", "c2", "gsil", "sq", "rbs", "qkn", "vf", "eyeb", "eyep",
        "Kexp", "Qexp", "tok", "sm", "KS", "QS", "vn", "ot", "tm", "osT", "Egx", "egb", "S0", "Vexp", "Sout", "S0b", "tokb"))
    kb.dma("sp", xs_t[0:NS, :], xs_d[:, :], W=[xs_t], key=xs_t)
    kb.dma("sp", sct[0:NS, :, :], sc_d[:, :, :], W=[sct], key=sct)
    kb.dma("sp", convs_d[:, 0:2, :], sc_d[:, 1:3, :], W=[], key=crow)
    kb.op("act", lambda e: e.activation(out=junk[:], in_=xs_t[:], func=AF.Square, accum_out=rr[:, 0:1]), R=[xs_t], W=[junk, rr])
    kb.op("act", lambda e: e.activation(out=rr[:, 1:2], in_=rr[:, 0:1], func=AF.Ln, scale=1.0 / 1024, bias=cbias[:, 2:3]), R=[rr, cbias], W=[rr])
    kb.op("act", lambda e: e.activation(out=rr[:, 2:3], in_=rr[:, 1:2], func=AF.Exp, scale=-0.5), R=[rr], W=[rr])
    kb.op("dve", lambda e: e.tensor_scalar(xsb[:], xs_t[:], rr[:, 2:3], None, op0=ALU.mult), R=[xs_t, rr], W=[xsb])
    pt = psT[0]
    ptB = pt.t[:, :]
    for kc in range(8):
        kb.op("pe", lambda e, kc=kc: e.transpose(out=ptB[:, kc * 128:(kc + 1) * 128], in_=xsb[:, kc * 128:(kc + 1) * 128], identity=ident_b[:]),
              R=[xsb, ident_b], W=[pt])
    kb.op("act", lambda e: e.activation(out=xsT[:], in_=ptB.rearrange("p (k t) -> p k t", k=8), func=AF.Copy), R=[pt], W=[xsT])
    ppf = PS()
    for ft in range(NF):
        for kc in range(8):
            kb.op("pe", lambda e, kc=kc: e.matmul(ppf[:, ft * NS:(ft + 1) * NS], lhsT=W[:, kc, ft * 128:(ft + 1) * 128], rhs=xsT[:, kc, 0:NS],
                                                 start=(kc == 0), stop=(kc == 7)), R=[W, xsT], W=[ppf])
    kb.op("act", lambda e: e.activation(out=pf[:], in_=ppf[:, 0:NF * NS].rearrange("p (a b) -> p a b", a=NF), func=AF.Copy), R=[ppf], W=[pf])
    for j in range(3):
        pc = PS()
        for kc in range(8):
            kb.op("pe", lambda e, kc=kc: e.matmul(pc[:, :], lhsT=xsT[:, kc, :], rhs=W[:, kc, j * 512:(j + 1) * 512], start=(kc == 0), stop=(kc == 7)),
                  R=[xsT, W], W=[pc])
        kb.op("act", lambda e: e.activation(out=crow[:, j * 512:(j + 1) * 512], in_=pc[:, :], func=AF.Copy), R=[pc], W=[crow])
    kb.dma("sp", convs_d[:, 2, :], crow[0:NS, :], R=[crow], W=[], key=crow)
    pab = PS()
    for kc in range(8):
        kb.op("pe", lambda e, kc=kc: e.matmul(pab[:, 0:2 * NH], lhsT=xsT[:, kc, :], rhs=W[:, kc, NF * 128:NF * 128 + 2 * NH], start=(kc == 0), stop=(kc == 7)),
              R=[xsT, W], W=[pab])
    kb.op("dve", lambda e: e.tensor_copy(ab[:], pab[:, 0:2 * NH]), R=[pab], W=[ab])
    g_ = sm[:, 0:NH]; eg_ = sm[:, NH:2 * NH]; be_ = sm[:, 2 * NH:3 * NH]; qk_ = sm[:, 3 * NH:4 * NH]
    tp_ = sm[:, 4 * NH:5 * NH]; ri_ = sm[:, 5 * NH:6 * NH]; tq_ = sm[:, 6 * NH:7 * NH]
    kb.op("dve", lambda e: e.tensor_tensor(tp_, ab[:, 0:NH], dtb[:, :], op=ALU.add), R=[ab, dtb], W=[sm])
    kb.op("act", lambda e: e.activation(out=tp_, in_=tp_, func=AF.Exp), R=[sm], W=[sm])
    kb.op("act", lambda e: e.activation(out=tq_, in_=ab[:, NH:2 * NH], func=AF.Exp, scale=-1.0), R=[ab], W=[sm])
    kb.op("act", lambda e: e.activation(out=tp_, in_=tp_, func=AF.Ln, bias=cbias[:, 3:4]), R=[sm, cbias], W=[sm])
    kb.op("act", lambda e: e.activation(out=tq_, in_=tq_, func=AF.Ln, bias=cbias[:, 3:4]), R=[sm, cbias], W=[sm])
    kb.op("dve", lambda e: e.tensor_tensor(g_, tp_, negA[:, :], op=ALU.mult), R=[sm, negA], W=[sm])
    kb.op("act", lambda e: e.activation(out=eg_, in_=g_, func=AF.Exp), R=[sm], W=[sm])
    kb.op("act", lambda e: e.activation(out=be_, in_=tq_, func=AF.Exp, scale=-1.0), R=[sm], W=[sm])
    psc = [PS(), PS()]
    idx = 0
    for ft in range(3 * NH):
        for tap in range(3):
            bank, off = (0, idx * NS) if idx < 32 else (1, (idx - 32) * NS)
            kb.op("pe", lambda e: e.transpose(out=psc[bank][:, off:off + NS], in_=sct[:, tap, ft * 128:(ft + 1) * 128][:, :], identity=ident_f[:])
                  if False else e.matmul(psc[bank][:, off:off + NS], lhsT=sct[:, tap, ft * 128:(ft + 1) * 128], rhs=ident_f[:, 0:NS], start=True, stop=True),
                  R=[sct, ident_f], W=[psc[bank]])
            idx += 1
    scTf = scT[:].rearrange("p a b c -> p (a b c)")
    kb.op("act", lambda e: e.activation(out=scTf[:, 0:512], in_=psc[0][:, 0:512], func=AF.Copy), R=[psc[0]], W=[scT])
    kb.op("act", lambda e: e.activation(out=scTf[:, 512:576], in_=psc[1][:, 0:64], func=AF.Copy), R=[psc[1]], W=[scT])
    def cwb(tap):
        return cw[:, :, tap:tap + 1].to_broadcast([128, 3 * NH, NS])
    kb.op("dve", lambda e: e.tensor_tensor(t_[:], scT[:, :, 0, :], cwb(0), op=ALU.mult), R=[scT, cw], W=[t_])
    for tap in (1, 2):
        kb.op("dve", lambda e, tap=tap: e.tensor_tensor(u_[:], scT[:, :, tap, :], cwb(tap), op=ALU.mult), R=[scT, cw], W=[u_])
        kb.op("dve", lambda e: e.tensor_tensor(t_[:], t_[:], u_[:], op=ALU.add), R=[t_, u_], W=[t_])
    kb.op("dve", lambda e: e.tensor_tensor(u_[:], pf[:, 0:3 * NH, :], cwb(3), op=ALU.mult), R=[pf, cw], W=[u_])
    kb.op("dve", lambda e: e.tensor_tensor(t_[:], t_[:], u_[:], op=ALU.add), R=[t_, u_], W=[t_])
    kb.op("act", lambda e: e.activation(out=u_[:], in_=t_[:], func=AF.Tanh, scale=0.5), R=[t_], W=[u_])
    kb.op("dve", lambda e: e.scalar_tensor_tensor(out=c2[:], in0=u_[:], scalar=1.0, in1=t_[:], op0=ALU.add, op1=ALU.mult), R=[u_, t_], W=[c2])
    kb.op("act", lambda e: e.activation(out=gsil[:], in_=pf[:, 3 * NH:4 * NH, :], func=AF.Tanh, scale=0.5), R=[pf], W=[gsil])
    kb.op("dve", lambda e: e.scalar_tensor_tensor(out=gsil[:], in0=gsil[:], scalar=1.0, in1=pf[:, 3 * NH:4 * NH, :], op0=ALU.add, op1=ALU.mult),
          R=[gsil, pf], W=[gsil])
    kb.op("act", lambda e: e.activation(out=sq[:], in_=c2[:, 0:2 * NH, :], func=AF.Square), R=[c2], W=[sq])
    pn = PS()
    kb.op("pe", lambda e: e.matmul(pn[:, 0:2 * NH * NS], lhsT=ones_b[:], rhs=sq[:].rearrange("p a b -> p (a b)"), start=True, stop=True),
          R=[ones_b, sq], W=[pn])
    pn3 = pn.t[:, 0:2 * NH * NS].rearrange("p (a b) -> p a b", a=2 * NH)
    kb.op("act", lambda e: e.activation(out=rbs[:, 0:NH, :], in_=pn3[:, 0:NH, :], func=AF.Ln, scale=128.0, bias=cbias[:, 1:2]), R=[pn, cbias], W=[rbs])
    kb.op("act", lambda e: e.activation(out=rbs[:, NH:2 * NH, :], in_=pn3[:, NH:2 * NH, :], func=AF.Ln, scale=1.0, bias=cbias[:, 0:1]), R=[pn, cbias], W=[rbs])
    kb.op("act", lambda e: e.activation(out=rbs[:], in_=rbs[:], func=AF.Exp, scale=-0.5), R=[rbs], W=[rbs])
    kb.op("dve", lambda e: e.tensor_tensor(qkn[:], c2[:, 0:2 * NH, :], rbs[:], op=ALU.mult), R=[c2, rbs], W=[qkn])
    kb.op("dve", lambda e: e.tensor_scalar(vf[:], c2[:, 2 * NH:3 * NH, :], 0.5, None, op0=ALU.mult), R=[c2], W=[vf])
    ptk = [PS(), PS(), PS()]
    for i in range(3 * NH):
        src = qkn[:, i, :] if i < 2 * NH else vf[:, i - 2 * NH, :]
        bank, off = i // 4, (i % 4) * 128
        kb.op("pe", lambda e: e.matmul(ptk[bank][0:NS, off:off + 128], lhsT=src, rhs=ident_f[:], start=True, stop=True),
              R=[qkn, vf, ident_f], W=[ptk[bank]])
    for bank in range(3):
        kb.op("act", lambda e, bank=bank: e.activation(out=tok[0:NS, bank * 4:(bank + 1) * 4, :],
                                                       in_=ptk[bank][0:NS, :].rearrange("p (a b) -> p a b", a=4), func=AF.Copy),
              R=[ptk[bank]], W=[tok])
    q_t = tok[:, 0:NH, :]; k_t = tok[:, NH:2 * NH, :]; v_t = tok[:, 2 * NH:3 * NH, :]
    kb.op("dve", lambda e: e.tensor_tensor(tm[:], q_t, k_t, op=ALU.mult), R=[tok], W=[tm])
    kb.op("dve", lambda e: e.tensor_reduce(out=qk_, in_=tm[:], axis=mybir.AxisListType.X, op=ALU.add), R=[tm], W=[sm])
    for h in range(NH):
        kb.op("dve", lambda e, h=h: e.tensor_tensor(Kexp[:, h, :, :], eyeb[:], qkn[:, NH + h, :].unsqueeze(2).to_broadcast([128, NS, NS]), op=ALU.mult),
              R=[eyeb, qkn], W=[Kexp])
        kb.op("dve", lambda e, h=h: e.tensor_tensor(Qexp[:, h, :, :], eyeb[:], qkn[:, h, :].unsqueeze(2).to_broadcast([128, NS, NS]), op=ALU.mult),
              R=[eyeb, qkn], W=[Qexp])
    kb.op("dve", lambda e: e.tensor_tensor(Egx[:], eyep[:, :].unsqueeze(2).to_broadcast([128, NS, NH]),
                                           eg_.unsqueeze(1).to_broadcast([128, NS, NH]), op=ALU.mult), R=[eyep, sm], W=[Egx])
    peg = PS()
    kb.op("pe", lambda e: e.matmul(peg[:, 0:NS * NH], lhsT=ones_f[:], rhs=Egx[:].rearrange("p a b -> p (a b)"), start=True, stop=True),
          R=[ones_f, Egx], W=[peg])
    kb.op("act", lambda e: e.activation(out=egb[:], in_=peg[:, 0:NS * NH].rearrange("p (a b) -> p a b", a=NS), func=AF.Copy), R=[peg], W=[egb])

    def bcs(ap):
        return ap.unsqueeze(2).to_broadcast([128, NH, 128])
    for h in range(NH):
        s0 = S0[h % 4]
        kb.dma("sp", s0[:], ss_d[:, h, :, :].rearrange("n k v -> k n v"), W=[s0], key=s0)
        s0b = S0b[h % 4]
        kb.op("act", lambda e: e.activation(out=s0b[:], in_=s0[:], func=AF.Copy), R=[s0], W=[s0b])
        if h == 0:
            kb.op("pool", lambda e: e.tensor_copy(tokb[:], tok[:, NH:2 * NH, :]), R=[tok], W=[tokb])
        pks = PS()
        for n in range(NS):
            kb.op("pe", lambda e, n=n: e.matmul(pks[0:NS, 0:128], lhsT=Kexp[:, h, n, :], rhs=s0b[:, n, :], start=(n == 0), stop=(n == NS - 1)),
                  R=[Kexp, s0b], W=[pks])
        for n in range(NS):
            kb.op("pe", lambda e, n=n: e.matmul(pks[0:NS, 128:256], lhsT=Qexp[:, h, n, :], rhs=s0b[:, n, :], start=(n == 0), stop=(n == NS - 1)),
                  R=[Qexp, s0b], W=[pks])
        kb.op("act", lambda e: e.activation(out=KS[0:NS, h, :], in_=pks[0:NS, 0:128], func=AF.Copy), R=[pks], W=[KS])
        kb.op("act", lambda e: e.activation(out=QS[0:NS, h, :], in_=pks[0:NS, 128:256], func=AF.Copy), R=[pks], W=[QS])
        r16 = slice(0, NS)
        kb.op("dve", lambda e: e.scalar_tensor_tensor(out=tm[r16, h, :], in0=KS[r16, h, :], scalar=eg_[r16, h:h + 1], in1=v_t[r16, h, :],
                                                      op0=ALU.mult, op1=ALU.subtract), R=[KS, sm, tok], W=[tm])
        kb.op("dve", lambda e: e.tensor_scalar(vn[r16, h, :], tm[r16, h, :], be_[r16, h:h + 1], -1.0, op0=ALU.mult, op1=ALU.mult),
              R=[tm, sm], W=[vn])
        kb.op("dve", lambda e: e.tensor_scalar(tm[r16, h, :], QS[r16, h, :], eg_[r16, h:h + 1], None, op0=ALU.mult), R=[QS, sm], W=[tm])
        kb.op("dve", lambda e: e.scalar_tensor_tensor(out=ot[r16, h, :], in0=vn[r16, h, :], scalar=qk_[r16, h:h + 1], in1=tm[r16, h, :],
                                                      op0=ALU.mult, op1=ALU.add), R=[vn, sm, tm], W=[ot])
        vx = Vexp[h % 4]
        kb.op("dve", lambda e: e.tensor_tensor(vx[:], vn[:, h, :].unsqueeze(1).to_broadcast([128, NS, 128]),
                                               eyep[:, :].unsqueeze(2).to_broadcast([128, NS, 128]), op=ALU.mult), R=[vn, eyep], W=[vx])
        for n4 in range(NS // 4):
            pss = PS()
            so = Sout[n4 % 4]
            for j in range(4):
                n = n4 * 4 + j
                kb.op("pe", lambda e, n=n, j=j: e.matmul(pss[:, j * 128:(j + 1) * 128], lhsT=tokb[:, h, :], rhs=vx[:, n, :], start=True, stop=True),
                      R=[tokb, vx], W=[pss])
            for j in range(4):
                n = n4 * 4 + j
                kb.op("dve", lambda e, n=n, j=j: e.scalar_tensor_tensor(out=so[:, j, :], in0=s0[:, n, :], scalar=egb[:, n, h:h + 1],
                                                                        in1=pss[:, j * 128:(j + 1) * 128], op0=ALU.mult, op1=ALU.add),
                      R=[s0, egb, pss], W=[so])
            kb.dma("sp", ssms_d[n4 * 4:(n4 + 1) * 4, h, :, :].rearrange("n k v -> k n v"), so[:], R=[so], W=[], key=so)
    kb.op("dve", lambda e: e.tensor_tensor(tm[:], ot[:], ot[:], op=ALU.mult), R=[ot], W=[tm])
    kb.op("dve", lambda e: e.tensor_reduce(out=ri_, in_=tm[:], axis=mybir.AxisListType.X, op=ALU.add), R=[tm], W=[sm])
    kb.op("act", lambda e: e.activation(out=ri_, in_=ri_, func=AF.Ln, scale=1.0 / 128, bias=cbias[:, 2:3]), R=[sm, cbias], W=[sm])
    kb.op("act", lambda e: e.activation(out=ri_, in_=ri_, func=AF.Exp, scale=-0.5), R=[sm], W=[sm])
    kb.op("dve", lambda e: e.tensor_tensor(ot[:], ot[:], bcs(ri_), op=ALU.mult), R=[ot, sm], W=[ot])
    pot = PS()
    for h in range(NH):
        kb.op("pe", lambda e, h=h: e.matmul(pot[:, h * NS:(h + 1) * NS], lhsT=ot[:, h, :], rhs=ident_f[:, 0:NS], start=True, stop=True),
              R=[ot, ident_f], W=[pot])
    kb.op("dve", lambda e: e.scalar_tensor_tensor(out=osT[:], in0=pot[:, 0:NH * NS].rearrange("p (a b) -> p a b", a=NH), scalar=ong[:, 0:1],
                                                  in1=gsil[:], op0=ALU.mult, op1=ALU.mult), R=[pot, ong, gsil], W=[osT])
    kb.dma("sp", os_scr.sloc(smp["grp"])[:, :, 0:NS], osT[:], R=[osT], W=[os_scr], key=osT)

import numpy as np

EPS = 1e-6
NQ = 16
NS = 16


def host_consts_b():
    c = {}
    i = np.arange(128)
    slopes = np.exp2(-8.0 * np.arange(1, NQ + 1, dtype=np.float32) / NQ).astype(np.float32)
    bias = np.zeros((128, 2, NQ, 128), np.float32)
    jj = i[:, None]
    ii = i[None, :]
    for h in range(NQ):
        cur = np.where(ii >= jj, -slopes[h] * (ii - jj), -30000.0)
        prv = np.where(jj >= ii, -slopes[h] * (128 + ii - jj), -30000.0)
        bias[:, 0, h, :] = prv
        bias[:, 1, h, :] = cur
    c["abias"] = bias
    bo = np.zeros((128, 128), np.float32)
    bo[:64, :64] = 1.0
    bo[64:, 64:] = 1.0
    c["bones"] = bo
    eo = np.zeros((128, 2, 128), np.float32)
    eo[:, 0, :64] = 1.0
    eo[:, 1, 64:] = 1.0
    c["eones"] = eo
    bs = np.zeros((NS * 4, 4, 128), np.float32)
    for n in range(NS):
        for g in range(4):
            for hh in range(4):
                bs[n * 4 + g, hh, :] = -slopes[4 * g + hh] * (128 - i)
    c["sbias"] = bs
    c["eye16"] = np.tile(np.eye(16, dtype=np.float32)[None], (128, 1, 1)).copy()
    pm = np.zeros((128, 64), np.float32)
    pm[np.arange(128), np.arange(128) % 64] = 1.0
    c["pairM"] = pm
    return c


def alloc_weights_b(kb):
    Woa = kb.sb([128, 8, 1024], BF16, "Woa")
    Wkv = kb.sb([128, 8, 512], BF16, "Wkv")
    Wb = kb.sb([128, 8, 2048], BF16, "Wb")
    Wob = kb.sb([128, 8, 1024], BF16, "Wob")
    gk = kb.sb([128, 8], F32, "gkv")
    gb = kb.sb([128, 8], F32, "gnb")
    return Woa, Wkv, Wb, Wob, gk, gb


def load_weights_b(kb, W6, woa_d, wkv_d, kvn_d, wb_d, nb_d, wob_d):
    Woa, Wkv, Wb, Wob, gk, gb = W6
    stg = [kb.sb([128, 2048], F32, f"stgb{i}") for i in range(4)]
    kb.dma("sp", gk[:], kvn_d[:, :], W=[gk], key=gk)
    kb.dma("sp", gb[:], nb_d[:, :], W=[gb], key=gb)
    i = 0
    for (dst, src, n, g) in ((Woa, woa_d, 1024, None), (Wkv, wkv_d, 512, gk), (Wb, wb_d, 2048, gb), (Wob, wob_d, 1024, None)):
        sv = src.rearrange("(kc p) n -> p kc n", p=128)
        for kc in range(8):
            s = stg[i % 4]
            q_ = ("sp", "act")[i % 2]
            i += 1
            kb.dma(q_, s[:, 0:n], sv[:, kc, :], W=[s], key=s)
            if g is None:
                kb.op("act" if kc % 2 else "dve",
                      (lambda e: e.activation(out=dst[:, kc, :], in_=s[:, 0:n], func=AF.Copy)) if kc % 2 else
                      (lambda e: e.tensor_copy(dst[:, kc, :], s[:, 0:n])), R=[s], W=[dst])
            else:
                kb.op("act" if kc % 2 else "dve",
                      (lambda e: e.activation(out=dst[:, kc, :], in_=s[:, 0:n], func=AF.Copy, scale=g[:, kc:kc + 1])) if kc % 2 else
                      (lambda e: e.tensor_scalar(dst[:, kc, :], s[:, 0:n], g[:, kc:kc + 1], None, op0=ALU.mult)),
                      R=[s, g], W=[dst])


def phase_b(kb, cb, x_d, o_scr, y_d, kwin_d, vwin_d, kng_d, qng_d, snk_d, Wts, NBLK, psT, psF, smp=None, idx_d=None, ab1_d=None, NBT=None, wsrc=None):
    Woa, Wkv, Wb, Wob = Wts[:4]
    pfi = [0]

    def PS():
        p = psF[pfi[0] % len(psF)]
        pfi[0] += 1
        return p

    def v4(p, a=4):
        return p.t[:, :].rearrange("p (a b) -> p a b", a=a)

    ident_f = kb.sb([128, 128], F32, "b_ident_f")
    ident_b = kb.sb([128, 128], BF16, "b_ident_b")
    bones_f = kb.sb([128, 128], F32, "bones_f")
    bones = kb.sb([128, 128], BF16, "bones")
    eones_f = kb.sb([128, 2, 128], F32, "eones_f")
    eones = kb.sb([128, 2, 128], BF16, "eones")
    kng = kb.sb([128, 64], F32, "kng")
    qng = kb.sb([128, 1], F32, "qng")
    esink = kb.sb([128, 8], F32, "esink")
    cbias = kb.sb([128, 4], F32, "b_cbias")
    cl = kb.sb([128, 1], F32, "b_cload")
    lds = []

    def ld(dst, src):
        lds.append((dst, src))
    ld(ident_f, cb["ident"][:, :]); ld(bones_f, cb["bones"][:, :]); ld(eones_f, cb["eones"][:, :, :])
    ld(kng, kng_d[:, :]); ld(qng, qng_d[:, :]); ld(esink, snk_d[:, :])
    kb.dma_multi("sp", [(d_[:], s_) for d_, s_ in lds], W=[d_ for d_, _ in lds], key=cl)
    kb.op("dve", lambda e: e.tensor_copy(ident_b[:], ident_f[:]), R=[ident_f], W=[ident_b])
    kb.op("dve", lambda e: e.tensor_copy(bones[:], bones_f[:]), R=[bones_f], W=[bones])
    kb.op("dve", lambda e: e.tensor_copy(eones[:], eones_f[:]), R=[eones_f], W=[eones])
    kb.op("act", lambda e: e.activation(out=esink[:], in_=esink[:], func=AF.Exp), R=[esink], W=[esink])
    kb.op("dve", lambda e: e.tensor_scalar(qng[:], qng[:], 0.125, None, op0=ALU.mult), R=[qng], W=[qng])
    for j, v in enumerate([EPS, 1.0]):
        kb.op("pool", lambda e, j=j, v=v: e.memset(cbias[:, j:j + 1], v), W=[cbias])

    idxt = kb.sb([128, (NBLK + 1) * 2], I32, "b_idx")
    kb.dma("sp", idxt[:], idx_d[:, :], W=[idxt], key=idxt)
    hs = kb.sb([128, 1024], BF16, "b_hs")
    junk = kb.sb([128, 1024], BF16, "b_junk")
    rr = kb.sb([128, 4], F32, "b_rr")
    hT = kb.sb([128, 8, 128], BF16, "b_hT")
    def rms_and_T(src, ntok, hT_out):
        kb.op("act", lambda e: e.activation(out=junk[0:ntok, :], in_=src[0:ntok, :], func=AF.Square,
                                            accum_out=rr[0:ntok, 0:1]), R=[src], W=[junk, rr])
        kb.op("act", lambda e: e.activation(out=rr[0:ntok, 1:2], in_=rr[0:ntok, 0:1], func=AF.Ln,
                                            scale=1.0 / 1024, bias=cbias[0:ntok, 0:1]), R=[rr, cbias], W=[rr])
        kb.op("act", lambda e: e.activation(out=rr[0:ntok, 2:3], in_=rr[0:ntok, 1:2], func=AF.Exp, scale=-0.5), R=[rr], W=[rr])
        kb.op("act", lambda e: e.activation(out=hs[0:ntok, :], in_=src[0:ntok, :], func=AF.Copy, scale=rr[0:ntok, 2:3]),
              R=[src, rr], W=[hs])
        pt = psT[0]
        ptB = pt.t[:, :]
        for kc in range(8):
            kb.op("pe", lambda e, kc=kc: e.transpose(out=ptB[:, kc * 128:kc * 128 + ntok], in_=hs[0:ntok, kc * 128:(kc + 1) * 128],
                                                     identity=ident_b[0:ntok, 0:ntok]), R=[hs, ident_b], W=[pt])
        kb.op("act", lambda e: e.activation(out=hT_out[:, :, 0:ntok],
                                            in_=ptB.rearrange("p (k t) -> p k t", k=8)[:, :, 0:ntok], func=AF.Copy),
              R=[pt], W=[hT_out])

    kb.push()
    load_weights_b(kb, Wts, *wsrc)
    if smp is not None:
        sample_b(kb, smp, locals(), PS, psT, rms_and_T)
    kb.pop()
    kb.push()
    abias = kb.sb([128, 2, NQ, 128], F32, "abias")
    kb.dma("sp", abias[:], cb["abias"][:, :, :, :], W=[abias], key=abias)
    abias1 = kb.sb([128, NQ, 128], F32, "abias1")
    kb.dma("sp", abias1[:], ab1_d[:, :, :], W=[abias1], key=abias1)
    xb = [kb.sb([128, 1024], F32, f"b_x{i}") for i in range(2)]
    ob = [kb.sb([128, 8, 128], BF16, f"b_o{i}") for i in range(2)]
    hb_2 = [kb.sb([128, 1024], F32, f"b_hb{i_}") for i_ in range(2)]
    kv_2 = [kb.sb([128, 512], F32, f"b_kv{i_}") for i_ in range(2)]
    kss_2 = [kb.sb([128, 8], F32, f"b_kss{i_}") for i_ in range(2)]
    kn_2 = [kb.sb([128, 4, 64], F32, f"b_kn{i_}") for i_ in range(2)]
    kdup_2 = [kb.sb([128, 4, 2, 64], BF16, f"b_kdup{i_}") for i_ in range(2)]
    kT2 = [kb.sb([128, 4, 128], BF16, f"b_kT2{i}") for i in range(2)]
    vE = [kb.sb([128, 4, 128], BF16, f"b_vE{i}") for i in range(2)]
    vO = [kb.sb([128, 4, 128], BF16, f"b_vO{i}") for i in range(2)]
    sq_2 = [kb.sb([128, 4, 128], BF16, f"b_sq{i_}") for i_ in range(2)]
    rq_2 = [kb.sb([128, 4, 128], F32, f"b_rq{i_}") for i_ in range(2)]
    qT_2 = [kb.sb([128, 2, 8, 128], BF16, f"b_qT{i_}") for i_ in range(2)]
    tg_2 = [kb.sb([128, 4, 128], F32, f"b_tg{i_}") for i_ in range(2)]
    gsl_2 = [kb.sb([128, 8, 128], BF16, f"b_gsl{i_}") for i_ in range(2)]
    ssb = [kb.sb([128, 4, 128], F32, f"b_ss{i}") for i in range(2)]
    PT = [kb.sb([128, 4, 128], BF16, f"b_PT{i}") for i in range(4)]
    den_2 = [kb.sb([128, 4, 128], F32, f"b_den{i_}") for i_ in range(2)]
    o1_2 = [kb.sb([128, 4, 128], F32, f"b_o1{i_}") for i_ in range(2)]
    oTb_2 = [kb.sb([128, 8, 128], BF16, f"b_oTb{i_}") for i_ in range(2)]
    yb = [kb.sb([128, 1024], F32, f"b_y{i}") for i in range(2)]
    for i_ in range(2):
        kb.op("pool", lambda e, i_=i_: e.memset(qT_2[i_][:], 0.0), W=[qT_2[i_]])
    for i in range(2):
        kb.op("pool", lambda e, i=i: e.memset(vE[i][:], 0.0), W=[vE[i]])
        kb.op("pool", lambda e, i=i: e.memset(vO[i][:], 0.0), W=[vO[i]])

    for blk in range(NBLK):
        t0 = blk * 128
        par = blk % 2
        x_ = xb[par]
        o_ = ob[par]
        hb, kv, kss, kn, kdup, sq, rq, qT, tg, gsl, den, o1, oTb = (hb_2[par], kv_2[par], kss_2[par], kn_2[par], kdup_2[par], sq_2[par],
                                                                  rq_2[par], qT_2[par], tg_2[par], gsl_2[par], den_2[par], o1_2[par], oTb_2[par])
        kb.dma("sp", x_[:], x_d[t0:t0 + 128, :], W=[x_], key=x_)
        kb.ind_dma_multi([(o_[:, 4 * hg:4 * hg + 4, :].rearrange("p h t -> p (h t)"), idxt[:, blk * 2 + hg:blk * 2 + hg + 1], o_scr.gsrc(blk)) for hg in range(2)],
                         None, o_scr.gnr(blk), R=[o_scr.gbuf(blk), idxt], W=[o_], key=o_)
        for half in range(2):
            ph = PS()
            for h in range(8):
                kb.op("pe", lambda e, h=h: e.matmul(ph[:, :], lhsT=o_[:, h, :], rhs=Woa[:, h, half * 512:(half + 1) * 512],
                                                   start=(h == 0), stop=(h == 7)), R=[o_, Woa], W=[ph])
            kb.op("dve", lambda e: e.tensor_tensor(hb[:, half * 512:(half + 1) * 512], ph[:, :], x_[:, half * 512:(half + 1) * 512],
                                                   op=ALU.add), R=[ph, x_], W=[hb])
        rms_and_T(hb, 128, hT)
        pk = PS()
        for kc in range(8):
            kb.op("pe", lambda e, kc=kc: e.matmul(pk[:, :], lhsT=hT[:, kc, :], rhs=Wkv[:, kc, :], start=(kc == 0), stop=(kc == 7)),
                  R=[hT, Wkv], W=[pk])
        kb.op("act", lambda e: e.activation(out=kv[:], in_=pk[:, :], func=AF.Copy), R=[pk], W=[kv])
        for g in range(4):
            kb.op("act", lambda e, g=g: e.activation(out=junk[:, 0:64], in_=kv[:, g * 64:(g + 1) * 64], func=AF.Square,
                                                    accum_out=kss[:, g:g + 1]), R=[kv], W=[junk, kss])
        kb.op("act", lambda e: e.activation(out=kss[:, 4:8], in_=kss[:, 0:4], func=AF.Ln, scale=1.0 / 64, bias=cbias[:, 0:1]),
              R=[kss, cbias], W=[kss])
        kb.op("act", lambda e: e.activation(out=kss[:, 4:8], in_=kss[:, 4:8], func=AF.Exp, scale=-0.5), R=[kss], W=[kss])
        kb.op("dve", lambda e: e.tensor_tensor(kn[:], kv[:, 0:256].rearrange("p (g d) -> p g d", g=4),
                                               kss[:, 4:8].unsqueeze(2).to_broadcast([128, 4, 64]), op=ALU.mult), R=[kv, kss], W=[kn])
        kb.op("dve", lambda e: e.tensor_tensor(kn[:], kn[:], kng[:, :].unsqueeze(1).to_broadcast([128, 4, 64]), op=ALU.mult),
              R=[kn, kng], W=[kn])
        kb.op("act", lambda e: e.activation(out=kdup[:, :, 0, :], in_=kn[:], func=AF.Copy), R=[kn], W=[kdup])
        kb.op("act", lambda e: e.activation(out=kdup[:, :, 1, :], in_=kn[:], func=AF.Copy), R=[kn], W=[kdup])
        vv = kv[:, 256:512].rearrange("p (g d) -> p g d", g=4)
        kb.op("act", lambda e: e.activation(out=vE[par][:, :, 0:64], in_=vv, func=AF.Copy), R=[kv], W=[vE[par]])
        kb.op("act", lambda e: e.activation(out=vO[par][:, :, 64:128], in_=vv, func=AF.Copy), R=[kv], W=[vO[par]])
        pt = psT[1]
        ptB = pt.t[:, :]
        for g in range(4):
            kb.op("pe", lambda e, g=g: e.transpose(out=ptB[:, g * 128:(g + 1) * 128], in_=kdup[:, g, :, :].rearrange("p a d -> p (a d)"),
                                                   identity=ident_b[:]), R=[kdup, ident_b], W=[pt])
        kb.op("act", lambda e: e.activation(out=kT2[par][:], in_=ptB[:, 0:512].rearrange("p (g t) -> p g t", g=4), func=AF.Copy),
              R=[pt], W=[kT2[par]])
        if blk == NBLK - 1:
            kb.dma("sp", kwin_d[:, :], kn[:].rearrange("p g d -> p (g d)"), R=[kn], W=[], key=kn)
            kb.dma("sp", vwin_d[:, :], kv[:, 256:512], R=[kv], W=[], key=kv)
        if blk == 0:
            continue
        for grp in range(4):
            pq = PS()
            for cc in range(4):
                col0 = (grp * 4 + cc) * 128
                for kc in range(8):
                    kb.op("pe", lambda e, kc=kc: e.matmul(pq[:, cc * 128:(cc + 1) * 128], lhsT=Wb[:, kc, col0:col0 + 128], rhs=hT[:, kc, :],
                                                         start=(kc == 0), stop=(kc == 7)), R=[Wb, hT], W=[pq])
            if grp < 2:
                kb.op("act", lambda e: e.activation(out=sq[:], in_=v4(pq), func=AF.Square), R=[pq], W=[sq])
                pn = PS()
                kb.op("pe", lambda e: e.matmul(pn[:, :], lhsT=bones[:], rhs=sq[:].rearrange("p a b -> p (a b)"), start=True, stop=True),
                      R=[bones, sq], W=[pn])
                kb.op("act", lambda e: e.activation(out=rq[:], in_=v4(pn), func=AF.Ln, scale=1.0 / 64, bias=cbias[:, 0:1]),
                      R=[pn, cbias], W=[rq])
                kb.op("act", lambda e: e.activation(out=rq[:], in_=rq[:], func=AF.Exp, scale=-0.5), R=[rq], W=[rq])
                for hf in range(2):
                    ps_ = slice(hf * 64, (hf + 1) * 64)
                    kb.op("dve", lambda e: e.scalar_tensor_tensor(out=qT[ps_, hf, grp * 4:(grp + 1) * 4, :], in0=v4(pq)[ps_], scalar=qng[ps_, 0:1],
                                                                  in1=rq[ps_], op0=ALU.mult, op1=ALU.mult), R=[pq, qng, rq], W=[qT])
            else:
                kb.op("act", lambda e: e.activation(out=tg[:], in_=v4(pq), func=AF.Tanh, scale=0.5), R=[pq], W=[tg])
                kb.op("dve", lambda e: e.scalar_tensor_tensor(out=gsl[:, (grp - 2) * 4:(grp - 1) * 4, :], in0=tg[:], scalar=1.0, in1=v4(pq),
                                                              op0=ALU.add, op1=ALU.mult), R=[tg, pq], W=[gsl])
        kbs = [0, 1]
        pO = [psF[0], psF[1]]
        pD = [psF[2], psF[3]]
        sidx = 0
        for g in range(4):
            pts = []
            for ki, kbk in enumerate(kbs):
                kpar = par if kbk == 1 else 1 - par
                psS = psF[4 + sidx % 2]
                sidx += 1
                for hh in range(4):
                    head = 4 * g + hh
                    c, hf = head // 2, head % 2
                    kb.op("pe", lambda e, hh=hh, c=c, hf=hf: e.matmul(psS[:, hh * 128:(hh + 1) * 128],
                                                                     lhsT=kT2[kpar][:, g, :],
                                                                     rhs=qT[:, hf, c, :], start=True, stop=True),
                          R=[kT2[kpar], qT], W=[psS])
                s_ = ssb[ki]
                bsrc = abias1[:, 4 * g:4 * g + 4, :] if (blk == 1 and kbk == 0) else abias[:, kbk, 4 * g:4 * g + 4, :]
                kb.op("dve", lambda e: e.tensor_tensor(s_[:], v4(psS), bsrc, op=ALU.add),
                      R=[psS, abias, abias1], W=[s_])
                p_ = PT[(g % 2) * 2 + ki]
                kb.op("act", lambda e: e.activation(out=p_[:], in_=s_[:], func=AF.Exp), R=[s_], W=[p_])
                pts.append((p_, kpar))
            for hh in range(4):
                head = 4 * g + hh
                c, hf = head // 2, head % 2
                bank, cc = c // 4, c % 4
                first = (hf == 0)
                for ki, (p_, kpar) in enumerate(pts):
                    vsrc = vE[kpar] if hf == 0 else vO[kpar]
                    st = first and ki == 0
                    sp_ = (hf == 1) and ki == len(pts) - 1
                    kb.op("pe", lambda e: e.matmul(pO[bank][:, cc * 128:(cc + 1) * 128], lhsT=vsrc[:, g, :], rhs=p_[:, hh, :],
                                                   start=st, stop=sp_), R=[vsrc, p_], W=[pO[bank]])
                    kb.op("pe", lambda e: e.matmul(pD[bank][:, cc * 128:(cc + 1) * 128], lhsT=eones[:, hf, :], rhs=p_[:, hh, :],
                                                   start=st, stop=sp_), R=[eones, p_], W=[pD[bank]])
        for bank in range(2):
            kb.op("dve", lambda e: e.tensor_tensor(den[:], v4(pD[bank]),
                                                   esink[:, bank * 4:(bank + 1) * 4].unsqueeze(2).to_broadcast([128, 4, 128]), op=ALU.add),
                  R=[pD[bank], esink], W=[den])
            kb.op("dve", lambda e: e.reciprocal(den[:], den[:]), R=[den], W=[den])
            kb.op("dve", lambda e: e.tensor_tensor(o1[:], v4(pO[bank]), den[:], op=ALU.mult), R=[pO[bank], den], W=[o1])
            kb.op("dve", lambda e: e.scalar_tensor_tensor(out=oTb[:, bank * 4:(bank + 1) * 4, :], in0=o1[:], scalar=0.5,
                                                          in1=gsl[:, bank * 4:(bank + 1) * 4, :], op0=ALU.mult, op1=ALU.mult),
                  R=[o1, gsl], W=[oTb])
        y_ = yb[par]
        for half in range(2):
            py = PS()
            for c in range(8):
                kb.op("pe", lambda e, c=c: e.matmul(py[:, :], lhsT=oTb[:, c, :], rhs=Wob[:, c, half * 512:(half + 1) * 512],
                                                   start=(c == 0), stop=(c == 7)), R=[oTb, Wob], W=[py])
            kb.op("dve", lambda e: e.tensor_tensor(y_[:, half * 512:(half + 1) * 512], py[:, :], hb[:, half * 512:(half + 1) * 512], op=ALU.add),
                  R=[py, hb], W=[y_])
        kb.dma("sp", y_d[t0 - 128:t0, :], y_[:], R=[y_], W=[], key=y_)

    kb.pop()


def sample_b(kb, smp, L, PS, psT, rms_and_T):
    NS_ = 16
    Woa, Wkv, Wb, Wob = L["Woa"], L["Wkv"], L["Wb"], L["Wob"]
    cbias, kng, ident_b, hT, rr = L["cbias"], L["kng"], L["ident_b"], L["hT"], L["rr"]
    xs_d, os_scr, ys_d, ck_d, cv_d, kws_d, vws_d = (smp[k] for k in ("xs", "os_scr", "ys", "ck", "cv", "kws", "vws"))
    q_scr, o2_scr, kn_scr, vn_scr = (smp[k] for k in ("q_scr", "o2_scr", "kn_scr", "vn_scr"))
    qgr_d, sb_d, sk_d = smp["qngr"], smp["sbias"], smp["snk64"]
    X = mybir.AxisListType.X
    xs_t = kb.sb([128, 1024], F32, "t_x")
    hs_t = kb.sb([128, 1024], F32, "t_h")
    osT = kb.sb([128, 8, 128], BF16, "t_osT")
    idxt, NBLK = L["idxt"], L["NBLK"]
    kv = kb.sb([128, 512], F32, "t_kv")
    kss = kb.sb([128, 8], F32, "t_kss")
    junk = kb.sb([128, 64], BF16, "t_junk")
    kn = kb.sb([128, 4, 64], F32, "t_kn")
    qg = kb.sb([128, 2048], F32, "t_qg")
    tq = kb.sb([128, 16, 64], F32, "t_tq")
    qss = kb.sb([128, 32], F32, "t_qss")
    qgr = kb.sb([128, 64], F32, "t_qgr")
    gsl = kb.sb([128, 1024], F32, "t_gsl")
    q64 = kb.sb([128, 4, 64], F32, "t_q64")
    kn64 = kb.sb([64, 64], F32, "t_kn64")
    vn64 = kb.sb([64, 64], F32, "t_vn64")
    bufA = kb.sb([128, 64, 64], F32, "t_bufA")
    bufB = kb.sb([128, 64, 64], F32, "t_bufB")
    s64 = kb.sb([128, 4, 64], F32, "t_s64")
    sbias = kb.sb([128, 4, 64], F32, "t_sbias")
    part = kb.sb([128, 4 * 64 + 4], F32, "t_part")
    pairM = kb.sb([128, 64], F32, "t_pairM")
    esk = kb.sb([64, 4], F32, "t_esk")
    sm = kb.sb([64, 16], F32, "t_sm")
    o64 = kb.sb([64, 4, 64], F32, "t_o64")
    t64 = kb.sb([64, 4, 64], F32, "t_t64")
    ot = kb.sb([128, 1024], F32, "t_ot")
    ob = kb.sb([128, 1024], BF16, "t_ob")
    oT = kb.sb([128, 8, 128], BF16, "t_oT")
    ys = kb.sb([128, 1024], F32, "t_ys")
    for b_ in (xs_t, hs_t, ot):
        kb.op("pool", lambda e, b_=b_: e.memset(b_[:], 0.0), W=[b_])
    kb.dma("sp", xs_t[0:NS_, :], xs_d[:, :], W=[xs_t], key=xs_t)
    kb.ind_dma_multi([(osT[:, 4 * hg:4 * hg + 4, :].rearrange("p h t -> p (h t)"), idxt[:, NBLK * 2 + hg:NBLK * 2 + hg + 1], os_scr.gsrc(NBLK)) for hg in range(2)],
                     None, os_scr.gnr(NBLK), R=[os_scr.gbuf(NBLK), idxt], W=[osT], key=osT)
    kb.dma("sp", qgr[:], qgr_d[:, :], W=[qgr], key=qgr)
    kb.op("dve", lambda e: e.tensor_scalar(qgr[:], qgr[:], 0.125, None, op0=ALU.mult), R=[qgr], W=[qgr])
    kb.dma_multi("sp", [(sbias[jh * 64:(jh + 1) * 64, :, :], sb_d[:, :, jh * 64:(jh + 1) * 64]) for jh in range(2)], W=[sbias], key=sbias)
    kb.dma("sp", pairM[:], smp["pairM"][:, :], W=[pairM], key=pairM)
    kb.dma("sp", esk[:], sk_d[:, :], W=[esk], key=esk)
    kb.op("act", lambda e: e.activation(out=esk[:], in_=esk[:], func=AF.Exp), R=[esk], W=[esk])
    kb.dma("sp", kws_d[:, 0:127, :], ck_d[:, 1:128, :], W=[], key=junk)
    kb.dma("sp", vws_d[:, 0:127, :], cv_d[:, 1:128, :], W=[], key=junk)
    for half in range(2):
        ph = PS()
        for h in range(8):
            kb.op("pe", lambda e, h=h: e.matmul(ph[0:NS_, :], lhsT=osT[:, h, 0:NS_], rhs=Woa[:, h, half * 512:(half + 1) * 512],
                                               start=(h == 0), stop=(h == 7)), R=[osT, Woa], W=[ph])
        kb.op("dve", lambda e: e.tensor_tensor(hs_t[0:NS_, half * 512:(half + 1) * 512], ph[0:NS_, :], xs_t[0:NS_, half * 512:(half + 1) * 512],
                                               op=ALU.add), R=[ph, xs_t], W=[hs_t])
    rms_and_T(hs_t, 128, hT)
    pk = PS()
    for kc in range(8):
        kb.op("pe", lambda e, kc=kc: e.matmul(pk[:, :], lhsT=hT[:, kc, :], rhs=Wkv[:, kc, :], start=(kc == 0), stop=(kc == 7)), R=[hT, Wkv], W=[pk])
    kb.op("act", lambda e: e.activation(out=kv[:], in_=pk[:, :], func=AF.Copy), R=[pk], W=[kv])
    for g in range(4):
        kb.op("act", lambda e, g=g: e.activation(out=junk[:, 0:64], in_=kv[:, g * 64:(g + 1) * 64], func=AF.Square, accum_out=kss[:, g:g + 1]),
              R=[kv], W=[junk, kss])
    kb.op("act", lambda e: e.activation(out=kss[:, 4:8], in_=kss[:, 0:4], func=AF.Ln, scale=1.0 / 64, bias=cbias[:, 0:1]), R=[kss, cbias], W=[kss])
    kb.op("act", lambda e: e.activation(out=kss[:, 4:8], in_=kss[:, 4:8], func=AF.Exp, scale=-0.5), R=[kss], W=[kss])
    kb.op("dve", lambda e: e.tensor_tensor(kn[:], kv[:, 0:256].rearrange("p (g d) -> p g d", g=4),
                                           kss[:, 4:8].unsqueeze(2).to_broadcast([128, 4, 64]), op=ALU.mult), R=[kv, kss], W=[kn])
    kb.op("dve", lambda e: e.tensor_tensor(kn[:], kn[:], kng[:, :].unsqueeze(1).to_broadcast([128, 4, 64]), op=ALU.mult), R=[kn, kng], W=[kn])
    kb.dma("sp", kws_d[:, 127, :], kn[0:NS_].rearrange("p g d -> p (g d)"), R=[kn], W=[], key=kn)
    kb.dma("sp", vws_d[:, 127, :], kv[0:NS_, 256:512], R=[kv], W=[], key=kv)
    kb.dma("sp", kn_scr.t[:, :], kn[0:NS_].rearrange("p g d -> p (g d)"), R=[kn], W=[kn_scr], key=kn)
    kb.dma("sp", vn_scr.t[:, :], kv[0:NS_, 256:512], R=[kv], W=[vn_scr], key=kv)
    for j in range(4):
        pq = PS()
        for kc in range(8):
            kb.op("pe", lambda e, kc=kc: e.matmul(pq[:, :], lhsT=hT[:, kc, :], rhs=Wb[:, kc, j * 512:(j + 1) * 512], start=(kc == 0), stop=(kc == 7)),
                  R=[hT, Wb], W=[pq])
        kb.op("act", lambda e: e.activation(out=qg[:, j * 512:(j + 1) * 512], in_=pq[:, :], func=AF.Copy), R=[pq], W=[qg])
    q3 = qg[:, 0:1024].rearrange("p (h d) -> p h d", h=16)
    kb.op("dve", lambda e: e.tensor_tensor(tq[:], q3, q3, op=ALU.mult), R=[qg], W=[tq])
    kb.op("dve", lambda e: e.tensor_reduce(out=qss[:, 0:16], in_=tq[:], axis=X, op=ALU.add), R=[tq], W=[qss])
    kb.op("act", lambda e: e.activation(out=qss[:, 16:32], in_=qss[:, 0:16], func=AF.Ln, scale=1.0 / 64, bias=cbias[:, 0:1]), R=[qss, cbias], W=[qss])
    kb.op("act", lambda e: e.activation(out=qss[:, 16:32], in_=qss[:, 16:32], func=AF.Exp, scale=-0.5), R=[qss], W=[qss])
    kb.op("dve", lambda e: e.tensor_tensor(tq[:], q3, qss[:, 16:32].unsqueeze(2).to_broadcast([128, 16, 64]), op=ALU.mult), R=[qg, qss], W=[tq])
    kb.op("dve", lambda e: e.tensor_tensor(tq[:], tq[:], qgr[:, :].unsqueeze(1).to_broadcast([128, 16, 64]), op=ALU.mult), R=[tq, qgr], W=[tq])
    kb.dma("sp", q_scr.t[:, :], tq[0:NS_].rearrange("p h d -> p (h d)"), R=[tq], W=[q_scr], key=tq)
    kb.op("act", lambda e: e.activation(out=gsl[:], in_=qg[:, 1024:2048], func=AF.Tanh, scale=0.5), R=[qg], W=[gsl])
    kb.op("dve", lambda e: e.scalar_tensor_tensor(out=gsl[:], in0=gsl[:], scalar=1.0, in1=qg[:, 1024:2048], op0=ALU.add, op1=ALU.mult),
          R=[gsl, qg], W=[gsl])
    kb.dma_multi("sp", [(q64[jh * 64:(jh + 1) * 64], q_scr.t[:, :].rearrange("n (g hh d) -> (n g) hh d", g=4, hh=4)) for jh in range(2)],
                 R=[q_scr], W=[q64], key=q64)
    kb.dma("sp", kn64[:], kn_scr.t[:, :].rearrange("n (g d) -> (n g) d", g=4), R=[kn_scr], W=[kn64], key=kn64)
    kb.dma("sp", vn64[:], vn_scr.t[:, :].rearrange("n (g d) -> (n g) d", g=4), R=[vn_scr], W=[vn64], key=vn64)
    kb.dma_multi("sp", [(bufA[jh * 64 + 4 * n:jh * 64 + 4 * n + 4, :, :], ck_d[n, jh * 64:(jh + 1) * 64, :].rearrange("j (g d) -> g j d", g=4))
                        for n in range(NS_) for jh in range(2)], W=[bufA], key=bufA)
    for hh in range(4):
        kb.op("dve", lambda e, hh=hh: e.tensor_tensor(bufB[:], bufA[:], q64[:, hh, :].unsqueeze(1).to_broadcast([128, 64, 64]), op=ALU.mult),
              R=[bufA, q64], W=[bufB])
        kb.op("dve", lambda e, hh=hh: e.tensor_reduce(out=s64[:, hh, :], in_=bufB[:], axis=X, op=ALU.add), R=[bufB], W=[s64])
    kb.op("dve", lambda e: e.tensor_tensor(s64[:], s64[:], sbias[:], op=ALU.add), R=[s64, sbias], W=[s64])
    kb.op("act", lambda e: e.activation(out=s64[:], in_=s64[:], func=AF.Exp), R=[s64], W=[s64])
    kb.op("dve", lambda e: e.tensor_reduce(out=part[:, 256:260], in_=s64[:], axis=X, op=ALU.add), R=[s64], W=[part])
    kb.op("dve", lambda e: e.tensor_tensor(t64[:], q64[0:64], kn64[:, :].unsqueeze(1).to_broadcast([64, 4, 64]), op=ALU.mult), R=[q64, kn64], W=[t64])
    kb.op("dve", lambda e: e.tensor_reduce(out=sm[:, 4:8], in_=t64[:], axis=X, op=ALU.add), R=[t64], W=[sm])
    kb.op("act", lambda e: e.activation(out=sm[:, 4:8], in_=sm[:, 4:8], func=AF.Exp), R=[sm], W=[sm])
    kb.dma_multi("sp", [(bufB[jh * 64 + 4 * n:jh * 64 + 4 * n + 4, :, :], cv_d[n, jh * 64:(jh + 1) * 64, :].rearrange("j (g d) -> g j d", g=4))
                        for n in range(NS_) for jh in range(2)], W=[bufB], key=bufB)
    for hh in range(4):
        kb.op("dve", lambda e, hh=hh: e.tensor_tensor(bufA[:].rearrange("p j d -> p d j"), bufB[:].rearrange("p j d -> p d j"),
                                                      s64[:, hh, :].unsqueeze(1).to_broadcast([128, 64, 64]), op=ALU.mult),
              R=[bufB, s64], W=[bufA])
        kb.op("dve", lambda e, hh=hh: e.tensor_reduce(out=part[:, hh * 64:(hh + 1) * 64], in_=bufA[:].rearrange("p j d -> p d j"), axis=X, op=ALU.add),
              R=[bufA], W=[part])
    pcm = PS()
    kb.op("pe", lambda e: e.matmul(pcm[0:64, 0:260], lhsT=pairM[:], rhs=part[:], start=True, stop=True), R=[pairM, part], W=[pcm])
    kb.op("act", lambda e: e.activation(out=o64[:], in_=pcm[0:64, 0:256].rearrange("p (a b) -> p a b", a=4), func=AF.Copy), R=[pcm], W=[o64])
    kb.op("act", lambda e: e.activation(out=sm[:, 0:4], in_=pcm[0:64, 256:260], func=AF.Copy), R=[pcm], W=[sm])
    kb.op("dve", lambda e: e.tensor_tensor(sm[:, 8:12], sm[:, 0:4], sm[:, 4:8], op=ALU.add), R=[sm], W=[sm])
    kb.op("dve", lambda e: e.tensor_tensor(sm[:, 8:12], sm[:, 8:12], esk[:], op=ALU.add), R=[sm, esk], W=[sm])
    kb.op("dve", lambda e: e.reciprocal(sm[:, 8:12], sm[:, 8:12]), R=[sm], W=[sm])
    kb.op("dve", lambda e: e.tensor_tensor(t64[:], vn64[:, :].unsqueeze(1).to_broadcast([64, 4, 64]),
                                           sm[:, 4:8].unsqueeze(2).to_broadcast([64, 4, 64]), op=ALU.mult), R=[vn64, sm], W=[t64])
    kb.op("dve", lambda e: e.tensor_tensor(o64[:], o64[:], t64[:], op=ALU.add), R=[o64, t64], W=[o64])
    kb.op("dve", lambda e: e.tensor_tensor(o64[:], o64[:], sm[:, 8:12].unsqueeze(2).to_broadcast([64, 4, 64]), op=ALU.mult), R=[o64, sm], W=[o64])
    kb.dma("sp", o2_scr.t[:, :].rearrange("n (g hh d) -> (n g) hh d", g=4, hh=4), o64[:], R=[o64], W=[o2_scr], key=o64)
    kb.dma("sp", ot[0:NS_, :], o2_scr.t[:, :], R=[o2_scr], W=[ot], key=ot)
    kb.op("dve", lambda e: e.scalar_tensor_tensor(out=ob[:], in0=ot[:], scalar=0.5, in1=gsl[:], op0=ALU.mult, op1=ALU.mult), R=[ot, gsl], W=[ob])
    pt = psT[1]
    ptB = pt.t[:, :]
    for c in range(8):
        kb.op("pe", lambda e, c=c: e.transpose(out=ptB[:, c * 128:(c + 1) * 128], in_=ob[:, c * 128:(c + 1) * 128], identity=ident_b[:]),
              R=[ob, ident_b], W=[pt])
    kb.op("act", lambda e: e.activation(out=oT[:], in_=ptB.rearrange("p (k t) -> p k t", k=8), func=AF.Copy), R=[pt], W=[oT])
    for half in range(2):
        py = PS()
        for c in range(8):
            kb.op("pe", lambda e, c=c: e.matmul(py[:, :], lhsT=oT[:, c, :], rhs=Wob[:, c, half * 512:(half + 1) * 512], start=(c == 0), stop=(c == 7)),
                  R=[oT, Wob], W=[py])
        kb.op("dve", lambda e: e.tensor_tensor(ys[:, half * 512:(half + 1) * 512], py[:, :], hs_t[:, half * 512:(half + 1) * 512], op=ALU.add),
              R=[py, hs_t], W=[ys])
    kb.dma("sp", ys_d[:, :], ys[0:NS_, :], R=[ys], W=[], key=ys)

from concourse.bass_utils import run_bass_kernel_spmd

NHG = 4
T_FULL = 4096
GROUPS = [[0, 4], [1, 5], [2, 6], [3, 7]]


KCH = 3
GATHER_BARRIER = False


class Exchange(Buf):
    def __init__(self, kb, T):
        Buf.__init__(self, None, "xch")
        self.kb = kb
        self.HB = T // 2 // 128
        self.NBLK = self.HB + 1
        nslot = self.NBLK + 1
        self.nch = (nslot + KCH - 1) // KCH
        self.kc = [2 * min(KCH, nslot - c * KCH) for c in range(self.nch)]
        self.i = [kb.dram(f"xi{c}", [128 * self.kc[c], 512], BF16, "Internal") for c in range(self.nch)]
        self.g = [kb.dram(f"xg{c}", [2 * 128 * self.kc[c], 512], BF16, "Internal") for c in range(self.nch)]
        self.iv = [self.i[c].t.rearrange("(p l) (h t) -> p l h t", l=self.kc[c], t=128) for c in range(self.nch)]

    def _slot(self, k, s):
        c = k // KCH
        return c, (k - c * KCH) * 2 + s

    def loc(self, colblk):
        out = []
        for s in range(2):
            k = colblk - s * self.HB
            if 0 <= k < self.NBLK:
                c, l = self._slot(k, s)
                out.append(self.iv[c][:, l, :, :])
        return out

    def sloc(self, grp):
        c, l = self._slot(self.NBLK, grp)
        return self.iv[c][:, l, :, :]

    def gsrc(self, k):
        return self.g[k // KCH].t[:, :]

    def gbuf(self, k):
        return self.g[k // KCH]

    def gnr(self, k):
        return 2 * 128 * self.kc[k // KCH]

    def row(self, k, s, hg, p):
        c, l = self._slot(k, s)
        return hg * 128 * self.kc[c] + p * self.kc[c] + l

    def gather(self):
        kb = self.kb
        for c in range(self.nch):
            key = f"cc{c}"
            kb.semh[key] = kb.es.enter_context(kb.nc.semaphore(key))
            kb.cnt[key] = 0
            kb.dma_keys.append(key)
            kb.flush()
            kb._wait("pool", kb._deps([self], []))
            ins = kb.eng["pool"].collective_compute("AllGather", ALU.bypass, replica_groups=GROUPS,
                                                    ins=[self.i[c].t], outs=[self.g[c].t])
            kb.cnt[key] += 1
            ins.then_inc(kb.semh[key], 1)
            kb.nins += 1
            self.g[c].w = (key, 1)
            self.g[c].r = []
        if GATHER_BARRIER:
            kb.barrier()


def _prep_a(inp, hg):
    heads = [hg * NHG + i for i in range(NHG)]
    w = inp["w_in_a"][0]
    cols = []
    for base in (0, 1024, 2048, 3072):
        for h in heads:
            cols.append(np.arange(base + h * 128, base + (h + 1) * 128))
    cols.append(np.array([4096 + h for h in heads]))
    cols.append(np.array([4104 + h for h in heads]))
    cols = np.concatenate(cols)
    d = {}
    d["wa"] = np.ascontiguousarray(w[:, cols])
    cwf = inp["conv_w_a"][0]
    cw = np.zeros((128, 3 * NHG, 4), np.float32)
    for g, base in enumerate((0, 1024, 2048)):
        for i, h in enumerate(heads):
            cw[:, g * NHG + i, :] = cwf[:, base + h * 128: base + (h + 1) * 128].T
    d["cw"] = cw
    d["alog"] = np.ascontiguousarray(np.tile(inp["a_log"][0][heads][None, :], (128, 1)).astype(np.float32))
    d["dtb"] = np.ascontiguousarray(np.tile(inp["dt_bias"][0][heads][None, :], (128, 1)).astype(np.float32))
    return d


def build(T=T_FULL):
    kb = KB()
    NSB = T // 512
    HALF = T // 2
    NBLK = HALF // 128 + 1
    NBT = 1 + T // 128 + 2
    I = lambda n, s: kb.dram(n, s, F32, "ExternalInput")
    O = lambda n, s: kb.dram(n, s, F32, "ExternalOutput")
    x_d = I("x", [T, 1024])
    xB_d = I("xB", [HALF + 128, 1024])
    na_d = I("na", [128, 8])
    ong_d = I("ong", [128, 1])
    wa_d = I("wa", [1024, 16 * 128 + 8]); cw_d = I("cw", [128, 12, 4]); alog_d = I("alog", [128, 4]); dtb_d = I("dtb", [128, 4])
    hc = host_consts()
    hcb = host_consts_b()
    cst = {k: I("c_" + k, list(v.shape)) for k, v in hc.items()}
    cstb = {k: I("cb_" + k, list(v.shape)) for k, v in hcb.items()}
    woa_d = I("woa", [1024, 1024]); wkv_d = I("wkv", [1024, 512]); kvn_d = I("kvn", [128, 8])
    wb_d = I("wb", [1024, 2048]); nb_d = I("nb", [128, 8]); wob_d = I("wob", [1024, 1024])
    kng_d = I("kng", [128, 64]); qng_d = I("qng", [128, 1]); snk_d = I("snk", [128, 8])
    idx_d = kb.dram("idxtab", [128, (NBLK + 1) * 2], I32, "ExternalInput")
    ab1_d = I("abias1", [128, 16, 128])
    y_d = O("y", [HALF, 1024])
    ssm_d = O("ssm", [4, 128, 128])
    convo_d = O("convo", [128, 12, 3])
    kwin_d = O("kwin", [128, 256]); vwin_d = O("vwin", [128, 256])
    xch = Exchange(kb, T)
    xs32_d = I("xs32", [32, 1024]); sc_d = I("sc", [32, 3, 1536]); ss_d = I("ss", [32, 4, 128, 128])
    xs_d = I("xs", [16, 1024])
    ck_d = I("ck", [16, 128, 256]); cv_d = I("cv", [16, 128, 256])
    qngr_d = I("qngr", [128, 64]); snk64_d = I("snk64", [64, 4])
    convs_d = O("convs", [32, 3, 1536]); ssms_d = O("ssms", [32, 4, 128, 128])
    kws_d = O("kws", [16, 128, 256]); vws_d = O("vws", [16, 128, 256]); ys_d = O("ys", [16, 1024])
    q_scr = kb.dram("q_scr", [16, 1024], F32, "Internal"); o2_scr = kb.dram("o2_scr", [16, 1024], F32, "Internal")
    kn_scr = kb.dram("kn_scr", [16, 256], F32, "Internal"); vn_scr = kb.dram("vn_scr", [16, 256], F32, "Internal")
    psT = [kb.ps([128, 1024], BF16, f"psT{i}") for i in range(2)]
    psF = [kb.ps([128, 512], F32, f"psF{i}") for i in range(6)]
    cst_t = {k: v.t for k, v in cst.items()}
    smps = []
    for grp in range(2):
        r = slice(16 * grp, 16 * grp + 16)
        smps.append(dict(xs=xs32_d.t[r, :], sc=sc_d.t[r, :, :], ss=ss_d.t[r, :, :, :], convs=convs_d.t[r, :, :], ssms=ssms_d.t[r, :, :, :],
                         os_scr=xch, eye=cstb["eye16"].t, grp=grp))
    phase_a(kb, x_d.t, wa_d.t, na_d.t, cw_d.t, alog_d.t, dtb_d.t, ong_d.t, cst_t, xch, ssm_d, convo_d, NSB, psT, psF,
            row0=0, smp=smps, col0=128)
    kb.new_scope()
    xch.gather()
    Wts = alloc_weights_b(kb)
    cb_t = {k: v.t for k, v in cstb.items()}
    cb_t["ident"] = cst["ident"].t
    phase_b(kb, cb_t, xB_d.t, xch, y_d.t, kwin_d.t, vwin_d.t, kng_d.t, qng_d.t, snk_d.t, Wts, NBLK, psT, psF,
            smp=dict(xs=xs_d.t, os_scr=xch, ys=ys_d.t, ck=ck_d.t, cv=cv_d.t, kws=kws_d.t, vws=vws_d.t, q_scr=q_scr, o2_scr=o2_scr,
                     kn_scr=kn_scr, vn_scr=vn_scr, qngr=qngr_d.t, sbias=cstb["sbias"].t, snk64=snk64_d.t, pairM=cstb["pairM"].t),
            idx_d=idx_d.t, ab1_d=ab1_d.t, NBT=NBT, wsrc=(woa_d.t, wkv_d.t, kvn_d.t, wb_d.t, nb_d.t, wob_d.t))
    kb.finish()
    return kb


def make_inputs(inp, T=T_FULL):
    HALF = T // 2
    NBLK = HALF // 128 + 1
    NBT = 1 + T // 128 + 2
    hc = host_consts()
    hcb = host_consts_b()
    shared = {}
    shared["na"] = np.ascontiguousarray(inp["norm_a"][0].reshape(8, 128).T)
    shared["ong"] = np.ascontiguousarray(inp["o_norm_a"][0].reshape(128, 1))
    for k, v in hc.items():
        shared["c_" + k] = v
    for k, v in hcb.items():
        shared["cb_" + k] = v
    shared["woa"] = np.ascontiguousarray(inp["w_out_a"][0])
    shared["wkv"] = np.ascontiguousarray(inp["w_kv"])
    shared["kvn"] = np.ascontiguousarray(inp["kv_norm"].reshape(8, 128).T)
    shared["wb"] = np.ascontiguousarray(inp["w_in_b"][0])
    shared["nb"] = np.ascontiguousarray(inp["norm_b"][0].reshape(8, 128).T)
    shared["wob"] = np.ascontiguousarray(inp["w_out_b"][0])
    shared["kng"] = np.ascontiguousarray(np.tile(inp["k_norm"][None, :], (128, 1)).astype(np.float32))
    shared["qng"] = np.ascontiguousarray(np.tile(inp["q_norm"][0], 2).reshape(128, 1).astype(np.float32))
    sk = inp["sinks"][0]
    snk = np.zeros((128, 8), np.float32)
    for c in range(8):
        snk[:64, c] = sk[2 * c]
        snk[64:, c] = sk[2 * c + 1]
    shared["snk"] = snk
    shared["qngr"] = np.ascontiguousarray(np.tile(inp["q_norm"][0][None, :], (128, 1)).astype(np.float32))
    s64 = np.zeros((64, 4), np.float32)
    for n in range(16):
        for g in range(4):
            s64[n * 4 + g, :] = sk[4 * g:4 * g + 4]
    shared["snk64"] = s64
    chan = []
    for hg in range(2):
        cols = []
        for base in (0, 1024, 2048):
            for i in range(NHG):
                h = hg * NHG + i
                cols.append(np.arange(base + h * 128, base + (h + 1) * 128))
        chan.append(np.concatenate(cols))
    pa = [_prep_a(inp, hg) for hg in range(2)]
    p = np.arange(128)
    maps = []
    for c in range(8):
        b, s = c % 4, c // 4
        m = dict(shared)
        m.update(pa[s])
        m["x"] = np.ascontiguousarray(inp["x_prompt"][b, :T])
        xB = np.zeros((HALF + 128, 1024), np.float32)
        lo = s * HALF - 128
        if lo < 0:
            xB[128:] = inp["x_prompt"][b, 0:HALF]
        else:
            xB[:] = inp["x_prompt"][b, lo:lo + HALF + 128]
        m["xB"] = xB
        idx = np.zeros((128, (NBLK + 1) * 2), np.int32)
        nslot = NBLK + 1
        for kk in range(nslot):
            cch = kk // KCH
            kc = 2 * min(KCH, nslot - cch * KCH)
            l = (kk - cch * KCH) * 2 + s
            for hg in range(2):
                idx[:, kk * 2 + hg] = hg * 128 * kc + p * kc + l
        m["idxtab"] = idx
        m["abias1"] = np.ascontiguousarray(hcb["abias"][:, 0]) if s == 1 else np.full((128, 16, 128), -30000.0, np.float32)
        n0 = 32 * b
        m["xs32"] = np.ascontiguousarray(inp["x_sample"][n0:n0 + 32, 0, :])
        m["sc"] = np.ascontiguousarray(inp["state_conv"][0, n0:n0 + 32][:, :, chan[s]])
        m["ss"] = np.ascontiguousarray(inp["state_ssm"][0, n0:n0 + 32, s * NHG:(s + 1) * NHG])
        n1 = n0 + 16 * s
        m["xs"] = np.ascontiguousarray(inp["x_sample"][n1:n1 + 16, 0, :])
        m["ck"] = np.ascontiguousarray(inp["cache_k_win"][n1:n1 + 16].reshape(16, 128, 256))
        m["cv"] = np.ascontiguousarray(inp["cache_v_win"][n1:n1 + 16].reshape(16, 128, 256))
        maps.append(m)
    return maps


def assemble(R, T=T_FULL):
    B = 4
    HALF = T // 2
    y_p = np.zeros((B, T, 1024), np.float32)
    conv_p = np.zeros((1, B, 3, 3072), np.float32)
    ssm_p = np.zeros((1, B, 8, 128, 128), np.float32)
    kw_p = np.zeros((B, 128, 4, 64), np.float32)
    vw_p = np.zeros((B, 128, 4, 64), np.float32)
    y_s = np.zeros((128, 1, 1024), np.float32)
    conv_s = np.zeros((1, 128, 3, 3072), np.float32)
    ssm_s = np.zeros((1, 128, 8, 128, 128), np.float32)
    kw_s = np.zeros((128, 128, 4, 64), np.float32)
    vw_s = np.zeros((128, 128, 4, 64), np.float32)
    for c in range(8):
        b, s = c % 4, c // 4
        r = R[c]
        y_p[b, s * HALF:(s + 1) * HALF] = r["y"]
        ssm_p[0, b, s * NHG:(s + 1) * NHG] = r["ssm"]
        n0 = 32 * b
        ssm_s[0, n0:n0 + 32, s * NHG:(s + 1) * NHG] = r["ssms"]
        for g, base in enumerate((0, 1024, 2048)):
            for i in range(NHG):
                h = s * NHG + i
                conv_p[0, b, :, base + h * 128: base + (h + 1) * 128] = r["convo"][:, g * NHG + i, :].T
                conv_s[0, n0:n0 + 32, :, base + h * 128: base + (h + 1) * 128] = r["convs"][:, :, (g * NHG + i) * 128:(g * NHG + i + 1) * 128]
        if s == 1:
            kw_p[b] = r["kwin"].reshape(128, 4, 64)
            vw_p[b] = r["vwin"].reshape(128, 4, 64)
        n1 = n0 + 16 * s
        y_s[n1:n1 + 16, 0] = r["ys"]
        kw_s[n1:n1 + 16] = r["kws"].reshape(16, 128, 4, 64)
        vw_s[n1:n1 + 16] = r["vws"].reshape(16, 128, 4, 64)
    return (y_p, y_s, conv_p, ssm_p, kw_p, vw_p, conv_s, ssm_s, kw_s, vw_s)


_CACHE = {}


def kernel(**inp):
    inp = {k: np.asarray(v) for k, v in inp.items()}
    if "kb" not in _CACHE:
        _CACHE["kb"] = build()
    kb = _CACHE["kb"]
    maps = make_inputs(inp)
    res = run_bass_kernel_spmd(kb.nc, maps, core_ids=list(range(8)))
    return assemble(res.results)
```

```python
import contextlib
import numpy as np
import concourse.bass as bass
import concourse.mybir as mybir

F32 = mybir.dt.float32
BF16 = mybir.dt.bfloat16
I32 = mybir.dt.int32
AF = mybir.ActivationFunctionType
ALU = mybir.AluOpType


class Buf:
    def __init__(self, t, name):
        self.t = t
        self.name = name
        self.w = None
        self.r = []
        self.dkey = None

    def __getitem__(self, k):
        return self.t[k]


TABLE_AWARE = False


class _Rec:
    def __init__(self):
        self.call = None

    def __getattr__(self, name):
        def f(*a, **k):
            self.call = (name, a, k)
            return self
        return f


def _fsize(ap):
    try:
        return int(ap.free_size())
    except Exception:
        return 128


def _nbytes(ap):
    try:
        return int(ap.nbytes())
    except Exception:
        return 65536


class KB:
    DEFER = True

    def __init__(self):
        self.nc = bass.Bass("TRN2", target_bir_lowering=False)
        nc = self.nc
        self.es = contextlib.ExitStack()
        self.eng = {"pe": nc.tensor, "act": nc.scalar, "dve": nc.vector,
                    "pool": nc.gpsimd, "sp": nc.sync}
        self.semh = {}
        self.cnt = {}
        self.seen = {e: {} for e in self.eng}
        for e in ("pe", "act", "dve", "pool"):
            self.semh[e] = self.es.enter_context(nc.semaphore("s_" + e))
            self.cnt[e] = 0
        self.nbuf = 0
        self.pend = []
        self.scope = contextlib.ExitStack()
        self.scopes = []
        self.dma_keys = []
        self.nins = 0

    def sb(self, shape, dt, name=None):
        self.nbuf += 1
        name = f"sb{self.nbuf}_" + (name or "b")
        t = self.scope.enter_context(self.nc.sbuf_tensor(name, list(shape), dt))
        return Buf(t, name)

    def push(self):
        self.scopes.append(self.scope)
        self.scope = contextlib.ExitStack()

    def pop(self):
        self.barrier()
        self.scope.close()
        self.scope = self.scopes.pop()

    def new_scope(self):
        self.barrier()
        self.scope.close()
        self.scope = contextlib.ExitStack()

    def ps(self, shape, dt, name=None):
        self.nbuf += 1
        name = f"ps{self.nbuf}_" + (name or "p")
        t = self.es.enter_context(self.nc.psum_tensor(name, list(shape), dt))
        return Buf(t, name)

    def dram(self, name, shape, dt, kind):
        t = self.nc.dram_tensor(name, list(shape), dt, kind=kind)
        return Buf(t.ap(), name)

    def _deps(self, R, W):
        need = {}
        for b in R:
            if b.w is not None:
                k, c = b.w
                need[k] = max(need.get(k, 0), c)
        for b in W:
            if b.w is not None:
                k, c = b.w
                need[k] = max(need.get(k, 0), c)
            for (k, c) in b.r:
                need[k] = max(need.get(k, 0), c)
        return need

    def _wait(self, e, need):
        E = self.eng[e]
        seen = self.seen[e]
        for k, c in need.items():
            if k == e and e == "pe":
                continue
            if seen.get(k, 0) >= c:
                continue
            E.wait_ge(self.semh[k], c)
            seen[k] = c

    def _mark(self, tok, R, W):
        for b in R:
            b.r.append(tok)
            if len(b.r) > 64:
                d = {}
                for k, c in b.r:
                    d[k] = max(d.get(k, 0), c)
                b.r = list(d.items())
        for b in W:
            b.w = tok
            b.r = []

    def op(self, e, fn, R=(), W=()):
        if self.DEFER:
            rec = _Rec()
            fn(rec)
            name, a, k = rec.call
            out = k.get("out", a[0] if a else None)
            n = _fsize(out) if out is not None else 128
            if e == "pe":
                if name == "transpose":
                    c = 0.07
                else:
                    c = 0.03 + max(n, 64) / 2400.0
                    l = k.get("lhsT")
                    if l is not None and l.dtype == F32:
                        c *= 4
            elif e == "dve":
                c = 0.06 + n / 960.0
            elif e == "act":
                c = 0.2 + n / 1200.0
            else:
                c = 0.1 + n / 480.0
            tb = 0
            if e == "act":
                fnc = k.get("func")
                if fnc == AF.Ln:
                    tb = 1
                elif fnc == AF.Tanh:
                    tb = 2
            self.pend.append(("op", e, rec.call, tuple(R), tuple(W), c, c, tb))
            return None
        return self._op_now(e, fn, R, W)

    def _op_now(self, e, fn, R=(), W=()):
        self._wait(e, self._deps(R, W))
        ins = fn(self.eng[e])
        self.cnt[e] += 1
        ins.then_inc(self.semh[e], 1)
        self._mark((e, self.cnt[e]), R, W)
        self.nins += 1
        return ins

    def dma(self, q, out, in_, R=(), W=(), key=None, **kw):
        if self.DEFER:
            lat = 1.5 + _nbytes(out) / 150000.0
            W = tuple(W) if any(b is key for b in W) else tuple(W) + (key,)
            self.pend.append(("dma", q, (out, in_, key, kw), tuple(R), W, 0.06, lat))
            return None
        return self._dma_now(q, out, in_, R, W, key, **kw)

    def _dma_now(self, q, out, in_, R=(), W=(), key=None, **kw):
        if key.dkey is None:
            key.dkey = "d_" + key.name
            self.semh[key.dkey] = self.es.enter_context(self.nc.semaphore(key.dkey))
            self.cnt[key.dkey] = 0
            self.dma_keys.append(key.dkey)
        self._wait(q, self._deps(R, W))
        ins = self.eng[q].dma_start(out=out, in_=in_, **kw)
        self.cnt[key.dkey] += 16
        ins.then_inc(self.semh[key.dkey], 16)
        self._mark((key.dkey, self.cnt[key.dkey]), R, W)
        self.nins += 1
        return ins

    def dma_multi(self, q, pairs, R=(), W=(), key=None):
        if self.DEFER:
            nb = sum(_nbytes(o) for o, _ in pairs)
            W = tuple(W) if any(b is key for b in W) else tuple(W) + (key,)
            self.pend.append(("dmam", q, (list(pairs), key), tuple(R), W, 0.06 * len(pairs), 1.5 + nb / 150000.0))
            return None
        return self._dma_multi_now(q, pairs, R, W, key)

    def _dma_multi_now(self, q, pairs, R=(), W=(), key=None):
        if key.dkey is None:
            key.dkey = "d_" + key.name
            self.semh[key.dkey] = self.es.enter_context(self.nc.semaphore(key.dkey))
            self.cnt[key.dkey] = 0
            self.dma_keys.append(key.dkey)
        self._wait(q, self._deps(R, W))
        for (out, in_) in pairs:
            ins = self.eng[q].dma_start(out=out, in_=in_)
            self.cnt[key.dkey] += 16
            ins.then_inc(self.semh[key.dkey], 16)
            self.nins += 1
        self._mark((key.dkey, self.cnt[key.dkey]), R, W)

    def ind_dma(self, out, in_, idx_ap, nrows, R=(), W=(), key=None):
        q = "pool"
        if key.dkey is None:
            key.dkey = "d_" + key.name
            self.semh[key.dkey] = self.es.enter_context(self.nc.semaphore(key.dkey))
            self.cnt[key.dkey] = 0
            self.dma_keys.append(key.dkey)
        self._wait(q, self._deps(R, W))
        ins = self.eng[q].indirect_dma_start(out=out, out_offset=None, in_=in_,
                                             in_offset=bass.IndirectOffsetOnAxis(ap=idx_ap, axis=0),
                                             bounds_check=nrows - 1, oob_is_err=False)
        self.cnt[key.dkey] += 16
        ins.then_inc(self.semh[key.dkey], 16)
        self._mark((key.dkey, self.cnt[key.dkey]), R, W)
        self.nins += 1

    def ind_dma_multi(self, items, in_, nrows, R=(), W=(), key=None):
        if self.DEFER:
            nb = sum(_nbytes(o) for o, _, _ in items)
            W = tuple(W) if any(b is key for b in W) else tuple(W) + (key,)
            self.pend.append(("indm", "pool", (list(items), nrows, key), tuple(R), W, 1.0 * len(items), 3.0 + nb / 150000.0))
            return None
        return self._ind_dma_multi_now(items, in_, nrows, R, W, key)

    def _ind_dma_multi_now(self, items, in_, nrows, R=(), W=(), key=None):
        q = "pool"
        if key.dkey is None:
            key.dkey = "d_" + key.name
            self.semh[key.dkey] = self.es.enter_context(self.nc.semaphore(key.dkey))
            self.cnt[key.dkey] = 0
            self.dma_keys.append(key.dkey)
        self._wait(q, self._deps(R, W))
        for (out, idx_ap, src) in items:
            ins = self.eng[q].indirect_dma_start(out=out, out_offset=None, in_=src,
                                                 in_offset=bass.IndirectOffsetOnAxis(ap=idx_ap, axis=0),
                                                 bounds_check=nrows - 1, oob_is_err=False)
            self.cnt[key.dkey] += 16
            ins.then_inc(self.semh[key.dkey], 16)
            self.nins += 1
        self._mark((key.dkey, self.cnt[key.dkey]), R, W)

    def all_gather(self, in_buf, out_buf, groups):
        key = "cc_" + out_buf.name
        self.semh[key] = self.es.enter_context(self.nc.semaphore(key))
        self.cnt[key] = 0
        self.dma_keys.append(key)
        self._wait("pool", self._deps([in_buf], [out_buf]))
        ins = self.eng["pool"].collective_compute("AllGather", ALU.bypass, replica_groups=groups,
                                                  ins=[in_buf.t], outs=[out_buf.t])
        self.cnt[key] += 1
        ins.then_inc(self.semh[key], 1)
        self._mark((key, 1), [in_buf], [out_buf])
        self.nins += 1

    def flush(self):
        P = self.pend
        self.pend = []
        M = len(P)
        if M == 0:
            return
        lastw = {}
        readers = {}
        deps = [None] * M
        succ = [[] for _ in range(M)]
        for j, rec_ in enumerate(P):
            e, R, W = rec_[1], rec_[3], rec_[4]
            d = set()
            for b in R:
                i = lastw.get(id(b))
                if i is not None:
                    d.add(i)
            for b in W:
                i = lastw.get(id(b))
                if i is not None:
                    d.add(i)
                for i in readers.get(id(b), ()):
                    d.add(i)
            d.discard(j)
            deps[j] = d
            for i in d:
                succ[i].append(j)
            for b in R:
                readers.setdefault(id(b), []).append(j)
            for b in W:
                lastw[id(b)] = j
                readers[id(b)] = []
        tail = [0.0] * M
        for j in range(M - 1, -1, -1):
            t = 0.0
            for k in succ[j]:
                if tail[k] > t:
                    t = tail[k]
            tail[j] = t + P[j][6]
        ndep = [len(d) for d in deps]
        dr = [0.0] * M
        fin = [0.0] * M
        free = {}
        cand = [j for j in range(M) if ndep[j] == 0]
        order = []
        LOOK = 96
        acttab = 0
        import heapq
        heapq.heapify(cand)
        pool = []
        while cand or pool:
            while cand and len(pool) < LOOK:
                pool.append(heapq.heappop(cand))
            best = None
            bk = None
            for j in pool:
                e = P[j][1]
                st = dr[j]
                f = free.get(e, 0.0)
                if f > st:
                    st = f
                if TABLE_AWARE and e == "act" and len(P[j]) > 7 and P[j][7] and P[j][7] != acttab:
                    st += 1.3
                key = (round(st, 2), -tail[j], j)
                if bk is None or key < bk:
                    bk = key
                    best = j
            pool.remove(best)
            j = best
            e = P[j][1]
            st = max(dr[j], free.get(e, 0.0))
            if TABLE_AWARE and e == "act" and len(P[j]) > 7 and P[j][7]:
                if P[j][7] != acttab:
                    st += 1.3
                acttab = P[j][7]
            free[e] = st + P[j][5]
            fin[j] = st + P[j][6]
            order.append(j)
            for k in succ[j]:
                t = fin[j] + (0.0 if (P[k][1] == e and P[j][0] == "op") else 0.05)
                if t > dr[k]:
                    dr[k] = t
                ndep[k] -= 1
                if ndep[k] == 0:
                    heapq.heappush(cand, k)
        self.est = getattr(self, "est", 0.0) + max(fin) if fin else 0.0
        sv = self.DEFER
        self.DEFER = False
        try:
            for j in order:
                kind, e, pay, R, W = P[j][:5]
                if kind == "op":
                    name, a, k = pay
                    self._op_now(e, lambda eng: getattr(eng, name)(*a, **k), R, W)
                elif kind == "dma":
                    out, in_, key, kw = pay
                    self._dma_now(e, out, in_, R, W, key, **kw)
                elif kind == "dmam":
                    pairs, key = pay
                    self._dma_multi_now(e, pairs, R, W, key)
                else:
                    items, nrows, key = pay
                    self._ind_dma_multi_now(items, None, nrows, R, W, key)
        finally:
            self.DEFER = sv

    def barrier(self):
        self.flush()
        for e in self.eng:
            need = {k: c for k, c in self.cnt.items() if c > 0 and k != e}
            self._wait(e, need)

    def finish(self):
        self.flush()
        need = {k: self.cnt[k] for k in self.dma_keys if self.cnt[k] > 0}
        self._wait("sp", need)


import numpy as np

NH = 4
DK = 128
EPS = 1e-6
NLEV = 7


def host_consts():
    c = {}
    c["ident"] = np.eye(128, dtype=np.float32)
    i = np.arange(128)
    c["U"] = (i[:, None] <= i[None, :]).astype(np.float32)
    mi = (i[None, :] >= i[:, None]).astype(np.float32)
    ms = np.where(i[None, :] > i[:, None], 0.0, -30000.0).astype(np.float32)
    c["maskUi"] = np.tile(mi[:, None, :], (1, NH, 1)).copy()
    c["maskUs"] = np.tile(ms[:, None, :], (1, NH, 1)).copy()
    lm = np.zeros((128, NLEV, NH, 128), np.float32)
    for l in range(NLEV):
        s = 1 << l
        bi = i // s
        m = ((bi[:, None] % 2 == 1) & (bi[None, :] == bi[:, None] - 1)).astype(np.float32)
        lm[:, l, :, :] = m[:, None, :]
    c["lmask"] = lm
    c["negm"] = np.where(i[None, :] >= i[:, None], 0.0, -30000.0).astype(np.float32)
    return c


def phase_a(kb, x_d, wa_d, na_d, cw_d, alog_d, dtb_d, ong_d, cst, o_scr, ssm_d, convo_d, NSB, psT, psF, row0=0, smp=None, col0=0):
    nc = kb.nc
    NF = 4 * NH
    NCOL = NF * 128 + 2 * NH
    SBT = 512

    ident_f = kb.sb([128, 128], F32, "ident_f")
    ident_b = kb.sb([128, 128], BF16, "ident_b")
    ones_b = kb.sb([128, 128], BF16, "ones_b")
    ones_f = kb.sb([128, 128], F32, "ones_f")
    U_f = kb.sb([128, 128], F32, "U_f")
    mUs = kb.sb([128, NH, 128], F32, "mUs")
    lmask = kb.sb([128, NLEV, NH, 128], BF16, "lmask")
    cbias = kb.sb([128, 8], F32, "cbias")
    na = kb.sb([128, 8], F32, "na")
    cw = kb.sb([128, 3 * NH, 4], F32, "cw")
    negA = kb.sb([128, NH], F32, "negA")
    dtb = kb.sb([128, NH], F32, "dtb")
    ong = kb.sb([128, 1], F32, "ong")
    cload = kb.sb([128, 1], F32, "cload")
    negm_f = kb.sb([128, 128], F32, "negm_f")
    negm = kb.sb([128, 128], BF16, "negm")
    negms = kb.sb([128, 128], BF16, "negms")
    identb4 = kb.sb([128, NH, 128], BF16, "identb4")

    lds = []

    def ld(dst, src):
        lds.append((dst, src))
    ld(ident_f, cst["ident"][:, :]); ld(U_f, cst["U"][:, :])
    ld(mUs, cst["maskUs"][:, :, :])
    ld(na, na_d[:, :]); ld(cw, cw_d[:, :, :]); ld(negA, alog_d[:, :]); ld(dtb, dtb_d[:, :])
    ld(ong, ong_d[:, :]); ld(negm_f, cst["negm"][:, :])
    kb.dma_multi("sp", [(d_[:], s_) for d_, s_ in lds], W=[d_ for d_, _ in lds], key=cload)
    kb.op("dve", lambda e: e.tensor_copy(ident_b[:], ident_f[:]), R=[ident_f], W=[ident_b])
    kb.op("pool", lambda e: e.memset(ones_b[:], 1.0), W=[ones_b])
    kb.op("dve", lambda e: e.tensor_copy(negm[:], negm_f[:]), R=[negm_f], W=[negm])
    U_b = kb.sb([128, 128], BF16, "U_b")
    kb.op("dve", lambda e: e.tensor_copy(U_b[:], U_f[:]), R=[U_f], W=[U_b])
    kb.op("dve", lambda e: e.tensor_copy(negms[:], mUs[:, 0, :]), R=[mUs], W=[negms])
    for h in range(NH):
        kb.op("dve", lambda e, h=h: e.tensor_copy(identb4[:, h, :], ident_f[:]), R=[ident_f], W=[identb4])
    kb.op("pool", lambda e: e.memset(ones_f[:], 1.0), W=[ones_f])
    kb.push()
    lmask_f = kb.sb([128, NLEV, NH, 128], F32, "lmask_f")
    kb.dma("sp", lmask_f[:], cst["lmask"][:, :, :, :], W=[lmask_f], key=lmask_f)
    kb.op("dve", lambda e: e.tensor_copy(lmask[:], lmask_f[:]), R=[lmask_f], W=[lmask])
    kb.pop()
    for j, v in enumerate([4 * EPS, 4 * EPS * 128, EPS, 1.0, 0.0]):
        kb.op("pool", lambda e, j=j, v=v: e.memset(cbias[:, j:j + 1], v), W=[cbias])
    kb.op("act", lambda e: e.activation(out=negA[:], in_=negA[:], func=AF.Exp), R=[negA], W=[negA])
    kb.op("dve", lambda e: e.tensor_scalar(negA[:], negA[:], -1.0, None, op0=ALU.mult), R=[negA], W=[negA])
    kb.op("dve", lambda e: e.tensor_scalar(ong[:], ong[:], 0.5, None, op0=ALU.mult), R=[ong], W=[ong])

    W = kb.sb([128, 8, NCOL], BF16, "Wa")
    kb.push()
    stg = [kb.sb([128, NCOL], F32, f"stg{i}") for i in range(2)]
    wa_v = wa_d.rearrange("(kc p) n -> p kc n", p=128)
    for kc in range(8):
        s = stg[kc % 2]
        kb.dma(("sp", "act")[kc % 2], s[:], wa_v[:, kc, :], W=[s], key=s)
        eng = "act" if kc % 2 == 0 else "dve"
        if eng == "act":
            kb.op("act", lambda e, kc=kc, s=s: e.activation(out=W[:, kc, :], in_=s[:], func=AF.Copy,
                                                         scale=na[:, kc:kc + 1]), R=[s, na], W=[W])
        else:
            kb.op("dve", lambda e, kc=kc, s=s: e.tensor_scalar(W[:, kc, :], s[:], na[:, kc:kc + 1], None,
                                                            op0=ALU.mult), R=[s, na], W=[W])

    pfi = [0]

    def PS(ring=0):
        p = psF[pfi[0] % len(psF)]
        pfi[0] += 1
        return p

    if smp is not None:
        zt = kb.sb([128, NH, 128], BF16, "zpad")
        kb.op("pool", lambda e: e.memset(zt[:], 0.0), W=[zt])
        kb.dma_multi("sp", [(dst, zt[:]) for dst in o_scr.loc(0)] + [(o_scr.sloc(g_), zt[:]) for g_ in range(2)],
                     R=[zt], W=[o_scr], key=zt)
        sample_a(kb, smp, locals(), PS, psT, row0)
    kb.pop()

    xt = [kb.sb([128, 1024], F32, f"xt{i}") for i in range(3)]
    xs = [kb.sb([128, 1024], BF16, f"xs{i}") for i in range(2)]
    rr = kb.sb([128, 4], F32, "rr")
    junk = kb.sb([128, 1024], BF16, "junk")
    xsT_1 = kb.sb([128, 8, SBT], BF16, "xsT")
    xsT_2 = [xsT_1, xsT_1]
    pre = kb.sb([128, 3 * NH, SBT + 3], F32, "pre")
    preb = [Buf(pre.t, f"pre{i}") for i in range(3 * NH)]
    acc = [kb.sb([128, SBT], F32, f"acc{i}") for i in range(2)]
    tnh = [kb.sb([128, SBT], F32, f"tnh{i}") for i in range(2)]
    qkv_2 = [kb.sb([128, 3 * NH, SBT], BF16, f"qkv{i_}") for i_ in range(2)]
    qb_2 = [[Buf(qkv_2[i_].t, f"qkv{i_}_{i}") for i in range(3 * NH)] for i_ in range(2)]
    gs_2 = [kb.sb([128, NH, SBT], BF16, f"gs{i_}") for i_ in range(2)]
    sqb = [kb.sb([128, SBT], BF16, f"sqb{i}") for i in range(2)]
    rb = [kb.sb([128, SBT], F32, f"rb{i}") for i in range(2)]
    ab_2 = [kb.sb([128, 4, 2 * NH], F32, f"ab{i_}") for i_ in range(2)]
    t1_2 = [kb.sb([128, 4, NH], F32, f"t1{i_}") for i_ in range(2)]
    gg_2 = [kb.sb([128, 4, NH], F32, f"gg{i_}") for i_ in range(2)]
    gh16_2 = [kb.sb([128, 4, NH], BF16, f"gh16{i_}") for i_ in range(2)]
    gh32_2 = [kb.sb([128, 4, NH], F32, f"gh32{i_}") for i_ in range(2)]
    gl32_2 = [kb.sb([128, 4, NH], F32, f"gl32{i_}") for i_ in range(2)]
    lnb_2 = [kb.sb([128, 4, NH], F32, f"lnb{i_}") for i_ in range(2)]
    beta_2 = [kb.sb([128, 4, NH], F32, f"beta{i_}") for i_ in range(2)]
    Gs_2 = [kb.sb([128, 2 * NH], F32, f"Gs{i_}") for i_ in range(2)]
    negG_2 = [kb.sb([128, NH], F32, f"negG{i_}") for i_ in range(2)]
    nGb_2 = [kb.sb([128, NH], F32, f"nGb{i_}") for i_ in range(2)]
    negeG_2 = [kb.sb([128, NH], F32, f"negeG{i_}") for i_ in range(2)]
    kdsc_2 = [kb.sb([128, NH], F32, f"kdsc{i_}") for i_ in range(2)]
    gtot_2 = [kb.sb([128, NH], F32, f"gtot{i_}") for i_ in range(2)]
    gB_2 = [kb.sb([128, 2, NH, 128], BF16, f"gB{i_}") for i_ in range(2)]
    E_2 = [kb.sb([128, NH, 128], F32, f"E{i_}") for i_ in range(2)]
    Eb_2 = [kb.sb([128, NH, 128], F32, f"Eb{i_}") for i_ in range(2)]
    eGb_2 = [kb.sb([128, NH, 128], F32, f"eGb{i_}") for i_ in range(2)]
    MT_2 = [kb.sb([128, NH, 128], BF16, f"MT{i_}") for i_ in range(2)]
    qkT_2 = [kb.sb([128, NH, 128], BF16, f"qkT{i_}") for i_ in range(2)]
    qdT_2 = [kb.sb([128, NH, 128], BF16, f"qdT{i_}") for i_ in range(2)]
    T_2 = [kb.sb([128, NH, 128], BF16, f"T{i_}") for i_ in range(2)]
    TT_2 = [kb.sb([128, NH, 128], BF16, f"TT{i_}") for i_ in range(2)]
    Pm_2 = [kb.sb([128, NH, 128], BF16, f"Pm{i_}") for i_ in range(2)]
    kd_2 = [kb.sb([128, NH, 128], BF16, f"kd{i_}") for i_ in range(2)]
    vtok_2 = [kb.sb([128, NH, 128], F32, f"vtok{i_}") for i_ in range(2)]
    Rb_2 = [kb.sb([128, NH, 128], BF16, f"Rb{i_}") for i_ in range(2)]
    vnew_2 = [kb.sb([128, NH, 128], BF16, f"vnew{i_}") for i_ in range(2)]
    S32 = kb.sb([128, NH, 128], F32, "S32")
    Sbf = kb.sb([128, NH, 128], BF16, "Sbf")
    osq_2 = [kb.sb([128, NH, 128], BF16, f"osq{i_}") for i_ in range(2)]
    rinv_2 = [kb.sb([128, NH, 128], F32, f"rinv{i_}") for i_ in range(2)]
    otmp_2 = [kb.sb([128, NH, 128], F32, f"otmp{i_}") for i_ in range(2)]
    oTf = [kb.sb([128, NH, SBT], BF16, f"oTf{i}") for i in range(2)]

    def v3(p):
        return p.t[:, :].rearrange("p (a b) -> p a b", a=NH)

    kb.op("pool", lambda e: e.memset(pre[:], 0.0), W=preb)
    kb.op("pool", lambda e: e.memset(S32[:], 0.0), W=[S32])
    kb.op("pool", lambda e: e.memset(Sbf[:], 0.0), W=[Sbf])

    def bc(buf, blk=None):
        a = buf[:, :] if blk is None else buf[:, blk, :]
        return a.unsqueeze(2).to_broadcast([128, NH, 128])

    for sbi in range(NSB):
        tok0 = sbi * SBT
        sp_ = sbi % 2
        xsT, ab, t1, gg, lnb, beta, gs = (xsT_2[sp_], ab_2[sp_], t1_2[sp_], gg_2[sp_], lnb_2[sp_], beta_2[sp_], gs_2[sp_])
        qkv, qb = qkv_2[sp_], qb_2[sp_]
        gh16, gh32, gl32 = gh16_2[sp_], gh32_2[sp_], gl32_2[sp_]
        for b4 in range(4):
            xb = xt[(sbi * 4 + b4) % 3]
            xsb = xs[b4 % 2]
            kb.dma("sp", xb[:], x_d[tok0 + b4 * 128: tok0 + (b4 + 1) * 128, :], W=[xb], key=xb)
            kb.op("act", lambda e: e.activation(out=junk[:], in_=xb[:], func=AF.Square,
                                                accum_out=rr[:, 0:1]), R=[xb], W=[junk, rr])
            kb.op("act", lambda e: e.activation(out=rr[:, 1:2], in_=rr[:, 0:1], func=AF.Ln,
                                                scale=1.0 / 1024, bias=cbias[:, 2:3]), R=[rr, cbias], W=[rr])
            kb.op("act", lambda e: e.activation(out=rr[:, 2:3], in_=rr[:, 1:2], func=AF.Exp, scale=-0.5), R=[rr], W=[rr])
            kb.op("act", lambda e: e.activation(out=xsb[:], in_=xb[:], func=AF.Copy, scale=rr[:, 2:3]),
                  R=[xb, rr], W=[xsb])
            pt = psT[b4 % 2]
            ptB = pt.t[:, :]
            for kc in range(8):
                kb.op("pe", lambda e, kc=kc: e.transpose(out=ptB[:, kc * 128:(kc + 1) * 128],
                                                         in_=xsb[:, kc * 128:(kc + 1) * 128],
                                                         identity=ident_b[:]), R=[xsb, ident_b], W=[pt])
            ptv = ptB.rearrange("p (k t) -> p k t", k=8)
            kb.op("act", lambda e: e.activation(out=xsT[:, :, b4 * 128:(b4 + 1) * 128], in_=ptv, func=AF.Copy),
                  R=[pt], W=[xsT])
        pab = PS()
        for b4 in range(4):
            for kc in range(8):
                kb.op("pe", lambda e, kc=kc: e.matmul(pab[:, b4 * 2 * NH:(b4 + 1) * 2 * NH],
                                                     lhsT=xsT[:, kc, b4 * 128:(b4 + 1) * 128],
                                                     rhs=W[:, kc, NF * 128:NF * 128 + 2 * NH],
                                                     start=(kc == 0), stop=(kc == 7)), R=[xsT, W], W=[pab])
        kb.op("dve", lambda e: e.tensor_copy(ab[:], pab[:, 0:8 * NH].rearrange("p (a b) -> p a b", a=4)),
              R=[pab], W=[ab])
        kb.op("dve", lambda e: e.tensor_tensor(t1[:], ab[:, :, 0:NH],
                                               dtb[:, :].unsqueeze(1).to_broadcast([128, 4, NH]), op=ALU.add),
              R=[ab, dtb], W=[t1])
        kb.op("act", lambda e: e.activation(out=t1[:], in_=t1[:], func=AF.Exp), R=[t1], W=[t1])
        kb.op("act", lambda e: e.activation(out=lnb[:], in_=ab[:, :, NH:2 * NH], func=AF.Exp, scale=-1.0),
              R=[ab], W=[lnb])
        kb.op("act", lambda e: e.activation(out=t1[:], in_=t1[:], func=AF.Ln, bias=cbias[:, 3:4]),
              R=[t1, cbias], W=[t1])
        kb.op("act", lambda e: e.activation(out=lnb[:], in_=lnb[:], func=AF.Ln, bias=cbias[:, 3:4]),
              R=[lnb, cbias], W=[lnb])
        kb.op("dve", lambda e: e.tensor_tensor(gg[:], t1[:], negA[:, :].unsqueeze(1).to_broadcast([128, 4, NH]),
                                               op=ALU.mult), R=[t1, negA], W=[gg])
        kb.op("dve", lambda e: e.tensor_scalar(lnb[:], lnb[:], -1.0, None, op0=ALU.mult), R=[lnb], W=[lnb])
        kb.op("dve", lambda e: e.tensor_copy(gh16[:], gg[:]), R=[gg], W=[gh16])
        kb.op("dve", lambda e: e.tensor_copy(gh32[:], gh16[:]), R=[gh16], W=[gh32])
        kb.op("dve", lambda e: e.tensor_tensor(gl32[:], gg[:], gh32[:], op=ALU.subtract), R=[gg, gh32], W=[gl32])
        kb.op("act", lambda e: e.activation(out=beta[:], in_=lnb[:], func=AF.Exp), R=[lnb], W=[beta])

        for ft in range(NF):
            pp = PS()
            for kc in range(8):
                kb.op("pe", lambda e, kc=kc: e.matmul(pp[:, :], lhsT=W[:, kc, ft * 128:(ft + 1) * 128],
                                                     rhs=xsT[:, kc, :], start=(kc == 0), stop=(kc == 7)),
                      R=[W, xsT], W=[pp])
            a_ = acc[ft % 2]
            t_ = tnh[ft % 2]
            if ft < 3 * NH:
                kb.op("act", lambda e: e.activation(out=pre[:, ft, 3:SBT + 3], in_=pp[:, :], func=AF.Copy),
                      R=[pp], W=[preb[ft]])
                kb.op("act", lambda e: e.activation(out=a_[:], in_=pp[:, :], func=AF.Copy,
                                                    scale=cw[:, ft, 3:4]), R=[pp, cw], W=[a_])
                for tap in range(3):
                    eng = "dve"
                    kb.op(eng, lambda e, tap=tap: e.scalar_tensor_tensor(
                        out=a_[:], in0=pre[:, ft, tap:tap + SBT], scalar=cw[:, ft, tap:tap + 1], in1=a_[:],
                        op0=ALU.mult, op1=ALU.add), R=[preb[ft], cw, a_], W=[a_])
                kb.op("act", lambda e: e.activation(out=pre[:, ft, 0:3], in_=pre[:, ft, SBT:SBT + 3], func=AF.Copy),
                      R=[preb[ft]], W=[preb[ft]])
                kb.op("act", lambda e: e.activation(out=t_[:], in_=a_[:], func=AF.Tanh, scale=0.5),
                      R=[a_], W=[t_])
                kb.op("dve", lambda e: e.scalar_tensor_tensor(out=qkv[:, ft, :], in0=t_[:], scalar=1.0,
                                                              in1=a_[:], op0=ALU.add, op1=ALU.mult),
                      R=[t_, a_], W=[qb[ft]])
            else:
                h = ft - 3 * NH
                kb.op("act", lambda e: e.activation(out=t_[:], in_=pp[:, :], func=AF.Tanh, scale=0.5),
                      R=[pp], W=[t_])
                kb.op("dve", lambda e: e.scalar_tensor_tensor(out=gs[:, h, :], in0=t_[:], scalar=1.0,
                                                              in1=pp[:, :], op0=ALU.add, op1=ALU.mult),
                      R=[t_, pp], W=[gs])
        for ft in range(2 * NH):
            s_ = sqb[ft % 2]
            r_ = rb[ft % 2]
            pn = PS()
            kb.op("act", lambda e: e.activation(out=s_[:], in_=qkv[:, ft, :], func=AF.Square), R=[qb[ft]], W=[s_])
            kb.op("pe", lambda e: e.matmul(pn[:, :], lhsT=ones_b[:], rhs=s_[:], start=True, stop=True),
                  R=[ones_b, s_], W=[pn])
            isq = ft < NH
            kb.op("act", lambda e: e.activation(out=r_[:], in_=pn[:, :], func=AF.Ln,
                                                scale=(128.0 if isq else 1.0),
                                                bias=cbias[:, 1:2] if isq else cbias[:, 0:1]),
                  R=[pn, cbias], W=[r_])
            kb.op("act", lambda e: e.activation(out=r_[:], in_=r_[:], func=AF.Exp, scale=-0.5), R=[r_], W=[r_])
            kb.op("dve", lambda e: e.tensor_tensor(qkv[:, ft, :], qkv[:, ft, :], r_[:], op=ALU.mult),
                  R=[qb[ft], r_], W=[qb[ft]])

        for b4 in range(4):
            c0 = b4 * 128
            cs = slice(c0, c0 + 128)
            cp_ = (sbi * 4 + b4) % 2
            (Gs, negG, nGb, negeG, kdsc, gtot, gB, E, Eb, eGb, MT, qkT, qdT, T, TT, Pm, kd, vtok, Rb, vnew, osq, rinv, otmp) = (
                Gs_2[cp_], negG_2[cp_], nGb_2[cp_], negeG_2[cp_], kdsc_2[cp_], gtot_2[cp_], gB_2[cp_], E_2[cp_], Eb_2[cp_], eGb_2[cp_],
                MT_2[cp_], qkT_2[cp_], qdT_2[cp_], T_2[cp_], TT_2[cp_], Pm_2[cp_], kd_2[cp_], vtok_2[cp_], Rb_2[cp_], vnew_2[cp_],
                osq_2[cp_], rinv_2[cp_], otmp_2[cp_])
            ptk = psT[(sbi * 4 + b4) % 2]
            ptv_ = ptk
            ptkB = ptk.t[:, :]
            for h in range(NH):
                kb.op("pe", lambda e, h=h: e.transpose(out=ptkB[:, h * 128:(h + 1) * 128],
                                                       in_=qkv[:, NH + h, cs], identity=ident_b[:]),
                      R=[qb[NH + h], ident_b], W=[ptk])
            for h in range(NH):
                kb.op("pe", lambda e, h=h: e.transpose(out=ptkB[:, (NH + h) * 128:(NH + h + 1) * 128],
                                                       in_=qkv[:, 2 * NH + h, cs], identity=ident_b[:]),
                      R=[qb[2 * NH + h], ident_b], W=[ptv_])
            pg = PS()
            kb.op("pe", lambda e: e.matmul(pg[:, 0:NH], lhsT=U_f[:], rhs=gg[:, b4, :], start=True, stop=True),
                  R=[U_f, gg], W=[pg])
            kb.op("pe", lambda e: e.matmul(pg[:, NH:2 * NH], lhsT=ones_f[:], rhs=gg[:, b4, :], start=True, stop=True),
                  R=[ones_f, gg], W=[pg])
            kb.op("dve", lambda e: e.tensor_copy(Gs[:], pg[:, 0:2 * NH]), R=[pg], W=[Gs])
            kb.op("dve", lambda e: e.tensor_scalar(negG[:], Gs[:, 0:NH], -1.0, None, op0=ALU.mult), R=[Gs], W=[negG])
            kb.op("dve", lambda e: e.tensor_tensor(nGb[:], lnb[:, b4, :], Gs[:, 0:NH], op=ALU.subtract),
                  R=[lnb, Gs], W=[nGb])
            kb.op("act", lambda e: e.activation(out=negeG[:], in_=Gs[:, 0:NH], func=AF.Exp), R=[Gs], W=[negeG])
            kb.op("dve", lambda e: e.tensor_scalar(negeG[:], negeG[:], -1.0, None, op0=ALU.mult), R=[negeG], W=[negeG])
            kb.op("dve", lambda e: e.tensor_tensor(kdsc[:], Gs[:, NH:2 * NH], Gs[:, 0:NH], op=ALU.subtract),
                  R=[Gs], W=[kdsc])
            kb.op("act", lambda e: e.activation(out=kdsc[:], in_=kdsc[:], func=AF.Exp), R=[kdsc], W=[kdsc])
            kb.op("act", lambda e: e.activation(out=gtot[:], in_=Gs[:, NH:2 * NH], func=AF.Exp), R=[Gs], W=[gtot])
            for h in range(NH):
                kb.op("act", lambda e, h=h: e.activation(out=kd[:, h, :], in_=ptkB[:, h * 128:(h + 1) * 128], func=AF.Copy,
                                                        scale=kdsc[:, h:h + 1]), R=[ptk, kdsc], W=[kd])
            kb.op("act", lambda e: e.activation(out=vtok[:], in_=ptkB[:, NH * 128:2 * NH * 128].rearrange("p (a b) -> p a b", a=NH),
                                                func=AF.Copy, scale=0.5), R=[ptv_], W=[vtok])
            for h in range(NH):
                kb.op("act", lambda e, h=h: e.activation(out=gB[:, 0, h, :], in_=ones_f[:], func=AF.Copy, scale=gh32[:, b4, h:h + 1]),
                      R=[ones_f, gh32], W=[gB])
                kb.op("act", lambda e, h=h: e.activation(out=gB[:, 1, h, :], in_=ones_f[:], func=AF.Copy, scale=gl32[:, b4, h:h + 1]),
                      R=[ones_f, gl32], W=[gB])
            pgb = PS()
            pgm = PS()
            for h in range(NH):
                kb.op("pe", lambda e, h=h: e.matmul(pgb[:, h * 128:(h + 1) * 128], lhsT=gB[:, 0, h, :], rhs=U_b[:],
                                                   start=True, stop=False), R=[gB, U_b], W=[pgb])
                kb.op("pe", lambda e, h=h: e.matmul(pgb[:, h * 128:(h + 1) * 128], lhsT=gB[:, 1, h, :], rhs=U_b[:],
                                                   start=False, stop=True), R=[gB, U_b], W=[pgb])
            for h in range(NH):
                kb.op("pe", lambda e, h=h: e.matmul(pgm[:, h * 128:(h + 1) * 128], lhsT=gB[:, 0, h, :], rhs=U_b[:],
                                                   start=True, stop=False), R=[gB, U_b], W=[pgm])
                kb.op("pe", lambda e, h=h: e.matmul(pgm[:, h * 128:(h + 1) * 128], lhsT=gB[:, 1, h, :], rhs=U_b[:],
                                                   start=False, stop=False), R=[gB, U_b], W=[pgm])
                kb.op("pe", lambda e, h=h: e.matmul(pgm[:, h * 128:(h + 1) * 128], lhsT=ident_b[:], rhs=negm[:],
                                                   start=False, stop=True), R=[ident_b, negm], W=[pgm])
            pgs = PS()
            for h in range(NH):
                kb.op("pe", lambda e, h=h: e.matmul(pgs[:, h * 128:(h + 1) * 128], lhsT=gB[:, 0, h, :], rhs=U_b[:],
                                                   start=True, stop=False), R=[gB, U_b], W=[pgs])
                kb.op("pe", lambda e, h=h: e.matmul(pgs[:, h * 128:(h + 1) * 128], lhsT=gB[:, 1, h, :], rhs=U_b[:],
                                                   start=False, stop=False), R=[gB, U_b], W=[pgs])
                kb.op("pe", lambda e, h=h: e.matmul(pgs[:, h * 128:(h + 1) * 128], lhsT=ident_b[:], rhs=negms[:],
                                                   start=False, stop=True), R=[ident_b, negms], W=[pgs])
            for h in range(NH):
                kb.op("act", lambda e, h=h: e.activation(out=E[:, h, :], in_=pgm[:, h * 128:(h + 1) * 128], func=AF.Exp,
                                                        bias=negG[:, h:h + 1]), R=[pgm, negG], W=[E])
                kb.op("act", lambda e, h=h: e.activation(out=Eb[:, h, :], in_=pgs[:, h * 128:(h + 1) * 128], func=AF.Exp,
                                                        bias=nGb[:, h:h + 1]), R=[pgs, nGb], W=[Eb])
            kb.op("act", lambda e: e.activation(out=eGb[:], in_=v3(pgb), func=AF.Exp), R=[pgb], W=[eGb])
            pA = PS()
            pKQ = PS()
            for h in range(NH):
                kb.op("pe", lambda e, h=h: e.matmul(pA[:, h * 128:(h + 1) * 128], lhsT=qkv[:, NH + h, cs],
                                                   rhs=qkv[:, NH + h, cs], start=True, stop=True), R=[qb[NH + h]], W=[pA])
                kb.op("pe", lambda e, h=h: e.matmul(pKQ[:, h * 128:(h + 1) * 128], lhsT=qkv[:, NH + h, cs],
                                                   rhs=qkv[:, h, cs], start=True, stop=True), R=[qb[NH + h], qb[h]], W=[pKQ])
            kb.op("dve", lambda e: e.tensor_tensor(MT[:], v3(pA), Eb[:], op=ALU.mult), R=[pA, Eb], W=[MT])
            kb.op("dve", lambda e: e.tensor_tensor(qkT[:], v3(pKQ), E[:], op=ALU.mult), R=[pKQ, E], W=[qkT])
            kb.op("dve", lambda e: e.tensor_tensor(qdT[:], qkv[:, 0:NH, cs], eGb[:], op=ALU.mult), R=qb[0:NH] + [eGb], W=[qdT])
            for l in range(NLEV):
                pP = PS()
                if l == 0:
                    for h in range(NH):
                        kb.op("pe", lambda e, h=h: e.matmul(pP[:, h * 128:(h + 1) * 128], lhsT=MT[:, h, :], rhs=ident_b[:],
                                                           start=True, stop=True), R=[MT, ident_b], W=[pP])
                    kb.op("dve", lambda e: e.tensor_tensor(Pm[:], v3(pP), lmask[:, 0, :, :], op=ALU.mult), R=[pP, lmask], W=[Pm])
                    pQT = PS()
                    for h in range(NH):
                        kb.op("pe", lambda e, h=h: e.matmul(pQT[:, h * 128:(h + 1) * 128], lhsT=Pm[:, h, :], rhs=ident_b[:],
                                                           start=True, stop=True), R=[Pm, ident_b], W=[pQT])
                    kb.op("dve", lambda e: e.tensor_tensor(T[:], identb4[:], Pm[:], op=ALU.subtract), R=[identb4, Pm], W=[T])
                    kb.op("dve", lambda e: e.tensor_tensor(TT[:], identb4[:], v3(pQT), op=ALU.subtract), R=[identb4, pQT], W=[TT])
                    continue
                for h in range(NH):
                    kb.op("pe", lambda e, h=h: e.matmul(pP[:, h * 128:(h + 1) * 128], lhsT=MT[:, h, :], rhs=T[:, h, :],
                                                       start=True, stop=True), R=[MT, T], W=[pP])
                kb.op("dve", lambda e: e.tensor_tensor(Pm[:], v3(pP), lmask[:, l, :, :], op=ALU.mult),
                      R=[pP, lmask], W=[Pm])
                last = (l == NLEV - 1)
                pQT = PS()
                if not last:
                    pQ = PS()
                for h in range(NH):
                    if not last:
                        kb.op("pe", lambda e, h=h: e.matmul(pQ[:, h * 128:(h + 1) * 128], lhsT=TT[:, h, :], rhs=Pm[:, h, :],
                                                           start=True, stop=True), R=[TT, Pm], W=[pQ])
                    kb.op("pe", lambda e, h=h: e.matmul(pQT[:, h * 128:(h + 1) * 128], lhsT=Pm[:, h, :], rhs=TT[:, h, :],
                                                       start=True, stop=True), R=[Pm, TT], W=[pQT])
                if not last:
                    kb.op("dve", lambda e: e.tensor_tensor(T[:], T[:], v3(pQ), op=ALU.subtract), R=[T, pQ], W=[T])
                kb.op("dve", lambda e: e.tensor_tensor(TT[:], TT[:], v3(pQT), op=ALU.subtract), R=[TT, pQT], W=[TT])
            pKS = PS()
            for h in range(NH):
                kb.op("pe", lambda e, h=h: e.matmul(pKS[:, h * 128:(h + 1) * 128], lhsT=qkv[:, NH + h, cs], rhs=Sbf[:, h, :],
                                                   start=True, stop=True), R=[qb[NH + h], Sbf], W=[pKS])
            for h in range(NH):
                kb.op("dve", lambda e, h=h: e.scalar_tensor_tensor(out=Rb[:, h, :], in0=pKS[:, h * 128:(h + 1) * 128],
                                                                   scalar=negeG[:, h:h + 1], in1=vtok[:, h, :],
                                                                   op0=ALU.mult, op1=ALU.add), R=[pKS, negeG, vtok], W=[Rb])
            pX = PS()
            for h in range(NH):
                kb.op("pe", lambda e, h=h: e.matmul(pX[:, h * 128:(h + 1) * 128], lhsT=TT[:, h, :], rhs=Rb[:, h, :],
                                                   start=True, stop=True), R=[TT, Rb], W=[pX])
            for h in range(NH):
                kb.op("act", lambda e, h=h: e.activation(out=vnew[:, h, :], in_=pX[:, h * 128:(h + 1) * 128], func=AF.Copy,
                                                        scale=beta[:, b4, h:h + 1]), R=[pX, beta], W=[vnew])
            pO = PS()
            pS = PS()
            for h in range(NH):
                kb.op("pe", lambda e, h=h: e.matmul(pO[:, h * 128:(h + 1) * 128], lhsT=Sbf[:, h, :], rhs=qdT[:, h, :],
                                                   start=True, stop=False), R=[Sbf, qdT], W=[pO])
                kb.op("pe", lambda e, h=h: e.matmul(pO[:, h * 128:(h + 1) * 128], lhsT=vnew[:, h, :], rhs=qkT[:, h, :],
                                                   start=False, stop=True), R=[vnew, qkT], W=[pO])
            for h in range(NH):
                kb.op("pe", lambda e, h=h: e.matmul(pS[:, h * 128:(h + 1) * 128], lhsT=kd[:, h, :], rhs=vnew[:, h, :],
                                                   start=True, stop=True), R=[kd, vnew], W=[pS])
            for h in range(NH):
                kb.op("dve", lambda e, h=h: e.scalar_tensor_tensor(out=S32[:, h, :], in0=S32[:, h, :], scalar=gtot[:, h:h + 1],
                                                                   in1=pS[:, h * 128:(h + 1) * 128], op0=ALU.mult, op1=ALU.add),
                      R=[S32, gtot, pS], W=[S32])
            kb.op("act", lambda e: e.activation(out=Sbf[:], in_=S32[:], func=AF.Copy), R=[S32], W=[Sbf])
            kb.op("act", lambda e: e.activation(out=osq[:], in_=v3(pO), func=AF.Square), R=[pO], W=[osq])
            pN = PS()
            kb.op("pe", lambda e: e.matmul(pN[:, :], lhsT=ones_b[:], rhs=osq[:].rearrange("p a b -> p (a b)"),
                                           start=True, stop=True), R=[ones_b, osq], W=[pN])
            kb.op("act", lambda e: e.activation(out=rinv[:], in_=v3(pN), func=AF.Ln, scale=1.0 / 128,
                                                bias=cbias[:, 2:3]), R=[pN, cbias], W=[rinv])
            kb.op("act", lambda e: e.activation(out=rinv[:], in_=rinv[:], func=AF.Exp, scale=-0.5), R=[rinv], W=[rinv])
            kb.op("dve", lambda e: e.tensor_tensor(otmp[:], v3(pO), rinv[:], op=ALU.mult), R=[pO, rinv], W=[otmp])
            of = oTf[sbi % 2]
            kb.op("dve", lambda e: e.scalar_tensor_tensor(out=of[:, :, cs], in0=otmp[:], scalar=ong[:, 0:1],
                                                          in1=gs[:, :, cs], op0=ALU.mult, op1=ALU.mult),
                  R=[otmp, ong, gs], W=[of])
        of = oTf[sbi % 2]
        kb0 = (col0 + tok0) // 128
        prs = []
        for j in range(SBT // 128):
            for dst in o_scr.loc(kb0 + j):
                prs.append((dst, of[:, :, j * 128:(j + 1) * 128]))
        kb.dma_multi("sp", prs, R=[of], W=[o_scr], key=of)
    for h in range(NH):
        kb.dma("sp", ssm_d[h, :, :], S32[:, h, :], R=[S32], W=[ssm_d], key=S32)
    kb.dma("sp", convo_d[:, :, :], pre[:, :, 0:3], R=preb, W=[convo_d], key=pre)


def sample_a(kb, smps, L, PS, psT, row0):
    NS = 16
    W, cw, negA, dtb, ong, cbias = L["W"], L["cw"], L["negA"], L["dtb"], L["ong"], L["cbias"]
    ident_f, ident_b, ones_b, ones_f = L["ident_f"], L["ident_b"], L["ones_b"], L["ones_f"]
    NF = 4 * NH
    eye_d = smps[0]["eye"]
    xs_t = kb.sb([128, 1024], F32, "s_x")
    xsb = kb.sb([128, 1024], BF16, "s_xb")
    junk = kb.sb([128, 1024], BF16, "s_junk")
    rr = kb.sb([128, 4], F32, "s_rr")
    xsT = kb.sb([128, 8, 128], BF16, "s_xsT")
    sct = kb.sb([128, 3, 3 * NH * 128], F32, "s_sct")
    scT = kb.sb([128, 3 * NH, 3, NS], F32, "s_scT")
    crow = kb.sb([128, 3 * NH * 128], F32, "s_crow")
    ab = kb.sb([128, 2 * NH], F32, "s_ab")
    pf = kb.sb([128, NF, NS], F32, "s_pf")
    t_ = kb.sb([128, 3 * NH, NS], F32, "s_t")
    u_ = kb.sb([128, 3 * NH, NS], F32, "s_u")
    c2 = kb.sb([128, 3 * NH, NS], F32, "s_c2")
    gsil = kb.sb([128, NH, NS], F32, "s_gsil")
    sq = kb.sb([128, 2 * NH, NS], BF16, "s_sq")
    rbs = kb.sb([128, 2 * NH, NS], F32, "s_rbs")
    qkn = kb.sb([128, 2 * NH, NS], F32, "s_qkn")
    vf = kb.sb([128, NH, NS], F32, "s_vf")
    eyeb = kb.sb([128, NS, NS], F32, "s_eyeb")
    eyep = kb.sb([128, NS], F32, "s_eyep")
    Kexp = kb.sb([128, NH, NS, NS], BF16, "s_Kexp")
    Qexp = kb.sb([128, NH, NS, NS], BF16, "s_Qexp")
    S0b = [kb.sb([128, NS, 128], BF16, f"s_S0b{i}") for i in range(4)]
    tokb = kb.sb([128, NH, 128], BF16, "s_tokb")
    tok = kb.sb([128, 3 * NH, 128], F32, "s_tok")
    sm = kb.sb([128, 8 * NH], F32, "s_sm")
    KS = kb.sb([128, NH, 128], F32, "s_KS")
    QS = kb.sb([128, NH, 128], F32, "s_QS")
    vn = kb.sb([128, NH, 128], F32, "s_vn")
    ot = kb.sb([128, NH, 128], F32, "s_ot")
    tm = kb.sb([128, NH, 128], F32, "s_tm")
    osT = kb.sb([128, NH, NS], BF16, "s_osT")
    Egx = kb.sb([128, NS, NH], F32, "s_Egx")
    egb = kb.sb([128, NS, NH], F32, "s_egb")
    S0 = [kb.sb([128, NS, 128], F32, f"s_S0{i}") for i in range(4)]
    Vexp = [kb.sb([128, NS, 128], BF16, f"s_Vexp{i}") for i in range(4)]
    Sout = [kb.sb([128, 4, 128], F32, f"s_Sout{i}") for i in range(4)]
    for b_ in (xs_t, sct, tok, sm, vn, ot, eyep, Egx, Vexp[0], Vexp[1], Vexp[2], Vexp[3]):
        kb.op("pool", lambda e, b_=b_: e.memset(b_[:], 0.0), W=[b_])
    kb.dma("sp", eyeb[:], eye_d[:, :, :], W=[eyeb], key=eyeb)
    kb.dma("sp", eyep[0:NS, :], eye_d[0, :, :], W=[eyep], key=eyep)
    for smp in smps:
        _sample_a_group(kb, smp, locals(), L, PS, psT)


def _sample_a_group(kb, smp, A, L, PS, psT):
    NS = 16
    NF = 4 * NH
    W, cw, negA, dtb, ong, cbias = L["W"], L["cw"], L["negA"], L["dtb"], L["ong"], L["cbias"]
    ident_f, ident_b, ones_b, ones_f = L["ident_f"], L["ident_b"], L["ones_b"], L["ones_f"]
    xs_d, sc_d, ss_d, convs_d, ssms_d, os_scr = (smp[k] for k in ("xs", "sc", "ss", "convs", "ssms", "os_scr"))
    (xs_t, xsb, junk, rr, xsT, sct, scT, crow, ab, pf, t_, u_, c2, gsil, sq, rbs, qkn, vf, eyeb, eyep, Kexp, Qexp, tok, sm, KS, QS, vn, ot, tm,
     osT, Egx, egb, S0, Vexp, Sout, S0b, tokb) = (A[k] for k in (
        "xs_t", "xsb", "junk", "rr", "xsT", "sct", "scT", "crow", "ab", "pf", "t_", "u_", "c2", "gsil", "sq", "rbs", "qkn", "vf", "eyeb", "eyep",
        "Kexp", "Qexp", "tok", "sm", "KS", "QS", "vn", "ot", "tm", "osT", "Egx", "egb", "S0", "Vexp", "Sout", "S0b", "tokb"))
    kb.dma("sp", xs_t[0:NS, :], xs_d[:, :], W=[xs_t], key=xs_t)
    kb.dma("sp", sct[0:NS, :, :], sc_d[:, :, :], W=[sct], key=sct)
    kb.dma("sp", convs_d[:, 0:2, :], sc_d[:, 1:3, :], W=[], key=crow)
    kb.op("act", lambda e: e.activation(out=junk[:], in_=xs_t[:], func=AF.Square, accum_out=rr[:, 0:1]), R=[xs_t], W=[junk, rr])
    kb.op("act", lambda e: e.activation(out=rr[:, 1:2], in_=rr[:, 0:1], func=AF.Ln, scale=1.0 / 1024, bias=cbias[:, 2:3]), R=[rr, cbias], W=[rr])
    kb.op("act", lambda e: e.activation(out=rr[:, 2:3], in_=rr[:, 1:2], func=AF.Exp, scale=-0.5), R=[rr], W=[rr])
    kb.op("dve", lambda e: e.tensor_scalar(xsb[:], xs_t[:], rr[:, 2:3], None, op0=ALU.mult), R=[xs_t, rr], W=[xsb])
    pt = psT[0]
    ptB = pt.t[:, :]
    for kc in range(8):
        kb.op("pe", lambda e, kc=kc: e.transpose(out=ptB[:, kc * 128:(kc + 1) * 128], in_=xsb[:, kc * 128:(kc + 1) * 128], identity=ident_b[:]),
              R=[xsb, ident_b], W=[pt])
    kb.op("act", lambda e: e.activation(out=xsT[:], in_=ptB.rearrange("p (k t) -> p k t", k=8), func=AF.Copy), R=[pt], W=[xsT])
    ppf = PS()
    for ft in range(NF):
        for kc in range(8):
            kb.op("pe", lambda e, kc=kc: e.matmul(ppf[:, ft * NS:(ft + 1) * NS], lhsT=W[:, kc, ft * 128:(ft + 1) * 128], rhs=xsT[:, kc, 0:NS],
                                                 start=(kc == 0), stop=(kc == 7)), R=[W, xsT], W=[ppf])
    kb.op("act", lambda e: e.activation(out=pf[:], in_=ppf[:, 0:NF * NS].rearrange("p (a b) -> p a b", a=NF), func=AF.Copy), R=[ppf], W=[pf])
    for j in range(3):
        pc = PS()
        for kc in range(8):
            kb.op("pe", lambda e, kc=kc: e.matmul(pc[:, :], lhsT=xsT[:, kc, :], rhs=W[:, kc, j * 512:(j + 1) * 512], start=(kc == 0), stop=(kc == 7)),
                  R=[xsT, W], W=[pc])
        kb.op("act", lambda e: e.activation(out=crow[:, j * 512:(j + 1) * 512], in_=pc[:, :], func=AF.Copy), R=[pc], W=[crow])
    kb.dma("sp", convs_d[:, 2, :], crow[0:NS, :], R=[crow], W=[], key=crow)
    pab = PS()
    for kc in range(8):
        kb.op("pe", lambda e, kc=kc: e.matmul(pab[:, 0:2 * NH], lhsT=xsT[:, kc, :], rhs=W[:, kc, NF * 128:NF * 128 + 2 * NH], start=(kc == 0), stop=(kc == 7)),
              R=[xsT, W], W=[pab])
    kb.op("dve", lambda e: e.tensor_copy(ab[:], pab[:, 0:2 * NH]), R=[pab], W=[ab])
    g_ = sm[:, 0:NH]; eg_ = sm[:, NH:2 * NH]; be_ = sm[:, 2 * NH:3 * NH]; qk_ = sm[:, 3 * NH:4 * NH]
    tp_ = sm[:, 4 * NH:5 * NH]; ri_ = sm[:, 5 * NH:6 * NH]; tq_ = sm[:, 6 * NH:7 * NH]
    kb.op("dve", lambda e: e.tensor_tensor(tp_, ab[:, 0:NH], dtb[:, :], op=ALU.add), R=[ab, dtb], W=[sm])
    kb.op("act", lambda e: e.activation(out=tp_, in_=tp_, func=AF.Exp), R=[sm], W=[sm])
    kb.op("act", lambda e: e.activation(out=tq_, in_=ab[:, NH:2 * NH], func=AF.Exp, scale=-1.0), R=[ab], W=[sm])
    kb.op("act", lambda e: e.activation(out=tp_, in_=tp_, func=AF.Ln, bias=cbias[:, 3:4]), R=[sm, cbias], W=[sm])
    kb.op("act", lambda e: e.activation(out=tq_, in_=tq_, func=AF.Ln, bias=cbias[:, 3:4]), R=[sm, cbias], W=[sm])
    kb.op("dve", lambda e: e.tensor_tensor(g_, tp_, negA[:, :], op=ALU.mult), R=[sm, negA], W=[sm])
    kb.op("act", lambda e: e.activation(out=eg_, in_=g_, func=AF.Exp), R=[sm], W=[sm])
    kb.op("act", lambda e: e.activation(out=be_, in_=tq_, func=AF.Exp, scale=-1.0), R=[sm], W=[sm])
    psc = [PS(), PS()]
    idx = 0
    for ft in range(3 * NH):
        for tap in range(3):
            bank, off = (0, idx * NS) if idx < 32 else (1, (idx - 32) * NS)
            kb.op("pe", lambda e: e.transpose(out=psc[bank][:, off:off + NS], in_=sct[:, tap, ft * 128:(ft + 1) * 128][:, :], identity=ident_f[:])
                  if False else e.matmul(psc[bank][:, off:off + NS], lhsT=sct[:, tap, ft * 128:(ft + 1) * 128], rhs=ident_f[:, 0:NS], start=True, stop=True),
                  R=[sct, ident_f], W=[psc[bank]])
            idx += 1
    scTf = scT[:].rearrange("p a b c -> p (a b c)")
    kb.op("act", lambda e: e.activation(out=scTf[:, 0:512], in_=psc[0][:, 0:512], func=AF.Copy), R=[psc[0]], W=[scT])
    kb.op("act", lambda e: e.activation(out=scTf[:, 512:576], in_=psc[1][:, 0:64], func=AF.Copy), R=[psc[1]], W=[scT])
    def cwb(tap):
        return cw[:, :, tap:tap + 1].to_broadcast([128, 3 * NH, NS])
    kb.op("dve", lambda e: e.tensor_tensor(t_[:], scT[:, :, 0, :], cwb(0), op=ALU.mult), R=[scT, cw], W=[t_])
    for tap in (1, 2):
        kb.op("dve", lambda e, tap=tap: e.tensor_tensor(u_[:], scT[:, :, tap, :], cwb(tap), op=ALU.mult), R=[scT, cw], W=[u_])
        kb.op("dve", lambda e: e.tensor_tensor(t_[:], t_[:], u_[:], op=ALU.add), R=[t_, u_], W=[t_])
    kb.op("dve", lambda e: e.tensor_tensor(u_[:], pf[:, 0:3 * NH, :], cwb(3), op=ALU.mult), R=[pf, cw], W=[u_])
    kb.op("dve", lambda e: e.tensor_tensor(t_[:], t_[:], u_[:], op=ALU.add), R=[t_, u_], W=[t_])
    kb.op("act", lambda e: e.activation(out=u_[:], in_=t_[:], func=AF.Tanh, scale=0.5), R=[t_], W=[u_])
    kb.op("dve", lambda e: e.scalar_tensor_tensor(out=c2[:], in0=u_[:], scalar=1.0, in1=t_[:], op0=ALU.add, op1=ALU.mult), R=[u_, t_], W=[c2])
    kb.op("act", lambda e: e.activation(out=gsil[:], in_=pf[:, 3 * NH:4 * NH, :], func=AF.Tanh, scale=0.5), R=[pf], W=[gsil])
    kb.op("dve", lambda e: e.scalar_tensor_tensor(out=gsil[:], in0=gsil[:], scalar=1.0, in1=pf[:, 3 * NH:4 * NH, :], op0=ALU.add, op1=ALU.mult),
          R=[gsil, pf], W=[gsil])
    kb.op("act", lambda e: e.activation(out=sq[:], in_=c2[:, 0:2 * NH, :], func=AF.Square), R=[c2], W=[sq])
    pn = PS()
    kb.op("pe", lambda e: e.matmul(pn[:, 0:2 * NH * NS], lhsT=ones_b[:], rhs=sq[:].rearrange("p a b -> p (a b)"), start=True, stop=True),
          R=[ones_b, sq], W=[pn])
    pn3 = pn.t[:, 0:2 * NH * NS].rearrange("p (a b) -> p a b", a=2 * NH)
    kb.op("act", lambda e: e.activation(out=rbs[:, 0:NH, :], in_=pn3[:, 0:NH, :], func=AF.Ln, scale=128.0, bias=cbias[:, 1:2]), R=[pn, cbias], W=[rbs])
    kb.op("act", lambda e: e.activation(out=rbs[:, NH:2 * NH, :], in_=pn3[:, NH:2 * NH, :], func=AF.Ln, scale=1.0, bias=cbias[:, 0:1]), R=[pn, cbias], W=[rbs])
    kb.op("act", lambda e: e.activation(out=rbs[:], in_=rbs[:], func=AF.Exp, scale=-0.5), R=[rbs], W=[rbs])
    kb.op("dve", lambda e: e.tensor_tensor(qkn[:], c2[:, 0:2 * NH, :], rbs[:], op=ALU.mult), R=[c2, rbs], W=[qkn])
    kb.op("dve", lambda e: e.tensor_scalar(vf[:], c2[:, 2 * NH:3 * NH, :], 0.5, None, op0=ALU.mult), R=[c2], W=[vf])
    ptk = [PS(), PS(), PS()]
    for i in range(3 * NH):
        src = qkn[:, i, :] if i < 2 * NH else vf[:, i - 2 * NH, :]
        bank, off = i // 4, (i % 4) * 128
        kb.op("pe", lambda e: e.matmul(ptk[bank][0:NS, off:off + 128], lhsT=src, rhs=ident_f[:], start=True, stop=True),
              R=[qkn, vf, ident_f], W=[ptk[bank]])
    for bank in range(3):
        kb.op("act", lambda e, bank=bank: e.activation(out=tok[0:NS, bank * 4:(bank + 1) * 4, :],
                                                       in_=ptk[bank][0:NS, :].rearrange("p (a b) -> p a b", a=4), func=AF.Copy),
              R=[ptk[bank]], W=[tok])
    q_t = tok[:, 0:NH, :]; k_t = tok[:, NH:2 * NH, :]; v_t = tok[:, 2 * NH:3 * NH, :]
    kb.op("dve", lambda e: e.tensor_tensor(tm[:], q_t, k_t, op=ALU.mult), R=[tok], W=[tm])
    kb.op("dve", lambda e: e.tensor_reduce(out=qk_, in_=tm[:], axis=mybir.AxisListType.X, op=ALU.add), R=[tm], W=[sm])
    for h in range(NH):
        kb.op("dve", lambda e, h=h: e.tensor_tensor(Kexp[:, h, :, :], eyeb[:], qkn[:, NH + h, :].unsqueeze(2).to_broadcast([128, NS, NS]), op=ALU.mult),
              R=[eyeb, qkn], W=[Kexp])
        kb.op("dve", lambda e, h=h: e.tensor_tensor(Qexp[:, h, :, :], eyeb[:], qkn[:, h, :].unsqueeze(2).to_broadcast([128, NS, NS]), op=ALU.mult),
              R=[eyeb, qkn], W=[Qexp])
    kb.op("dve", lambda e: e.tensor_tensor(Egx[:], eyep[:, :].unsqueeze(2).to_broadcast([128, NS, NH]),
                                           eg_.unsqueeze(1).to_broadcast([128, NS, NH]), op=ALU.mult), R=[eyep, sm], W=[Egx])
    peg = PS()
    kb.op("pe", lambda e: e.matmul(peg[:, 0:NS * NH], lhsT=ones_f[:], rhs=Egx[:].rearrange("p a b -> p (a b)"), start=True, stop=True),
          R=[ones_f, Egx], W=[peg])
    kb.op("act", lambda e: e.activation(out=egb[:], in_=peg[:, 0:NS * NH].rearrange("p (a b) -> p a b", a=NS), func=AF.Copy), R=[peg], W=[egb])

    def bcs(ap):
        return ap.unsqueeze(2).to_broadcast([128, NH, 128])
    for h in range(NH):
        s0 = S0[h % 4]
        kb.dma("sp", s0[:], ss_d[:, h, :, :].rearrange("n k v -> k n v"), W=[s0], key=s0)
        s0b = S0b[h % 4]
        kb.op("act", lambda e: e.activation(out=s0b[:], in_=s0[:], func=AF.Copy), R=[s0], W=[s0b])
        if h == 0:
            kb.op("pool", lambda e: e.tensor_copy(tokb[:], tok[:, NH:2 * NH, :]), R=[tok], W=[tokb])
        pks = PS()
        for n in range(NS):
            kb.op("pe", lambda e, n=n: e.matmul(pks[0:NS, 0:128], lhsT=Kexp[:, h, n, :], rhs=s0b[:, n, :], start=(n == 0), stop=(n == NS - 1)),
                  R=[Kexp, s0b], W=[pks])
        for n in range(NS):
            kb.op("pe", lambda e, n=n: e.matmul(pks[0:NS, 128:256], lhsT=Qexp[:, h, n, :], rhs=s0b[:, n, :], start=(n == 0), stop=(n == NS - 1)),
                  R=[Qexp, s0b], W=[pks])
        kb.op("act", lambda e: e.activation(out=KS[0:NS, h, :], in_=pks[0:NS, 0:128], func=AF.Copy), R=[pks], W=[KS])
        kb.op("act", lambda e: e.activation(out=QS[0:NS, h, :], in_=pks[0:NS, 128:256], func=AF.Copy), R=[pks], W=[QS])
        r16 = slice(0, NS)
        kb.op("dve", lambda e: e.scalar_tensor_tensor(out=tm[r16, h, :], in0=KS[r16, h, :], scalar=eg_[r16, h:h + 1], in1=v_t[r16, h, :],
                                                      op0=ALU.mult, op1=ALU.subtract), R=[KS, sm, tok], W=[tm])
        kb.op("dve", lambda e: e.tensor_scalar(vn[r16, h, :], tm[r16, h, :], be_[r16, h:h + 1], -1.0, op0=ALU.mult, op1=ALU.mult),
              R=[tm, sm], W=[vn])
        kb.op("dve", lambda e: e.tensor_scalar(tm[r16, h, :], QS[r16, h, :], eg_[r16, h:h + 1], None, op0=ALU.mult), R=[QS, sm], W=[tm])
        kb.op("dve", lambda e: e.scalar_tensor_tensor(out=ot[r16, h, :], in0=vn[r16, h, :], scalar=qk_[r16, h:h + 1], in1=tm[r16, h, :],
                                                      op0=ALU.mult, op1=ALU.add), R=[vn, sm, tm], W=[ot])
        vx = Vexp[h % 4]
        kb.op("dve", lambda e: e.tensor_tensor(vx[:], vn[:, h, :].unsqueeze(1).to_broadcast([128, NS, 128]),
                                               eyep[:, :].unsqueeze(2).to_broadcast([128, NS, 128]), op=ALU.mult), R=[vn, eyep], W=[vx])
        for n4 in range(NS // 4):
            pss = PS()
            so = Sout[n4 % 4]
            for j in range(4):
                n = n4 * 4 + j
                kb.op("pe", lambda e, n=n, j=j: e.matmul(pss[:, j * 128:(j + 1) * 128], lhsT=tokb[:, h, :], rhs=vx[:, n, :], start=True, stop=True),
                      R=[tokb, vx], W=[pss])
            for j in range(4):
                n = n4 * 4 + j
                kb.op("dve", lambda e, n=n, j=j: e.scalar_tensor_tensor(out=so[:, j, :], in0=s0[:, n, :], scalar=egb[:, n, h:h + 1],
                                                                        in1=pss[:, j * 128:(j + 1) * 128], op0=ALU.mult, op1=ALU.add),
                      R=[s0, egb, pss], W=[so])
            kb.dma("sp", ssms_d[n4 * 4:(n4 + 1) * 4, h, :, :].rearrange("n k v -> k n v"), so[:], R=[so], W=[], key=so)
    kb.op("dve", lambda e: e.tensor_tensor(tm[:], ot[:], ot[:], op=ALU.mult), R=[ot], W=[tm])
    kb.op("dve", lambda e: e.tensor_reduce(out=ri_, in_=tm[:], axis=mybir.AxisListType.X, op=ALU.add), R=[tm], W=[sm])
    kb.op("act", lambda e: e.activation(out=ri_, in_=ri_, func=AF.Ln, scale=1.0 / 128, bias=cbias[:, 2:3]), R=[sm, cbias], W=[sm])
    kb.op("act", lambda e: e.activation(out=ri_, in_=ri_, func=AF.Exp, scale=-0.5), R=[sm], W=[sm])
    kb.op("dve", lambda e: e.tensor_tensor(ot[:], ot[:], bcs(ri_), op=ALU.mult), R=[ot, sm], W=[ot])
    pot = PS()
    for h in range(NH):
        kb.op("pe", lambda e, h=h: e.matmul(pot[:, h * NS:(h + 1) * NS], lhsT=ot[:, h, :], rhs=ident_f[:, 0:NS], start=True, stop=True),
              R=[ot, ident_f], W=[pot])
    kb.op("dve", lambda e: e.scalar_tensor_tensor(out=osT[:], in0=pot[:, 0:NH * NS].rearrange("p (a b) -> p a b", a=NH), scalar=ong[:, 0:1],
                                                  in1=gsil[:], op0=ALU.mult, op1=ALU.mult), R=[pot, ong, gsil], W=[osT])
    kb.dma("sp", os_scr.sloc(smp["grp"])[:, :, 0:NS], osT[:], R=[osT], W=[os_scr], key=osT)

import numpy as np

EPS = 1e-6
NQ = 16
NS = 16


def host_consts_b():
    c = {}
    i = np.arange(128)
    slopes = np.exp2(-8.0 * np.arange(1, NQ + 1, dtype=np.float32) / NQ).astype(np.float32)
    bias = np.zeros((128, 2, NQ, 128), np.float32)
    jj = i[:, None]
    ii = i[None, :]
    for h in range(NQ):
        cur = np.where(ii >= jj, -slopes[h] * (ii - jj), -30000.0)
        prv = np.where(jj >= ii, -slopes[h] * (128 + ii - jj), -30000.0)
        bias[:, 0, h, :] = prv
        bias[:, 1, h, :] = cur
    c["abias"] = bias
    bo = np.zeros((128, 128), np.float32)
    bo[:64, :64] = 1.0
    bo[64:, 64:] = 1.0
    c["bones"] = bo
    eo = np.zeros((128, 2, 128), np.float32)
    eo[:, 0, :64] = 1.0
    eo[:, 1, 64:] = 1.0
    c["eones"] = eo
    bs = np.zeros((NS * 4, 4, 128), np.float32)
    for n in range(NS):
        for g in range(4):
            for hh in range(4):
                bs[n * 4 + g, hh, :] = -slopes[4 * g + hh] * (128 - i)
    c["sbias"] = bs
    c["eye16"] = np.tile(np.eye(16, dtype=np.float32)[None], (128, 1, 1)).copy()
    pm = np.zeros((128, 64), np.float32)
    pm[np.arange(128), np.arange(128) % 64] = 1.0
    c["pairM"] = pm
    return c


def alloc_weights_b(kb):
    Woa = kb.sb([128, 8, 1024], BF16, "Woa")
    Wkv = kb.sb([128, 8, 512], BF16, "Wkv")
    Wb = kb.sb([128, 8, 2048], BF16, "Wb")
    Wob = kb.sb([128, 8, 1024], BF16, "Wob")
    gk = kb.sb([128, 8], F32, "gkv")
    gb = kb.sb([128, 8], F32, "gnb")
    return Woa, Wkv, Wb, Wob, gk, gb


def load_weights_b(kb, W6, woa_d, wkv_d, kvn_d, wb_d, nb_d, wob_d):
    Woa, Wkv, Wb, Wob, gk, gb = W6
    stg = [kb.sb([128, 2048], F32, f"stgb{i}") for i in range(4)]
    kb.dma("sp", gk[:], kvn_d[:, :], W=[gk], key=gk)
    kb.dma("sp", gb[:], nb_d[:, :], W=[gb], key=gb)
    i = 0
    for (dst, src, n, g) in ((Woa, woa_d, 1024, None), (Wkv, wkv_d, 512, gk), (Wb, wb_d, 2048, gb), (Wob, wob_d, 1024, None)):
        sv = src.rearrange("(kc p) n -> p kc n", p=128)
        for kc in range(8):
            s = stg[i % 4]
            q_ = ("sp", "act")[i % 2]
            i += 1
            kb.dma(q_, s[:, 0:n], sv[:, kc, :], W=[s], key=s)
            if g is None:
                kb.op("act" if kc % 2 else "dve",
                      (lambda e: e.activation(out=dst[:, kc, :], in_=s[:, 0:n], func=AF.Copy)) if kc % 2 else
                      (lambda e: e.tensor_copy(dst[:, kc, :], s[:, 0:n])), R=[s], W=[dst])
            else:
                kb.op("act" if kc % 2 else "dve",
                      (lambda e: e.activation(out=dst[:, kc, :], in_=s[:, 0:n], func=AF.Copy, scale=g[:, kc:kc + 1])) if kc % 2 else
                      (lambda e: e.tensor_scalar(dst[:, kc, :], s[:, 0:n], g[:, kc:kc + 1], None, op0=ALU.mult)),
                      R=[s, g], W=[dst])


def phase_b(kb, cb, x_d, o_scr, y_d, kwin_d, vwin_d, kng_d, qng_d, snk_d, Wts, NBLK, psT, psF, smp=None, idx_d=None, ab1_d=None, NBT=None, wsrc=None):
    Woa, Wkv, Wb, Wob = Wts[:4]
    pfi = [0]

    def PS():
        p = psF[pfi[0] % len(psF)]
        pfi[0] += 1
        return p

    def v4(p, a=4):
        return p.t[:, :].rearrange("p (a b) -> p a b", a=a)

    ident_f = kb.sb([128, 128], F32, "b_ident_f")
    ident_b = kb.sb([128, 128], BF16, "b_ident_b")
    bones_f = kb.sb([128, 128], F32, "bones_f")
    bones = kb.sb([128, 128], BF16, "bones")
    eones_f = kb.sb([128, 2, 128], F32, "eones_f")
    eones = kb.sb([128, 2, 128], BF16, "eones")
    kng = kb.sb([128, 64], F32, "kng")
    qng = kb.sb([128, 1], F32, "qng")
    esink = kb.sb([128, 8], F32, "esink")
    cbias = kb.sb([128, 4], F32, "b_cbias")
    cl = kb.sb([128, 1], F32, "b_cload")
    lds = []

    def ld(dst, src):
        lds.append((dst, src))
    ld(ident_f, cb["ident"][:, :]); ld(bones_f, cb["bones"][:, :]); ld(eones_f, cb["eones"][:, :, :])
    ld(kng, kng_d[:, :]); ld(qng, qng_d[:, :]); ld(esink, snk_d[:, :])
    kb.dma_multi("sp", [(d_[:], s_) for d_, s_ in lds], W=[d_ for d_, _ in lds], key=cl)
    kb.op("dve", lambda e: e.tensor_copy(ident_b[:], ident_f[:]), R=[ident_f], W=[ident_b])
    kb.op("dve", lambda e: e.tensor_copy(bones[:], bones_f[:]), R=[bones_f], W=[bones])
    kb.op("dve", lambda e: e.tensor_copy(eones[:], eones_f[:]), R=[eones_f], W=[eones])
    kb.op("act", lambda e: e.activation(out=esink[:], in_=esink[:], func=AF.Exp), R=[esink], W=[esink])
    kb.op("dve", lambda e: e.tensor_scalar(qng[:], qng[:], 0.125, None, op0=ALU.mult), R=[qng], W=[qng])
    for j, v in enumerate([EPS, 1.0]):
        kb.op("pool", lambda e, j=j, v=v: e.memset(cbias[:, j:j + 1], v), W=[cbias])

    idxt = kb.sb([128, (NBLK + 1) * 2], I32, "b_idx")
    kb.dma("sp", idxt[:], idx_d[:, :], W=[idxt], key=idxt)
    hs = kb.sb([128, 1024], BF16, "b_hs")
    junk = kb.sb([128, 1024], BF16, "b_junk")
    rr = kb.sb([128, 4], F32, "b_rr")
    hT = kb.sb([128, 8, 128], BF16, "b_hT")
    def rms_and_T(src, ntok, hT_out):
        kb.op("act", lambda e: e.activation(out=junk[0:ntok, :], in_=src[0:ntok, :], func=AF.Square,
                                            accum_out=rr[0:ntok, 0:1]), R=[src], W=[junk, rr])
        kb.op("act", lambda e: e.activation(out=rr[0:ntok, 1:2], in_=rr[0:ntok, 0:1], func=AF.Ln,
                                            scale=1.0 / 1024, bias=cbias[0:ntok, 0:1]), R=[rr, cbias], W=[rr])
        kb.op("act", lambda e: e.activation(out=rr[0:ntok, 2:3], in_=rr[0:ntok, 1:2], func=AF.Exp, scale=-0.5), R=[rr], W=[rr])
        kb.op("act", lambda e: e.activation(out=hs[0:ntok, :], in_=src[0:ntok, :], func=AF.Copy, scale=rr[0:ntok, 2:3]),
              R=[src, rr], W=[hs])
        pt = psT[0]
        ptB = pt.t[:, :]
        for kc in range(8):
            kb.op("pe", lambda e, kc=kc: e.transpose(out=ptB[:, kc * 128:kc * 128 + ntok], in_=hs[0:ntok, kc * 128:(kc + 1) * 128],
                                                     identity=ident_b[0:ntok, 0:ntok]), R=[hs, ident_b], W=[pt])
        kb.op("act", lambda e: e.activation(out=hT_out[:, :, 0:ntok],
                                            in_=ptB.rearrange("p (k t) -> p k t", k=8)[:, :, 0:ntok], func=AF.Copy),
              R=[pt], W=[hT_out])

    kb.push()
    load_weights_b(kb, Wts, *wsrc)
    if smp is not None:
        sample_b(kb, smp, locals(), PS, psT, rms_and_T)
    kb.pop()
    kb.push()
    abias = kb.sb([128, 2, NQ, 128], F32, "abias")
    kb.dma("sp", abias[:], cb["abias"][:, :, :, :], W=[abias], key=abias)
    abias1 = kb.sb([128, NQ, 128], F32, "abias1")
    kb.dma("sp", abias1[:], ab1_d[:, :, :], W=[abias1], key=abias1)
    xb = [kb.sb([128, 1024], F32, f"b_x{i}") for i in range(2)]
    ob = [kb.sb([128, 8, 128], BF16, f"b_o{i}") for i in range(2)]
    hb_2 = [kb.sb([128, 1024], F32, f"b_hb{i_}") for i_ in range(2)]
    kv_2 = [kb.sb([128, 512], F32, f"b_kv{i_}") for i_ in range(2)]
    kss_2 = [kb.sb([128, 8], F32, f"b_kss{i_}") for i_ in range(2)]
    kn_2 = [kb.sb([128, 4, 64], F32, f"b_kn{i_}") for i_ in range(2)]
    kdup_2 = [kb.sb([128, 4, 2, 64], BF16, f"b_kdup{i_}") for i_ in range(2)]
    kT2 = [kb.sb([128, 4, 128], BF16, f"b_kT2{i}") for i in range(2)]
    vE = [kb.sb([128, 4, 128], BF16, f"b_vE{i}") for i in range(2)]
    vO = [kb.sb([128, 4, 128], BF16, f"b_vO{i}") for i in range(2)]
    sq_2 = [kb.sb([128, 4, 128], BF16, f"b_sq{i_}") for i_ in range(2)]
    rq_2 = [kb.sb([128, 4, 128], F32, f"b_rq{i_}") for i_ in range(2)]
    qT_2 = [kb.sb([128, 2, 8, 128], BF16, f"b_qT{i_}") for i_ in range(2)]
    tg_2 = [kb.sb([128, 4, 128], F32, f"b_tg{i_}") for i_ in range(2)]
    gsl_2 = [kb.sb([128, 8, 128], BF16, f"b_gsl{i_}") for i_ in range(2)]
    ssb = [kb.sb([128, 4, 128], F32, f"b_ss{i}") for i in range(2)]
    PT = [kb.sb([128, 4, 128], BF16, f"b_PT{i}") for i in range(4)]
    den_2 = [kb.sb([128, 4, 128], F32, f"b_den{i_}") for i_ in range(2)]
    o1_2 = [kb.sb([128, 4, 128], F32, f"b_o1{i_}") for i_ in range(2)]
    oTb_2 = [kb.sb([128, 8, 128], BF16, f"b_oTb{i_}") for i_ in range(2)]
    yb = [kb.sb([128, 1024], F32, f"b_y{i}") for i in range(2)]
    for i_ in range(2):
        kb.op("pool", lambda e, i_=i_: e.memset(qT_2[i_][:], 0.0), W=[qT_2[i_]])
    for i in range(2):
        kb.op("pool", lambda e, i=i: e.memset(vE[i][:], 0.0), W=[vE[i]])
        kb.op("pool", lambda e, i=i: e.memset(vO[i][:], 0.0), W=[vO[i]])

    for blk in range(NBLK):
        t0 = blk * 128
        par = blk % 2
        x_ = xb[par]
        o_ = ob[par]
        hb, kv, kss, kn, kdup, sq, rq, qT, tg, gsl, den, o1, oTb = (hb_2[par], kv_2[par], kss_2[par], kn_2[par], kdup_2[par], sq_2[par],
                                                                  rq_2[par], qT_2[par], tg_2[par], gsl_2[par], den_2[par], o1_2[par], oTb_2[par])
        kb.dma("sp", x_[:], x_d[t0:t0 + 128, :], W=[x_], key=x_)
        kb.ind_dma_multi([(o_[:, 4 * hg:4 * hg + 4, :].rearrange("p h t -> p (h t)"), idxt[:, blk * 2 + hg:blk * 2 + hg + 1], o_scr.gsrc(blk)) for hg in range(2)],
                         None, o_scr.gnr(blk), R=[o_scr.gbuf(blk), idxt], W=[o_], key=o_)
        for half in range(2):
            ph = PS()
            for h in range(8):
                kb.op("pe", lambda e, h=h: e.matmul(ph[:, :], lhsT=o_[:, h, :], rhs=Woa[:, h, half * 512:(half + 1) * 512],
                                                   start=(h == 0), stop=(h == 7)), R=[o_, Woa], W=[ph])
            kb.op("dve", lambda e: e.tensor_tensor(hb[:, half * 512:(half + 1) * 512], ph[:, :], x_[:, half * 512:(half + 1) * 512],
                                                   op=ALU.add), R=[ph, x_], W=[hb])
        rms_and_T(hb, 128, hT)
        pk = PS()
        for kc in range(8):
            kb.op("pe", lambda e, kc=kc: e.matmul(pk[:, :], lhsT=hT[:, kc, :], rhs=Wkv[:, kc, :], start=(kc == 0), stop=(kc == 7)),
                  R=[hT, Wkv], W=[pk])
        kb.op("act", lambda e: e.activation(out=kv[:], in_=pk[:, :], func=AF.Copy), R=[pk], W=[kv])
        for g in range(4):
            kb.op("act", lambda e, g=g: e.activation(out=junk[:, 0:64], in_=kv[:, g * 64:(g + 1) * 64], func=AF.Square,
                                                    accum_out=kss[:, g:g + 1]), R=[kv], W=[junk, kss])
        kb.op("act", lambda e: e.activation(out=kss[:, 4:8], in_=kss[:, 0:4], func=AF.Ln, scale=1.0 / 64, bias=cbias[:, 0:1]),
              R=[kss, cbias], W=[kss])
        kb.op("act", lambda e: e.activation(out=kss[:, 4:8], in_=kss[:, 4:8], func=AF.Exp, scale=-0.5), R=[kss], W=[kss])
        kb.op("dve", lambda e: e.tensor_tensor(kn[:], kv[:, 0:256].rearrange("p (g d) -> p g d", g=4),
                                               kss[:, 4:8].unsqueeze(2).to_broadcast([128, 4, 64]), op=ALU.mult), R=[kv, kss], W=[kn])
        kb.op("dve", lambda e: e.tensor_tensor(kn[:], kn[:], kng[:, :].unsqueeze(1).to_broadcast([128, 4, 64]), op=ALU.mult),
              R=[kn, kng], W=[kn])
        kb.op("act", lambda e: e.activation(out=kdup[:, :, 0, :], in_=kn[:], func=AF.Copy), R=[kn], W=[kdup])
        kb.op("act", lambda e: e.activation(out=kdup[:, :, 1, :], in_=kn[:], func=AF.Copy), R=[kn], W=[kdup])
        vv = kv[:, 256:512].rearrange("p (g d) -> p g d", g=4)
        kb.op("act", lambda e: e.activation(out=vE[par][:, :, 0:64], in_=vv, func=AF.Copy), R=[kv], W=[vE[par]])
        kb.op("act", lambda e: e.activation(out=vO[par][:, :, 64:128], in_=vv, func=AF.Copy), R=[kv], W=[vO[par]])
        pt = psT[1]
        ptB = pt.t[:, :]
        for g in range(4):
            kb.op("pe", lambda e, g=g: e.transpose(out=ptB[:, g * 128:(g + 1) * 128], in_=kdup[:, g, :, :].rearrange("p a d -> p (a d)"),
                                                   identity=ident_b[:]), R=[kdup, ident_b], W=[pt])
        kb.op("act", lambda e: e.activation(out=kT2[par][:], in_=ptB[:, 0:512].rearrange("p (g t) -> p g t", g=4), func=AF.Copy),
              R=[pt], W=[kT2[par]])
        if blk == NBLK - 1:
            kb.dma("sp", kwin_d[:, :], kn[:].rearrange("p g d -> p (g d)"), R=[kn], W=[], key=kn)
            kb.dma("sp", vwin_d[:, :], kv[:, 256:512], R=[kv], W=[], key=kv)
        if blk == 0:
            continue
        for grp in range(4):
            pq = PS()
            for cc in range(4):
                col0 = (grp * 4 + cc) * 128
                for kc in range(8):
                    kb.op("pe", lambda e, kc=kc: e.matmul(pq[:, cc * 128:(cc + 1) * 128], lhsT=Wb[:, kc, col0:col0 + 128], rhs=hT[:, kc, :],
                                                         start=(kc == 0), stop=(kc == 7)), R=[Wb, hT], W=[pq])
            if grp < 2:
                kb.op("act", lambda e: e.activation(out=sq[:], in_=v4(pq), func=AF.Square), R=[pq], W=[sq])
                pn = PS()
                kb.op("pe", lambda e: e.matmul(pn[:, :], lhsT=bones[:], rhs=sq[:].rearrange("p a b -> p (a b)"), start=True, stop=True),
                      R=[bones, sq], W=[pn])
                kb.op("act", lambda e: e.activation(out=rq[:], in_=v4(pn), func=AF.Ln, scale=1.0 / 64, bias=cbias[:, 0:1]),
                      R=[pn, cbias], W=[rq])
                kb.op("act", lambda e: e.activation(out=rq[:], in_=rq[:], func=AF.Exp, scale=-0.5), R=[rq], W=[rq])
                for hf in range(2):
                    ps_ = slice(hf * 64, (hf + 1) * 64)
                    kb.op("dve", lambda e: e.scalar_tensor_tensor(out=qT[ps_, hf, grp * 4:(grp + 1) * 4, :], in0=v4(pq)[ps_], scalar=qng[ps_, 0:1],
                                                                  in1=rq[ps_], op0=ALU.mult, op1=ALU.mult), R=[pq, qng, rq], W=[qT])
            else:
                kb.op("act", lambda e: e.activation(out=tg[:], in_=v4(pq), func=AF.Tanh, scale=0.5), R=[pq], W=[tg])
                kb.op("dve", lambda e: e.scalar_tensor_tensor(out=gsl[:, (grp - 2) * 4:(grp - 1) * 4, :], in0=tg[:], scalar=1.0, in1=v4(pq),
                                                              op0=ALU.add, op1=ALU.mult), R=[tg, pq], W=[gsl])
        kbs = [0, 1]
        pO = [psF[0], psF[1]]
        pD = [psF[2], psF[3]]
        sidx = 0
        for g in range(4):
            pts = []
            for ki, kbk in enumerate(kbs):
                kpar = par if kbk == 1 else 1 - par
                psS = psF[4 + sidx % 2]
                sidx += 1
                for hh in range(4):
                    head = 4 * g + hh
                    c, hf = head // 2, head % 2
                    kb.op("pe", lambda e, hh=hh, c=c, hf=hf: e.matmul(psS[:, hh * 128:(hh + 1) * 128],
                                                                     lhsT=kT2[kpar][:, g, :],
                                                                     rhs=qT[:, hf, c, :], start=True, stop=True),
                          R=[kT2[kpar], qT], W=[psS])
                s_ = ssb[ki]
                bsrc = abias1[:, 4 * g:4 * g + 4, :] if (blk == 1 and kbk == 0) else abias[:, kbk, 4 * g:4 * g + 4, :]
                kb.op("dve", lambda e: e.tensor_tensor(s_[:], v4(psS), bsrc, op=ALU.add),
                      R=[psS, abias, abias1], W=[s_])
                p_ = PT[(g % 2) * 2 + ki]
                kb.op("act", lambda e: e.activation(out=p_[:], in_=s_[:], func=AF.Exp), R=[s_], W=[p_])
                pts.append((p_, kpar))
            for hh in range(4):
                head = 4 * g + hh
                c, hf = head // 2, head % 2
                bank, cc = c // 4, c % 4
                first = (hf == 0)
                for ki, (p_, kpar) in enumerate(pts):
                    vsrc = vE[kpar] if hf == 0 else vO[kpar]
                    st = first and ki == 0
                    sp_ = (hf == 1) and ki == len(pts) - 1
                    kb.op("pe", lambda e: e.matmul(pO[bank][:, cc * 128:(cc + 1) * 128], lhsT=vsrc[:, g, :], rhs=p_[:, hh, :],
                                                   start=st, stop=sp_), R=[vsrc, p_], W=[pO[bank]])
                    kb.op("pe", lambda e: e.matmul(pD[bank][:, cc * 128:(cc + 1) * 128], lhsT=eones[:, hf, :], rhs=p_[:, hh, :],
                                                   start=st, stop=sp_), R=[eones, p_], W=[pD[bank]])
        for bank in range(2):
            kb.op("dve", lambda e: e.tensor_tensor(den[:], v4(pD[bank]),
                                                   esink[:, bank * 4:(bank + 1) * 4].unsqueeze(2).to_broadcast([128, 4, 128]), op=ALU.add),
                  R=[pD[bank], esink], W=[den])
            kb.op("dve", lambda e: e.reciprocal(den[:], den[:]), R=[den], W=[den])
            kb.op("dve", lambda e: e.tensor_tensor(o1[:], v4(pO[bank]), den[:], op=ALU.mult), R=[pO[bank], den], W=[o1])
            kb.op("dve", lambda e: e.scalar_tensor_tensor(out=oTb[:, bank * 4:(bank + 1) * 4, :], in0=o1[:], scalar=0.5,
                                                          in1=gsl[:, bank * 4:(bank + 1) * 4, :], op0=ALU.mult, op1=ALU.mult),
                  R=[o1, gsl], W=[oTb])
        y_ = yb[par]
        for half in range(2):
            py = PS()
            for c in range(8):
                kb.op("pe", lambda e, c=c: e.matmul(py[:, :], lhsT=oTb[:, c, :], rhs=Wob[:, c, half * 512:(half + 1) * 512],
                                                   start=(c == 0), stop=(c == 7)), R=[oTb, Wob], W=[py])
            kb.op("dve", lambda e: e.tensor_tensor(y_[:, half * 512:(half + 1) * 512], py[:, :], hb[:, half * 512:(half + 1) * 512], op=ALU.add),
                  R=[py, hb], W=[y_])
        kb.dma("sp", y_d[t0 - 128:t0, :], y_[:], R=[y_], W=[], key=y_)

    kb.pop()


def sample_b(kb, smp, L, PS, psT, rms_and_T):
    NS_ = 16
    Woa, Wkv, Wb, Wob = L["Woa"], L["Wkv"], L["Wb"], L["Wob"]
    cbias, kng, ident_b, hT, rr = L["cbias"], L["kng"], L["ident_b"], L["hT"], L["rr"]
    xs_d, os_scr, ys_d, ck_d, cv_d, kws_d, vws_d = (smp[k] for k in ("xs", "os_scr", "ys", "ck", "cv", "kws", "vws"))
    q_scr, o2_scr, kn_scr, vn_scr = (smp[k] for k in ("q_scr", "o2_scr", "kn_scr", "vn_scr"))
    qgr_d, sb_d, sk_d = smp["qngr"], smp["sbias"], smp["snk64"]
    X = mybir.AxisListType.X
    xs_t = kb.sb([128, 1024], F32, "t_x")
    hs_t = kb.sb([128, 1024], F32, "t_h")
    osT = kb.sb([128, 8, 128], BF16, "t_osT")
    idxt, NBLK = L["idxt"], L["NBLK"]
    kv = kb.sb([128, 512], F32, "t_kv")
    kss = kb.sb([128, 8], F32, "t_kss")
    junk = kb.sb([128, 64], BF16, "t_junk")
    kn = kb.sb([128, 4, 64], F32, "t_kn")
    qg = kb.sb([128, 2048], F32, "t_qg")
    tq = kb.sb([128, 16, 64], F32, "t_tq")
    qss = kb.sb([128, 32], F32, "t_qss")
    qgr = kb.sb([128, 64], F32, "t_qgr")
    gsl = kb.sb([128, 1024], F32, "t_gsl")
    q64 = kb.sb([128, 4, 64], F32, "t_q64")
    kn64 = kb.sb([64, 64], F32, "t_kn64")
    vn64 = kb.sb([64, 64], F32, "t_vn64")
    bufA = kb.sb([128, 64, 64], F32, "t_bufA")
    bufB = kb.sb([128, 64, 64], F32, "t_bufB")
    s64 = kb.sb([128, 4, 64], F32, "t_s64")
    sbias = kb.sb([128, 4, 64], F32, "t_sbias")
    part = kb.sb([128, 4 * 64 + 4], F32, "t_part")
    pairM = kb.sb([128, 64], F32, "t_pairM")
    esk = kb.sb([64, 4], F32, "t_esk")
    sm = kb.sb([64, 16], F32, "t_sm")
    o64 = kb.sb([64, 4, 64], F32, "t_o64")
    t64 = kb.sb([64, 4, 64], F32, "t_t64")
    ot = kb.sb([128, 1024], F32, "t_ot")
    ob = kb.sb([128, 1024], BF16, "t_ob")
    oT = kb.sb([128, 8, 128], BF16, "t_oT")
    ys = kb.sb([128, 1024], F32, "t_ys")
    for b_ in (xs_t, hs_t, ot):
        kb.op("pool", lambda e, b_=b_: e.memset(b_[:], 0.0), W=[b_])
    kb.dma("sp", xs_t[0:NS_, :], xs_d[:, :], W=[xs_t], key=xs_t)
    kb.ind_dma_multi([(osT[:, 4 * hg:4 * hg + 4, :].rearrange("p h t -> p (h t)"), idxt[:, NBLK * 2 + hg:NBLK * 2 + hg + 1], os_scr.gsrc(NBLK)) for hg in range(2)],
                     None, os_scr.gnr(NBLK), R=[os_scr.gbuf(NBLK), idxt], W=[osT], key=osT)
    kb.dma("sp", qgr[:], qgr_d[:, :], W=[qgr], key=qgr)
    kb.op("dve", lambda e: e.tensor_scalar(qgr[:], qgr[:], 0.125, None, op0=ALU.mult), R=[qgr], W=[qgr])
    kb.dma_multi("sp", [(sbias[jh * 64:(jh + 1) * 64, :, :], sb_d[:, :, jh * 64:(jh + 1) * 64]) for jh in range(2)], W=[sbias], key=sbias)
    kb.dma("sp", pairM[:], smp["pairM"][:, :], W=[pairM], key=pairM)
    kb.dma("sp", esk[:], sk_d[:, :], W=[esk], key=esk)
    kb.op("act", lambda e: e.activation(out=esk[:], in_=esk[:], func=AF.Exp), R=[esk], W=[esk])
    kb.dma("sp", kws_d[:, 0:127, :], ck_d[:, 1:128, :], W=[], key=junk)
    kb.dma("sp", vws_d[:, 0:127, :], cv_d[:, 1:128, :], W=[], key=junk)
    for half in range(2):
        ph = PS()
        for h in range(8):
            kb.op("pe", lambda e, h=h: e.matmul(ph[0:NS_, :], lhsT=osT[:, h, 0:NS_], rhs=Woa[:, h, half * 512:(half + 1) * 512],
                                               start=(h == 0), stop=(h == 7)), R=[osT, Woa], W=[ph])
        kb.op("dve", lambda e: e.tensor_tensor(hs_t[0:NS_, half * 512:(half + 1) * 512], ph[0:NS_, :], xs_t[0:NS_, half * 512:(half + 1) * 512],
                                               op=ALU.add), R=[ph, xs_t], W=[hs_t])
    rms_and_T(hs_t, 128, hT)
    pk = PS()
    for kc in range(8):
        kb.op("pe", lambda e, kc=kc: e.matmul(pk[:, :], lhsT=hT[:, kc, :], rhs=Wkv[:, kc, :], start=(kc == 0), stop=(kc == 7)), R=[hT, Wkv], W=[pk])
    kb.op("act", lambda e: e.activation(out=kv[:], in_=pk[:, :], func=AF.Copy), R=[pk], W=[kv])
    for g in range(4):
        kb.op("act", lambda e, g=g: e.activation(out=junk[:, 0:64], in_=kv[:, g * 64:(g + 1) * 64], func=AF.Square, accum_out=kss[:, g:g + 1]),
              R=[kv], W=[junk, kss])
    kb.op("act", lambda e: e.activation(out=kss[:, 4:8], in_=kss[:, 0:4], func=AF.Ln, scale=1.0 / 64, bias=cbias[:, 0:1]), R=[kss, cbias], W=[kss])
    kb.op("act", lambda e: e.activation(out=kss[:, 4:8], in_=kss[:, 4:8], func=AF.Exp, scale=-0.5), R=[kss], W=[kss])
    kb.op("dve", lambda e: e.tensor_tensor(kn[:], kv[:, 0:256].rearrange("p (g d) -> p g d", g=4),
                                           kss[:, 4:8].unsqueeze(2).to_broadcast([128, 4, 64]), op=ALU.mult), R=[kv, kss], W=[kn])
    kb.op("dve", lambda e: e.tensor_tensor(kn[:], kn[:], kng[:, :].unsqueeze(1).to_broadcast([128, 4, 64]), op=ALU.mult), R=[kn, kng], W=[kn])
    kb.dma("sp", kws_d[:, 127, :], kn[0:NS_].rearrange("p g d -> p (g d)"), R=[kn], W=[], key=kn)
    kb.dma("sp", vws_d[:, 127, :], kv[0:NS_, 256:512], R=[kv], W=[], key=kv)
    kb.dma("sp", kn_scr.t[:, :], kn[0:NS_].rearrange("p g d -> p (g d)"), R=[kn], W=[kn_scr], key=kn)
    kb.dma("sp", vn_scr.t[:, :], kv[0:NS_, 256:512], R=[kv], W=[vn_scr], key=kv)
    for j in range(4):
        pq = PS()
        for kc in range(8):
            kb.op("pe", lambda e, kc=kc: e.matmul(pq[:, :], lhsT=hT[:, kc, :], rhs=Wb[:, kc, j * 512:(j + 1) * 512], start=(kc == 0), stop=(kc == 7)),
                  R=[hT, Wb], W=[pq])
        kb.op("act", lambda e: e.activation(out=qg[:, j * 512:(j + 1) * 512], in_=pq[:, :], func=AF.Copy), R=[pq], W=[qg])
    q3 = qg[:, 0:1024].rearrange("p (h d) -> p h d", h=16)
    kb.op("dve", lambda e: e.tensor_tensor(tq[:], q3, q3, op=ALU.mult), R=[qg], W=[tq])
    kb.op("dve", lambda e: e.tensor_reduce(out=qss[:, 0:16], in_=tq[:], axis=X, op=ALU.add), R=[tq], W=[qss])
    kb.op("act", lambda e: e.activation(out=qss[:, 16:32], in_=qss[:, 0:16], func=AF.Ln, scale=1.0 / 64, bias=cbias[:, 0:1]), R=[qss, cbias], W=[qss])
    kb.op("act", lambda e: e.activation(out=qss[:, 16:32], in_=qss[:, 16:32], func=AF.Exp, scale=-0.5), R=[qss], W=[qss])
    kb.op("dve", lambda e: e.tensor_tensor(tq[:], q3, qss[:, 16:32].unsqueeze(2).to_broadcast([128, 16, 64]), op=ALU.mult), R=[qg, qss], W=[tq])
    kb.op("dve", lambda e: e.tensor_tensor(tq[:], tq[:], qgr[:, :].unsqueeze(1).to_broadcast([128, 16, 64]), op=ALU.mult), R=[tq, qgr], W=[tq])
    kb.dma("sp", q_scr.t[:, :], tq[0:NS_].rearrange("p h d -> p (h d)"), R=[tq], W=[q_scr], key=tq)
    kb.op("act", lambda e: e.activation(out=gsl[:], in_=qg[:, 1024:2048], func=AF.Tanh, scale=0.5), R=[qg], W=[gsl])
    kb.op("dve", lambda e: e.scalar_tensor_tensor(out=gsl[:], in0=gsl[:], scalar=1.0, in1=qg[:, 1024:2048], op0=ALU.add, op1=ALU.mult),
          R=[gsl, qg], W=[gsl])
    kb.dma_multi("sp", [(q64[jh * 64:(jh + 1) * 64], q_scr.t[:, :].rearrange("n (g hh d) -> (n g) hh d", g=4, hh=4)) for jh in range(2)],
                 R=[q_scr], W=[q64], key=q64)
    kb.dma("sp", kn64[:], kn_scr.t[:, :].rearrange("n (g d) -> (n g) d", g=4), R=[kn_scr], W=[kn64], key=kn64)
    kb.dma("sp", vn64[:], vn_scr.t[:, :].rearrange("n (g d) -> (n g) d", g=4), R=[vn_scr], W=[vn64], key=vn64)
    kb.dma_multi("sp", [(bufA[jh * 64 + 4 * n:jh * 64 + 4 * n + 4, :, :], ck_d[n, jh * 64:(jh + 1) * 64, :].rearrange("j (g d) -> g j d", g=4))
                        for n in range(NS_) for jh in range(2)], W=[bufA], key=bufA)
    for hh in range(4):
        kb.op("dve", lambda e, hh=hh: e.tensor_tensor(bufB[:], bufA[:], q64[:, hh, :].unsqueeze(1).to_broadcast([128, 64, 64]), op=ALU.mult),
              R=[bufA, q64], W=[bufB])
        kb.op("dve", lambda e, hh=hh: e.tensor_reduce(out=s64[:, hh, :], in_=bufB[:], axis=X, op=ALU.add), R=[bufB], W=[s64])
    kb.op("dve", lambda e: e.tensor_tensor(s64[:], s64[:], sbias[:], op=ALU.add), R=[s64, sbias], W=[s64])
    kb.op("act", lambda e: e.activation(out=s64[:], in_=s64[:], func=AF.Exp), R=[s64], W=[s64])
    kb.op("dve", lambda e: e.tensor_reduce(out=part[:, 256:260], in_=s64[:], axis=X, op=ALU.add), R=[s64], W=[part])
    kb.op("dve", lambda e: e.tensor_tensor(t64[:], q64[0:64], kn64[:, :].unsqueeze(1).to_broadcast([64, 4, 64]), op=ALU.mult), R=[q64, kn64], W=[t64])
    kb.op("dve", lambda e: e.tensor_reduce(out=sm[:, 4:8], in_=t64[:], axis=X, op=ALU.add), R=[t64], W=[sm])
    kb.op("act", lambda e: e.activation(out=sm[:, 4:8], in_=sm[:, 4:8], func=AF.Exp), R=[sm], W=[sm])
    kb.dma_multi("sp", [(bufB[jh * 64 + 4 * n:jh * 64 + 4 * n + 4, :, :], cv_d[n, jh * 64:(jh + 1) * 64, :].rearrange("j (g d) -> g j d", g=4))
                        for n in range(NS_) for jh in range(2)], W=[bufB], key=bufB)
    for hh in range(4):
        kb.op("dve", lambda e, hh=hh: e.tensor_tensor(bufA[:].rearrange("p j d -> p d j"), bufB[:].rearrange("p j d -> p d j"),
                                                      s64[:, hh, :].unsqueeze(1).to_broadcast([128, 64, 64]), op=ALU.mult),
              R=[bufB, s64], W=[bufA])
        kb.op("dve", lambda e, hh=hh: e.tensor_reduce(out=part[:, hh * 64:(hh + 1) * 64], in_=bufA[:].rearrange("p j d -> p d j"), axis=X, op=ALU.add),
              R=[bufA], W=[part])
    pcm = PS()
    kb.op("pe", lambda e: e.matmul(pcm[0:64, 0:260], lhsT=pairM[:], rhs=part[:], start=True, stop=True), R=[pairM, part], W=[pcm])
    kb.op("act", lambda e: e.activation(out=o64[:], in_=pcm[0:64, 0:256].rearrange("p (a b) -> p a b", a=4), func=AF.Copy), R=[pcm], W=[o64])
    kb.op("act", lambda e: e.activation(out=sm[:, 0:4], in_=pcm[0:64, 256:260], func=AF.Copy), R=[pcm], W=[sm])
    kb.op("dve", lambda e: e.tensor_tensor(sm[:, 8:12], sm[:, 0:4], sm[:, 4:8], op=ALU.add), R=[sm], W=[sm])
    kb.op("dve", lambda e: e.tensor_tensor(sm[:, 8:12], sm[:, 8:12], esk[:], op=ALU.add), R=[sm, esk], W=[sm])
    kb.op("dve", lambda e: e.reciprocal(sm[:, 8:12], sm[:, 8:12]), R=[sm], W=[sm])
    kb.op("dve", lambda e: e.tensor_tensor(t64[:], vn64[:, :].unsqueeze(1).to_broadcast([64, 4, 64]),
                                           sm[:, 4:8].unsqueeze(2).to_broadcast([64, 4, 64]), op=ALU.mult), R=[vn64, sm], W=[t64])
    kb.op("dve", lambda e: e.tensor_tensor(o64[:], o64[:], t64[:], op=ALU.add), R=[o64, t64], W=[o64])
    kb.op("dve", lambda e: e.tensor_tensor(o64[:], o64[:], sm[:, 8:12].unsqueeze(2).to_broadcast([64, 4, 64]), op=ALU.mult), R=[o64, sm], W=[o64])
    kb.dma("sp", o2_scr.t[:, :].rearrange("n (g hh d) -> (n g) hh d", g=4, hh=4), o64[:], R=[o64], W=[o2_scr], key=o64)
    kb.dma("sp", ot[0:NS_, :], o2_scr.t[:, :], R=[o2_scr], W=[ot], key=ot)
    kb.op("dve", lambda e: e.scalar_tensor_tensor(out=ob[:], in0=ot[:], scalar=0.5, in1=gsl[:], op0=ALU.mult, op1=ALU.mult), R=[ot, gsl], W=[ob])
    pt = psT[1]
    ptB = pt.t[:, :]
    for c in range(8):
        kb.op("pe", lambda e, c=c: e.transpose(out=ptB[:, c * 128:(c + 1) * 128], in_=ob[:, c * 128:(c + 1) * 128], identity=ident_b[:]),
              R=[ob, ident_b], W=[pt])
    kb.op("act", lambda e: e.activation(out=oT[:], in_=ptB.rearrange("p (k t) -> p k t", k=8), func=AF.Copy), R=[pt], W=[oT])
    for half in range(2):
        py = PS()
        for c in range(8):
            kb.op("pe", lambda e, c=c: e.matmul(py[:, :], lhsT=oT[:, c, :], rhs=Wob[:, c, half * 512:(half + 1) * 512], start=(c == 0), stop=(c == 7)),
                  R=[oT, Wob], W=[py])
        kb.op("dve", lambda e: e.tensor_tensor(ys[:, half * 512:(half + 1) * 512], py[:, :], hs_t[:, half * 512:(half + 1) * 512], op=ALU.add),
              R=[py, hs_t], W=[ys])
    kb.dma("sp", ys_d[:, :], ys[0:NS_, :], R=[ys], W=[], key=ys)

from concourse.bass_utils import run_bass_kernel_spmd

NHG = 4
T_FULL = 4096
GROUPS = [[0, 4], [1, 5], [2, 6], [3, 7]]


KCH = 3
GATHER_BARRIER = False


class Exchange(Buf):
    def __init__(self, kb, T):
        Buf.__init__(self, None, "xch")
        self.kb = kb
        self.HB = T // 2 // 128
        self.NBLK = self.HB + 1
        nslot = self.NBLK + 1
        self.nch = (nslot + KCH - 1) // KCH
        self.kc = [2 * min(KCH, nslot - c * KCH) for c in range(self.nch)]
        self.i = [kb.dram(f"xi{c}", [128 * self.kc[c], 512], BF16, "Internal") for c in range(self.nch)]
        self.g = [kb.dram(f"xg{c}", [2 * 128 * self.kc[c], 512], BF16, "Internal") for c in range(self.nch)]
        self.iv = [self.i[c].t.rearrange("(p l) (h t) -> p l h t", l=self.kc[c], t=128) for c in range(self.nch)]

    def _slot(self, k, s):
        c = k // KCH
        return c, (k - c * KCH) * 2 + s

    def loc(self, colblk):
        out = []
        for s in range(2):
            k = colblk - s * self.HB
            if 0 <= k < self.NBLK:
                c, l = self._slot(k, s)
                out.append(self.iv[c][:, l, :, :])
        return out

    def sloc(self, grp):
        c, l = self._slot(self.NBLK, grp)
        return self.iv[c][:, l, :, :]

    def gsrc(self, k):
        return self.g[k // KCH].t[:, :]

    def gbuf(self, k):
        return self.g[k // KCH]

    def gnr(self, k):
        return 2 * 128 * self.kc[k // KCH]

    def row(self, k, s, hg, p):
        c, l = self._slot(k, s)
        return hg * 128 * self.kc[c] + p * self.kc[c] + l

    def gather(self):
        kb = self.kb
        for c in range(self.nch):
            key = f"cc{c}"
            kb.semh[key] = kb.es.enter_context(kb.nc.semaphore(key))
            kb.cnt[key] = 0
            kb.dma_keys.append(key)
            kb.flush()
            kb._wait("pool", kb._deps([self], []))
            ins = kb.eng["pool"].collective_compute("AllGather", ALU.bypass, replica_groups=GROUPS,
                                                    ins=[self.i[c].t], outs=[self.g[c].t])
            kb.cnt[key] += 1
            ins.then_inc(kb.semh[key], 1)
            kb.nins += 1
            self.g[c].w = (key, 1)
            self.g[c].r = []
        if GATHER_BARRIER:
            kb.barrier()


def _prep_a(inp, hg):
    heads = [hg * NHG + i for i in range(NHG)]
    w = inp["w_in_a"][0]
    cols = []
    for base in (0, 1024, 2048, 3072):
        for h in heads:
            cols.append(np.arange(base + h * 128, base + (h + 1) * 128))
    cols.append(np.array([4096 + h for h in heads]))
    cols.append(np.array([4104 + h for h in heads]))
    cols = np.concatenate(cols)
    d = {}
    d["wa"] = np.ascontiguousarray(w[:, cols])
    cwf = inp["conv_w_a"][0]
    cw = np.zeros((128, 3 * NHG, 4), np.float32)
    for g, base in enumerate((0, 1024, 2048)):
        for i, h in enumerate(heads):
            cw[:, g * NHG + i, :] = cwf[:, base + h * 128: base + (h + 1) * 128].T
    d["cw"] = cw
    d["alog"] = np.ascontiguousarray(np.tile(inp["a_log"][0][heads][None, :], (128, 1)).astype(np.float32))
    d["dtb"] = np.ascontiguousarray(np.tile(inp["dt_bias"][0][heads][None, :], (128, 1)).astype(np.float32))
    return d


def build(T=T_FULL):
    kb = KB()
    NSB = T // 512
    HALF = T // 2
    NBLK = HALF // 128 + 1
    NBT = 1 + T // 128 + 2
    I = lambda n, s: kb.dram(n, s, F32, "ExternalInput")
    O = lambda n, s: kb.dram(n, s, F32, "ExternalOutput")
    x_d = I("x", [T, 1024])
    xB_d = I("xB", [HALF + 128, 1024])
    na_d = I("na", [128, 8])
    ong_d = I("ong", [128, 1])
    wa_d = I("wa", [1024, 16 * 128 + 8]); cw_d = I("cw", [128, 12, 4]); alog_d = I("alog", [128, 4]); dtb_d = I("dtb", [128, 4])
    hc = host_consts()
    hcb = host_consts_b()
    cst = {k: I("c_" + k, list(v.shape)) for k, v in hc.items()}
    cstb = {k: I("cb_" + k, list(v.shape)) for k, v in hcb.items()}
    woa_d = I("woa", [1024, 1024]); wkv_d = I("wkv", [1024, 512]); kvn_d = I("kvn", [128, 8])
    wb_d = I("wb", [1024, 2048]); nb_d = I("nb", [128, 8]); wob_d = I("wob", [1024, 1024])
    kng_d = I("kng", [128, 64]); qng_d = I("qng", [128, 1]); snk_d = I("snk", [128, 8])
    idx_d = kb.dram("idxtab", [128, (NBLK + 1) * 2], I32, "ExternalInput")
    ab1_d = I("abias1", [128, 16, 128])
    y_d = O("y", [HALF, 1024])
    ssm_d = O("ssm", [4, 128, 128])
    convo_d = O("convo", [128, 12, 3])
    kwin_d = O("kwin", [128, 256]); vwin_d = O("vwin", [128, 256])
    xch = Exchange(kb, T)
    xs32_d = I("xs32", [32, 1024]); sc_d = I("sc", [32, 3, 1536]); ss_d = I("ss", [32, 4, 128, 128])
    xs_d = I("xs", [16, 1024])
    ck_d = I("ck", [16, 128, 256]); cv_d = I("cv", [16, 128, 256])
    qngr_d = I("qngr", [128, 64]); snk64_d = I("snk64", [64, 4])
    convs_d = O("convs", [32, 3, 1536]); ssms_d = O("ssms", [32, 4, 128, 128])
    kws_d = O("kws", [16, 128, 256]); vws_d = O("vws", [16, 128, 256]); ys_d = O("ys", [16, 1024])
    q_scr = kb.dram("q_scr", [16, 1024], F32, "Internal"); o2_scr = kb.dram("o2_scr", [16, 1024], F32, "Internal")
    kn_scr = kb.dram("kn_scr", [16, 256], F32, "Internal"); vn_scr = kb.dram("vn_scr", [16, 256], F32, "Internal")
    psT = [kb.ps([128, 1024], BF16, f"psT{i}") for i in range(2)]
    psF = [kb.ps([128, 512], F32, f"psF{i}") for i in range(6)]
    cst_t = {k: v.t for k, v in cst.items()}
    smps = []
    for grp in range(2):
        r = slice(16 * grp, 16 * grp + 16)
        smps.append(dict(xs=xs32_d.t[r, :], sc=sc_d.t[r, :, :], ss=ss_d.t[r, :, :, :], convs=convs_d.t[r, :, :], ssms=ssms_d.t[r, :, :, :],
                         os_scr=xch, eye=cstb["eye16"].t, grp=grp))
    phase_a(kb, x_d.t, wa_d.t, na_d.t, cw_d.t, alog_d.t, dtb_d.t, ong_d.t, cst_t, xch, ssm_d, convo_d, NSB, psT, psF,
            row0=0, smp=smps, col0=128)
    kb.new_scope()
    xch.gather()
    Wts = alloc_weights_b(kb)
    cb_t = {k: v.t for k, v in cstb.items()}
    cb_t["ident"] = cst["ident"].t
    phase_b(kb, cb_t, xB_d.t, xch, y_d.t, kwin_d.t, vwin_d.t, kng_d.t, qng_d.t, snk_d.t, Wts, NBLK, psT, psF,
            smp=dict(xs=xs_d.t, os_scr=xch, ys=ys_d.t, ck=ck_d.t, cv=cv_d.t, kws=kws_d.t, vws=vws_d.t, q_scr=q_scr, o2_scr=o2_scr,
                     kn_scr=kn_scr, vn_scr=vn_scr, qngr=qngr_d.t, sbias=cstb["sbias"].t, snk64=snk64_d.t, pairM=cstb["pairM"].t),
            idx_d=idx_d.t, ab1_d=ab1_d.t, NBT=NBT, wsrc=(woa_d.t, wkv_d.t, kvn_d.t, wb_d.t, nb_d.t, wob_d.t))
    kb.finish()
    return kb


def make_inputs(inp, T=T_FULL):
    HALF = T // 2
    NBLK = HALF // 128 + 1
    NBT = 1 + T // 128 + 2
    hc = host_consts()
    hcb = host_consts_b()
    shared = {}
    shared["na"] = np.ascontiguousarray(inp["norm_a"][0].reshape(8, 128).T)
    shared["ong"] = np.ascontiguousarray(inp["o_norm_a"][0].reshape(128, 1))
    for k, v in hc.items():
        shared["c_" + k] = v
    for k, v in hcb.items():
        shared["cb_" + k] = v
    shared["woa"] = np.ascontiguousarray(inp["w_out_a"][0])
    shared["wkv"] = np.ascontiguousarray(inp["w_kv"])
    shared["kvn"] = np.ascontiguousarray(inp["kv_norm"].reshape(8, 128).T)
    shared["wb"] = np.ascontiguousarray(inp["w_in_b"][0])
    shared["nb"] = np.ascontiguousarray(inp["norm_b"][0].reshape(8, 128).T)
    shared["wob"] = np.ascontiguousarray(inp["w_out_b"][0])
    shared["kng"] = np.ascontiguousarray(np.tile(inp["k_norm"][None, :], (128, 1)).astype(np.float32))
    shared["qng"] = np.ascontiguousarray(np.tile(inp["q_norm"][0], 2).reshape(128, 1).astype(np.float32))
    sk = inp["sinks"][0]
    snk = np.zeros((128, 8), np.float32)
    for c in range(8):
        snk[:64, c] = sk[2 * c]
        snk[64:, c] = sk[2 * c + 1]
    shared["snk"] = snk
    shared["qngr"] = np.ascontiguousarray(np.tile(inp["q_norm"][0][None, :], (128, 1)).astype(np.float32))
    s64 = np.zeros((64, 4), np.float32)
    for n in range(16):
        for g in range(4):
            s64[n * 4 + g, :] = sk[4 * g:4 * g + 4]
    shared["snk64"] = s64
    chan = []
    for hg in range(2):
        cols = []
        for base in (0, 1024, 2048):
            for i in range(NHG):
                h = hg * NHG + i
                cols.append(np.arange(base + h * 128, base + (h + 1) * 128))
        chan.append(np.concatenate(cols))
    pa = [_prep_a(inp, hg) for hg in range(2)]
    p = np.arange(128)
    maps = []
    for c in range(8):
        b, s = c % 4, c // 4
        m = dict(shared)
        m.update(pa[s])
        m["x"] = np.ascontiguousarray(inp["x_prompt"][b, :T])
        xB = np.zeros((HALF + 128, 1024), np.float32)
        lo = s * HALF - 128
        if lo < 0:
            xB[128:] = inp["x_prompt"][b, 0:HALF]
        else:
            xB[:] = inp["x_prompt"][b, lo:lo + HALF + 128]
        m["xB"] = xB
        idx = np.zeros((128, (NBLK + 1) * 2), np.int32)
        nslot = NBLK + 1
        for kk in range(nslot):
            cch = kk // KCH
            kc = 2 * min(KCH, nslot - cch * KCH)
            l = (kk - cch * KCH) * 2 + s
            for hg in range(2):
                idx[:, kk * 2 + hg] = hg * 128 * kc + p * kc + l
        m["idxtab"] = idx
        m["abias1"] = np.ascontiguousarray(hcb["abias"][:, 0]) if s == 1 else np.full((128, 16, 128), -30000.0, np.float32)
        n0 = 32 * b
        m["xs32"] = np.ascontiguousarray(inp["x_sample"][n0:n0 + 32, 0, :])
        m["sc"] = np.ascontiguousarray(inp["state_conv"][0, n0:n0 + 32][:, :, chan[s]])
        m["ss"] = np.ascontiguousarray(inp["state_ssm"][0, n0:n0 + 32, s * NHG:(s + 1) * NHG])
        n1 = n0 + 16 * s
        m["xs"] = np.ascontiguousarray(inp["x_sample"][n1:n1 + 16, 0, :])
        m["ck"] = np.ascontiguousarray(inp["cache_k_win"][n1:n1 + 16].reshape(16, 128, 256))
        m["cv"] = np.ascontiguousarray(inp["cache_v_win"][n1:n1 + 16].reshape(16, 128, 256))
        maps.append(m)
    return maps


def assemble(R, T=T_FULL):
    B = 4
    HALF = T // 2
    y_p = np.zeros((B, T, 1024), np.float32)
    conv_p = np.zeros((1, B, 3, 3072), np.float32)
    ssm_p = np.zeros((1, B, 8, 128, 128), np.float32)
    kw_p = np.zeros((B, 128, 4, 64), np.float32)
    vw_p = np.zeros((B, 128, 4, 64), np.float32)
    y_s = np.zeros((128, 1, 1024), np.float32)
    conv_s = np.zeros((1, 128, 3, 3072), np.float32)
    ssm_s = np.zeros((1, 128, 8, 128, 128), np.float32)
    kw_s = np.zeros((128, 128, 4, 64), np.float32)
    vw_s = np.zeros((128, 128, 4, 64), np.float32)
    for c in range(8):
        b, s = c % 4, c // 4
        r = R[c]
        y_p[b, s * HALF:(s + 1) * HALF] = r["y"]
        ssm_p[0, b, s * NHG:(s + 1) * NHG] = r["ssm"]
        n0 = 32 * b
        ssm_s[0, n0:n0 + 32, s * NHG:(s + 1) * NHG] = r["ssms"]
        for g, base in enumerate((0, 1024, 2048)):
            for i in range(NHG):
                h = s * NHG + i
                conv_p[0, b, :, base + h * 128: base + (h + 1) * 128] = r["convo"][:, g * NHG + i, :].T
                conv_s[0, n0:n0 + 32, :, base + h * 128: base + (h + 1) * 128] = r["convs"][:, :, (g * NHG + i) * 128:(g * NHG + i + 1) * 128]
        if s == 1:
            kw_p[b] = r["kwin"].reshape(128, 4, 64)
            vw_p[b] = r["vwin"].reshape(128, 4, 64)
        n1 = n0 + 16 * s
        y_s[n1:n1 + 16, 0] = r["ys"]
        kw_s[n1:n1 + 16] = r["kws"].reshape(16, 128, 4, 64)
        vw_s[n1:n1 + 16] = r["vws"].reshape(16, 128, 4, 64)
    return (y_p, y_s, conv_p, ssm_p, kw_p, vw_p, conv_s, ssm_s, kw_s, vw_s)


_CACHE = {}


def kernel(**inp):
    inp = {k: np.asarray(v) for k, v in inp.items()}
    if "kb" not in _CACHE:
        _CACHE["kb"] = build()
    kb = _CACHE["kb"]
    maps = make_inputs(inp)
    res = run_bass_kernel_spmd(kb.nc, maps, core_ids=list(range(8)))
    return assemble(res.results)
```
